# Optimizing a Trainium2 kernel written in Bass

```python
import jax, jax.numpy as jnp
from jax import lax
import numpy as np

D_MODEL = 1024
BATCH = 2
SEQ = 16384
DEPTH = 2
DEC_BATCH = 8
DEC_SEQ = 32
PAST_LEN = 1024

CHUNK = 64
N_MIXERS = 2
N_POOL_LAYERS = (DEPTH + 1) // 2
N_SSD_LAYERS = DEPTH // 2
ALPHA = (2 * DEPTH) ** 0.25
BETA = (8 * DEPTH) ** -0.25
LN_EPS = 1e-5
POOL_WINDOWS = (2, 4, 8, 16)
N_POOL_GROUPS = len(POOL_WINDOWS)
POOL_GROUP = D_MODEL // N_POOL_GROUPS
POOL_HIST = max(POOL_WINDOWS) - 1
SSM_EXPAND = 2
D_INNER = SSM_EXPAND * D_MODEL
HEAD_DIM = 64
N_SSM_HEADS = D_INNER // HEAD_DIM
N_SSM_GROUPS = 8
HEADS_PER_GROUP = N_SSM_HEADS // N_SSM_GROUPS
D_STATE = 128
SSM_CONV = 4
CONV_DIM = D_INNER + 2 * N_SSM_GROUPS * D_STATE
IN_PROJ_DIM = D_INNER + CONV_DIM + N_SSM_HEADS
SSD_BLOCK = CHUNK
RMS_EPS = 1e-5
D_FF = 2816
FFN_CONV = 3
PLE_DIM = 256

kernel_name = "pool_ssd_hybrid_stream_step"


def layer_norm(x, g, b):
    xf = x.astype(jnp.float32)
    mu = jnp.mean(xf, axis=-1, keepdims=True)
    xc = xf - mu
    var = jnp.mean(xc * xc, axis=-1, keepdims=True)
    return (xc * lax.rsqrt(var + LN_EPS) * g.astype(jnp.float32) + b.astype(jnp.float32)).astype(x.dtype)


def causal_dwconv(xh, w, b):
    c = xh.shape[-1]
    y = lax.conv_general_dilated(xh, w[:, None, :].astype(xh.dtype), window_strides=(1,), padding='VALID',
                                 dimension_numbers=('NWC', 'WIO', 'NWC'), feature_group_count=c)
    return y + b.astype(xh.dtype)


def pool_mixer(x, hist, pos0, w, scale):
    bsz, L, _ = x.shape
    xh = jnp.concatenate([hist.astype(x.dtype), x], axis=1)
    cs = jnp.pad(jnp.cumsum(xh.astype(jnp.float32), axis=1), ((0, 0), (1, 0), (0, 0)))
    end = cs[:, POOL_HIST + 1:POOL_HIST + 1 + L]
    pos = pos0 + jnp.arange(L)
    outs = []
    for gi, wsz in enumerate(POOL_WINDOWS):
        sl = slice(gi * POOL_GROUP, (gi + 1) * POOL_GROUP)
        s = end[..., sl] - cs[:, POOL_HIST + 1 - wsz:POOL_HIST + 1 - wsz + L, sl]
        cnt = jnp.minimum(wsz, pos + 1).astype(jnp.float32)[None, :, None]
        outs.append(s / cnt)
    pooled = (jnp.concatenate(outs, axis=-1) - x.astype(jnp.float32)).astype(x.dtype)
    g = pooled.reshape(bsz, L, N_POOL_GROUPS, POOL_GROUP)
    y = jnp.einsum('blgc,gcd->blgd', g, w).reshape(bsz, L, D_MODEL) * scale
    return y.astype(x.dtype), xh[:, -POOL_HIST:]


def ssd_scan(xs, dt, A, Bm, Cm, S0, q):
    bsz, L = xs.shape[:2]
    nc = L // q
    f32 = jnp.float32

    def to_blocks(t):
        return jnp.moveaxis(t.reshape(bsz, nc, q, *t.shape[2:]), 1, 0)

    xdt = xs.astype(f32) * dt[..., None]
    a = dt * A
    causal = jnp.tril(jnp.ones((q, q), dtype=bool))[None, :, :, None, None]

    def step(S, inp):
        xc, ac, bc, cc = inp
        acs = jnp.cumsum(ac, axis=1)
        seg = acs[:, :, None] - acs[:, None, :]
        Lm = jnp.exp(jnp.where(causal, seg, -jnp.inf))
        cb = jnp.einsum('bqgn,bsgn->bqsg', cc, bc)
        y = jnp.einsum('bqsg,bqsgr,bsgrp->bqgrp', cb, Lm, xc)
        y = y + jnp.einsum('bqgn,bgrpn->bqgrp', cc, S) * jnp.exp(acs)[..., None]
        decay = jnp.exp(acs[:, -1:] - acs)
        S = S * jnp.exp(acs[:, -1])[..., None, None] + jnp.einsum('bsgn,bsgr,bsgrp->bgrpn', bc, decay, xc)
        return S, y

    S, ys = lax.scan(step, S0.astype(f32),
                     (to_blocks(xdt), to_blocks(a), to_blocks(Bm.astype(f32)), to_blocks(Cm.astype(f32))))
    y = jnp.moveaxis(ys, 0, 1).reshape(xs.shape)
    return y, S


def ssm_mixer(x, conv_hist, S0, in_proj, conv_w, conv_b, dt_bias, A_log, D_skip, norm_w, out_proj, q):
    bsz, L, _ = x.shape
    f32 = jnp.float32
    zxbcdt = x @ in_proj
    z = zxbcdt[..., :D_INNER]
    xBC = zxbcdt[..., D_INNER:D_INNER + CONV_DIM]
    dt_raw = zxbcdt[..., D_INNER + CONV_DIM:]
    xBC_h = jnp.concatenate([conv_hist.astype(x.dtype), xBC], axis=1)
    xBC_c = jax.nn.silu(causal_dwconv(xBC_h, conv_w, conv_b))
    new_conv = xBC_h[:, -(SSM_CONV - 1):]
    xs = xBC_c[..., :D_INNER].reshape(bsz, L, N_SSM_GROUPS, HEADS_PER_GROUP, HEAD_DIM)
    Bm = xBC_c[..., D_INNER:D_INNER + N_SSM_GROUPS * D_STATE].reshape(bsz, L, N_SSM_GROUPS, D_STATE)
    Cm = xBC_c[..., D_INNER + N_SSM_GROUPS * D_STATE:].reshape(bsz, L, N_SSM_GROUPS, D_STATE)
    dt = jax.nn.softplus(dt_raw.astype(f32) + dt_bias.astype(f32)).reshape(bsz, L, N_SSM_GROUPS, HEADS_PER_GROUP)
    A = -jnp.exp(A_log.astype(f32)).reshape(N_SSM_GROUPS, HEADS_PER_GROUP)
    S0 = S0.reshape(bsz, N_SSM_GROUPS, HEADS_PER_GROUP, HEAD_DIM, D_STATE)
    y, S = ssd_scan(xs, dt, A, Bm, Cm, S0, q)
    y = y + D_skip.astype(f32).reshape(N_SSM_GROUPS, HEADS_PER_GROUP)[..., None] * xs.astype(f32)
    y = y.reshape(bsz, L, D_INNER) * jax.nn.silu(z.astype(f32))
    yg = y.reshape(bsz, L, N_SSM_GROUPS, D_INNER // N_SSM_GROUPS)
    yg = yg * lax.rsqrt(jnp.mean(yg * yg, axis=-1, keepdims=True) + RMS_EPS)
    y = (yg.reshape(bsz, L, D_INNER) * norm_w.astype(f32)).astype(x.dtype)
    out = y @ out_proj
    return out, new_conv, S.reshape(bsz, N_SSM_HEADS, HEAD_DIM, D_STATE).astype(x.dtype)


def conv_ffn(x, hist, w_up, conv_w, conv_b, w_down):
    h = x @ w_up
    hh = jnp.concatenate([hist.astype(x.dtype), h], axis=1)
    hc = causal_dwconv(hh, conv_w, conv_b)
    g, u = hc[..., :D_FF], hc[..., D_FF:]
    y = (jax.nn.silu(g) * u) @ w_down
    return y, hh[:, -(FFN_CONV - 1):]


def trunk(x, p, pool_hist, ssm_conv_hist, ssm_state, ffn_hist, pos0, ssd_q, w):
    new_pool, new_sconv, new_sstate, new_ffn = [], [], [], []
    for i in range(DEPTH):
        j = i // N_MIXERS
        if i % N_MIXERS == 0:
            m, ph = pool_mixer(x, pool_hist[j], pos0, w['pool_w'][j], w['pool_scale'][j])
            new_pool.append(ph)
        else:
            m, ch, S = ssm_mixer(x, ssm_conv_hist[j], ssm_state[j], w['ssm_in_proj'][j], w['ssm_conv_w'][j],
                                 w['ssm_conv_b'][j], w['ssm_dt_bias'][j], w['ssm_A_log'][j], w['ssm_D'][j],
                                 w['ssm_norm_w'][j], w['ssm_out_proj'][j], ssd_q)
            new_sconv.append(ch)
            new_sstate.append(S)
        x = layer_norm(ALPHA * x + m, w['ln_mix_g'][i], w['ln_mix_b'][i])
        f, fh = conv_ffn(x, ffn_hist[i], w['ffn_up'][i], w['ffn_conv_w'][i], w['ffn_conv_b'][i], w['ffn_down'][i])
        new_ffn.append(fh)
        x = layer_norm(ALPHA * x + f, w['ln_ffn_g'][i], w['ln_ffn_b'][i])
        gate = jax.nn.sigmoid(x @ w['ple_gate_w'][i] + w['ple_gate_b'][i])
        x = x + gate * (p[i] @ w['ple_proj'][i])
    return x, jnp.stack(new_pool), jnp.stack(new_sconv), jnp.stack(new_sstate), jnp.stack(new_ffn)


def setup_inputs(seed: int = 0) -> dict:
    key = jax.random.key(seed)
    ks = iter(jax.random.split(key, 40))
    nrm = lambda shape, s=1.0: jax.random.normal(next(ks), shape, jnp.float32) * s
    NP, NS = N_POOL_LAYERS, N_SSD_LAYERS
    dt0 = jnp.exp(jax.random.uniform(next(ks), (NS, N_SSM_HEADS), jnp.float32, np.log(1e-3), np.log(1e-1)))
    dt_bias = dt0 + jnp.log(-jnp.expm1(-dt0))
    A_log = jnp.log(jax.random.uniform(next(ks), (NS, N_SSM_HEADS), jnp.float32, 1.0, 16.0))
    return {
        'x_prompt': nrm((BATCH, SEQ, D_MODEL)),
        'x_sample': nrm((DEC_BATCH, DEC_SEQ, D_MODEL)),
        'p_prompt': nrm((DEPTH, BATCH, SEQ, PLE_DIM)),
        'p_sample': nrm((DEPTH, DEC_BATCH, DEC_SEQ, PLE_DIM)),
        'cache_pool': nrm((NP, DEC_BATCH, POOL_HIST, D_MODEL)),
        'cache_ssm_conv': nrm((NS, DEC_BATCH, SSM_CONV - 1, CONV_DIM)),
        'state_ssm': nrm((NS, DEC_BATCH, N_SSM_HEADS, HEAD_DIM, D_STATE), 0.1),
        'cache_ffn_conv': nrm((DEPTH, DEC_BATCH, FFN_CONV - 1, 2 * D_FF)),
        'pool_w': nrm((NP, N_POOL_GROUPS, POOL_GROUP, POOL_GROUP), BETA * POOL_GROUP ** -0.5),
        'pool_scale': 1.0 + nrm((NP, D_MODEL), 0.02),
        'ssm_in_proj': nrm((NS, D_MODEL, IN_PROJ_DIM), D_MODEL ** -0.5),
        'ssm_conv_w': nrm((NS, SSM_CONV, CONV_DIM), SSM_CONV ** -0.5),
        'ssm_conv_b': nrm((NS, CONV_DIM), 0.02),
        'ssm_dt_bias': dt_bias,
        'ssm_A_log': A_log,
        'ssm_D': 1.0 + nrm((NS, N_SSM_HEADS), 0.02),
        'ssm_norm_w': 1.0 + nrm((NS, D_INNER), 0.02),
        'ssm_out_proj': nrm((NS, D_INNER, D_MODEL), BETA * D_INNER ** -0.5),
        'ln_mix_g': 1.0 + nrm((DEPTH, D_MODEL), 0.02),
        'ln_mix_b': nrm((DEPTH, D_MODEL), 0.02),
        'ffn_up': nrm((DEPTH, D_MODEL, 2 * D_FF), D_MODEL ** -0.5),
        'ffn_conv_w': nrm((DEPTH, FFN_CONV, 2 * D_FF), FFN_CONV ** -0.5),
        'ffn_conv_b': nrm((DEPTH, 2 * D_FF), 0.02),
        'ffn_down': nrm((DEPTH, D_FF, D_MODEL), BETA * D_FF ** -0.5),
        'ln_ffn_g': 1.0 + nrm((DEPTH, D_MODEL), 0.02),
        'ln_ffn_b': nrm((DEPTH, D_MODEL), 0.02),
        'ple_proj': nrm((DEPTH, PLE_DIM, D_MODEL), BETA * PLE_DIM ** -0.5),
        'ple_gate_w': nrm((DEPTH, D_MODEL, D_MODEL), D_MODEL ** -0.5),
        'ple_gate_b': nrm((DEPTH, D_MODEL), 0.02),
    }


def reference(x_prompt, x_sample, p_prompt, p_sample, cache_pool, cache_ssm_conv, state_ssm, cache_ffn_conv,
              pool_w, pool_scale, ssm_in_proj, ssm_conv_w, ssm_conv_b, ssm_dt_bias, ssm_A_log, ssm_D, ssm_norm_w,
              ssm_out_proj, ln_mix_g, ln_mix_b, ffn_up, ffn_conv_w, ffn_conv_b, ffn_down, ln_ffn_g, ln_ffn_b,
              ple_proj, ple_gate_w, ple_gate_b):
    w = dict(pool_w=pool_w, pool_scale=pool_scale, ssm_in_proj=ssm_in_proj, ssm_conv_w=ssm_conv_w,
             ssm_conv_b=ssm_conv_b, ssm_dt_bias=ssm_dt_bias, ssm_A_log=ssm_A_log, ssm_D=ssm_D,
             ssm_norm_w=ssm_norm_w, ssm_out_proj=ssm_out_proj, ln_mix_g=ln_mix_g, ln_mix_b=ln_mix_b,
             ffn_up=ffn_up, ffn_conv_w=ffn_conv_w, ffn_conv_b=ffn_conv_b, ffn_down=ffn_down,
             ln_ffn_g=ln_ffn_g, ln_ffn_b=ln_ffn_b, ple_proj=ple_proj, ple_gate_w=ple_gate_w,
             ple_gate_b=ple_gate_b)
    bp = x_prompt.shape[0]
    dt_ = x_prompt.dtype
    z_pool = jnp.zeros((N_POOL_LAYERS, bp, POOL_HIST, D_MODEL), dt_)
    z_sconv = jnp.zeros((N_SSD_LAYERS, bp, SSM_CONV - 1, CONV_DIM), dt_)
    z_state = jnp.zeros((N_SSD_LAYERS, bp, N_SSM_HEADS, HEAD_DIM, D_STATE), dt_)
    z_ffn = jnp.zeros((DEPTH, bp, FFN_CONV - 1, 2 * D_FF), dt_)
    y_prompt, pool_p, sconv_p, state_p, ffn_p = trunk(x_prompt, p_prompt, z_pool, z_sconv, z_state, z_ffn,
                                                      0, SSD_BLOCK, w)
    y_sample, pool_s, sconv_s, state_s, ffn_s = trunk(x_sample, p_sample, cache_pool, cache_ssm_conv, state_ssm,
                                                      cache_ffn_conv, PAST_LEN, x_sample.shape[1], w)
    return (y_prompt, y_sample, pool_p, pool_s, sconv_p, sconv_s, state_p, state_s, ffn_p, ffn_s)
```

```python
import numpy as np
import concourse.bass as bass
import concourse.mybir as mybir
from concourse.bass_utils import run_bass_kernel_spmd

F32 = mybir.dt.float32
BF16 = mybir.dt.bfloat16
AF = mybir.ActivationFunctionType
ALU = mybir.AluOpType


class Buf:
    __slots__ = ("name", "lw", "rd")

    def __init__(self, name):
        self.name = name
        self.lw = {}
        self.rd = {}


class Eng:
    def __init__(self, prog, name, handle, sem, is_pe=False, compute=True):
        self.prog = prog
        self.name = name
        self.h = handle
        self.sem = sem
        self.n = 0
        self.seen = {}
        self.q = []
        self.is_pe = is_pe
        self.compute = compute

    def _sync(self, deps):
        for key, cnt in deps.items():
            if key == self.name and self.is_pe:
                continue
            if self.seen.get(key, 0) >= cnt:
                continue
            self.seen[key] = cnt
            sem = self.prog.sems[key]
            self.q.append(("w", sem, cnt))

    def op(self, fn, reads=(), writes=()):
        deps = {}
        for b in reads:
            for k, c in b.lw.items():
                if deps.get(k, 0) < c:
                    deps[k] = c
        for b in writes:
            for k, c in b.lw.items():
                if deps.get(k, 0) < c:
                    deps[k] = c
            for k, c in b.rd.items():
                if deps.get(k, 0) < c:
                    deps[k] = c
        self._sync(deps)
        self.n += 1
        self.q.append(("o", fn, self.sem, 1))
        for b in reads:
            if b.rd.get(self.name, 0) < self.n:
                b.rd[self.name] = self.n
        for b in writes:
            b.lw = {self.name: self.n}
            b.rd = {}

    def dma(self, out_ap, in_ap, slot, reads=(), writes=()):
        deps = {}
        for b in reads:
            for k, c in b.lw.items():
                if deps.get(k, 0) < c:
                    deps[k] = c
        for b in writes:
            for k, c in b.lw.items():
                if deps.get(k, 0) < c:
                    deps[k] = c
            for k, c in b.rd.items():
                if deps.get(k, 0) < c:
                    deps[k] = c
        self._sync(deps)
        slot.n += 16
        self.q.append(("d", out_ap, in_ap, slot.sem))
        for b in reads:
            if b.rd.get(slot.key, 0) < slot.n:
                b.rd[slot.key] = slot.n
        for b in writes:
            b.lw = dict(b.lw) if False else {slot.key: slot.n}
            b.rd = {}

    def dma_multi_begin(self):
        pass

    def wait_all(self, bufs):
        deps = {}
        for b in bufs:
            for k, c in list(b.lw.items()) + list(b.rd.items()):
                if deps.get(k, 0) < c:
                    deps[k] = c
        self._sync(deps)

    def emit(self, h):
        for it in self.q:
            if it[0] == "w":
                h.wait_ge(it[1], it[2])
            elif it[0] == "o":
                ins = it[1](h)
                ins.then_inc(it[2], it[3])
            else:
                h.dma_start(out=it[1], in_=it[2]).then_inc(it[3], 16)


class Slot:
    def __init__(self, prog, key, sem):
        self.key = key
        self.sem = sem
        self.n = 0


class Prog:
    def __init__(self, nc, stack):
        self.nc = nc
        self.stack = stack
        self.sems = {}
        self.nslots = 0
        mk = lambda nm: stack.enter_context(nc.semaphore(nm))
        self.pe = Eng(self, "pe", nc.tensor, mk("s_pe"), is_pe=True)
        self.act = Eng(self, "act", nc.scalar, mk("s_act"))
        self.dve = Eng(self, "dve", nc.vector, mk("s_dve"))
        self.pool = Eng(self, "pool", nc.gpsimd, mk("s_pool"))
        self.sp = Eng(self, "sp", nc.sync, None, compute=False)
        for e in (self.pe, self.act, self.dve, self.pool):
            self.sems[e.name] = e.sem

    def slot(self, name=None):
        self.nslots += 1
        key = "dma%d" % self.nslots
        sem = self.stack.enter_context(self.nc.semaphore("s_" + key))
        self.sems[key] = sem
        return Slot(self, key, sem)

    def sb(self, name, shape, dtype):
        return self.stack.enter_context(self.nc.sbuf_tensor(name, list(shape), dtype))

    def ps(self, name, shape, dtype):
        return self.stack.enter_context(self.nc.psum_tensor(name, list(shape), dtype))

    def finish(self, final_bufs):
        self.sp.wait_all(final_bufs)
        with self.nc.Block() as block:
            @block.tensor
            def _(e):
                self.pe.emit(e)

            @block.scalar
            def _(e):
                self.act.emit(e)

            @block.vector
            def _(e):
                self.dve.emit(e)

            @block.gpsimd
            def _(e):
                self.pool.emit(e)

            @block.sync
            def _(e):
                self.sp.emit(e)

D = 1024
DFF = 2816
NJ = 22
DIN = 2048
CONVD = 4096
NH = 32
ALPHA_ = (2 * 2) ** 0.25
LN_EPS_ = 1e-5
RMS_EPS_ = 1e-5


def build_program(L, LS=32):
    import contextlib
    nc = bass.Bass("TRN2", target_bir_lowering=False)
    NBLK = L // 128

    def din(name, shape, dt=F32):
        return nc.dram_tensor(name, list(shape), dt, kind="ExternalInput").ap()

    def dout(name, shape, dt=F32):
        return nc.dram_tensor(name, list(shape), dt, kind="ExternalOutput").ap()

    def dint(name, shape, dt=BF16):
        return nc.dram_tensor(name, list(shape), dt).ap()

    Ls = [L, LS]
    xT_d = [din("xT%d" % s, [128, 8, 15 + Ls[s]]) for s in range(2)]
    x_d = [din("x%d" % s, [Ls[s], D]) for s in range(2)]
    pT_d = [[din("pT%d_%d" % (s, i), [128, 2, Ls[s]]) for i in range(2)] for s in range(2)]
    ftail_d = [din("ftail%d" % s, [128, 2, 2, NJ, 2]) for s in range(2)]
    ctail_d = [din("ctail%d" % s, [128, 32, 3]) for s in range(2)]
    st_d = [din("st%d" % s, [128, DIN]) for s in range(2)]
    invc_d = [din("invc%d" % s, [128, 8, 16]) for s in range(2)]
    pool_w_d = din("pool_w", [4, 256, 256])
    pool_scale_d = din("pool_scale", [1, D])
    in_proj_d = din("in_proj", [D, 6176])
    scw_d = din("scw", [128, 32, 4])
    scb_d = din("scb", [128, 32])
    dtb_d = din("dtb", [1, NH])
    alog_d = din("alog", [1, NH])
    dsk_d = din("dsk", [1, NH])
    normw_d = din("normw", [128, 16])
    out_proj_d = din("out_proj", [DIN, D])
    lng_d = din("lng", [4, D])
    lnb_d = din("lnb", [4, D])
    ffn_up_d = [din("ffn_up%d" % i, [D, 2 * DFF]) for i in range(2)]
    fcw_d = din("fcw", [128, 2, 2, NJ, 3])
    fcb_d = din("fcb", [128, 2, 2, NJ])
    ffn_down_d = [din("ffn_down%d" % i, [DFF, D]) for i in range(2)]
    ple_proj_d = [din("ple_proj%d" % i, [256, D]) for i in range(2)]
    gate_w_d = [din("gate_w%d" % i, [D, D]) for i in range(2)]
    gate_b_d = din("gate_b", [2, D])
    y_d = [dout("y%d" % s, [Ls[s], D]) for s in range(2)]
    pool_o = [dout("pool_o%d" % s, [15, D]) for s in range(2)]
    ftail_o = [dout("ftail_o%d" % s, [128, 2, 2, NJ, 2]) for s in range(2)]
    ctail_o = [dout("ctail_o%d" % s, [128, 32, 3]) for s in range(2)]
    st_o = [dout("st_o%d" % s, [128, DIN]) for s in range(2)]
    wup_s = [dint("wup_s%d" % i, [NJ, 128, 8, 256]) for i in range(2)]
    wdn_s = [dint("wdn_s%d" % i, [NJ, 128, D]) for i in range(2)]
    gate_s = [dint("gate_s%d" % i, [8, 128, D]) for i in range(2)]
    proj_s = [dint("proj_s%d" % i, [128, 2, D]) for i in range(2)]
    gb_s = dint("gb_s", [2, D])
    inz_s = dint("inz_s", [8, 128, DIN])
    inx_s = dint("inx_s", [32, 128, 8, 128])
    indt_s = dint("indt_s", [128, 8, NH])
    outp_s = dint("outp_s", [16, 128, D])

    with contextlib.ExitStack() as stack:
        P = Prog(nc, stack)
        pe, act, dve, pool, sp = P.pe, P.act, P.dve, P.pool, P.sp

        class T_:
            def __init__(self, name, shape, dt, psum=False):
                self.t = (P.ps if psum else P.sb)(name, shape, dt)
                self.b = Buf(name)

            def __getitem__(self, k):
                return self.t[k]

            def load(self, dst, src, reads=()):
                if not hasattr(self, "slot"):
                    self.slot = P.slot()
                sp.dma(dst, src, self.slot, reads=list(reads), writes=[self.b])

        cnt = [0]

        def SB(shape, dt=F32, name=None):
            cnt[0] += 1
            return T_(name or ("t%d" % cnt[0]), shape, dt)

        wbuf = Buf("wscratch")
        cs = P.slot()
        for i in range(2):
            for j in range(NJ):
                for gu in range(2):
                    src = ffn_up_d[i][:, gu * DFF + j * 128: gu * DFF + (j + 1) * 128].rearrange("(k p) n -> p k n", p=128)
                    pool.dma(wup_s[i][j, :, :, gu * 128:(gu + 1) * 128], src, cs, writes=[wbuf])
            for j0 in range(0, NJ, 2):
                pool.dma(wdn_s[i][j0:j0 + 2].rearrange("j p n -> p j n"),
                         ffn_down_d[i][j0 * 128:(j0 + 2) * 128, :].rearrange("(j p) n -> p j n", p=128), cs, writes=[wbuf])
            for k0 in range(0, 8, 2):
                pool.dma(gate_s[i][k0:k0 + 2].rearrange("j p n -> p j n"),
                         gate_w_d[i][k0 * 128:(k0 + 2) * 128, :].rearrange("(j p) n -> p j n", p=128), cs, writes=[wbuf])
            pool.dma(proj_s[i][:, :, :], ple_proj_d[i].rearrange("(k p) n -> p k n", p=128), cs, writes=[wbuf])
        pool.dma(gb_s[:, :], gate_b_d[:, :], cs, writes=[wbuf])
        for k in range(8):
            pool.dma(inz_s[k], in_proj_d[k * 128:(k + 1) * 128, 0:DIN], cs, writes=[wbuf])
        for c in range(32):
            pool.dma(inx_s[c], in_proj_d[:, DIN + c * 128: DIN + (c + 1) * 128].rearrange("(k p) n -> p k n", p=128),
                     cs, writes=[wbuf])
        pool.dma(indt_s[:, :, :], in_proj_d[:, 6144:6176].rearrange("(k p) n -> p k n", p=128), cs, writes=[wbuf])
        for k0 in range(0, 16, 2):
            pool.dma(outp_s[k0:k0 + 2].rearrange("j p n -> p j n"),
                     out_proj_d[k0 * 128:(k0 + 2) * 128, :].rearrange("(j p) n -> p j n", p=128), cs, writes=[wbuf])

        ld = P.slot()
        identf = SB([128, 128]); ident = SB([128, 128], BF16)
        tri = SB([128, 128]); trib = SB([128, 128], BF16); onesf = SB([128, 128]); mstr = SB([128, 128], BF16)
        mtmp = SB([128, 128])
        onesb = SB([1, 128], BF16)
        pool.op(lambda e: e.memset(onesf[:], 1.0), writes=[onesf.b])
        pool.op(lambda e: e.memset(onesb[:], 1.0), writes=[onesb.b])
        pool.op(lambda e: e.affine_select(out=identf[:], in_=onesf[:], pattern=[[-1, 128]], compare_op=ALU.is_equal,
                                          fill=0.0, base=0, channel_multiplier=1), reads=[onesf.b], writes=[identf.b])
        dve.op(lambda e: e.tensor_copy(out=ident[:], in_=identf[:]), reads=[identf.b], writes=[ident.b])
        pool.op(lambda e: e.affine_select(out=tri[:], in_=onesf[:], pattern=[[1, 128]], compare_op=ALU.is_ge,
                                          fill=0.0, base=0, channel_multiplier=-1), reads=[onesf.b], writes=[tri.b])
        dve.op(lambda e: e.tensor_copy(out=trib[:], in_=tri[:]), reads=[tri.b], writes=[trib.b])
        pool.op(lambda e: e.affine_select(out=mtmp[:], in_=onesf[:], pattern=[[-1, 128]], compare_op=ALU.is_gt,
                                          fill=0.0, base=0, channel_multiplier=1), reads=[onesf.b], writes=[mtmp.b])
        dve.op(lambda e: e.tensor_copy(out=mstr[:], in_=mtmp[:]), reads=[mtmp.b], writes=[mstr.b])

        lng = [SB([128, D]) for _ in range(4)]
        lnb = [SB([128, D]) for _ in range(4)]
        for q in range(4):
            lng[q].load(lng[q][:], lng_d[q:q + 1, :].partition_broadcast(128))
            lnb[q].load(lnb[q][:], lnb_d[q:q + 1, :].partition_broadcast(128))
        gbb = SB([1, 2, D], BF16)
        gbb.load(gbb[:], gb_s[:, :].rearrange("(o a) n -> o a n", o=1), reads=[wbuf])
        dtb = SB([128, NH]); abc = SB([128, NH]); dbc = SB([128, NH])
        dtb.load(dtb[:], dtb_d[0:1, :].partition_broadcast(128))
        abc.load(abc[:], alog_d[0:1, :].partition_broadcast(128))
        dbc.load(dbc[:], dsk_d[0:1, :].partition_broadcast(128))
        act.op(lambda e: e.activation(out=abc[:], in_=abc[:], func=AF.Exp), reads=[abc.b], writes=[abc.b])
        dve.op(lambda e: e.tensor_scalar_mul(out=abc[:], in0=abc[:], scalar1=-1.0), reads=[abc.b], writes=[abc.b])
        scw = SB([128, 32, 4]); scb = SB([128, 32]); normw = SB([128, 16])
        fcw = SB([128, 2, 2, NJ, 3]); fcb = SB([128, 2, 2, NJ])
        scw.load(scw[:], scw_d[:, :, :])
        scb.load(scb[:], scb_d[:, :])
        normw.load(normw[:], normw_d[:, :])
        fcw.load(fcw[:], fcw_d[:, :, :, :, :])
        fcb.load(fcb[:], fcb_d[:, :, :, :])
        poolw = SB([128, 8, 256], BF16)
        POOLW_SETUP = True
        wdt = SB([128, 8, NH], BF16)
        wdt.load(wdt[:], indt_s[:, :, :], reads=[wbuf])

        pA = T_("pA", [128, 2048], F32, psum=True)
        pB = T_("pB", [128, 1024], F32, psum=True)
        pT = T_("pT", [128, 1024], BF16, psum=True)
        pS = T_("pS", [128, 512], F32, psum=True)
        pAh = [Buf("pA0"), Buf("pA1")]
        pBh = [Buf("pB0"), Buf("pB1")]

        class Ring:
            def __init__(self, shape, depth):
                self.t = [SB(shape, BF16) for _ in range(depth)]
                self.sl = [P.slot() for _ in range(depth)]
                self.depth = depth

            def load(self, i, src):
                t = self.t[i % self.depth]
                sp.dma(t[:], src, self.sl[i % self.depth], reads=[wbuf], writes=[t.b])
                return t

        r_up = Ring([128, 8, 256], 2)
        r_dn = Ring([128, D], 2)
        r_gate = Ring([128, D], 2)
        r_proj = Ring([128, 2, D], 1)
        r_inz = Ring([128, DIN], 2)
        r_inx = Ring([128, 8, 128], 3)
        r_out = Ring([128, D], 2)

        def stream(n, ring, srcf, body):
            tl = {}
            for i in range(min(ring.depth, n)):
                tl[i] = ring.load(i, srcf(i))
            for i in range(n):
                body(i, tl[i])
                if i + ring.depth < n:
                    tl[i + ring.depth] = ring.load(i + ring.depth, srcf(i + ring.depth))

        xin = SB([128, 8, 143])
        xtok = [SB([128, D]) for _ in range(3)]
        sA = SB([128, 2, 143]); sBf = SB([128, 2, 143]); t16 = SB([128, 2, 16])
        pooled = SB([128, 8, 128], BF16)
        xT = SB([128, 8, 128], BF16)
        xb16 = SB([128, D], BF16)
        U1 = SB([128, 32 * 128], BF16)
        hb = [[SB([128, 130]) for _ in range(2)] for _ in range(2)]
        acc = [[SB([128, 128]) for _ in range(2)] for _ in range(2)]
        sg = SB([128, 128])
        pst = SB([128, 2, 128]); pTb = SB([128, 2, 128], BF16)
        gate = SB([128, D]); tmpt = SB([128, D])
        stats = SB([128, 2, 6]); mv = SB([128, 8])
        ftail = SB([128, 2, 2, NJ, 2]); ctail = SB([128, 32, 3])
        invc = SB([128, 8, 16])
        sz = SB([128, DIN]); ybuf = SB([128, DIN]); ytmp = SB([128, DIN])
        xsT = SB([128, 16, 128], BF16); BT = SB([128, 8, 128], BF16); CT = SB([128, 8, 128], BF16)
        cbuf = [SB([128, 131]) for _ in range(2)]; cacc = [SB([128, 128]) for _ in range(2)]
        xdt = SB([128, DIN], BF16); xdtd = SB([128, DIN], BF16); Btok = SB([128, 1024], BF16)
        cbm = SB([128, 8, 128], BF16)
        rseg = [SB([128, 8, 128], BF16) for _ in range(2)]
        expL = [SB([128, 8, 128], BF16) for _ in range(2)]
        ST = SB([128, DIN]); STb = SB([128, DIN], BF16)
        ynb = SB([128, DIN], BF16); ynT = SB([128, 16, 128], BF16)
        sm = SB([128, 8, NH])
        ss = SB([128, 4, 8])
        io = P.slot(); io2 = P.slot(); oslot = P.slot()
        sz.load(sz[:].rearrange("p (c d) -> p c d", d=256), pool_w_d.rearrange("g (h p) d -> p (g h) d", p=128))
        ytmp.load(ytmp[:, 0:D], pool_scale_d[0:1, :].partition_broadcast(128))
        for c in range(8):
            g = c // 2
            dve.op(lambda e, c=c, g=g: e.tensor_tensor(out=poolw[:, c, :], in0=sz[:, c * 256:(c + 1) * 256],
                                                       in1=ytmp[:, g * 256:(g + 1) * 256], op=ALU.mult),
                   reads=[sz.b, ytmp.b], writes=[poolw.b])
        outbufs = []

        def layer_norm(T, u, q, xo, use_xT=True):
            for h in range(2):
                dve.op(lambda e, h=h: e.bn_stats(out=stats[:T, h, :], in_=u[:T, h * 512:(h + 1) * 512]),
                       reads=[u.b], writes=[stats.b])
            dve.op(lambda e: e.bn_aggr(out=mv[:T, 0:2], in_=stats[:T].rearrange("p a b -> p (a b)")), reads=[stats.b], writes=[mv.b])
            dve.op(lambda e: e.tensor_scalar_add(out=mv[:T, 2:3], in0=mv[:T, 1:2], scalar1=LN_EPS_), reads=[mv.b], writes=[mv.b])
            act.op(lambda e: e.activation(out=mv[:T, 3:4], in_=mv[:T, 2:3], func=AF.Sqrt), reads=[mv.b], writes=[mv.b])
            dve.op(lambda e: e.reciprocal(out=mv[:T, 4:5], in_=mv[:T, 3:4]), reads=[mv.b], writes=[mv.b])
            dve.op(lambda e: e.scalar_tensor_tensor(out=mv[:T, 5:6], in0=mv[:T, 0:1], scalar=-1.0, in1=mv[:T, 4:5],
                                                    op0=ALU.mult, op1=ALU.mult), reads=[mv.b], writes=[mv.b])
            act.op(lambda e: e.activation(out=tmpt[:T, :], in_=u[:T, :], func=AF.Identity, scale=mv[:T, 4:5], bias=mv[:T, 5:6]),
                   reads=[u.b, mv.b], writes=[tmpt.b])
            pool.op(lambda e: e.tensor_tensor(out=tmpt[:T, :], in0=tmpt[:T, :], in1=lng[q][:T, :], op=ALU.mult),
                    reads=[tmpt.b, lng[q].b], writes=[tmpt.b])
            pool.op(lambda e: e.tensor_tensor(out=xo[:T, :], in0=tmpt[:T, :], in1=lnb[q][:T, :], op=ALU.add),
                    reads=[tmpt.b, lnb[q].b], writes=[xo.b])
            to_T(T, xo)

        def to_T(T, xo):
            act.op(lambda e: e.activation(out=xb16[:T, :], in_=xo[:T, :], func=AF.Copy), reads=[xo.b], writes=[xb16.b])
            for c in range(8):
                pe.op(lambda e, c=c: e.transpose(out=pT[:, c * 128:c * 128 + T], in_=xb16[:T, c * 128:(c + 1) * 128],
                                                 identity=ident[:T, :T]), reads=[xb16.b, ident.b], writes=[pT.b])
            dve.op(lambda e: e.tensor_copy(out=xT[:, :, :T], in_=pT[:].rearrange("p (c t) -> p c t", t=128)[:, :, :T]),
                   reads=[pT.b], writes=[xT.b])

        def ffn(T, i):
            hT = U1
            st_ = {}

            def up(j, w):
                par = j % 2
                for gu in range(2):
                    for k in range(8):
                        pe.op(lambda e, gu=gu, k=k: e.matmul(pB[:, gu * 512:gu * 512 + T], lhsT=w[:, k, gu * 128:(gu + 1) * 128],
                                                             rhs=xT[:, k, :T], start=(k == 0), stop=(k == 7)),
                              reads=[w.b, xT.b], writes=[pBh[gu]])
                for gu in range(2):
                    h_ = hb[par][gu]; a_ = acc[par][gu]
                    act.op(lambda e, gu=gu, h_=h_: e.activation(out=h_[:, 2:2 + T], in_=pB[:, gu * 512:gu * 512 + T], func=AF.Copy),
                           reads=[pBh[gu]], writes=[h_.b])
                    pool.op(lambda e, gu=gu, h_=h_: e.tensor_copy(out=h_[:, 0:2], in_=ftail[:, i, gu, j, :]),
                            reads=[ftail.b], writes=[h_.b])
                    pool.op(lambda e, gu=gu, h_=h_: e.tensor_copy(out=ftail[:, i, gu, j, :], in_=h_[:, T:T + 2]),
                            reads=[h_.b], writes=[ftail.b])
                    dve.op(lambda e, gu=gu, h_=h_, a_=a_: e.tensor_scalar(out=a_[:, :T], in0=h_[:, 2:2 + T],
                                                                          scalar1=fcw[:, i, gu, j, 2:3], scalar2=fcb[:, i, gu, j:j + 1],
                                                                          op0=ALU.mult, op1=ALU.add),
                           reads=[h_.b, fcw.b, fcb.b], writes=[a_.b])
                    for tap in (1, 0):
                        dve.op(lambda e, gu=gu, h_=h_, a_=a_, tap=tap: e.scalar_tensor_tensor(
                            out=a_[:, :T], in0=h_[:, tap:tap + T], scalar=fcw[:, i, gu, j, tap:tap + 1], in1=a_[:, :T],
                            op0=ALU.mult, op1=ALU.add), reads=[h_.b, fcw.b, a_.b], writes=[a_.b])
                act.op(lambda e: e.activation(out=sg[:, :T], in_=acc[par][0][:, :T], func=AF.Silu), reads=[acc[par][0].b], writes=[sg.b])
                pool.op(lambda e: e.tensor_tensor(out=hT[:, j * 128:j * 128 + T], in0=sg[:, :T], in1=acc[par][1][:, :T], op=ALU.mult),
                        reads=[sg.b, acc[par][1].b], writes=[hT.b])

            def down(j, w):
                for h in range(2):
                    pe.op(lambda e, h=h: e.matmul(pA[:T, h * 512:(h + 1) * 512], lhsT=hT[:, j * 128:j * 128 + T],
                                                  rhs=w[:, h * 512:(h + 1) * 512], start=(j == 0), stop=(j == NJ - 1)),
                          reads=[hT.b, w.b], writes=[pAh[0]])

            upt = {}; dnt = {}
            for jj in range(min(2, NJ)):
                upt[jj] = r_up.load(jj, wup_s[i][jj]); dnt[jj] = r_dn.load(jj, wdn_s[i][jj])
            for j in range(NJ):
                up(j, upt[j])
                down(j, dnt[j])
                if j + 2 < NJ:
                    upt[j + 2] = r_up.load(j + 2, wup_s[i][j + 2]); dnt[j + 2] = r_dn.load(j + 2, wdn_s[i][j + 2])

        def ple(T, i, s, t0, x2, xo):
            for h in range(2):
                pe.op(lambda e, h=h: e.matmul(pA[:T, h * 512:(h + 1) * 512], lhsT=onesb[0:1, :T], rhs=gbb[0:1, i, h * 512:(h + 1) * 512],
                                              start=True, stop=False), reads=[onesb.b, gbb.b], writes=[pAh[0]])

            def body(k, w):
                for h in range(2):
                    pe.op(lambda e, h=h: e.matmul(pA[:T, h * 512:(h + 1) * 512], lhsT=xT[:, k, :T], rhs=w[:, h * 512:(h + 1) * 512],
                                                  start=False, stop=(k == 7)), reads=[xT.b, w.b], writes=[pAh[0]])
            stream(8, r_gate, lambda k: gate_s[i][k], body)
            pst.load(pst[:, :, :T], pT_d[s][i][:, :, t0:t0 + T])
            pool.op(lambda e: e.tensor_copy(out=pTb[:, :, :T], in_=pst[:, :, :T]), reads=[pst.b], writes=[pTb.b])
            wp = r_proj.load(0, proj_s[i][:, :, :])
            for h in range(2):
                for k in range(2):
                    pe.op(lambda e, h=h, k=k: e.matmul(pA[:T, 1024 + h * 512:1024 + (h + 1) * 512], lhsT=pTb[:, k, :T],
                                                       rhs=wp[:, k, h * 512:(h + 1) * 512], start=(k == 0), stop=(k == 1)),
                          reads=[pTb.b, wp.b], writes=[pAh[1]])
            act.op(lambda e: e.activation(out=gate[:T, :], in_=pA[:T, 0:1024], func=AF.Sigmoid), reads=[pAh[0]], writes=[gate.b])
            dve.op(lambda e: e.tensor_tensor(out=gate[:T, :], in0=gate[:T, :], in1=pA[:T, 1024:2048], op=ALU.mult),
                   reads=[gate.b, pAh[1]], writes=[gate.b])
            pool.op(lambda e: e.tensor_tensor(out=xo[:T, :], in0=gate[:T, :], in1=x2[:T, :], op=ALU.add),
                    reads=[gate.b, x2.b], writes=[xo.b])

        def residual(T, xres, pbuf, uo):
            dve.op(lambda e: e.scalar_tensor_tensor(out=uo[:T, :], in0=xres[:T, :], scalar=ALPHA_, in1=pA[:T, 0:1024],
                                                    op0=ALU.mult, op1=ALU.add), reads=[xres.b, pbuf], writes=[uo.b])

        def bc3(ap2, T, n):
            return ap2.unsqueeze(2).to_broadcast([T, NH, n])

        def block(s, t0, T, first):
            xa, xb, xc = xtok
            xin.load(xin[:, :, :15 + T], xT_d[s][:, :, t0:t0 + 15 + T])
            xa.load(xa[:T, :], x_d[s][t0:t0 + T, :])
            W = 15 + T
            for g in range(4):
                cs_ = slice(2 * g, 2 * g + 2)
                wsz = 2 ** (g + 1)
                pool.op(lambda e, cs_=cs_: e.tensor_tensor(out=sA[:, :, 1:W], in0=xin[:, cs_, 1:W], in1=xin[:, cs_, 0:W - 1], op=ALU.add),
                        reads=[xin.b], writes=[sA.b])
                cur, oth = sA, sBf
                sh = 1
                for lvl in range(g):
                    sh2 = sh * 2
                    lo = 2 * sh2 - 1
                    pool.op(lambda e, cur=cur, oth=oth, lo=lo, sh2=sh2: e.tensor_tensor(
                        out=oth[:, :, lo:W], in0=cur[:, :, lo:W], in1=cur[:, :, lo - sh2:W - sh2], op=ALU.add),
                        reads=[cur.b], writes=[oth.b])
                    cur, oth = oth, cur
                    sh = sh2
                dve.op(lambda e, cur=cur, cs_=cs_, wsz=wsz: e.scalar_tensor_tensor(
                    out=pooled[:, cs_, :T], in0=cur[:, :, 15:W], scalar=1.0 / wsz, in1=xin[:, cs_, 15:W],
                    op0=ALU.mult, op1=ALU.subtract), reads=[cur.b, xin.b], writes=[pooled.b])
                if first:
                    dve.op(lambda e, cur=cur, cs_=cs_: e.tensor_tensor(out=t16[:], in0=cur[:, :, 15:31], in1=invc[:, cs_, :], op=ALU.mult),
                           reads=[cur.b, invc.b], writes=[t16.b])
                    dve.op(lambda e, cs_=cs_: e.tensor_tensor(out=pooled[:, cs_, 0:16], in0=t16[:], in1=xin[:, cs_, 15:31], op=ALU.subtract),
                           reads=[t16.b, xin.b], writes=[pooled.b])
            for g in range(4):
                for h in range(2):
                    c = 2 * g + h
                    pe.op(lambda e, g=g, h=h, c=c: e.matmul(pA[:T, g * 256:(g + 1) * 256], lhsT=pooled[:, c, :T], rhs=poolw[:, c, :],
                                                            start=(h == 0), stop=(h == 1)),
                          reads=[pooled.b, poolw.b], writes=[pAh[0]])
            residual(T, xa, pAh[0], xb)
            layer_norm(T, xb, 0, xc)
            ffn(T, 0)
            residual(T, xc, pAh[0], xb)
            layer_norm(T, xb, 1, xa)
            ple(T, 0, s, t0, xa, xc)
            to_T(T, xc)

            def zbody(k, w):
                for cc in range(4):
                    pe.op(lambda e, cc=cc: e.matmul(pA[:T, cc * 512:(cc + 1) * 512], lhsT=xT[:, k, :T], rhs=w[:, cc * 512:(cc + 1) * 512],
                                                    start=(k == 0), stop=(k == 7)), reads=[xT.b, w.b], writes=[pAh[cc // 2]])
            stream(8, r_inz, lambda k: inz_s[k], zbody)
            act.op(lambda e: e.activation(out=sz[:T, :], in_=pA[:T, :], func=AF.Silu), reads=[pAh[0], pAh[1]], writes=[sz.b])
            for k in range(8):
                pe.op(lambda e, k=k: e.matmul(pS[:T, 0:NH], lhsT=xT[:, k, :T], rhs=wdt[:, k, :], start=(k == 0), stop=(k == 7)),
                      reads=[xT.b, wdt.b], writes=[pS.b])
            S_ = lambda n: sm[:T, n, :]
            dve.op(lambda e: e.tensor_tensor(out=S_(0), in0=pS[:T, 0:NH], in1=dtb[:T, :], op=ALU.add), reads=[pS.b, dtb.b], writes=[sm.b])
            dve.op(lambda e: e.scalar_tensor_tensor(out=S_(1), in0=S_(0), scalar=-1.0, in1=S_(0), op0=ALU.mult, op1=ALU.max), reads=[sm.b], writes=[sm.b])
            act.op(lambda e: e.activation(out=S_(1), in_=S_(1), func=AF.Exp, scale=-1.0), reads=[sm.b], writes=[sm.b])
            act.op(lambda e: e.activation(out=S_(1), in_=S_(1), func=AF.Ln, bias=1.0), reads=[sm.b], writes=[sm.b])
            dve.op(lambda e: e.tensor_scalar_max(out=S_(0), in0=S_(0), scalar1=0.0), reads=[sm.b], writes=[sm.b])
            dve.op(lambda e: e.tensor_tensor(out=S_(2), in0=S_(0), in1=S_(1), op=ALU.add), reads=[sm.b], writes=[sm.b])
            dve.op(lambda e: e.tensor_tensor(out=S_(3), in0=S_(2), in1=abc[:T, :], op=ALU.mult), reads=[sm.b, abc.b], writes=[sm.b])

            def xbody(c, w):
                par = c % 2
                for k in range(8):
                    pe.op(lambda e, k=k: e.matmul(pB[:, par * 512:par * 512 + T], lhsT=w[:, k, :], rhs=xT[:, k, :T],
                                                  start=(k == 0), stop=(k == 7)), reads=[w.b, xT.b], writes=[pBh[par]])
                cb_ = cbuf[par]; ca_ = cacc[par]
                act.op(lambda e: e.activation(out=cb_[:, 3:3 + T], in_=pB[:, par * 512:par * 512 + T], func=AF.Copy),
                       reads=[pBh[par]], writes=[cb_.b])
                pool.op(lambda e: e.tensor_copy(out=cb_[:, 0:3], in_=ctail[:, c, :]), reads=[ctail.b], writes=[cb_.b])
                pool.op(lambda e: e.tensor_copy(out=ctail[:, c, :], in_=cb_[:, T:T + 3]), reads=[cb_.b], writes=[ctail.b])
                dve.op(lambda e: e.tensor_scalar(out=ca_[:, :T], in0=cb_[:, 3:3 + T], scalar1=scw[:, c, 3:4], scalar2=scb[:, c:c + 1],
                                                 op0=ALU.mult, op1=ALU.add), reads=[cb_.b, scw.b, scb.b], writes=[ca_.b])
                for tap in (2, 1, 0):
                    dve.op(lambda e, tap=tap: e.scalar_tensor_tensor(out=ca_[:, :T], in0=cb_[:, tap:tap + T], scalar=scw[:, c, tap:tap + 1],
                                                                     in1=ca_[:, :T], op0=ALU.mult, op1=ALU.add),
                           reads=[cb_.b, scw.b, ca_.b], writes=[ca_.b])
                if c < 16:
                    dst, db = xsT[:, c, :T], xsT.b
                elif c < 24:
                    dst, db = BT[:, c - 16, :T], BT.b
                else:
                    dst, db = CT[:, c - 24, :T], CT.b
                act.op(lambda e: e.activation(out=dst, in_=ca_[:, :T], func=AF.Silu), reads=[ca_.b], writes=[db])
            stream(32, r_inx, lambda c: inx_s[c], xbody)
            for r in range(2):
                for c8 in range(8):
                    c = r * 8 + c8
                    pe.op(lambda e, c=c, c8=c8: e.transpose(out=pT[:T, c8 * 128:(c8 + 1) * 128], in_=xsT[:, c, :T], identity=ident[:, :]),
                          reads=[xsT.b, ident.b], writes=[pT.b])
                hs = slice(r * 16, (r + 1) * 16)
                dve.op(lambda e, r=r, hs=hs: e.tensor_tensor(
                    out=xdt[:T, r * 1024:(r + 1) * 1024].rearrange("p (h d) -> p h d", d=64),
                    in0=pT[:T, :].rearrange("p (h d) -> p h d", d=64),
                    in1=sm[:T, 2, hs].unsqueeze(2).to_broadcast([T, 16, 64]), op=ALU.mult),
                    reads=[pT.b, sm.b], writes=[xdt.b])
                dve.op(lambda e, r=r, hs=hs: e.tensor_tensor(
                    out=ybuf[:T, r * 1024:(r + 1) * 1024].rearrange("p (h d) -> p h d", d=64),
                    in0=pT[:T, :].rearrange("p (h d) -> p h d", d=64),
                    in1=dbc[:T, hs].unsqueeze(2).to_broadcast([T, 16, 64]), op=ALU.mult),
                    reads=[pT.b, dbc.b], writes=[ybuf.b])
            for g in range(8):
                pe.op(lambda e, g=g: e.transpose(out=pT[:T, g * 128:(g + 1) * 128], in_=BT[:, g, :T], identity=ident[:, :]),
                      reads=[BT.b, ident.b], writes=[pT.b])
            dve.op(lambda e: e.tensor_copy(out=Btok[:T, :], in_=pT[:T, :]), reads=[pT.b], writes=[Btok.b])
            pe.op(lambda e: e.matmul(pS[:T, 32:64], lhsT=tri[:T, :T], rhs=sm[:T, 3, :], start=True, stop=True),
                  reads=[tri.b, sm.b], writes=[pS.b])
            pe.op(lambda e: e.matmul(pS[:, 64:96], lhsT=onesf[:T, :], rhs=sm[:T, 3, :], start=True, stop=True),
                  reads=[onesf.b, sm.b], writes=[pS.b])
            act.op(lambda e: e.activation(out=S_(4), in_=pS[:T, 32:64], func=AF.Exp), reads=[pS.b], writes=[sm.b])
            act.op(lambda e: e.activation(out=S_(5), in_=pS[:T, 32:64], func=AF.Copy), reads=[pS.b], writes=[sm.b])
            dve.op(lambda e: e.tensor_tensor(out=S_(6), in0=pS[:T, 64:96], in1=S_(5), op=ALU.subtract), reads=[pS.b, sm.b], writes=[sm.b])
            act.op(lambda e: e.activation(out=S_(6), in_=S_(6), func=AF.Exp), reads=[sm.b], writes=[sm.b])
            act.op(lambda e: e.activation(out=sm[:, 7, :], in_=pS[:, 64:96], func=AF.Exp), reads=[pS.b], writes=[sm.b])
            for g in range(8):
                pe.op(lambda e, g=g: e.matmul(pB[:T, g * 128:g * 128 + T], lhsT=BT[:, g, :T], rhs=CT[:, g, :T], start=True, stop=True),
                      reads=[BT.b, CT.b], writes=[pBh[g // 4]])
            dve.op(lambda e: e.tensor_tensor(out=cbm[:T, :, :T], in0=pB[:T, :].rearrange("p (g q) -> p g q", q=128)[:, :, :T],
                                             in1=tri[:T, :T].unsqueeze(1).to_broadcast([T, 8, T]), op=ALU.mult),
                   reads=[pBh[0], pBh[1], tri.b], writes=[cbm.b])
            MT = U1
            for r in range(4):
                rs_ = rseg[r % 2]; ex_ = expL[r % 2]
                pool.op(lambda e, r=r, rs_=rs_: e.tensor_tensor(
                    out=rs_[:T, :, :T], in0=sm[:T, 3, r * 8:(r + 1) * 8].unsqueeze(2).to_broadcast([T, 8, T]),
                    in1=trib[:T, :T].unsqueeze(1).to_broadcast([T, 8, T]), op=ALU.mult),
                    reads=[sm.b, trib.b], writes=[rs_.b])
                hp = r % 2
                for q2 in range(2):
                    pe.op(lambda e, rs_=rs_, q2=q2, hp=hp: e.matmul(
                        pA[:T, hp * 1024 + q2 * 512: hp * 1024 + q2 * 512 + 4 * T].rearrange("p (h q) -> p h q", q=T),
                        lhsT=mstr[:T, :T], rhs=rs_[:T, q2 * 4:(q2 + 1) * 4, :T], start=True, stop=True),
                        reads=[mstr.b, rs_.b], writes=[pAh[hp]])
                for q2 in range(2):
                    act.op(lambda e, ex_=ex_, q2=q2, hp=hp: e.activation(
                        out=ex_[:T, q2 * 4:(q2 + 1) * 4, :T],
                        in_=pA[:T, hp * 1024 + q2 * 512: hp * 1024 + q2 * 512 + 4 * T].rearrange("p (h q) -> p h q", q=T),
                        func=AF.Exp), reads=[pAh[hp]], writes=[ex_.b])
                for g2 in range(2):
                    gg = 2 * r + g2
                    dve.op(lambda e, ex_=ex_, g2=g2, gg=gg, r=r: e.tensor_tensor(
                        out=MT[:T, (r * 8 + g2 * 4) * 128:(r * 8 + g2 * 4 + 4) * 128].rearrange("p (h q) -> p h q", q=128)[:, :, :T],
                        in0=ex_[:T, g2 * 4:(g2 + 1) * 4, :T],
                        in1=cbm[:T, gg:gg + 1, :T].to_broadcast([T, 4, T]), op=ALU.mult),
                        reads=[ex_.b, cbm.b], writes=[MT.b])
            for g in range(8):
                pe.op(lambda e, g=g: e.matmul(pA[:T, g * 256:(g + 1) * 256], lhsT=CT[:, g, :T], rhs=STb[:, g * 256:(g + 1) * 256],
                                              start=True, stop=True), reads=[CT.b, STb.b], writes=[pAh[g // 4]])
            dve.op(lambda e: e.tensor_tensor(out=ytmp[:T, :].rearrange("p (h d) -> p h d", d=64),
                                             in0=pA[:T, :].rearrange("p (h d) -> p h d", d=64),
                                             in1=bc3(sm[:T, 4, :], T, 64), op=ALU.mult),
                   reads=[pAh[0], pAh[1], sm.b], writes=[ytmp.b])
            pool.op(lambda e: e.tensor_tensor(out=ybuf[:T, :], in0=ybuf[:T, :], in1=ytmp[:T, :], op=ALU.add),
                    reads=[ybuf.b, ytmp.b], writes=[ybuf.b])
            for h in range(NH):
                pe.op(lambda e, h=h: e.matmul(pA[:T, h * 64:(h + 1) * 64], lhsT=MT[:T, h * 128:h * 128 + T], rhs=xdt[:T, h * 64:(h + 1) * 64],
                                              start=True, stop=True), reads=[MT.b, xdt.b], writes=[pAh[h // 16]])
            dve.op(lambda e: e.tensor_tensor(out=ybuf[:T, :], in0=ybuf[:T, :], in1=pA[:T, :], op=ALU.add),
                   reads=[ybuf.b, pAh[0], pAh[1]], writes=[ybuf.b])
            pool.op(lambda e: e.tensor_tensor(out=xdtd[:T, :].rearrange("p (h d) -> p h d", d=64),
                                              in0=xdt[:T, :].rearrange("p (h d) -> p h d", d=64),
                                              in1=bc3(sm[:T, 6, :], T, 64), op=ALU.mult), reads=[xdt.b, sm.b], writes=[xdtd.b])
            for g in range(8):
                pe.op(lambda e, g=g: e.matmul(pA[:, g * 256:(g + 1) * 256], lhsT=Btok[:T, g * 128:(g + 1) * 128],
                                              rhs=xdtd[:T, g * 256:(g + 1) * 256], start=True, stop=True),
                      reads=[Btok.b, xdtd.b], writes=[pAh[g // 4]])
            pool.op(lambda e: e.tensor_tensor(out=ST[:, :].rearrange("p (h d) -> p h d", d=64),
                                              in0=ST[:, :].rearrange("p (h d) -> p h d", d=64),
                                              in1=sm[:, 7, :].unsqueeze(2).to_broadcast([128, NH, 64]), op=ALU.mult),
                    reads=[ST.b, sm.b], writes=[ST.b])
            dve.op(lambda e: e.tensor_tensor(out=ST[:, :], in0=ST[:, :], in1=pA[:, :], op=ALU.add),
                   reads=[ST.b, pAh[0], pAh[1]], writes=[ST.b])
            act.op(lambda e: e.activation(out=STb[:, :], in_=ST[:, :], func=AF.Copy), reads=[ST.b], writes=[STb.b])
            pool.op(lambda e: e.tensor_tensor(out=ybuf[:T, :], in0=ybuf[:T, :], in1=sz[:T, :], op=ALU.mult),
                    reads=[ybuf.b, sz.b], writes=[ybuf.b])
            for g in range(8):
                act.op(lambda e, g=g: e.activation(out=ytmp[:T, g * 256:(g + 1) * 256], in_=ybuf[:T, g * 256:(g + 1) * 256], func=AF.Square,
                                                   accum_out=ss[:T, 0, g:g + 1]), reads=[ybuf.b], writes=[ytmp.b, ss.b])
            dve.op(lambda e: e.tensor_scalar(out=ss[:T, 1, :], in0=ss[:T, 0, :], scalar1=1.0 / 256, scalar2=RMS_EPS_,
                                             op0=ALU.mult, op1=ALU.add), reads=[ss.b], writes=[ss.b])
            act.op(lambda e: e.activation(out=ss[:T, 2, :], in_=ss[:T, 1, :], func=AF.Sqrt), reads=[ss.b], writes=[ss.b])
            dve.op(lambda e: e.reciprocal(out=ss[:T, 3, :], in_=ss[:T, 2, :]), reads=[ss.b], writes=[ss.b])
            dve.op(lambda e: e.tensor_tensor(out=ynb[:T, :].rearrange("p (g d) -> p g d", d=256),
                                             in0=ybuf[:T, :].rearrange("p (g d) -> p g d", d=256),
                                             in1=ss[:T, 3, :].unsqueeze(2).to_broadcast([T, 8, 256]), op=ALU.mult),
                   reads=[ybuf.b, ss.b], writes=[ynb.b])
            for r in range(2):
                for c8 in range(8):
                    c = r * 8 + c8
                    pe.op(lambda e, c=c, c8=c8: e.transpose(out=pT[:, c8 * 128:c8 * 128 + T], in_=ynb[:T, c * 128:(c + 1) * 128],
                                                            identity=ident[:T, :T]), reads=[ynb.b, ident.b], writes=[pT.b])
                for c8 in range(8):
                    c = r * 8 + c8
                    act.op(lambda e, c=c, c8=c8: e.activation(out=ynT[:, c, :T], in_=pT[:, c8 * 128:c8 * 128 + T], func=AF.Identity,
                                                              scale=normw[:, c:c + 1]), reads=[pT.b, normw.b], writes=[ynT.b])

            def obody(k, w):
                for h in range(2):
                    pe.op(lambda e, h=h: e.matmul(pA[:T, h * 512:(h + 1) * 512], lhsT=ynT[:, k, :T], rhs=w[:, h * 512:(h + 1) * 512],
                                                  start=(k == 0), stop=(k == 15)), reads=[ynT.b, w.b], writes=[pAh[0]])
            stream(16, r_out, lambda k: outp_s[k], obody)
            residual(T, xc, pAh[0], xb)
            layer_norm(T, xb, 2, xa)
            ffn(T, 1)
            residual(T, xa, pAh[0], xb)
            layer_norm(T, xb, 3, xc)
            ple(T, 1, s, t0, xc, xa)
            sp.dma(y_d[s][t0:t0 + T, :], xa[:T, :], oslot, reads=[xa.b])

        def run_seq(s, nblk, T):
            ftail.load(ftail[:], ftail_d[s][:, :, :, :, :])
            ctail.load(ctail[:], ctail_d[s][:, :, :])
            ST.load(ST[:], st_d[s][:, :])
            invc.load(invc[:], invc_d[s][:, :, :])
            act.op(lambda e: e.activation(out=STb[:, :], in_=ST[:, :], func=AF.Copy), reads=[ST.b], writes=[STb.b])
            for b in range(nblk):
                block(s, b * T, T, b == 0)
            sp.dma(ftail_o[s][:, :, :, :, :], ftail[:], oslot, reads=[ftail.b])
            sp.dma(ctail_o[s][:, :, :], ctail[:], oslot, reads=[ctail.b])
            sp.dma(st_o[s][:, :], ST[:], oslot, reads=[ST.b])
            sp.dma(pool_o[s][:, :], x_d[s][Ls[s] - 15:Ls[s], :], oslot)

        run_seq(0, NBLK, 128)
        run_seq(1, 1, LS)
        fin = Buf("fin")
        fin.rd = {oslot.key: oslot.n}
        P.finish([fin, xtok[0], ftail, ctail, ST] if False else [fin])
    return nc


def _fm(a, nchunk):
    r, c = a.shape
    return np.ascontiguousarray(a.reshape(r, nchunk, 128).transpose(2, 1, 0))


def _prep_inputs(x_prompt, x_sample, p_prompt, p_sample, cache_pool, cache_ssm_conv, state_ssm, cache_ffn_conv,
                 pool_w, pool_scale, ssm_in_proj, ssm_conv_w, ssm_conv_b, ssm_dt_bias, ssm_A_log, ssm_D, ssm_norm_w,
                 ssm_out_proj, ln_mix_g, ln_mix_b, ffn_up, ffn_conv_w, ffn_conv_b, ffn_down, ln_ffn_g, ln_ffn_b,
                 ple_proj, ple_gate_w, ple_gate_b, n_cores=8):
    f = lambda a: np.ascontiguousarray(np.asarray(a, dtype=np.float32))
    L = x_prompt.shape[1]
    LS = x_sample.shape[1]
    shared = {
        "pool_w": f(pool_w[0]), "pool_scale": f(pool_scale), "in_proj": f(ssm_in_proj[0]),
        "scw": f(np.asarray(ssm_conv_w[0]).reshape(4, 32, 128).transpose(2, 1, 0)),
        "scb": f(np.asarray(ssm_conv_b[0]).reshape(32, 128).T),
        "dtb": f(ssm_dt_bias), "alog": f(ssm_A_log), "dsk": f(ssm_D),
        "normw": f(np.asarray(ssm_norm_w[0]).reshape(16, 128).T),
        "out_proj": f(ssm_out_proj[0]),
        "lng": f(np.stack([ln_mix_g[0], ln_ffn_g[0], ln_mix_g[1], ln_ffn_g[1]])),
        "lnb": f(np.stack([ln_mix_b[0], ln_ffn_b[0], ln_mix_b[1], ln_ffn_b[1]])),
        "fcw": f(np.asarray(ffn_conv_w).reshape(2, 3, 2, 22, 128).transpose(4, 0, 2, 3, 1)),
        "fcb": f(np.asarray(ffn_conv_b).reshape(2, 2, 22, 128).transpose(3, 0, 1, 2)),
        "gate_b": f(ple_gate_b),
    }
    for i in range(2):
        shared["ffn_up%d" % i] = f(ffn_up[i])
        shared["ffn_down%d" % i] = f(ffn_down[i])
        shared["ple_proj%d" % i] = f(ple_proj[i])
        shared["gate_w%d" % i] = f(ple_gate_w[i])
    wins = (2, 4, 8, 16)
    invc_p = np.zeros((128, 8, 16), np.float32)
    invc_s = np.zeros((128, 8, 16), np.float32)
    for c in range(8):
        w = wins[c // 2]
        invc_p[:, c, :] = 1.0 / np.minimum(w, np.arange(16) + 1)
        invc_s[:, c, :] = 1.0 / w
    in_maps = []
    for core in range(n_cores):
        bp = core % x_prompt.shape[0]
        bs = core % x_sample.shape[0]
        m = dict(shared)
        xp = f(x_prompt[bp])
        m["x0"] = xp
        m["xT0"] = _fm(np.concatenate([np.zeros((15, 1024), np.float32), xp], 0), 8)
        xs = f(x_sample[bs])
        m["x1"] = xs
        m["xT1"] = _fm(np.concatenate([f(cache_pool[0, bs]), xs], 0), 8)
        for i in range(2):
            m["pT0_%d" % i] = _fm(f(p_prompt[i, bp]), 2)
            m["pT1_%d" % i] = _fm(f(p_sample[i, bs]), 2)
        m["ftail0"] = np.zeros((128, 2, 2, 22, 2), np.float32)
        m["ftail1"] = f(np.asarray(cache_ffn_conv[:, bs]).reshape(2, 2, 2, 22, 128).transpose(4, 0, 2, 3, 1))
        m["ctail0"] = np.zeros((128, 32, 3), np.float32)
        m["ctail1"] = f(np.asarray(cache_ssm_conv[0, bs]).reshape(3, 32, 128).transpose(2, 1, 0))
        m["st0"] = np.zeros((128, 2048), np.float32)
        m["st1"] = f(np.asarray(state_ssm[0, bs]).reshape(2048, 128).T)
        m["invc0"] = invc_p
        m["invc1"] = invc_s
        in_maps.append(m)
    return in_maps, L, LS


def _unft(a):
    return np.ascontiguousarray(a.transpose(1, 4, 2, 3, 0).reshape(2, 2, 5632))


def _assemble(results, nb_p, nb_s):
    y_p = np.stack([results[b]["y0"] for b in range(nb_p)])
    y_s = np.stack([results[b]["y1"] for b in range(nb_s)])
    pool_p = np.stack([results[b]["pool_o0"] for b in range(nb_p)])[None]
    pool_s = np.stack([results[b]["pool_o1"] for b in range(nb_s)])[None]
    unc = lambda a: np.ascontiguousarray(a.transpose(2, 1, 0).reshape(3, 4096))
    sconv_p = np.stack([unc(results[b]["ctail_o0"]) for b in range(nb_p)])[None]
    sconv_s = np.stack([unc(results[b]["ctail_o1"]) for b in range(nb_s)])[None]
    uns = lambda a: np.ascontiguousarray(a.T.reshape(32, 64, 128))
    st_p = np.stack([uns(results[b]["st_o0"]) for b in range(nb_p)])[None]
    st_s = np.stack([uns(results[b]["st_o1"]) for b in range(nb_s)])[None]
    ffn_p = np.stack([_unft(results[b]["ftail_o0"]) for b in range(nb_p)], 1)
    ffn_s = np.stack([_unft(results[b]["ftail_o1"]) for b in range(nb_s)], 1)
    return (y_p, y_s, pool_p, pool_s, sconv_p, sconv_s, st_p, st_s, ffn_p, ffn_s)


_NC_CACHE = {}


def kernel(**inputs):
    in_maps, L, LS = _prep_inputs(**inputs)
    key = (L, LS)
    if key not in _NC_CACHE:
        _NC_CACHE[key] = build_program(L, LS)
    nc = _NC_CACHE[key]
    res = run_bass_kernel_spmd(nc, in_maps, core_ids=list(range(8)))
    outs = _assemble(res.results, inputs["x_prompt"].shape[0], inputs["x_sample"].shape[0])
    return tuple(np.asarray(o, dtype=np.float32) for o in outs)
```

```python
import numpy as np
import concourse.bass as bass
import concourse.mybir as mybir
from concourse.bass_utils import run_bass_kernel_spmd

F32 = mybir.dt.float32
BF16 = mybir.dt.bfloat16
AF = mybir.ActivationFunctionType
ALU = mybir.AluOpType


class Buf:
    __slots__ = ("name", "lw", "rd")

    def __init__(self, name):
        self.name = name
        self.lw = {}
        self.rd = {}


class Eng:
    def __init__(self, prog, name, handle, sem, is_pe=False, compute=True):
        self.prog = prog
        self.name = name
        self.h = handle
        self.sem = sem
        self.n = 0
        self.seen = {}
        self.q = []
        self.is_pe = is_pe
        self.compute = compute

    def _sync(self, deps):
        for key, cnt in deps.items():
            if key == self.name and self.is_pe:
                continue
            if self.seen.get(key, 0) >= cnt:
                continue
            self.seen[key] = cnt
            sem = self.prog.sems[key]
            self.q.append(("w", sem, cnt))

    def op(self, fn, reads=(), writes=()):
        deps = {}
        for b in reads:
            for k, c in b.lw.items():
                if deps.get(k, 0) < c:
                    deps[k] = c
        for b in writes:
            for k, c in b.lw.items():
                if deps.get(k, 0) < c:
                    deps[k] = c
            for k, c in b.rd.items():
                if deps.get(k, 0) < c:
                    deps[k] = c
        self._sync(deps)
        self.n += 1
        self.q.append(("o", fn, self.sem, 1))
        for b in reads:
            if b.rd.get(self.name, 0) < self.n:
                b.rd[self.name] = self.n
        for b in writes:
            b.lw = {self.name: self.n}
            b.rd = {}

    def dma(self, out_ap, in_ap, slot, reads=(), writes=()):
        deps = {}
        for b in reads:
            for k, c in b.lw.items():
                if deps.get(k, 0) < c:
                    deps[k] = c
        for b in writes:
            for k, c in b.lw.items():
                if deps.get(k, 0) < c:
                    deps[k] = c
            for k, c in b.rd.items():
                if deps.get(k, 0) < c:
                    deps[k] = c
        self._sync(deps)
        slot.n += 16
        self.q.append(("d", out_ap, in_ap, slot.sem))
        for b in reads:
            if b.rd.get(slot.key, 0) < slot.n:
                b.rd[slot.key] = slot.n
        for b in writes:
            b.lw = dict(b.lw) if False else {slot.key: slot.n}
            b.rd = {}

    def collective(self, tin, tout, slot, reads=(), writes=()):
        deps = {}
        for b in reads:
            for k, c in b.lw.items():
                if deps.get(k, 0) < c:
                    deps[k] = c
        for b in writes:
            for k, c in list(b.lw.items()) + list(b.rd.items()):
                if deps.get(k, 0) < c:
                    deps[k] = c
        self._sync(deps)
        slot.n += 1
        self.q.append(("c", tin, tout, slot.sem))
        for b in reads:
            b.rd[slot.key] = slot.n
        for b in writes:
            b.lw = {slot.key: slot.n}
            b.rd = {}

    def wait_all(self, bufs):
        deps = {}
        for b in bufs:
            for k, c in list(b.lw.items()) + list(b.rd.items()):
                if deps.get(k, 0) < c:
                    deps[k] = c
        self._sync(deps)

    def emit(self, h):
        for it in self.q:
            if it[0] == "w":
                h.wait_ge(it[1], it[2])
            elif it[0] == "o":
                ins = it[1](h)
                ins.then_inc(it[2], it[3])
            elif it[0] == "c":
                h.collective_compute("AllGather", ALU.bypass, replica_groups=[list(range(8))],
                                     ins=[it[1].ap().opt()], outs=[it[2].ap().opt()]).then_inc(it[3], 1)
            else:
                h.dma_start(out=it[1], in_=it[2]).then_inc(it[3], 16)


class Slot:
    def __init__(self, prog, key, sem):
        self.key = key
        self.sem = sem
        self.n = 0


class Prog:
    def __init__(self, nc, stack):
        self.nc = nc
        self.stack = stack
        self.sems = {}
        self.nslots = 0
        mk = lambda nm: stack.enter_context(nc.semaphore(nm))
        self.pe = Eng(self, "pe", nc.tensor, mk("s_pe"), is_pe=True)
        self.act = Eng(self, "act", nc.scalar, mk("s_act"))
        self.dve = Eng(self, "dve", nc.vector, mk("s_dve"))
        self.pool = Eng(self, "pool", nc.gpsimd, mk("s_pool"))
        self.sp = Eng(self, "sp", nc.sync, None, compute=False)
        for e in (self.pe, self.act, self.dve, self.pool):
            self.sems[e.name] = e.sem

    def slot(self, name=None):
        self.nslots += 1
        key = "dma%d" % self.nslots
        sem = self.stack.enter_context(self.nc.semaphore("s_" + key))
        self.sems[key] = sem
        return Slot(self, key, sem)

    def sb(self, name, shape, dtype):
        return self.stack.enter_context(self.nc.sbuf_tensor(name, list(shape), dtype))

    def ps(self, name, shape, dtype):
        return self.stack.enter_context(self.nc.psum_tensor(name, list(shape), dtype))

    def finish(self, final_bufs):
        self.sp.wait_all(final_bufs)
        with self.nc.Block() as block:
            @block.tensor
            def _(e):
                self.pe.emit(e)

            @block.scalar
            def _(e):
                self.act.emit(e)

            @block.vector
            def _(e):
                self.dve.emit(e)

            @block.gpsimd
            def _(e):
                self.pool.emit(e)

            @block.sync
            def _(e):
                self.sp.emit(e)

D = 1024
DFF = 2816
NJ = 22
DIN = 2048
CONVD = 4096
NH = 32
ALPHA_ = (2 * 2) ** 0.25
LN_EPS_ = 1e-5
RMS_EPS_ = 1e-5


def build_program(L, LS=32):
    import contextlib
    nc = bass.Bass("TRN2", target_bir_lowering=False)
    NOWN = L // 128
    NST = 3 * NOWN
    NBLK = NST + 1 + NOWN
    LP = NBLK * 128

    def din(name, shape, dt=F32):
        return nc.dram_tensor(name, list(shape), dt, kind="ExternalInput").ap()

    def dout(name, shape, dt=F32):
        return nc.dram_tensor(name, list(shape), dt, kind="ExternalOutput").ap()

    def dint(name, shape, dt=BF16):
        return nc.dram_tensor(name, list(shape), dt).ap()

    Ls = [LP, LS]
    xT_d = [din("xT%d" % s, [128, 8, 15 + Ls[s]]) for s in range(2)]
    x_d = [din("x%d" % s, [Ls[s], D]) for s in range(2)]
    pT_d = [[din("pT%d_%d" % (s, i), [128, 2, Ls[s]]) for i in range(2)] for s in range(2)]
    ftail_d = [din("ftail%d" % s, [128, 2, 2, NJ, 2]) for s in range(2)]
    ctail_d = [din("ctail%d" % s, [128, 32, 3]) for s in range(2)]
    st_d = [din("st%d" % s, [128, DIN]) for s in range(2)]
    invc_d = [din("invc0", [4, 128, 8, 16]), din("invc1", [128, 8, 16])]
    pool_w_d = din("pool_w", [4, 256, 256])
    pool_scale_d = din("pool_scale", [1, D])
    in_proj_d = din("in_proj", [D, 6176])
    scw_d = din("scw", [128, 32, 4])
    scb_d = din("scb", [128, 32])
    dtb_d = din("dtb", [1, NH])
    alog_d = din("alog", [1, NH])
    dsk_d = din("dsk", [1, NH])
    normw_d = din("normw", [128, 16])
    out_proj_d = din("out_proj", [DIN, D])
    lng_d = din("lng", [4, D])
    lnb_d = din("lnb", [4, D])
    ffn_up_d = [din("ffn_up%d" % i, [D, 2 * DFF]) for i in range(2)]
    fcw_d = din("fcw", [128, 2, 2, NJ, 3])
    fcb_d = din("fcb", [128, 2, 2, NJ])
    ffn_down_d = [din("ffn_down%d" % i, [DFF, D]) for i in range(2)]
    ple_proj_d = [din("ple_proj%d" % i, [256, D]) for i in range(2)]
    gate_w_d = [din("gate_w%d" % i, [D, D]) for i in range(2)]
    gate_b_d = din("gate_b", [2, D])
    y_d = [dout("y0", [L, D]), dout("y1", [LS, D])]
    vfl_d = din("vfl", [128, NBLK])
    pool_o = [dout("pool_o%d" % s, [15, D]) for s in range(2)]
    ftail_o = [dout("ftail_o%d" % s, [128, 2, 2, NJ, 2]) for s in range(2)]
    ctail_o = [dout("ctail_o%d" % s, [128, 32, 3]) for s in range(2)]
    st_o = [dout("st_o%d" % s, [128, DIN]) for s in range(2)]
    wup_s = [dint("wup_s%d" % i, [NJ, 128, 8, 256]) for i in range(2)]
    wdn_s = [dint("wdn_s%d" % i, [NJ, 128, D]) for i in range(2)]
    gate_s = [dint("gate_s%d" % i, [8, 128, D]) for i in range(2)]
    proj_s = [dint("proj_s%d" % i, [128, 2, D]) for i in range(2)]
    gb_s = dint("gb_s", [2, D])
    inz_s = dint("inz_s", [8, 128, DIN])
    inx_s = dint("inx_s", [32, 128, 8, 128])
    indt_s = dint("indt_s", [128, 8, NH])
    outp_s = dint("outp_s", [16, 128, D])

    with contextlib.ExitStack() as stack:
        P = Prog(nc, stack)
        pe, act, dve, pool, sp = P.pe, P.act, P.dve, P.pool, P.sp

        class T_:
            def __init__(self, name, shape, dt, psum=False):
                self.t = (P.ps if psum else P.sb)(name, shape, dt)
                self.b = Buf(name)

            def __getitem__(self, k):
                return self.t[k]

            def store(self, dst, src, writes=()):
                if not hasattr(self, "sslot"):
                    self.sslot = P.slot()
                sp.dma(dst, src, self.sslot, reads=[self.b], writes=list(writes))

            def load(self, dst, src, reads=()):
                if not hasattr(self, "slot"):
                    self.slot = P.slot()
                sp.dma(dst, src, self.slot, reads=list(reads), writes=[self.b])

        cnt = [0]

        def SB(shape, dt=F32, name=None):
            cnt[0] += 1
            return T_(name or ("t%d" % cnt[0]), shape, dt)

        wbuf = Buf("wscratch")
        cs = P.slot()
        for i in range(2):
            for j in range(NJ):
                for gu in range(2):
                    src = ffn_up_d[i][:, gu * DFF + j * 128: gu * DFF + (j + 1) * 128].rearrange("(k p) n -> p k n", p=128)
                    pool.dma(wup_s[i][j, :, :, gu * 128:(gu + 1) * 128], src, cs, writes=[wbuf])
            for j0 in range(0, NJ, 2):
                pool.dma(wdn_s[i][j0:j0 + 2].rearrange("j p n -> p j n"),
                         ffn_down_d[i][j0 * 128:(j0 + 2) * 128, :].rearrange("(j p) n -> p j n", p=128), cs, writes=[wbuf])
            for k0 in range(0, 8, 2):
                pool.dma(gate_s[i][k0:k0 + 2].rearrange("j p n -> p j n"),
                         gate_w_d[i][k0 * 128:(k0 + 2) * 128, :].rearrange("(j p) n -> p j n", p=128), cs, writes=[wbuf])
            pool.dma(proj_s[i][:, :, :], ple_proj_d[i].rearrange("(k p) n -> p k n", p=128), cs, writes=[wbuf])
        pool.dma(gb_s[:, :], gate_b_d[:, :], cs, writes=[wbuf])
        for k in range(8):
            pool.dma(inz_s[k], in_proj_d[k * 128:(k + 1) * 128, 0:DIN], cs, writes=[wbuf])
        for c in range(32):
            pool.dma(inx_s[c], in_proj_d[:, DIN + c * 128: DIN + (c + 1) * 128].rearrange("(k p) n -> p k n", p=128),
                     cs, writes=[wbuf])
        pool.dma(indt_s[:, :, :], in_proj_d[:, 6144:6176].rearrange("(k p) n -> p k n", p=128), cs, writes=[wbuf])
        for k0 in range(0, 16, 2):
            pool.dma(outp_s[k0:k0 + 2].rearrange("j p n -> p j n"),
                     out_proj_d[k0 * 128:(k0 + 2) * 128, :].rearrange("(j p) n -> p j n", p=128), cs, writes=[wbuf])

        ld = P.slot()
        identf = SB([128, 128]); ident = SB([128, 128], BF16)
        tri = SB([128, 128]); trib = SB([128, 128], BF16); onesf = SB([128, 128]); mstr = SB([128, 128], BF16)
        mtmp = identf
        onesb = SB([1, 128], BF16)
        pool.op(lambda e: e.memset(onesf[:], 1.0), writes=[onesf.b])
        pool.op(lambda e: e.memset(onesb[:], 1.0), writes=[onesb.b])
        pool.op(lambda e: e.affine_select(out=identf[:], in_=onesf[:], pattern=[[-1, 128]], compare_op=ALU.is_equal,
                                          fill=0.0, base=0, channel_multiplier=1), reads=[onesf.b], writes=[identf.b])
        dve.op(lambda e: e.tensor_copy(out=ident[:], in_=identf[:]), reads=[identf.b], writes=[ident.b])
        pool.op(lambda e: e.affine_select(out=tri[:], in_=onesf[:], pattern=[[1, 128]], compare_op=ALU.is_ge,
                                          fill=0.0, base=0, channel_multiplier=-1), reads=[onesf.b], writes=[tri.b])
        dve.op(lambda e: e.tensor_copy(out=trib[:], in_=tri[:]), reads=[tri.b], writes=[trib.b])
        pool.op(lambda e: e.affine_select(out=mtmp[:], in_=onesf[:], pattern=[[-1, 128]], compare_op=ALU.is_gt,
                                          fill=0.0, base=0, channel_multiplier=1), reads=[onesf.b], writes=[mtmp.b])
        dve.op(lambda e: e.tensor_copy(out=mstr[:], in_=mtmp[:]), reads=[mtmp.b], writes=[mstr.b])

        lng = [SB([128, D]) for _ in range(4)]
        lnb = [SB([128, D]) for _ in range(4)]
        for q in range(4):
            lng[q].load(lng[q][:], lng_d[q:q + 1, :].partition_broadcast(128))
            lnb[q].load(lnb[q][:], lnb_d[q:q + 1, :].partition_broadcast(128))
        gbb = SB([1, 2, D], BF16)
        gbb.load(gbb[:], gb_s[:, :].rearrange("(o a) n -> o a n", o=1), reads=[wbuf])
        dtb = SB([128, NH]); abc = SB([128, NH]); dbc = SB([128, NH])
        dtb.load(dtb[:], dtb_d[0:1, :].partition_broadcast(128))
        abc.load(abc[:], alog_d[0:1, :].partition_broadcast(128))
        dbc.load(dbc[:], dsk_d[0:1, :].partition_broadcast(128))
        act.op(lambda e: e.activation(out=abc[:], in_=abc[:], func=AF.Exp), reads=[abc.b], writes=[abc.b])
        dve.op(lambda e: e.tensor_scalar_mul(out=abc[:], in0=abc[:], scalar1=-1.0), reads=[abc.b], writes=[abc.b])
        scw = SB([128, 32, 4]); scb = SB([128, 32]); normw = SB([128, 16])
        fcw = SB([128, 2, 2, NJ, 3]); fcb = SB([128, 2, 2, NJ])
        scw.load(scw[:], scw_d[:, :, :])
        scb.load(scb[:], scb_d[:, :])
        normw.load(normw[:], normw_d[:, :])
        fcw.load(fcw[:], fcw_d[:, :, :, :, :])
        fcb.load(fcb[:], fcb_d[:, :, :, :])
        poolw = SB([128, 8, 256], BF16)
        POOLW_SETUP = True
        wdt = SB([128, 8, NH], BF16)
        wdt.load(wdt[:], indt_s[:, :, :], reads=[wbuf])

        pA = T_("pA", [128, 2048], F32, psum=True)
        pB = T_("pB", [128, 1024], F32, psum=True)
        pT = T_("pT", [128, 1024], BF16, psum=True)
        pS = T_("pS", [128, 512], F32, psum=True)
        pAh = [Buf("pA0"), Buf("pA1")]
        pBh = [Buf("pB0"), Buf("pB1")]
        pA23 = [Buf("pA2"), Buf("pA3")]

        class Ring:
            def __init__(self, shape, depth):
                self.t = [SB(shape, BF16) for _ in range(depth)]
                self.sl = [P.slot() for _ in range(depth)]
                self.depth = depth

            def load(self, i, src):
                t = self.t[i % self.depth]
                sp.dma(t[:], src, self.sl[i % self.depth], reads=[wbuf], writes=[t.b])
                return t

        r_up = Ring([128, 8, 256], 2)
        r_dn = Ring([128, D], 2)
        r_gate = Ring([128, D], 2)
        r_proj = Ring([128, 2, D], 1)
        r_inz = Ring([128, DIN], 2)
        r_inx = Ring([128, 8, 128], 2)
        r_out = Ring([128, D], 2)

        def stream(n, ring, srcf, body):
            tl = {}
            for i in range(min(ring.depth, n)):
                tl[i] = ring.load(i, srcf(i))
            for i in range(n):
                body(i, tl[i])
                if i + ring.depth < n:
                    tl[i + ring.depth] = ring.load(i + ring.depth, srcf(i + ring.depth))

        xin = SB([128, 8, 143])
        xtok = [SB([128, D]) for _ in range(3)]
        sA = SB([128, 2, 143]); sBf = SB([128, 2, 143]); t16 = SB([128, 2, 16])
        pooled = SB([128, 8, 128], BF16)
        xT = SB([128, 8, 128], BF16)
        xb16 = SB([128, D], BF16)
        U1 = SB([128, 32 * 128], BF16)
        hb = [[SB([128, 130]) for _ in range(2)] for _ in range(2)]
        acc = [[SB([128, 128]) for _ in range(2)] for _ in range(2)]
        sg = SB([128, 128])
        pst = SB([128, 2, 128]); pTb = SB([128, 2, 128], BF16)
        gate = SB([128, D]); tmpt = SB([128, D])
        stats = SB([128, 2, 6]); mv = SB([128, 8])
        ftail = SB([128, 2, 2, NJ, 2]); ctail = SB([128, 32, 3])
        invc = SB([128, 8, 16])
        sz = SB([128, DIN]); ybuf = SB([128, DIN]); ytmp = SB([128, DIN])
        xsT = SB([128, 16, 128], BF16); BT = SB([128, 8, 128], BF16); CT = SB([128, 8, 128], BF16)
        cbuf = [SB([128, 131]) for _ in range(2)]; cacc = [SB([128, 128]) for _ in range(2)]
        xdt = SB([128, DIN], BF16); xdtd = SB([128, DIN], BF16); Btok = SB([128, 1024], BF16)
        cbm = SB([128, 8, 128], BF16)
        rseg = [SB([128, 8, 128], BF16) for _ in range(2)]
        _ex = SB([128, 8, 128], BF16); expL = [_ex, _ex]
        ST = SB([128, DIN]); STb = SB([128, DIN], BF16)
        ynb = SB([128, DIN], BF16); ynT = SB([128, 16, 128], BF16)
        sm = SB([128, 8, NH])
        ss = SB([128, 4, 8])
        io = P.slot(); io2 = P.slot(); oslot = P.slot()
        vfl = SB([128, NBLK]); vfl.load(vfl[:], vfl_d[:, :])
        invt = [SB([128, 8, 16]) for _ in range(4)]
        for q in range(4):
            invt[q].load(invt[q][:], invc_d[0][q])
        scr = Buf("scratch")
        sz.load(sz[:].rearrange("p (c d) -> p c d", d=256), pool_w_d.rearrange("g (h p) d -> p (g h) d", p=128))
        ytmp.load(ytmp[:, 0:D], pool_scale_d[0:1, :].partition_broadcast(128))
        for c in range(8):
            g = c // 2
            dve.op(lambda e, c=c, g=g: e.tensor_tensor(out=poolw[:, c, :], in0=sz[:, c * 256:(c + 1) * 256],
                                                       in1=ytmp[:, g * 256:(g + 1) * 256], op=ALU.mult),
                   reads=[sz.b, ytmp.b], writes=[poolw.b])
        outbufs = []

        def layer_norm(T, u, q, xo, use_xT=True):
            for h in range(2):
                dve.op(lambda e, h=h: e.bn_stats(out=stats[:T, h, :], in_=u[:T, h * 512:(h + 1) * 512]),
                       reads=[u.b], writes=[stats.b])
            dve.op(lambda e: e.bn_aggr(out=mv[:T, 0:2], in_=stats[:T].rearrange("p a b -> p (a b)")), reads=[stats.b], writes=[mv.b])
            dve.op(lambda e: e.tensor_scalar_add(out=mv[:T, 2:3], in0=mv[:T, 1:2], scalar1=LN_EPS_), reads=[mv.b], writes=[mv.b])
            act.op(lambda e: e.activation(out=mv[:T, 3:4], in_=mv[:T, 2:3], func=AF.Sqrt), reads=[mv.b], writes=[mv.b])
            dve.op(lambda e: e.reciprocal(out=mv[:T, 4:5], in_=mv[:T, 3:4]), reads=[mv.b], writes=[mv.b])
            dve.op(lambda e: e.scalar_tensor_tensor(out=mv[:T, 5:6], in0=mv[:T, 0:1], scalar=-1.0, in1=mv[:T, 4:5],
                                                    op0=ALU.mult, op1=ALU.mult), reads=[mv.b], writes=[mv.b])
            act.op(lambda e: e.activation(out=tmpt[:T, :], in_=u[:T, :], func=AF.Identity, scale=mv[:T, 4:5], bias=mv[:T, 5:6]),
                   reads=[u.b, mv.b], writes=[tmpt.b])
            pool.op(lambda e: e.tensor_tensor(out=tmpt[:T, :], in0=tmpt[:T, :], in1=lng[q][:T, :], op=ALU.mult),
                    reads=[tmpt.b, lng[q].b], writes=[tmpt.b])
            pool.op(lambda e: e.tensor_tensor(out=xo[:T, :], in0=tmpt[:T, :], in1=lnb[q][:T, :], op=ALU.add),
                    reads=[tmpt.b, lnb[q].b], writes=[xo.b])
            to_T(T, xo)

        def to_T(T, xo):
            act.op(lambda e: e.activation(out=xb16[:T, :], in_=xo[:T, :], func=AF.Copy), reads=[xo.b], writes=[xb16.b])
            for c in range(8):
                pe.op(lambda e, c=c: e.transpose(out=pT[:, c * 128:c * 128 + T], in_=xb16[:T, c * 128:(c + 1) * 128],
                                                 identity=ident[:T, :T]), reads=[xb16.b, ident.b], writes=[pT.b])
            dve.op(lambda e: e.tensor_copy(out=xT[:, :, :T], in_=pT[:].rearrange("p (c t) -> p c t", t=128)[:, :, :T]),
                   reads=[pT.b], writes=[xT.b])

        def ffn(T, i):
            hT = U1
            st_ = {}

            def up(j, w):
                par = j % 2
                if par == 0:
                    pp_, po_, pbs_ = pB, 0, [[pBh[0]], [pBh[1]]]
                else:
                    pp_, po_, pbs_ = pA, 1024, [[pA23[0], pAh[1]], [pA23[1], pAh[1]]]
                for gu in range(2):
                    for k in range(8):
                        pe.op(lambda e, gu=gu, k=k: e.matmul(pp_[:, po_ + gu * 512:po_ + gu * 512 + T], lhsT=w[:, k, gu * 128:(gu + 1) * 128],
                                                             rhs=xT[:, k, :T], start=(k == 0), stop=(k == 7)),
                              reads=[w.b, xT.b], writes=pbs_[gu])
                for gu in range(2):
                    h_ = hb[par][gu]; a_ = acc[par][gu]
                    act.op(lambda e, gu=gu, h_=h_: e.activation(out=h_[:, 2:2 + T], in_=pp_[:, po_ + gu * 512:po_ + gu * 512 + T], func=AF.Copy),
                           reads=pbs_[gu], writes=[h_.b])
                    pool.op(lambda e, gu=gu, h_=h_: e.tensor_copy(out=h_[:, 0:2], in_=ftail[:, i, gu, j, :]),
                            reads=[ftail.b], writes=[h_.b])
                    pool.op(lambda e, gu=gu, h_=h_: e.tensor_copy(out=ftail[:, i, gu, j, :], in_=h_[:, T:T + 2]),
                            reads=[h_.b], writes=[ftail.b])
                    dve.op(lambda e, gu=gu, h_=h_, a_=a_: e.tensor_scalar(out=a_[:, :T], in0=h_[:, 2:2 + T],
                                                                          scalar1=fcw[:, i, gu, j, 2:3], scalar2=fcb[:, i, gu, j:j + 1],
                                                                          op0=ALU.mult, op1=ALU.add),
                           reads=[h_.b, fcw.b, fcb.b], writes=[a_.b])
                    for tap in (1, 0):
                        dve.op(lambda e, gu=gu, h_=h_, a_=a_, tap=tap: e.scalar_tensor_tensor(
                            out=a_[:, :T], in0=h_[:, tap:tap + T], scalar=fcw[:, i, gu, j, tap:tap + 1], in1=a_[:, :T],
                            op0=ALU.mult, op1=ALU.add), reads=[h_.b, fcw.b, a_.b], writes=[a_.b])
                act.op(lambda e: e.activation(out=sg[:, :T], in_=acc[par][0][:, :T], func=AF.Silu), reads=[acc[par][0].b], writes=[sg.b])
                pool.op(lambda e: e.tensor_tensor(out=hT[:, j * 128:j * 128 + T], in0=sg[:, :T], in1=acc[par][1][:, :T], op=ALU.mult),
                        reads=[sg.b, acc[par][1].b], writes=[hT.b])

            def down(j, w):
                for h in range(2):
                    pe.op(lambda e, h=h: e.matmul(pA[:T, h * 512:(h + 1) * 512], lhsT=hT[:, j * 128:j * 128 + T],
                                                  rhs=w[:, h * 512:(h + 1) * 512], start=(j == 0), stop=(j == NJ - 1)),
                          reads=[hT.b, w.b], writes=[pAh[0]])

            upt = {}; dnt = {}
            for jj in range(min(2, NJ)):
                upt[jj] = r_up.load(jj, wup_s[i][jj]); dnt[jj] = r_dn.load(jj, wdn_s[i][jj])
            for j in range(NJ):
                up(j, upt[j])
                down(j, dnt[j])
                if j + 2 < NJ:
                    upt[j + 2] = r_up.load(j + 2, wup_s[i][j + 2]); dnt[j + 2] = r_dn.load(j + 2, wdn_s[i][j + 2])

        def ple(T, i, s, t0, x2, xo):
            for h in range(2):
                pe.op(lambda e, h=h: e.matmul(pA[:T, h * 512:(h + 1) * 512], lhsT=onesb[0:1, :T], rhs=gbb[0:1, i, h * 512:(h + 1) * 512],
                                              start=True, stop=False), reads=[onesb.b, gbb.b], writes=[pAh[0]])

            def body(k, w):
                for h in range(2):
                    pe.op(lambda e, h=h: e.matmul(pA[:T, h * 512:(h + 1) * 512], lhsT=xT[:, k, :T], rhs=w[:, h * 512:(h + 1) * 512],
                                                  start=False, stop=(k == 7)), reads=[xT.b, w.b], writes=[pAh[0]])
            stream(8, r_gate, lambda k: gate_s[i][k], body)
            pst.load(pst[:, :, :T], pT_d[s][i][:, :, t0:t0 + T])
            pool.op(lambda e: e.tensor_copy(out=pTb[:, :, :T], in_=pst[:, :, :T]), reads=[pst.b], writes=[pTb.b])
            wp = r_proj.load(0, proj_s[i][:, :, :])
            for h in range(2):
                for k in range(2):
                    pe.op(lambda e, h=h, k=k: e.matmul(pA[:T, 1024 + h * 512:1024 + (h + 1) * 512], lhsT=pTb[:, k, :T],
                                                       rhs=wp[:, k, h * 512:(h + 1) * 512], start=(k == 0), stop=(k == 1)),
                          reads=[pTb.b, wp.b], writes=[pAh[1]])
            act.op(lambda e: e.activation(out=gate[:T, :], in_=pA[:T, 0:1024], func=AF.Sigmoid), reads=[pAh[0]], writes=[gate.b])
            dve.op(lambda e: e.tensor_tensor(out=gate[:T, :], in0=gate[:T, :], in1=pA[:T, 1024:2048], op=ALU.mult),
                   reads=[gate.b, pAh[1]], writes=[gate.b])
            pool.op(lambda e: e.tensor_tensor(out=xo[:T, :], in0=gate[:T, :], in1=x2[:T, :], op=ALU.add),
                    reads=[gate.b, x2.b], writes=[xo.b])

        def residual(T, xres, pbuf, uo):
            dve.op(lambda e: e.scalar_tensor_tensor(out=uo[:T, :], in0=xres[:T, :], scalar=ALPHA_, in1=pA[:T, 0:1024],
                                                    op0=ALU.mult, op1=ALU.add), reads=[xres.b, pbuf], writes=[uo.b])

        def bc3(ap2, T, n):
            return ap2.unsqueeze(2).to_broadcast([T, NH, n])

        def block_p1(s, t0, T, first, mode, blk, vcol=None, invtab=None):
            invc_ = invtab if invtab is not None else invc
            xa, xb, xc = xtok
            xin.load(xin[:, :, :15 + T], xT_d[s][:, :, t0:t0 + 15 + T])
            xa.load(xa[:T, :], x_d[s][t0:t0 + T, :])
            W = 15 + T
            for g in range(4):
                cs_ = slice(2 * g, 2 * g + 2)
                wsz = 2 ** (g + 1)
                pool.op(lambda e, cs_=cs_: e.tensor_tensor(out=sA[:, :, 1:W], in0=xin[:, cs_, 1:W], in1=xin[:, cs_, 0:W - 1], op=ALU.add),
                        reads=[xin.b], writes=[sA.b])
                cur, oth = sA, sBf
                sh = 1
                for lvl in range(g):
                    sh2 = sh * 2
                    lo = 2 * sh2 - 1
                    pool.op(lambda e, cur=cur, oth=oth, lo=lo, sh2=sh2: e.tensor_tensor(
                        out=oth[:, :, lo:W], in0=cur[:, :, lo:W], in1=cur[:, :, lo - sh2:W - sh2], op=ALU.add),
                        reads=[cur.b], writes=[oth.b])
                    cur, oth = oth, cur
                    sh = sh2
                dve.op(lambda e, cur=cur, cs_=cs_, wsz=wsz: e.scalar_tensor_tensor(
                    out=pooled[:, cs_, :T], in0=cur[:, :, 15:W], scalar=1.0 / wsz, in1=xin[:, cs_, 15:W],
                    op0=ALU.mult, op1=ALU.subtract), reads=[cur.b, xin.b], writes=[pooled.b])
                if first:
                    dve.op(lambda e, cur=cur, cs_=cs_: e.tensor_tensor(out=t16[:], in0=cur[:, :, 15:31], in1=invc_[:, cs_, :], op=ALU.mult),
                           reads=[cur.b, invc_.b], writes=[t16.b])
                    dve.op(lambda e, cs_=cs_: e.tensor_tensor(out=pooled[:, cs_, 0:16], in0=t16[:], in1=xin[:, cs_, 15:31], op=ALU.subtract),
                           reads=[t16.b, xin.b], writes=[pooled.b])
            for g in range(4):
                for h in range(2):
                    c = 2 * g + h
                    pe.op(lambda e, g=g, h=h, c=c: e.matmul(pA[:T, g * 256:(g + 1) * 256], lhsT=pooled[:, c, :T], rhs=poolw[:, c, :],
                                                            start=(h == 0), stop=(h == 1)),
                          reads=[pooled.b, poolw.b], writes=[pAh[0]])
            residual(T, xa, pAh[0], xb)
            layer_norm(T, xb, 0, xc)
            ffn(T, 0)
            residual(T, xc, pAh[0], xb)
            layer_norm(T, xb, 1, xa)
            ple(T, 0, s, t0, xa, xc)
            to_T(T, xc)

            if mode != "state":
                def zbody(k, w):
                    for cc in range(4):
                        pe.op(lambda e, cc=cc: e.matmul(pA[:T, cc * 512:(cc + 1) * 512], lhsT=xT[:, k, :T], rhs=w[:, cc * 512:(cc + 1) * 512],
                                                        start=(k == 0), stop=(k == 7)), reads=[xT.b, w.b], writes=[pAh[cc // 2]])
                stream(8, r_inz, lambda k: inz_s[k], zbody)
                act.op(lambda e: e.activation(out=sz[:T, :], in_=pA[:T, :], func=AF.Silu), reads=[pAh[0], pAh[1]], writes=[sz.b])
            for k in range(8):
                pe.op(lambda e, k=k: e.matmul(pS[:T, 0:NH], lhsT=xT[:, k, :T], rhs=wdt[:, k, :], start=(k == 0), stop=(k == 7)),
                      reads=[xT.b, wdt.b], writes=[pS.b])
            S_ = lambda n: sm[:T, n, :]
            dve.op(lambda e: e.tensor_tensor(out=S_(0), in0=pS[:T, 0:NH], in1=dtb[:T, :], op=ALU.add), reads=[pS.b, dtb.b], writes=[sm.b])
            dve.op(lambda e: e.scalar_tensor_tensor(out=S_(1), in0=S_(0), scalar=-1.0, in1=S_(0), op0=ALU.mult, op1=ALU.max), reads=[sm.b], writes=[sm.b])
            act.op(lambda e: e.activation(out=S_(1), in_=S_(1), func=AF.Exp, scale=-1.0), reads=[sm.b], writes=[sm.b])
            act.op(lambda e: e.activation(out=S_(1), in_=S_(1), func=AF.Ln, bias=1.0), reads=[sm.b], writes=[sm.b])
            dve.op(lambda e: e.tensor_scalar_max(out=S_(0), in0=S_(0), scalar1=0.0), reads=[sm.b], writes=[sm.b])
            dve.op(lambda e: e.tensor_tensor(out=S_(2), in0=S_(0), in1=S_(1), op=ALU.add), reads=[sm.b], writes=[sm.b])
            if vcol is not None:
                dve.op(lambda e: e.tensor_scalar_mul(out=S_(2), in0=S_(2), scalar1=vfl[:T, vcol:vcol + 1]), reads=[sm.b, vfl.b], writes=[sm.b])
            dve.op(lambda e: e.tensor_tensor(out=S_(3), in0=S_(2), in1=abc[:T, :], op=ALU.mult), reads=[sm.b, abc.b], writes=[sm.b])

            def xbody(c, w):
                par = c % 2
                for k in range(8):
                    pe.op(lambda e, k=k: e.matmul(pB[:, par * 512:par * 512 + T], lhsT=w[:, k, :], rhs=xT[:, k, :T],
                                                  start=(k == 0), stop=(k == 7)), reads=[w.b, xT.b], writes=[pBh[par]])
                cb_ = cbuf[par]; ca_ = cacc[par]
                act.op(lambda e: e.activation(out=cb_[:, 3:3 + T], in_=pB[:, par * 512:par * 512 + T], func=AF.Copy),
                       reads=[pBh[par]], writes=[cb_.b])
                pool.op(lambda e: e.tensor_copy(out=cb_[:, 0:3], in_=ctail[:, c, :]), reads=[ctail.b], writes=[cb_.b])
                pool.op(lambda e: e.tensor_copy(out=ctail[:, c, :], in_=cb_[:, T:T + 3]), reads=[cb_.b], writes=[ctail.b])
                dve.op(lambda e: e.tensor_scalar(out=ca_[:, :T], in0=cb_[:, 3:3 + T], scalar1=scw[:, c, 3:4], scalar2=scb[:, c:c + 1],
                                                 op0=ALU.mult, op1=ALU.add), reads=[cb_.b, scw.b, scb.b], writes=[ca_.b])
                for tap in (2, 1, 0):
                    dve.op(lambda e, tap=tap: e.scalar_tensor_tensor(out=ca_[:, :T], in0=cb_[:, tap:tap + T], scalar=scw[:, c, tap:tap + 1],
                                                                     in1=ca_[:, :T], op0=ALU.mult, op1=ALU.add),
                           reads=[cb_.b, scw.b, ca_.b], writes=[ca_.b])
                if c < 16:
                    dst, db = xsT[:, c, :T], xsT.b
                elif c < 24:
                    dst, db = BT[:, c - 16, :T], BT.b
                else:
                    dst, db = CT[:, c - 24, :T], CT.b
                act.op(lambda e: e.activation(out=dst, in_=ca_[:, :T], func=AF.Silu), reads=[ca_.b], writes=[db])
            stream(24 if mode == "state" else 32, r_inx, lambda c: inx_s[c], xbody)
            for r in range(2):
                for c8 in range(8):
                    c = r * 8 + c8
                    pe.op(lambda e, c=c, c8=c8: e.transpose(out=pT[:T, c8 * 128:(c8 + 1) * 128], in_=xsT[:, c, :T], identity=ident[:, :]),
                          reads=[xsT.b, ident.b], writes=[pT.b])
                hs = slice(r * 16, (r + 1) * 16)
                dve.op(lambda e, r=r, hs=hs: e.tensor_tensor(
                    out=xdt[:T, r * 1024:(r + 1) * 1024].rearrange("p (h d) -> p h d", d=64),
                    in0=pT[:T, :].rearrange("p (h d) -> p h d", d=64),
                    in1=sm[:T, 2, hs].unsqueeze(2).to_broadcast([T, 16, 64]), op=ALU.mult),
                    reads=[pT.b, sm.b], writes=[xdt.b])
                dve.op(lambda e, r=r, hs=hs: e.tensor_tensor(
                    out=ybuf[:T, r * 1024:(r + 1) * 1024].rearrange("p (h d) -> p h d", d=64),
                    in0=pT[:T, :].rearrange("p (h d) -> p h d", d=64),
                    in1=dbc[:T, hs].unsqueeze(2).to_broadcast([T, 16, 64]), op=ALU.mult),
                    reads=[pT.b, dbc.b], writes=[ybuf.b])
            for g in range(8):
                pe.op(lambda e, g=g: e.transpose(out=pT[:T, g * 128:(g + 1) * 128], in_=BT[:, g, :T], identity=ident[:, :]),
                      reads=[BT.b, ident.b], writes=[pT.b])
            dve.op(lambda e: e.tensor_copy(out=Btok[:T, :], in_=pT[:T, :]), reads=[pT.b], writes=[Btok.b])
            pe.op(lambda e: e.matmul(pS[:T, 32:64], lhsT=tri[:T, :T], rhs=sm[:T, 3, :], start=True, stop=True),
                  reads=[tri.b, sm.b], writes=[pS.b])
            pe.op(lambda e: e.matmul(pS[:, 64:96], lhsT=onesf[:T, :], rhs=sm[:T, 3, :], start=True, stop=True),
                  reads=[onesf.b, sm.b], writes=[pS.b])
            act.op(lambda e: e.activation(out=S_(4), in_=pS[:T, 32:64], func=AF.Exp), reads=[pS.b], writes=[sm.b])
            act.op(lambda e: e.activation(out=S_(5), in_=pS[:T, 32:64], func=AF.Copy), reads=[pS.b], writes=[sm.b])
            dve.op(lambda e: e.tensor_tensor(out=S_(6), in0=pS[:T, 64:96], in1=S_(5), op=ALU.subtract), reads=[pS.b, sm.b], writes=[sm.b])
            act.op(lambda e: e.activation(out=S_(6), in_=S_(6), func=AF.Exp), reads=[sm.b], writes=[sm.b])
            act.op(lambda e: e.activation(out=sm[:, 7, :], in_=pS[:, 64:96], func=AF.Exp), reads=[pS.b], writes=[sm.b])
            if False:
                dve.op(lambda e: e.tensor_tensor(out=S_(0), in0=pS[:T, 32:64], in1=carry[:T, :], op=ALU.add), reads=[pS.b, carry.b], writes=[sm.b])
                act.op(lambda e: e.activation(out=S_(0), in_=S_(0), func=AF.Exp), reads=[sm.b], writes=[sm.b])
                dve.op(lambda e: e.tensor_tensor(out=carry[:, :], in0=carry[:, :], in1=pS[:, 64:96], op=ALU.add), reads=[pS.b, carry.b], writes=[carry.b])
            if mode != "state":
                for g in range(8):
                    pe.op(lambda e, g=g: e.matmul(pB[:T, g * 128:g * 128 + T], lhsT=BT[:, g, :T], rhs=CT[:, g, :T], start=True, stop=True),
                          reads=[BT.b, CT.b], writes=[pBh[g // 4]])
                dve.op(lambda e: e.tensor_tensor(out=cbm[:T, :, :T], in0=pB[:T, :].rearrange("p (g q) -> p g q", q=128)[:, :, :T],
                                                 in1=tri[:T, :T].unsqueeze(1).to_broadcast([T, 8, T]), op=ALU.mult),
                       reads=[pBh[0], pBh[1], tri.b], writes=[cbm.b])
                MT = U1
                for r in range(4):
                    rs_ = rseg[r % 2]; ex_ = expL[r % 2]
                    pool.op(lambda e, r=r, rs_=rs_: e.tensor_tensor(
                        out=rs_[:T, :, :T], in0=sm[:T, 3, r * 8:(r + 1) * 8].unsqueeze(2).to_broadcast([T, 8, T]),
                        in1=trib[:T, :T].unsqueeze(1).to_broadcast([T, 8, T]), op=ALU.mult),
                        reads=[sm.b, trib.b], writes=[rs_.b])
                    hp = r % 2
                    for q2 in range(2):
                        pe.op(lambda e, rs_=rs_, q2=q2, hp=hp: e.matmul(
                            pA[:T, hp * 1024 + q2 * 512: hp * 1024 + q2 * 512 + 4 * T].rearrange("p (h q) -> p h q", q=T),
                            lhsT=mstr[:T, :T], rhs=rs_[:T, q2 * 4:(q2 + 1) * 4, :T], start=True, stop=True),
                            reads=[mstr.b, rs_.b], writes=[pAh[hp]])
                    for q2 in range(2):
                        act.op(lambda e, ex_=ex_, q2=q2, hp=hp: e.activation(
                            out=ex_[:T, q2 * 4:(q2 + 1) * 4, :T],
                            in_=pA[:T, hp * 1024 + q2 * 512: hp * 1024 + q2 * 512 + 4 * T].rearrange("p (h q) -> p h q", q=T),
                            func=AF.Exp), reads=[pAh[hp]], writes=[ex_.b])
                    for g2 in range(2):
                        gg = 2 * r + g2
                        dve.op(lambda e, ex_=ex_, g2=g2, gg=gg, r=r: e.tensor_tensor(
                            out=MT[:T, (r * 8 + g2 * 4) * 128:(r * 8 + g2 * 4 + 4) * 128].rearrange("p (h q) -> p h q", q=128)[:, :, :T],
                            in0=ex_[:T, g2 * 4:(g2 + 1) * 4, :T],
                            in1=cbm[:T, gg:gg + 1, :T].to_broadcast([T, 4, T]), op=ALU.mult),
                            reads=[ex_.b, cbm.b], writes=[MT.b])
                for g in range(8):
                    pe.op(lambda e, g=g: e.matmul(pA[:T, g * 256:(g + 1) * 256], lhsT=CT[:, g, :T], rhs=STb[:, g * 256:(g + 1) * 256],
                                                  start=True, stop=True), reads=[CT.b, STb.b], writes=[pAh[g // 4]])
                dve.op(lambda e: e.tensor_tensor(out=ytmp[:T, :].rearrange("p (h d) -> p h d", d=64),
                                                 in0=pA[:T, :].rearrange("p (h d) -> p h d", d=64),
                                                 in1=bc3(sm[:T, 4, :], T, 64), op=ALU.mult),
                       reads=[pAh[0], pAh[1], sm.b], writes=[ytmp.b])
                pool.op(lambda e: e.tensor_tensor(out=ybuf[:T, :], in0=ybuf[:T, :], in1=ytmp[:T, :], op=ALU.add),
                        reads=[ybuf.b, ytmp.b], writes=[ybuf.b])
                for h in range(NH):
                    pe.op(lambda e, h=h: e.matmul(pA[:T, h * 64:(h + 1) * 64], lhsT=MT[:T, h * 128:h * 128 + T], rhs=xdt[:T, h * 64:(h + 1) * 64],
                                                  start=True, stop=True), reads=[MT.b, xdt.b], writes=[pAh[h // 16]])
                dve.op(lambda e: e.tensor_tensor(out=ybuf[:T, :], in0=ybuf[:T, :], in1=pA[:T, :], op=ALU.add),
                       reads=[ybuf.b, pAh[0], pAh[1]], writes=[ybuf.b])
            pool.op(lambda e: e.tensor_tensor(out=xdtd[:T, :].rearrange("p (h d) -> p h d", d=64),
                                              in0=xdt[:T, :].rearrange("p (h d) -> p h d", d=64),
                                              in1=bc3(sm[:T, 6, :], T, 64), op=ALU.mult), reads=[xdt.b, sm.b], writes=[xdtd.b])
            for g in range(8):
                pe.op(lambda e, g=g: e.matmul(pA[:, g * 256:(g + 1) * 256], lhsT=Btok[:T, g * 128:(g + 1) * 128],
                                              rhs=xdtd[:T, g * 256:(g + 1) * 256], start=True, stop=True),
                      reads=[Btok.b, xdtd.b], writes=[pAh[g // 4]])
            pool.op(lambda e: e.tensor_tensor(out=ST[:, :].rearrange("p (h d) -> p h d", d=64),
                                              in0=ST[:, :].rearrange("p (h d) -> p h d", d=64),
                                              in1=sm[:, 7, :].unsqueeze(2).to_broadcast([128, NH, 64]), op=ALU.mult),
                    reads=[ST.b, sm.b], writes=[ST.b])
            dve.op(lambda e: e.tensor_tensor(out=ST[:, :], in0=ST[:, :], in1=pA[:, :], op=ALU.add),
                   reads=[ST.b, pAh[0], pAh[1]], writes=[ST.b])
            act.op(lambda e: e.activation(out=STb[:, :], in_=ST[:, :], func=AF.Copy), reads=[ST.b], writes=[STb.b])
            if False:
                xc.store(x3s[blk], xc[:, :], writes=[scr])
                sz.store(szs[blk], sz[:, :], writes=[scr])
                ybuf.store(yss[blk], ybuf[:, :], writes=[scr])
                CT.store(cts[blk], CT[:, :, :], writes=[scr])
                sm.store(ess[blk], sm[:, 0, :], writes=[scr])

        def block_p2(s, t0, T, mode, blk, yrow):
            xa, xb, xc = xtok
            if False:
                xc.load(xc[:, :], x3s[blk], reads=[scr])
                sz.load(sz[:, :], szs[blk], reads=[scr])
                ybuf.load(ybuf[:, :], yss[blk], reads=[scr])
                CT.load(CT[:, :, :], cts[blk], reads=[scr])
                sm.load(sm[:, 0, :], ess[blk], reads=[scr])
                for g in range(8):
                    pe.op(lambda e, g=g: e.matmul(pA[:T, g * 256:(g + 1) * 256], lhsT=CT[:, g, :T], rhs=STb[:, g * 256:(g + 1) * 256],
                                                  start=True, stop=True), reads=[CT.b, STb.b], writes=[pAh[g // 4]])
                dve.op(lambda e: e.tensor_tensor(out=ytmp[:T, :].rearrange("p (h d) -> p h d", d=64),
                                                 in0=pA[:T, :].rearrange("p (h d) -> p h d", d=64),
                                                 in1=bc3(sm[:T, 0, :], T, 64), op=ALU.mult),
                       reads=[pAh[0], pAh[1], sm.b], writes=[ytmp.b])
                pool.op(lambda e: e.tensor_tensor(out=ybuf[:T, :], in0=ybuf[:T, :], in1=ytmp[:T, :], op=ALU.add),
                        reads=[ybuf.b, ytmp.b], writes=[ybuf.b])
            pool.op(lambda e: e.tensor_tensor(out=ybuf[:T, :], in0=ybuf[:T, :], in1=sz[:T, :], op=ALU.mult),
                    reads=[ybuf.b, sz.b], writes=[ybuf.b])
            for g in range(8):
                act.op(lambda e, g=g: e.activation(out=ytmp[:T, g * 256:(g + 1) * 256], in_=ybuf[:T, g * 256:(g + 1) * 256], func=AF.Square,
                                                   accum_out=ss[:T, 0, g:g + 1]), reads=[ybuf.b], writes=[ytmp.b, ss.b])
            dve.op(lambda e: e.tensor_scalar(out=ss[:T, 1, :], in0=ss[:T, 0, :], scalar1=1.0 / 256, scalar2=RMS_EPS_,
                                             op0=ALU.mult, op1=ALU.add), reads=[ss.b], writes=[ss.b])
            act.op(lambda e: e.activation(out=ss[:T, 2, :], in_=ss[:T, 1, :], func=AF.Sqrt), reads=[ss.b], writes=[ss.b])
            dve.op(lambda e: e.reciprocal(out=ss[:T, 3, :], in_=ss[:T, 2, :]), reads=[ss.b], writes=[ss.b])
            dve.op(lambda e: e.tensor_tensor(out=ynb[:T, :].rearrange("p (g d) -> p g d", d=256),
                                             in0=ybuf[:T, :].rearrange("p (g d) -> p g d", d=256),
                                             in1=ss[:T, 3, :].unsqueeze(2).to_broadcast([T, 8, 256]), op=ALU.mult),
                   reads=[ybuf.b, ss.b], writes=[ynb.b])
            for r in range(2):
                for c8 in range(8):
                    c = r * 8 + c8
                    pe.op(lambda e, c=c, c8=c8: e.transpose(out=pT[:, c8 * 128:c8 * 128 + T], in_=ynb[:T, c * 128:(c + 1) * 128],
                                                            identity=ident[:T, :T]), reads=[ynb.b, ident.b], writes=[pT.b])
                for c8 in range(8):
                    c = r * 8 + c8
                    act.op(lambda e, c=c, c8=c8: e.activation(out=ynT[:, c, :T], in_=pT[:, c8 * 128:c8 * 128 + T], func=AF.Identity,
                                                              scale=normw[:, c:c + 1]), reads=[pT.b, normw.b], writes=[ynT.b])

            def obody(k, w):
                for h in range(2):
                    pe.op(lambda e, h=h: e.matmul(pA[:T, h * 512:(h + 1) * 512], lhsT=ynT[:, k, :T], rhs=w[:, h * 512:(h + 1) * 512],
                                                  start=(k == 0), stop=(k == 15)), reads=[ynT.b, w.b], writes=[pAh[0]])
            stream(16, r_out, lambda k: outp_s[k], obody)
            residual(T, xc, pAh[0], xb)
            layer_norm(T, xb, 2, xa)
            ffn(T, 1)
            residual(T, xa, pAh[0], xb)
            layer_norm(T, xb, 3, xc)
            ple(T, 1, s, t0, xc, xa)
            if mode != "halo":
                sp.dma(y_d[s][yrow:yrow + T, :], xa[:T, :], oslot, reads=[xa.b])

        def run_sample():
            s = 1
            ftail.load(ftail[:], ftail_d[s][:, :, :, :, :])
            ctail.load(ctail[:], ctail_d[s][:, :, :])
            ST.load(ST[:], st_d[s][:, :])
            invc.load(invc[:], invc_d[s][:, :, :])
            act.op(lambda e: e.activation(out=STb[:, :], in_=ST[:, :], func=AF.Copy), reads=[ST.b], writes=[STb.b])
            block_p1(s, 0, LS, True, "sample", 0)
            block_p2(s, 0, LS, "sample", 0, 0)
            sp.dma(ftail_o[s][:, :, :, :, :], ftail[:], oslot, reads=[ftail.b])
            sp.dma(ctail_o[s][:, :, :], ctail[:], oslot, reads=[ctail.b])
            sp.dma(st_o[s][:, :], ST[:], oslot, reads=[ST.b])
            sp.dma(pool_o[s][:, :], x_d[s][LS - 15:LS, :], oslot)

        def mask_all(b):
            for ap_, tb in ((ftail[:].rearrange("p a b c d -> p (a b c d)"), ftail), (ctail[:].rearrange("p a b -> p (a b)"), ctail)):
                dve.op(lambda e, ap_=ap_: e.tensor_scalar_mul(out=ap_, in0=ap_, scalar1=vfl[:, b:b + 1]),
                       reads=[tb.b, vfl.b], writes=[tb.b])

        def run_prompt():
            s = 0
            ftail.load(ftail[:], ftail_d[s][:, :, :, :, :])
            ctail.load(ctail[:], ctail_d[s][:, :, :])
            ST.load(ST[:], st_d[s][:, :])
            act.op(lambda e: e.activation(out=STb[:, :], in_=ST[:, :], func=AF.Copy), reads=[ST.b], writes=[STb.b])
            firsts = {NST + 1 - k * NOWN: k for k in range(4)}
            for b in range(NBLK):
                mode = "state" if b < NST else ("halo" if b == NST else "own")
                block_p1(s, b * 128, 128, b in firsts, mode, b, vcol=(b if mode != "own" else None),
                         invtab=invt[firsts[b]] if b in firsts else None)
                if mode != "state":
                    block_p2(s, b * 128, 128, mode, b, (b - NST - 1) * 128)
                if (b + 1) in firsts:
                    mask_all(b)
            sp.dma(ftail_o[s][:, :, :, :, :], ftail[:], oslot, reads=[ftail.b])
            sp.dma(ctail_o[s][:, :, :], ctail[:], oslot, reads=[ctail.b])
            sp.dma(st_o[s][:, :], ST[:], oslot, reads=[ST.b])
            sp.dma(pool_o[s][:, :], x_d[s][LP - 15:LP, :], oslot)

        ccslot = P.slot()
        run_sample()
        run_prompt()
        fin = Buf("fin")
        fin.rd = {oslot.key: oslot.n}
        P.finish([fin, xtok[0], ftail, ctail, ST] if False else [fin])
    return nc


def _fm(a, nchunk):
    r, c = a.shape
    return np.ascontiguousarray(a.reshape(r, nchunk, 128).transpose(2, 1, 0))


def _prep_inputs(x_prompt, x_sample, p_prompt, p_sample, cache_pool, cache_ssm_conv, state_ssm, cache_ffn_conv,
                 pool_w, pool_scale, ssm_in_proj, ssm_conv_w, ssm_conv_b, ssm_dt_bias, ssm_A_log, ssm_D, ssm_norm_w,
                 ssm_out_proj, ln_mix_g, ln_mix_b, ffn_up, ffn_conv_w, ffn_conv_b, ffn_down, ln_ffn_g, ln_ffn_b,
                 ple_proj, ple_gate_w, ple_gate_b, n_cores=8):
    f = lambda a: np.ascontiguousarray(np.asarray(a, dtype=np.float32))
    L = x_prompt.shape[1]
    LS = x_sample.shape[1]
    shared = {
        "pool_w": f(pool_w[0]), "pool_scale": f(pool_scale), "in_proj": f(ssm_in_proj[0]),
        "scw": f(np.asarray(ssm_conv_w[0]).reshape(4, 32, 128).transpose(2, 1, 0)),
        "scb": f(np.asarray(ssm_conv_b[0]).reshape(32, 128).T),
        "dtb": f(ssm_dt_bias), "alog": f(ssm_A_log), "dsk": f(ssm_D),
        "normw": f(np.asarray(ssm_norm_w[0]).reshape(16, 128).T),
        "out_proj": f(ssm_out_proj[0]),
        "lng": f(np.stack([ln_mix_g[0], ln_ffn_g[0], ln_mix_g[1], ln_ffn_g[1]])),
        "lnb": f(np.stack([ln_mix_b[0], ln_ffn_b[0], ln_mix_b[1], ln_ffn_b[1]])),
        "fcw": f(np.asarray(ffn_conv_w).reshape(2, 3, 2, 22, 128).transpose(4, 0, 2, 3, 1)),
        "fcb": f(np.asarray(ffn_conv_b).reshape(2, 2, 22, 128).transpose(3, 0, 1, 2)),
        "gate_b": f(ple_gate_b),
    }
    for i in range(2):
        shared["ffn_up%d" % i] = f(ffn_up[i])
        shared["ffn_down%d" % i] = f(ffn_down[i])
        shared["ple_proj%d" % i] = f(ple_proj[i])
        shared["gate_w%d" % i] = f(ple_gate_w[i])
    wins = (2, 4, 8, 16)
    invc_p = np.zeros((128, 8, 16), np.float32)
    invc_s = np.zeros((128, 8, 16), np.float32)
    for c in range(8):
        w = wins[c // 2]
        invc_p[:, c, :] = 1.0 / np.minimum(w, np.arange(16) + 1)
        invc_s[:, c, :] = 1.0 / w
    in_maps = []
    nseq = x_prompt.shape[0]
    nseg = n_cores // nseq
    Lo = L // nseg
    NOWN = Lo // 128
    NST = 3 * NOWN
    NBLK = NST + 1 + NOWN
    LP = NBLK * 128

    def ext(a, start, n):
        out = np.zeros((n,) + a.shape[1:], np.float32)
        lo = max(start, 0)
        if start + n > lo:
            out[lo - start:] = a[lo:start + n]
        return out

    for core in range(n_cores):
        bp = core // nseg
        seg = core % nseg
        bs = core % x_sample.shape[0]
        start = (seg * NOWN - NST - 1) * 128
        nv = NST + 1 - seg * NOWN
        m = dict(shared)
        xp = f(x_prompt[bp])
        m["x0"] = ext(xp, start, LP)
        m["xT0"] = _fm(ext(xp, start - 15, LP + 15), 8)
        xs = f(x_sample[bs])
        m["x1"] = xs
        m["xT1"] = _fm(np.concatenate([f(cache_pool[0, bs]), xs], 0), 8)
        for i in range(2):
            m["pT0_%d" % i] = _fm(ext(f(p_prompt[i, bp]), start, LP), 2)
            m["pT1_%d" % i] = _fm(f(p_sample[i, bs]), 2)
        m["ftail0"] = np.zeros((128, 2, 2, 22, 2), np.float32)
        m["ftail1"] = f(np.asarray(cache_ffn_conv[:, bs]).reshape(2, 2, 2, 22, 128).transpose(4, 0, 2, 3, 1))
        m["ctail0"] = np.zeros((128, 32, 3), np.float32)
        m["ctail1"] = f(np.asarray(cache_ssm_conv[0, bs]).reshape(3, 32, 128).transpose(2, 1, 0))
        m["st0"] = np.zeros((128, 2048), np.float32)
        m["st1"] = f(np.asarray(state_ssm[0, bs]).reshape(2048, 128).T)
        m["invc0"] = np.stack([invc_p if k == seg else invc_s for k in range(4)])
        m["invc1"] = invc_s
        vfl = np.ones((128, NBLK), np.float32)
        vfl[:, :nv] = 0.0
        m["vfl"] = vfl
        in_maps.append(m)
    return in_maps, Lo, LS


def _unft(a):
    return np.ascontiguousarray(a.transpose(1, 4, 2, 3, 0).reshape(2, 2, 5632))


def _assemble(results, nb_p, nb_s):
    nseg = len(results) // nb_p
    last = [b * nseg + nseg - 1 for b in range(nb_p)]
    y_p = np.stack([np.concatenate([results[b * nseg + g]["y0"] for g in range(nseg)], 0) for b in range(nb_p)])
    y_s = np.stack([results[b]["y1"] for b in range(nb_s)])
    pool_p = np.stack([results[c]["pool_o0"] for c in last])[None]
    pool_s = np.stack([results[b]["pool_o1"] for b in range(nb_s)])[None]
    unc = lambda a: np.ascontiguousarray(a.transpose(2, 1, 0).reshape(3, 4096))
    sconv_p = np.stack([unc(results[c]["ctail_o0"]) for c in last])[None]
    sconv_s = np.stack([unc(results[b]["ctail_o1"]) for b in range(nb_s)])[None]
    uns = lambda a: np.ascontiguousarray(a.T.reshape(32, 64, 128))
    st_p = np.stack([uns(results[c]["st_o0"]) for c in last])[None]
    st_s = np.stack([uns(results[b]["st_o1"]) for b in range(nb_s)])[None]
    ffn_p = np.stack([_unft(results[c]["ftail_o0"]) for c in last], 1)
    ffn_s = np.stack([_unft(results[b]["ftail_o1"]) for b in range(nb_s)], 1)
    return (y_p, y_s, pool_p, pool_s, sconv_p, sconv_s, st_p, st_s, ffn_p, ffn_s)


_NC_CACHE = {}


def kernel(**inputs):
    in_maps, L, LS = _prep_inputs(**inputs)
    key = (L, LS)
    if key not in _NC_CACHE:
        _NC_CACHE[key] = build_program(L, LS)
    nc = _NC_CACHE[key]
    res = run_bass_kernel_spmd(nc, in_maps, core_ids=list(range(8)))
    outs = _assemble(res.results, inputs["x_prompt"].shape[0], inputs["x_sample"].shape[0])
    return tuple(np.asarray(o, dtype=np.float32) for o in outs)
```

```python
import numpy as np
import concourse.bass as bass
import concourse.mybir as mybir
from concourse.bass_utils import run_bass_kernel_spmd

F32 = mybir.dt.float32
BF16 = mybir.dt.bfloat16
AF = mybir.ActivationFunctionType
ALU = mybir.AluOpType


class Buf:
    __slots__ = ("name", "lw", "rd")

    def __init__(self, name):
        self.name = name
        self.lw = {}
        self.rd = {}


class Eng:
    def __init__(self, prog, name, handle, sem, is_pe=False, compute=True):
        self.prog = prog
        self.name = name
        self.h = handle
        self.sem = sem
        self.n = 0
        self.seen = {}
        self.q = []
        self.is_pe = is_pe
        self.compute = compute

    def _sync(self, deps):
        for key, cnt in deps.items():
            if key == self.name and self.is_pe:
                continue
            if self.seen.get(key, 0) >= cnt:
                continue
            self.seen[key] = cnt
            sem = self.prog.sems[key]
            self.q.append(("w", sem, cnt))

    def op(self, fn, reads=(), writes=()):
        deps = {}
        for b in reads:
            for k, c in b.lw.items():
                if deps.get(k, 0) < c:
                    deps[k] = c
        for b in writes:
            for k, c in b.lw.items():
                if deps.get(k, 0) < c:
                    deps[k] = c
            for k, c in b.rd.items():
                if deps.get(k, 0) < c:
                    deps[k] = c
        self._sync(deps)
        self.n += 1
        self.q.append(("o", fn, self.sem, 1))
        for b in reads:
            if b.rd.get(self.name, 0) < self.n:
                b.rd[self.name] = self.n
        for b in writes:
            b.lw = {self.name: self.n}
            b.rd = {}

    def dma(self, out_ap, in_ap, slot, reads=(), writes=()):
        deps = {}
        for b in reads:
            for k, c in b.lw.items():
                if deps.get(k, 0) < c:
                    deps[k] = c
        for b in writes:
            for k, c in b.lw.items():
                if deps.get(k, 0) < c:
                    deps[k] = c
            for k, c in b.rd.items():
                if deps.get(k, 0) < c:
                    deps[k] = c
        self._sync(deps)
        slot.n += 16
        self.q.append(("d", out_ap, in_ap, slot.sem))
        for b in reads:
            if b.rd.get(slot.key, 0) < slot.n:
                b.rd[slot.key] = slot.n
        for b in writes:
            b.lw = dict(b.lw) if False else {slot.key: slot.n}
            b.rd = {}

    def collective(self, tin, tout, slot, reads=(), writes=()):
        deps = {}
        for b in reads:
            for k, c in b.lw.items():
                if deps.get(k, 0) < c:
                    deps[k] = c
        for b in writes:
            for k, c in list(b.lw.items()) + list(b.rd.items()):
                if deps.get(k, 0) < c:
                    deps[k] = c
        self._sync(deps)
        slot.n += 1
        self.q.append(("c", tin, tout, slot.sem))
        for b in reads:
            b.rd[slot.key] = slot.n
        for b in writes:
            b.lw = {slot.key: slot.n}
            b.rd = {}

    def wait_all(self, bufs):
        deps = {}
        for b in bufs:
            for k, c in list(b.lw.items()) + list(b.rd.items()):
                if deps.get(k, 0) < c:
                    deps[k] = c
        self._sync(deps)

    def emit(self, h):
        for it in self.q:
            if it[0] == "w":
                h.wait_ge(it[1], it[2])
            elif it[0] == "o":
                ins = it[1](h)
                ins.then_inc(it[2], it[3])
            elif it[0] == "c":
                h.collective_compute("AllGather", ALU.bypass, replica_groups=[list(range(8))],
                                     ins=[it[1].ap().opt()], outs=[it[2].ap().opt()]).then_inc(it[3], 1)
            else:
                h.dma_start(out=it[1], in_=it[2]).then_inc(it[3], 16)


class Slot:
    def __init__(self, prog, key, sem):
        self.key = key
        self.sem = sem
        self.n = 0


class Prog:
    def __init__(self, nc, stack):
        self.nc = nc
        self.stack = stack
        self.sems = {}
        self.nslots = 0
        mk = lambda nm: stack.enter_context(nc.semaphore(nm))
        self.pe = Eng(self, "pe", nc.tensor, mk("s_pe"), is_pe=True)
        self.act = Eng(self, "act", nc.scalar, mk("s_act"))
        self.dve = Eng(self, "dve", nc.vector, mk("s_dve"))
        self.pool = Eng(self, "pool", nc.gpsimd, mk("s_pool"))
        self.sp = Eng(self, "sp", nc.sync, None, compute=False)
        for e in (self.pe, self.act, self.dve, self.pool):
            self.sems[e.name] = e.sem

    def slot(self, name=None):
        self.nslots += 1
        key = "dma%d" % self.nslots
        sem = self.stack.enter_context(self.nc.semaphore("s_" + key))
        self.sems[key] = sem
        return Slot(self, key, sem)

    def sb(self, name, shape, dtype):
        return self.stack.enter_context(self.nc.sbuf_tensor(name, list(shape), dtype))

    def ps(self, name, shape, dtype):
        return self.stack.enter_context(self.nc.psum_tensor(name, list(shape), dtype))

    def finish(self, final_bufs):
        self.sp.wait_all(final_bufs)
        with self.nc.Block() as block:
            @block.tensor
            def _(e):
                self.pe.emit(e)

            @block.scalar
            def _(e):
                self.act.emit(e)

            @block.vector
            def _(e):
                self.dve.emit(e)

            @block.gpsimd
            def _(e):
                self.pool.emit(e)

            @block.sync
            def _(e):
                self.sp.emit(e)

D = 1024
DFF = 2816
NJ = 22
DIN = 2048
CONVD = 4096
NH = 32
ALPHA_ = (2 * 2) ** 0.25
LN_EPS_ = 1e-5
RMS_EPS_ = 1e-5


def build_program(L, LS=32):
    import contextlib
    nc = bass.Bass("TRN2", target_bir_lowering=False)
    NOWN = L // 128
    NST = 3 * NOWN
    NBLK = NST + 1 + NOWN
    LP = NBLK * 128

    def din(name, shape, dt=F32):
        return nc.dram_tensor(name, list(shape), dt, kind="ExternalInput").ap()

    def dout(name, shape, dt=F32):
        return nc.dram_tensor(name, list(shape), dt, kind="ExternalOutput").ap()

    def dint(name, shape, dt=BF16):
        return nc.dram_tensor(name, list(shape), dt).ap()

    Ls = [LP, LS]
    xT_d = [din("xT%d" % s, [128, 8, 15 + Ls[s]]) for s in range(2)]
    x_d = [din("x%d" % s, [Ls[s], D]) for s in range(2)]
    pT_d = [[din("pT%d_%d" % (s, i), [128, 2, Ls[s]]) for i in range(2)] for s in range(2)]
    ftail_d = [din("ftail%d" % s, [128, 2, 2, NJ, 2]) for s in range(2)]
    ctail_d = [din("ctail%d" % s, [128, 32, 3]) for s in range(2)]
    st_d = [din("st%d" % s, [128, DIN]) for s in range(2)]
    invc_d = [din("invc0", [4, 128, 8, 16]), din("invc1", [128, 8, 16])]
    pool_w_d = din("pool_w", [4, 256, 256])
    pool_scale_d = din("pool_scale", [1, D])
    in_proj_d = din("in_proj", [D, 6176])
    scw_d = din("scw", [128, 32, 4])
    scb_d = din("scb", [128, 32])
    dtb_d = din("dtb", [1, NH])
    alog_d = din("alog", [1, NH])
    dsk_d = din("dsk", [1, NH])
    normw_d = din("normw", [128, 16])
    out_proj_d = din("out_proj", [DIN, D])
    lng_d = din("lng", [4, D])
    lnb_d = din("lnb", [4, D])
    ffn_up_d = [din("ffn_up%d" % i, [D, 2 * DFF]) for i in range(2)]
    fcw_d = din("fcw", [128, 2, 2, NJ, 3])
    fcb_d = din("fcb", [128, 2, 2, NJ])
    ffn_down_d = [din("ffn_down%d" % i, [DFF, D]) for i in range(2)]
    ple_proj_d = [din("ple_proj%d" % i, [256, D]) for i in range(2)]
    gate_w_d = [din("gate_w%d" % i, [D, D]) for i in range(2)]
    gate_b_d = din("gate_b", [2, D])
    y_d = [dout("y0", [L, D]), dout("y1", [LS, D])]
    vfl_d = din("vfl", [128, NBLK])
    pool_o = [dout("pool_o%d" % s, [15, D]) for s in range(2)]
    ftail_o = [dout("ftail_o%d" % s, [128, 2, 2, NJ, 2]) for s in range(2)]
    ctail_o = [dout("ctail_o%d" % s, [128, 32, 3]) for s in range(2)]
    st_o = [dout("st_o%d" % s, [128, DIN]) for s in range(2)]
    wup_s = [dint("wup_s%d" % i, [NJ, 128, 8, 256]) for i in range(2)]
    wdn_s = [dint("wdn_s%d" % i, [NJ, 128, D]) for i in range(2)]
    gate_s = [dint("gate_s%d" % i, [8, 128, D]) for i in range(2)]
    proj_s = [dint("proj_s%d" % i, [128, 2, D]) for i in range(2)]
    gb_s = dint("gb_s", [2, D])
    inz_s = dint("inz_s", [8, 128, DIN])
    inx_s = dint("inx_s", [32, 128, 8, 128])
    indt_s = dint("indt_s", [128, 8, NH])
    outp_s = dint("outp_s", [16, 128, D])

    with contextlib.ExitStack() as stack:
        P = Prog(nc, stack)
        pe, act, dve, pool, sp = P.pe, P.act, P.dve, P.pool, P.sp

        class T_:
            def __init__(self, name, shape, dt, psum=False):
                self.t = (P.ps if psum else P.sb)(name, shape, dt)
                self.b = Buf(name)

            def __getitem__(self, k):
                return self.t[k]

            def store(self, dst, src, writes=()):
                if not hasattr(self, "sslot"):
                    self.sslot = P.slot()
                sp.dma(dst, src, self.sslot, reads=[self.b], writes=list(writes))

            def load(self, dst, src, reads=()):
                if not hasattr(self, "slot"):
                    self.slot = P.slot()
                sp.dma(dst, src, self.slot, reads=list(reads), writes=[self.b])

        cnt = [0]

        def SB(shape, dt=F32, name=None):
            cnt[0] += 1
            return T_(name or ("t%d" % cnt[0]), shape, dt)

        WB = {}
        WS = {}

        def conv(key, dst, src):
            if key not in WB:
                WB[key] = Buf("w_" + key)
                WS[key] = P.slot()
            pool.dma(dst, src, WS[key], writes=[WB[key]])

        def conv_layer(i):
            for j in range(NJ):
                for gu in range(2):
                    src = ffn_up_d[i][:, gu * DFF + j * 128: gu * DFF + (j + 1) * 128].rearrange("(k p) n -> p k n", p=128)
                    conv("wup%d" % i, wup_s[i][j, :, :, gu * 128:(gu + 1) * 128], src)
            for j0 in range(0, NJ, 2):
                conv("wdn%d" % i, wdn_s[i][j0:j0 + 2].rearrange("j p n -> p j n"),
                     ffn_down_d[i][j0 * 128:(j0 + 2) * 128, :].rearrange("(j p) n -> p j n", p=128))
            for k0 in range(0, 8, 2):
                conv("gate%d" % i, gate_s[i][k0:k0 + 2].rearrange("j p n -> p j n"),
                     gate_w_d[i][k0 * 128:(k0 + 2) * 128, :].rearrange("(j p) n -> p j n", p=128))
            conv("proj%d" % i, proj_s[i][:, :, :], ple_proj_d[i].rearrange("(k p) n -> p k n", p=128))

        conv("gb", gb_s[:, :], gate_b_d[:, :])
        conv("indt", indt_s[:, :, :], in_proj_d[:, 6144:6176].rearrange("(k p) n -> p k n", p=128))
        conv_layer(0)
        for k in range(8):
            conv("inz", inz_s[k], in_proj_d[k * 128:(k + 1) * 128, 0:DIN])
        for c in range(32):
            conv("inx", inx_s[c], in_proj_d[:, DIN + c * 128: DIN + (c + 1) * 128].rearrange("(k p) n -> p k n", p=128))
        for k0 in range(0, 16, 2):
            conv("outp", outp_s[k0:k0 + 2].rearrange("j p n -> p j n"),
                 out_proj_d[k0 * 128:(k0 + 2) * 128, :].rearrange("(j p) n -> p j n", p=128))
        conv_layer(1)

        ld = P.slot()
        identf = SB([128, 128]); ident = SB([128, 128], BF16)
        tri = SB([128, 128]); trib = SB([128, 128], BF16); onesf = SB([128, 128]); mstr = SB([128, 128], BF16)
        mtmp = identf
        onesb = SB([1, 128], BF16)
        pool.op(lambda e: e.memset(onesf[:], 1.0), writes=[onesf.b])
        pool.op(lambda e: e.memset(onesb[:], 1.0), writes=[onesb.b])
        pool.op(lambda e: e.affine_select(out=identf[:], in_=onesf[:], pattern=[[-1, 128]], compare_op=ALU.is_equal,
                                          fill=0.0, base=0, channel_multiplier=1), reads=[onesf.b], writes=[identf.b])
        dve.op(lambda e: e.tensor_copy(out=ident[:], in_=identf[:]), reads=[identf.b], writes=[ident.b])
        pool.op(lambda e: e.affine_select(out=tri[:], in_=onesf[:], pattern=[[1, 128]], compare_op=ALU.is_ge,
                                          fill=0.0, base=0, channel_multiplier=-1), reads=[onesf.b], writes=[tri.b])
        dve.op(lambda e: e.tensor_copy(out=trib[:], in_=tri[:]), reads=[tri.b], writes=[trib.b])
        pool.op(lambda e: e.affine_select(out=mtmp[:], in_=onesf[:], pattern=[[-1, 128]], compare_op=ALU.is_gt,
                                          fill=0.0, base=0, channel_multiplier=1), reads=[onesf.b], writes=[mtmp.b])
        dve.op(lambda e: e.tensor_copy(out=mstr[:], in_=mtmp[:]), reads=[mtmp.b], writes=[mstr.b])

        lng = [SB([128, D]) for _ in range(4)]
        lnb = [SB([128, D]) for _ in range(4)]
        for q in range(4):
            lng[q].load(lng[q][:], lng_d[q:q + 1, :].partition_broadcast(128))
            lnb[q].load(lnb[q][:], lnb_d[q:q + 1, :].partition_broadcast(128))
        gbb = SB([1, 2, D], BF16)
        gbb.load(gbb[:], gb_s[:, :].rearrange("(o a) n -> o a n", o=1), reads=[WB["gb"]])
        dtb = SB([128, NH]); abc = SB([128, NH]); dbc = SB([128, NH])
        dtb.load(dtb[:], dtb_d[0:1, :].partition_broadcast(128))
        abc.load(abc[:], alog_d[0:1, :].partition_broadcast(128))
        dbc.load(dbc[:], dsk_d[0:1, :].partition_broadcast(128))
        act.op(lambda e: e.activation(out=abc[:], in_=abc[:], func=AF.Exp), reads=[abc.b], writes=[abc.b])
        dve.op(lambda e: e.tensor_scalar_mul(out=abc[:], in0=abc[:], scalar1=-1.0), reads=[abc.b], writes=[abc.b])
        scw = SB([128, 32, 4]); scb = SB([128, 32]); normw = SB([128, 16])
        fcw = SB([128, 2, 2, NJ, 3]); fcb = SB([128, 2, 2, NJ])
        scw.load(scw[:], scw_d[:, :, :])
        scb.load(scb[:], scb_d[:, :])
        normw.load(normw[:], normw_d[:, :])
        fcw.load(fcw[:], fcw_d[:, :, :, :, :])
        fcb.load(fcb[:], fcb_d[:, :, :, :])
        poolw = SB([128, 8, 256], BF16)
        POOLW_SETUP = True
        wdt = SB([128, 8, NH], BF16)
        wdt.load(wdt[:], indt_s[:, :, :], reads=[WB["indt"]])

        pA = T_("pA", [128, 2048], F32, psum=True)
        pB = T_("pB", [128, 1024], F32, psum=True)
        pT = T_("pT", [128, 1024], BF16, psum=True)
        pS = T_("pS", [128, 512], F32, psum=True)
        pAh = [Buf("pA0"), Buf("pA1")]
        pBh = [Buf("pB0"), Buf("pB1")]
        pA23 = [Buf("pA2"), Buf("pA3")]

        NSLOT = 9
        wslots = [SB([128, 2048], BF16, name="wslot%d" % q) for q in range(NSLOT)]
        for w_ in wslots:
            w_.slot = P.slot()

        class WView:
            def __init__(self, slot_t, view):
                self.v = view
                self.b = slot_t.b

            def __getitem__(self, k):
                return self.v[k]

        class Ring:
            def __init__(self, kind, lo, hi, srcbuf=None):
                self.kind = kind
                self.lo = lo
                self.depth = hi - lo

            def view(self, t):
                k = self.kind
                if k == "up":
                    return t.t[:, :].rearrange("p (k n) -> p k n", n=256)
                if k == "row1024":
                    return t.t[:, 0:1024]
                if k == "proj":
                    return t.t[:, :].rearrange("p (k n) -> p k n", n=1024)
                if k == "row2048":
                    return t.t[:, :]
                if k == "inx":
                    return t.t[:, 0:1024].rearrange("p (k n) -> p k n", n=128)
                raise KeyError(k)

            def load(self, i, src, wb):
                t = wslots[self.lo + i % self.depth]
                v = self.view(t)
                sp.dma(v, src, t.slot, reads=[wb], writes=[t.b])
                return WView(t, v)

        r_up = Ring("up", 0, 4)
        r_dn = Ring("row1024", 4, 8)
        r_gate = Ring("row1024", 0, 8)
        r_proj = Ring("proj", 8, 9)
        r_inz = Ring("row2048", 0, 8)
        r_inx = Ring("inx", 0, 8)
        r_out = Ring("row1024", 0, 8)

        def stream(n, ring, srcf, body, wb):
            tl = {}
            for i in range(min(ring.depth, n)):
                tl[i] = ring.load(i, srcf(i), wb)
            for i in range(n):
                body(i, tl[i])
                if i + ring.depth < n:
                    tl[i + ring.depth] = ring.load(i + ring.depth, srcf(i + ring.depth), wb)

        xin = SB([128, 8, 143])
        xtok = [SB([128, D]) for _ in range(3)]
        sA = SB([128, 2, 143]); sBf = SB([128, 2, 143]); t16 = SB([128, 2, 16])
        pooled = SB([128, 8, 128], BF16)
        xT = SB([128, 8, 128], BF16)
        xb16 = SB([128, D], BF16)
        U1 = SB([128, 32 * 128], BF16)
        hb = [[SB([128, 130]) for _ in range(2)] for _ in range(2)]
        acc = [[SB([128, 128]) for _ in range(2)] for _ in range(2)]
        sg = SB([128, 128])
        pst = SB([128, 2, 128]); pTb = SB([128, 2, 128], BF16)
        gate = SB([128, D]); tmpt = SB([128, D])
        stats = SB([128, 2, 6]); mv = SB([128, 8])
        ftail = SB([128, 2, 2, NJ, 2]); ctail = SB([128, 32, 3])
        invc = SB([128, 8, 16])
        sz = SB([128, DIN]); ybuf = SB([128, DIN]); ytmp = SB([128, DIN])
        xsT = SB([128, 16, 128], BF16); BT = SB([128, 8, 128], BF16); CT = SB([128, 8, 128], BF16)
        cbuf = [SB([128, 131]) for _ in range(2)]; cacc = [SB([128, 128]) for _ in range(2)]
        xdt = SB([128, DIN], BF16); xdtd = SB([128, DIN], BF16); Btok = SB([128, 1024], BF16)
        cbm = SB([128, 8, 128], BF16)
        rseg = [SB([128, 8, 128], BF16) for _ in range(2)]
        _ex = SB([128, 8, 128], BF16); expL = [_ex, _ex]
        ST = SB([128, DIN]); STb = SB([128, DIN], BF16)
        ynb = SB([128, DIN], BF16); ynT = SB([128, 16, 128], BF16)
        sm = SB([128, 8, NH])
        ss = SB([128, 4, 8])
        io = P.slot(); io2 = P.slot(); oslot = P.slot()
        vfl = SB([128, NBLK]); vfl.load(vfl[:], vfl_d[:, :])
        invt = [SB([128, 8, 16]) for _ in range(4)]
        for q in range(4):
            invt[q].load(invt[q][:], invc_d[0][q])
        scr = Buf("scratch")
        sz.load(sz[:].rearrange("p (c d) -> p c d", d=256), pool_w_d.rearrange("g (h p) d -> p (g h) d", p=128))
        ytmp.load(ytmp[:, 0:D], pool_scale_d[0:1, :].partition_broadcast(128))
        for c in range(8):
            g = c // 2
            dve.op(lambda e, c=c, g=g: e.tensor_tensor(out=poolw[:, c, :], in0=sz[:, c * 256:(c + 1) * 256],
                                                       in1=ytmp[:, g * 256:(g + 1) * 256], op=ALU.mult),
                   reads=[sz.b, ytmp.b], writes=[poolw.b])
        outbufs = []

        def layer_norm(T, u, q, xo, use_xT=True):
            for h in range(2):
                dve.op(lambda e, h=h: e.bn_stats(out=stats[:T, h, :], in_=u[:T, h * 512:(h + 1) * 512]),
                       reads=[u.b], writes=[stats.b])
            dve.op(lambda e: e.bn_aggr(out=mv[:T, 0:2], in_=stats[:T].rearrange("p a b -> p (a b)")), reads=[stats.b], writes=[mv.b])
            dve.op(lambda e: e.tensor_scalar_add(out=mv[:T, 2:3], in0=mv[:T, 1:2], scalar1=LN_EPS_), reads=[mv.b], writes=[mv.b])
            act.op(lambda e: e.activation(out=mv[:T, 3:4], in_=mv[:T, 2:3], func=AF.Sqrt), reads=[mv.b], writes=[mv.b])
            dve.op(lambda e: e.reciprocal(out=mv[:T, 4:5], in_=mv[:T, 3:4]), reads=[mv.b], writes=[mv.b])
            dve.op(lambda e: e.scalar_tensor_tensor(out=mv[:T, 5:6], in0=mv[:T, 0:1], scalar=-1.0, in1=mv[:T, 4:5],
                                                    op0=ALU.mult, op1=ALU.mult), reads=[mv.b], writes=[mv.b])
            act.op(lambda e: e.activation(out=tmpt[:T, :], in_=u[:T, :], func=AF.Identity, scale=mv[:T, 4:5], bias=mv[:T, 5:6]),
                   reads=[u.b, mv.b], writes=[tmpt.b])
            pool.op(lambda e: e.tensor_tensor(out=tmpt[:T, :], in0=tmpt[:T, :], in1=lng[q][:T, :], op=ALU.mult),
                    reads=[tmpt.b, lng[q].b], writes=[tmpt.b])
            pool.op(lambda e: e.tensor_tensor(out=xo[:T, :], in0=tmpt[:T, :], in1=lnb[q][:T, :], op=ALU.add),
                    reads=[tmpt.b, lnb[q].b], writes=[xo.b])
            to_T(T, xo)

        def to_T(T, xo):
            act.op(lambda e: e.activation(out=xb16[:T, :], in_=xo[:T, :], func=AF.Copy), reads=[xo.b], writes=[xb16.b])
            for c in range(8):
                pe.op(lambda e, c=c: e.transpose(out=pT[:, c * 128:c * 128 + T], in_=xb16[:T, c * 128:(c + 1) * 128],
                                                 identity=ident[:T, :T]), reads=[xb16.b, ident.b], writes=[pT.b])
            dve.op(lambda e: e.tensor_copy(out=xT[:, :, :T], in_=pT[:].rearrange("p (c t) -> p c t", t=128)[:, :, :T]),
                   reads=[pT.b], writes=[xT.b])

        def ffn(T, i):
            hT = U1
            st_ = {}

            def up(j, w):
                par = j % 2
                if par == 0:
                    pp_, po_, pbs_ = pB, 0, [[pBh[0]], [pBh[1]]]
                else:
                    pp_, po_, pbs_ = pA, 1024, [[pA23[0], pAh[1]], [pA23[1], pAh[1]]]
                for gu in range(2):
                    for k in range(8):
                        pe.op(lambda e, gu=gu, k=k: e.matmul(pp_[:, po_ + gu * 512:po_ + gu * 512 + T], lhsT=w[:, k, gu * 128:(gu + 1) * 128],
                                                             rhs=xT[:, k, :T], start=(k == 0), stop=(k == 7)),
                              reads=[w.b, xT.b], writes=pbs_[gu])
                for gu in range(2):
                    h_ = hb[par][gu]
                    act.op(lambda e, gu=gu, h_=h_: e.activation(out=h_[:, 2:2 + T], in_=pp_[:, po_ + gu * 512:po_ + gu * 512 + T], func=AF.Copy),
                           reads=pbs_[gu], writes=[h_.b])
                    pool.op(lambda e, gu=gu, h_=h_: e.tensor_copy(out=h_[:, 0:2], in_=ftail[:, i, gu, j, :]),
                            reads=[ftail.b], writes=[h_.b])
                    pool.op(lambda e, gu=gu, h_=h_: e.tensor_copy(out=ftail[:, i, gu, j, :], in_=h_[:, T:T + 2]),
                            reads=[h_.b], writes=[ftail.b])
                for gu in range(2):
                    h_ = hb[par][gu]; a_ = acc[par][gu]
                    dve.op(lambda e, gu=gu, h_=h_, a_=a_: e.tensor_scalar(out=a_[:, :T], in0=h_[:, 2:2 + T],
                                                                          scalar1=fcw[:, i, gu, j, 2:3], scalar2=fcb[:, i, gu, j:j + 1],
                                                                          op0=ALU.mult, op1=ALU.add),
                           reads=[h_.b, fcw.b, fcb.b], writes=[a_.b])
                for tap in (1, 0):
                    for gu in range(2):
                        h_ = hb[par][gu]; a_ = acc[par][gu]
                        dve.op(lambda e, gu=gu, h_=h_, a_=a_, tap=tap: e.scalar_tensor_tensor(
                            out=a_[:, :T], in0=h_[:, tap:tap + T], scalar=fcw[:, i, gu, j, tap:tap + 1], in1=a_[:, :T],
                            op0=ALU.mult, op1=ALU.add), reads=[h_.b, fcw.b, a_.b], writes=[a_.b])
                act.op(lambda e: e.activation(out=sg[:, :T], in_=acc[par][0][:, :T], func=AF.Silu), reads=[acc[par][0].b], writes=[sg.b])
                pool.op(lambda e: e.tensor_tensor(out=hT[:, j * 128:j * 128 + T], in0=sg[:, :T], in1=acc[par][1][:, :T], op=ALU.mult),
                        reads=[sg.b, acc[par][1].b], writes=[hT.b])

            def down(j, w):
                for h in range(2):
                    pe.op(lambda e, h=h: e.matmul(pA[:T, h * 512:(h + 1) * 512], lhsT=hT[:, j * 128:j * 128 + T],
                                                  rhs=w[:, h * 512:(h + 1) * 512], start=(j == 0), stop=(j == NJ - 1)),
                          reads=[hT.b, w.b], writes=[pAh[0]])

            upt = {}; dnt = {}
            for jj in range(min(4, NJ)):
                upt[jj] = r_up.load(jj, wup_s[i][jj], WB["wup%d" % i]); dnt[jj] = r_dn.load(jj, wdn_s[i][jj], WB["wdn%d" % i])
            for j in range(NJ):
                up(j, upt[j])
                down(j, dnt[j])
                if j + 4 < NJ:
                    upt[j + 4] = r_up.load(j + 4, wup_s[i][j + 4], WB["wup%d" % i]); dnt[j + 4] = r_dn.load(j + 4, wdn_s[i][j + 4], WB["wdn%d" % i])

        def ple(T, i, s, t0, x2, xo):
            for h in range(2):
                pe.op(lambda e, h=h: e.matmul(pA[:T, h * 512:(h + 1) * 512], lhsT=onesb[0:1, :T], rhs=gbb[0:1, i, h * 512:(h + 1) * 512],
                                              start=True, stop=False), reads=[onesb.b, gbb.b], writes=[pAh[0]])

            def body(k, w):
                for h in range(2):
                    pe.op(lambda e, h=h: e.matmul(pA[:T, h * 512:(h + 1) * 512], lhsT=xT[:, k, :T], rhs=w[:, h * 512:(h + 1) * 512],
                                                  start=False, stop=(k == 7)), reads=[xT.b, w.b], writes=[pAh[0]])
            stream(8, r_gate, lambda k: gate_s[i][k], body, WB["gate%d" % i])
            pst.load(pst[:, :, :T], pT_d[s][i][:, :, t0:t0 + T])
            pool.op(lambda e: e.tensor_copy(out=pTb[:, :, :T], in_=pst[:, :, :T]), reads=[pst.b], writes=[pTb.b])
            wp = r_proj.load(0, proj_s[i][:, :, :], WB["proj%d" % i])
            for h in range(2):
                for k in range(2):
                    pe.op(lambda e, h=h, k=k: e.matmul(pA[:T, 1024 + h * 512:1024 + (h + 1) * 512], lhsT=pTb[:, k, :T],
                                                       rhs=wp[:, k, h * 512:(h + 1) * 512], start=(k == 0), stop=(k == 1)),
                          reads=[pTb.b, wp.b], writes=[pAh[1]])
            act.op(lambda e: e.activation(out=gate[:T, :], in_=pA[:T, 0:1024], func=AF.Sigmoid), reads=[pAh[0]], writes=[gate.b])
            dve.op(lambda e: e.tensor_tensor(out=gate[:T, :], in0=gate[:T, :], in1=pA[:T, 1024:2048], op=ALU.mult),
                   reads=[gate.b, pAh[1]], writes=[gate.b])
            pool.op(lambda e: e.tensor_tensor(out=xo[:T, :], in0=gate[:T, :], in1=x2[:T, :], op=ALU.add),
                    reads=[gate.b, x2.b], writes=[xo.b])

        def residual(T, xres, pbuf, uo):
            dve.op(lambda e: e.scalar_tensor_tensor(out=uo[:T, :], in0=xres[:T, :], scalar=ALPHA_, in1=pA[:T, 0:1024],
                                                    op0=ALU.mult, op1=ALU.add), reads=[xres.b, pbuf], writes=[uo.b])

        def bc3(ap2, T, n):
            return ap2.unsqueeze(2).to_broadcast([T, NH, n])

        def block_p1(s, t0, T, first, mode, blk, vcol=None, invtab=None):
            invc_ = invtab if invtab is not None else invc
            xa, xb, xc = xtok
            xin.load(xin[:, :, :15 + T], xT_d[s][:, :, t0:t0 + 15 + T])
            xa.load(xa[:T, :], x_d[s][t0:t0 + T, :])
            W = 15 + T
            for g in range(4):
                cs_ = slice(2 * g, 2 * g + 2)
                wsz = 2 ** (g + 1)
                pool.op(lambda e, cs_=cs_: e.tensor_tensor(out=sA[:, :, 1:W], in0=xin[:, cs_, 1:W], in1=xin[:, cs_, 0:W - 1], op=ALU.add),
                        reads=[xin.b], writes=[sA.b])
                cur, oth = sA, sBf
                sh = 1
                for lvl in range(g):
                    sh2 = sh * 2
                    lo = 2 * sh2 - 1
                    pool.op(lambda e, cur=cur, oth=oth, lo=lo, sh2=sh2: e.tensor_tensor(
                        out=oth[:, :, lo:W], in0=cur[:, :, lo:W], in1=cur[:, :, lo - sh2:W - sh2], op=ALU.add),
                        reads=[cur.b], writes=[oth.b])
                    cur, oth = oth, cur
                    sh = sh2
                dve.op(lambda e, cur=cur, cs_=cs_, wsz=wsz: e.scalar_tensor_tensor(
                    out=pooled[:, cs_, :T], in0=cur[:, :, 15:W], scalar=1.0 / wsz, in1=xin[:, cs_, 15:W],
                    op0=ALU.mult, op1=ALU.subtract), reads=[cur.b, xin.b], writes=[pooled.b])
                if first:
                    dve.op(lambda e, cur=cur, cs_=cs_: e.tensor_tensor(out=t16[:], in0=cur[:, :, 15:31], in1=invc_[:, cs_, :], op=ALU.mult),
                           reads=[cur.b, invc_.b], writes=[t16.b])
                    dve.op(lambda e, cs_=cs_: e.tensor_tensor(out=pooled[:, cs_, 0:16], in0=t16[:], in1=xin[:, cs_, 15:31], op=ALU.subtract),
                           reads=[t16.b, xin.b], writes=[pooled.b])
            for g in range(4):
                for h in range(2):
                    c = 2 * g + h
                    pe.op(lambda e, g=g, h=h, c=c: e.matmul(pA[:T, g * 256:(g + 1) * 256], lhsT=pooled[:, c, :T], rhs=poolw[:, c, :],
                                                            start=(h == 0), stop=(h == 1)),
                          reads=[pooled.b, poolw.b], writes=[pAh[0]])
            residual(T, xa, pAh[0], xb)
            layer_norm(T, xb, 0, xc)
            ffn(T, 0)
            residual(T, xc, pAh[0], xb)
            layer_norm(T, xb, 1, xa)
            ple(T, 0, s, t0, xa, xc)
            to_T(T, xc)

            if mode != "state":
                def zbody(k, w):
                    for cc in range(4):
                        pe.op(lambda e, cc=cc: e.matmul(pA[:T, cc * 512:(cc + 1) * 512], lhsT=xT[:, k, :T], rhs=w[:, cc * 512:(cc + 1) * 512],
                                                        start=(k == 0), stop=(k == 7)), reads=[xT.b, w.b], writes=[pAh[cc // 2]])
                stream(8, r_inz, lambda k: inz_s[k], zbody, WB["inz"])
                act.op(lambda e: e.activation(out=sz[:T, :], in_=pA[:T, :], func=AF.Silu), reads=[pAh[0], pAh[1]], writes=[sz.b])
            for k in range(8):
                pe.op(lambda e, k=k: e.matmul(pS[:T, 0:NH], lhsT=xT[:, k, :T], rhs=wdt[:, k, :], start=(k == 0), stop=(k == 7)),
                      reads=[xT.b, wdt.b], writes=[pS.b])
            S_ = lambda n: sm[:T, n, :]
            dve.op(lambda e: e.tensor_tensor(out=S_(0), in0=pS[:T, 0:NH], in1=dtb[:T, :], op=ALU.add), reads=[pS.b, dtb.b], writes=[sm.b])
            dve.op(lambda e: e.scalar_tensor_tensor(out=S_(1), in0=S_(0), scalar=-1.0, in1=S_(0), op0=ALU.mult, op1=ALU.max), reads=[sm.b], writes=[sm.b])
            act.op(lambda e: e.activation(out=S_(1), in_=S_(1), func=AF.Exp, scale=-1.0), reads=[sm.b], writes=[sm.b])
            act.op(lambda e: e.activation(out=S_(1), in_=S_(1), func=AF.Ln, bias=1.0), reads=[sm.b], writes=[sm.b])
            dve.op(lambda e: e.tensor_scalar_max(out=S_(0), in0=S_(0), scalar1=0.0), reads=[sm.b], writes=[sm.b])
            dve.op(lambda e: e.tensor_tensor(out=S_(2), in0=S_(0), in1=S_(1), op=ALU.add), reads=[sm.b], writes=[sm.b])
            if vcol is not None:
                dve.op(lambda e: e.tensor_scalar_mul(out=S_(2), in0=S_(2), scalar1=vfl[:T, vcol:vcol + 1]), reads=[sm.b, vfl.b], writes=[sm.b])
            dve.op(lambda e: e.tensor_tensor(out=S_(3), in0=S_(2), in1=abc[:T, :], op=ALU.mult), reads=[sm.b, abc.b], writes=[sm.b])

            def xbody(c, w):
                par = c % 2
                for k in range(8):
                    pe.op(lambda e, k=k: e.matmul(pB[:, par * 512:par * 512 + T], lhsT=w[:, k, :], rhs=xT[:, k, :T],
                                                  start=(k == 0), stop=(k == 7)), reads=[w.b, xT.b], writes=[pBh[par]])
                cb_ = cbuf[par]; ca_ = cacc[par]
                act.op(lambda e: e.activation(out=cb_[:, 3:3 + T], in_=pB[:, par * 512:par * 512 + T], func=AF.Copy),
                       reads=[pBh[par]], writes=[cb_.b])
                pool.op(lambda e: e.tensor_copy(out=cb_[:, 0:3], in_=ctail[:, c, :]), reads=[ctail.b], writes=[cb_.b])
                pool.op(lambda e: e.tensor_copy(out=ctail[:, c, :], in_=cb_[:, T:T + 3]), reads=[cb_.b], writes=[ctail.b])
                dve.op(lambda e: e.tensor_scalar(out=ca_[:, :T], in0=cb_[:, 3:3 + T], scalar1=scw[:, c, 3:4], scalar2=scb[:, c:c + 1],
                                                 op0=ALU.mult, op1=ALU.add), reads=[cb_.b, scw.b, scb.b], writes=[ca_.b])
                for tap in (2, 1, 0):
                    dve.op(lambda e, tap=tap: e.scalar_tensor_tensor(out=ca_[:, :T], in0=cb_[:, tap:tap + T], scalar=scw[:, c, tap:tap + 1],
                                                                     in1=ca_[:, :T], op0=ALU.mult, op1=ALU.add),
                           reads=[cb_.b, scw.b, ca_.b], writes=[ca_.b])
                if c < 16:
                    dst, db = xsT[:, c, :T], xsT.b
                elif c < 24:
                    dst, db = BT[:, c - 16, :T], BT.b
                else:
                    dst, db = CT[:, c - 24, :T], CT.b
                act.op(lambda e: e.activation(out=dst, in_=ca_[:, :T], func=AF.Silu), reads=[ca_.b], writes=[db])
            stream(24 if mode == "state" else 32, r_inx, lambda c: inx_s[c], xbody, WB["inx"])
            for r in range(2):
                for c8 in range(8):
                    c = r * 8 + c8
                    pe.op(lambda e, c=c, c8=c8: e.transpose(out=pT[:T, c8 * 128:(c8 + 1) * 128], in_=xsT[:, c, :T], identity=ident[:, :]),
                          reads=[xsT.b, ident.b], writes=[pT.b])
                hs = slice(r * 16, (r + 1) * 16)
                dve.op(lambda e, r=r, hs=hs: e.tensor_tensor(
                    out=xdt[:T, r * 1024:(r + 1) * 1024].rearrange("p (h d) -> p h d", d=64),
                    in0=pT[:T, :].rearrange("p (h d) -> p h d", d=64),
                    in1=sm[:T, 2, hs].unsqueeze(2).to_broadcast([T, 16, 64]), op=ALU.mult),
                    reads=[pT.b, sm.b], writes=[xdt.b])
                dve.op(lambda e, r=r, hs=hs: e.tensor_tensor(
                    out=ybuf[:T, r * 1024:(r + 1) * 1024].rearrange("p (h d) -> p h d", d=64),
                    in0=pT[:T, :].rearrange("p (h d) -> p h d", d=64),
                    in1=dbc[:T, hs].unsqueeze(2).to_broadcast([T, 16, 64]), op=ALU.mult),
                    reads=[pT.b, dbc.b], writes=[ybuf.b])
            for g in range(8):
                pe.op(lambda e, g=g: e.transpose(out=pT[:T, g * 128:(g + 1) * 128], in_=BT[:, g, :T], identity=ident[:, :]),
                      reads=[BT.b, ident.b], writes=[pT.b])
            dve.op(lambda e: e.tensor_copy(out=Btok[:T, :], in_=pT[:T, :]), reads=[pT.b], writes=[Btok.b])
            pe.op(lambda e: e.matmul(pS[:T, 32:64], lhsT=tri[:T, :T], rhs=sm[:T, 3, :], start=True, stop=True),
                  reads=[tri.b, sm.b], writes=[pS.b])
            pe.op(lambda e: e.matmul(pS[:, 64:96], lhsT=onesf[:T, :], rhs=sm[:T, 3, :], start=True, stop=True),
                  reads=[onesf.b, sm.b], writes=[pS.b])
            act.op(lambda e: e.activation(out=S_(4), in_=pS[:T, 32:64], func=AF.Exp), reads=[pS.b], writes=[sm.b])
            act.op(lambda e: e.activation(out=S_(5), in_=pS[:T, 32:64], func=AF.Copy), reads=[pS.b], writes=[sm.b])
            dve.op(lambda e: e.tensor_tensor(out=S_(6), in0=pS[:T, 64:96], in1=S_(5), op=ALU.subtract), reads=[pS.b, sm.b], writes=[sm.b])
            act.op(lambda e: e.activation(out=S_(6), in_=S_(6), func=AF.Exp), reads=[sm.b], writes=[sm.b])
            act.op(lambda e: e.activation(out=sm[:, 7, :], in_=pS[:, 64:96], func=AF.Exp), reads=[pS.b], writes=[sm.b])
            if False:
                dve.op(lambda e: e.tensor_tensor(out=S_(0), in0=pS[:T, 32:64], in1=carry[:T, :], op=ALU.add), reads=[pS.b, carry.b], writes=[sm.b])
                act.op(lambda e: e.activation(out=S_(0), in_=S_(0), func=AF.Exp), reads=[sm.b], writes=[sm.b])
                dve.op(lambda e: e.tensor_tensor(out=carry[:, :], in0=carry[:, :], in1=pS[:, 64:96], op=ALU.add), reads=[pS.b, carry.b], writes=[carry.b])
            if mode != "state":
                for g in range(8):
                    pe.op(lambda e, g=g: e.matmul(pB[:T, g * 128:g * 128 + T], lhsT=BT[:, g, :T], rhs=CT[:, g, :T], start=True, stop=True),
                          reads=[BT.b, CT.b], writes=[pBh[g // 4]])
                dve.op(lambda e: e.tensor_tensor(out=cbm[:T, :, :T], in0=pB[:T, :].rearrange("p (g q) -> p g q", q=128)[:, :, :T],
                                                 in1=tri[:T, :T].unsqueeze(1).to_broadcast([T, 8, T]), op=ALU.mult),
                       reads=[pBh[0], pBh[1], tri.b], writes=[cbm.b])
                MT = U1
                for r in range(4):
                    rs_ = rseg[r % 2]; ex_ = expL[r % 2]
                    pool.op(lambda e, r=r, rs_=rs_: e.tensor_tensor(
                        out=rs_[:T, :, :T], in0=sm[:T, 3, r * 8:(r + 1) * 8].unsqueeze(2).to_broadcast([T, 8, T]),
                        in1=trib[:T, :T].unsqueeze(1).to_broadcast([T, 8, T]), op=ALU.mult),
                        reads=[sm.b, trib.b], writes=[rs_.b])
                    hp = r % 2
                    for q2 in range(2):
                        pe.op(lambda e, rs_=rs_, q2=q2, hp=hp: e.matmul(
                            pA[:T, hp * 1024 + q2 * 512: hp * 1024 + q2 * 512 + 4 * T].rearrange("p (h q) -> p h q", q=T),
                            lhsT=mstr[:T, :T], rhs=rs_[:T, q2 * 4:(q2 + 1) * 4, :T], start=True, stop=True),
                            reads=[mstr.b, rs_.b], writes=[pAh[hp]])
                    for q2 in range(2):
                        act.op(lambda e, ex_=ex_, q2=q2, hp=hp: e.activation(
                            out=ex_[:T, q2 * 4:(q2 + 1) * 4, :T],
                            in_=pA[:T, hp * 1024 + q2 * 512: hp * 1024 + q2 * 512 + 4 * T].rearrange("p (h q) -> p h q", q=T),
                            func=AF.Exp), reads=[pAh[hp]], writes=[ex_.b])
                    for g2 in range(2):
                        gg = 2 * r + g2
                        dve.op(lambda e, ex_=ex_, g2=g2, gg=gg, r=r: e.tensor_tensor(
                            out=MT[:T, (r * 8 + g2 * 4) * 128:(r * 8 + g2 * 4 + 4) * 128].rearrange("p (h q) -> p h q", q=128)[:, :, :T],
                            in0=ex_[:T, g2 * 4:(g2 + 1) * 4, :T],
                            in1=cbm[:T, gg:gg + 1, :T].to_broadcast([T, 4, T]), op=ALU.mult),
                            reads=[ex_.b, cbm.b], writes=[MT.b])
                for g in range(8):
                    pe.op(lambda e, g=g: e.matmul(pA[:T, g * 256:(g + 1) * 256], lhsT=CT[:, g, :T], rhs=STb[:, g * 256:(g + 1) * 256],
                                                  start=True, stop=True), reads=[CT.b, STb.b], writes=[pAh[g // 4]])
                dve.op(lambda e: e.tensor_tensor(out=ytmp[:T, :].rearrange("p (h d) -> p h d", d=64),
                                                 in0=pA[:T, :].rearrange("p (h d) -> p h d", d=64),
                                                 in1=bc3(sm[:T, 4, :], T, 64), op=ALU.mult),
                       reads=[pAh[0], pAh[1], sm.b], writes=[ytmp.b])
                pool.op(lambda e: e.tensor_tensor(out=ybuf[:T, :], in0=ybuf[:T, :], in1=ytmp[:T, :], op=ALU.add),
                        reads=[ybuf.b, ytmp.b], writes=[ybuf.b])
                for h in range(NH):
                    pe.op(lambda e, h=h: e.matmul(pA[:T, h * 64:(h + 1) * 64], lhsT=MT[:T, h * 128:h * 128 + T], rhs=xdt[:T, h * 64:(h + 1) * 64],
                                                  start=True, stop=True), reads=[MT.b, xdt.b], writes=[pAh[h // 16]])
                dve.op(lambda e: e.tensor_tensor(out=ybuf[:T, :], in0=ybuf[:T, :], in1=pA[:T, :], op=ALU.add),
                       reads=[ybuf.b, pAh[0], pAh[1]], writes=[ybuf.b])
            pool.op(lambda e: e.tensor_tensor(out=xdtd[:T, :].rearrange("p (h d) -> p h d", d=64),
                                              in0=xdt[:T, :].rearrange("p (h d) -> p h d", d=64),
                                              in1=bc3(sm[:T, 6, :], T, 64), op=ALU.mult), reads=[xdt.b, sm.b], writes=[xdtd.b])
            for g in range(8):
                pe.op(lambda e, g=g: e.matmul(pA[:, g * 256:(g + 1) * 256], lhsT=Btok[:T, g * 128:(g + 1) * 128],
                                              rhs=xdtd[:T, g * 256:(g + 1) * 256], start=True, stop=True),
                      reads=[Btok.b, xdtd.b], writes=[pAh[g // 4]])
            pool.op(lambda e: e.tensor_tensor(out=ST[:, :].rearrange("p (h d) -> p h d", d=64),
                                              in0=ST[:, :].rearrange("p (h d) -> p h d", d=64),
                                              in1=sm[:, 7, :].unsqueeze(2).to_broadcast([128, NH, 64]), op=ALU.mult),
                    reads=[ST.b, sm.b], writes=[ST.b])
            dve.op(lambda e: e.tensor_tensor(out=ST[:, :], in0=ST[:, :], in1=pA[:, :], op=ALU.add),
                   reads=[ST.b, pAh[0], pAh[1]], writes=[ST.b])
            act.op(lambda e: e.activation(out=STb[:, :], in_=ST[:, :], func=AF.Copy), reads=[ST.b], writes=[STb.b])
            if False:
                xc.store(x3s[blk], xc[:, :], writes=[scr])
                sz.store(szs[blk], sz[:, :], writes=[scr])
                ybuf.store(yss[blk], ybuf[:, :], writes=[scr])
                CT.store(cts[blk], CT[:, :, :], writes=[scr])
                sm.store(ess[blk], sm[:, 0, :], writes=[scr])

        def block_p2(s, t0, T, mode, blk, yrow):
            xa, xb, xc = xtok
            if False:
                xc.load(xc[:, :], x3s[blk], reads=[scr])
                sz.load(sz[:, :], szs[blk], reads=[scr])
                ybuf.load(ybuf[:, :], yss[blk], reads=[scr])
                CT.load(CT[:, :, :], cts[blk], reads=[scr])
                sm.load(sm[:, 0, :], ess[blk], reads=[scr])
                for g in range(8):
                    pe.op(lambda e, g=g: e.matmul(pA[:T, g * 256:(g + 1) * 256], lhsT=CT[:, g, :T], rhs=STb[:, g * 256:(g + 1) * 256],
                                                  start=True, stop=True), reads=[CT.b, STb.b], writes=[pAh[g // 4]])
                dve.op(lambda e: e.tensor_tensor(out=ytmp[:T, :].rearrange("p (h d) -> p h d", d=64),
                                                 in0=pA[:T, :].rearrange("p (h d) -> p h d", d=64),
                                                 in1=bc3(sm[:T, 0, :], T, 64), op=ALU.mult),
                       reads=[pAh[0], pAh[1], sm.b], writes=[ytmp.b])
                pool.op(lambda e: e.tensor_tensor(out=ybuf[:T, :], in0=ybuf[:T, :], in1=ytmp[:T, :], op=ALU.add),
                        reads=[ybuf.b, ytmp.b], writes=[ybuf.b])
            pool.op(lambda e: e.tensor_tensor(out=ybuf[:T, :], in0=ybuf[:T, :], in1=sz[:T, :], op=ALU.mult),
                    reads=[ybuf.b, sz.b], writes=[ybuf.b])
            for g in range(8):
                act.op(lambda e, g=g: e.activation(out=ytmp[:T, g * 256:(g + 1) * 256], in_=ybuf[:T, g * 256:(g + 1) * 256], func=AF.Square,
                                                   accum_out=ss[:T, 0, g:g + 1]), reads=[ybuf.b], writes=[ytmp.b, ss.b])
            dve.op(lambda e: e.tensor_scalar(out=ss[:T, 1, :], in0=ss[:T, 0, :], scalar1=1.0 / 256, scalar2=RMS_EPS_,
                                             op0=ALU.mult, op1=ALU.add), reads=[ss.b], writes=[ss.b])
            act.op(lambda e: e.activation(out=ss[:T, 2, :], in_=ss[:T, 1, :], func=AF.Sqrt), reads=[ss.b], writes=[ss.b])
            dve.op(lambda e: e.reciprocal(out=ss[:T, 3, :], in_=ss[:T, 2, :]), reads=[ss.b], writes=[ss.b])
            dve.op(lambda e: e.tensor_tensor(out=ynb[:T, :].rearrange("p (g d) -> p g d", d=256),
                                             in0=ybuf[:T, :].rearrange("p (g d) -> p g d", d=256),
                                             in1=ss[:T, 3, :].unsqueeze(2).to_broadcast([T, 8, 256]), op=ALU.mult),
                   reads=[ybuf.b, ss.b], writes=[ynb.b])
            for r in range(2):
                for c8 in range(8):
                    c = r * 8 + c8
                    pe.op(lambda e, c=c, c8=c8: e.transpose(out=pT[:, c8 * 128:c8 * 128 + T], in_=ynb[:T, c * 128:(c + 1) * 128],
                                                            identity=ident[:T, :T]), reads=[ynb.b, ident.b], writes=[pT.b])
                for c8 in range(8):
                    c = r * 8 + c8
                    act.op(lambda e, c=c, c8=c8: e.activation(out=ynT[:, c, :T], in_=pT[:, c8 * 128:c8 * 128 + T], func=AF.Identity,
                                                              scale=normw[:, c:c + 1]), reads=[pT.b, normw.b], writes=[ynT.b])

            def obody(k, w):
                for h in range(2):
                    pe.op(lambda e, h=h: e.matmul(pA[:T, h * 512:(h + 1) * 512], lhsT=ynT[:, k, :T], rhs=w[:, h * 512:(h + 1) * 512],
                                                  start=(k == 0), stop=(k == 15)), reads=[ynT.b, w.b], writes=[pAh[0]])
            stream(16, r_out, lambda k: outp_s[k], obody, WB["outp"])
            residual(T, xc, pAh[0], xb)
            layer_norm(T, xb, 2, xa)
            ffn(T, 1)
            residual(T, xa, pAh[0], xb)
            layer_norm(T, xb, 3, xc)
            ple(T, 1, s, t0, xc, xa)
            if mode != "halo":
                sp.dma(y_d[s][yrow:yrow + T, :], xa[:T, :], oslot, reads=[xa.b])

        def run_sample():
            s = 1
            ftail.load(ftail[:], ftail_d[s][:, :, :, :, :])
            ctail.load(ctail[:], ctail_d[s][:, :, :])
            ST.load(ST[:], st_d[s][:, :])
            invc.load(invc[:], invc_d[s][:, :, :])
            act.op(lambda e: e.activation(out=STb[:, :], in_=ST[:, :], func=AF.Copy), reads=[ST.b], writes=[STb.b])
            block_p1(s, 0, LS, True, "sample", 0)
            block_p2(s, 0, LS, "sample", 0, 0)
            sp.dma(ftail_o[s][:, :, :, :, :], ftail[:], oslot, reads=[ftail.b])
            sp.dma(ctail_o[s][:, :, :], ctail[:], oslot, reads=[ctail.b])
            sp.dma(st_o[s][:, :], ST[:], oslot, reads=[ST.b])
            sp.dma(pool_o[s][:, :], x_d[s][LS - 15:LS, :], oslot)

        def mask_all(b):
            for ap_, tb in ((ftail[:].rearrange("p a b c d -> p (a b c d)"), ftail), (ctail[:].rearrange("p a b -> p (a b)"), ctail)):
                dve.op(lambda e, ap_=ap_: e.tensor_scalar_mul(out=ap_, in0=ap_, scalar1=vfl[:, b:b + 1]),
                       reads=[tb.b, vfl.b], writes=[tb.b])

        def run_prompt():
            s = 0
            ftail.load(ftail[:], ftail_d[s][:, :, :, :, :])
            ctail.load(ctail[:], ctail_d[s][:, :, :])
            ST.load(ST[:], st_d[s][:, :])
            act.op(lambda e: e.activation(out=STb[:, :], in_=ST[:, :], func=AF.Copy), reads=[ST.b], writes=[STb.b])
            firsts = {NST + 1 - k * NOWN: k for k in range(4)}
            for b in range(NBLK):
                mode = "state" if b < NST else ("halo" if b == NST else "own")
                block_p1(s, b * 128, 128, b in firsts, mode, b, vcol=(b if mode != "own" else None),
                         invtab=invt[firsts[b]] if b in firsts else None)
                if mode != "state":
                    block_p2(s, b * 128, 128, mode, b, (b - NST - 1) * 128)
                if (b + 1) in firsts:
                    mask_all(b)
            sp.dma(ftail_o[s][:, :, :, :, :], ftail[:], oslot, reads=[ftail.b])
            sp.dma(ctail_o[s][:, :, :], ctail[:], oslot, reads=[ctail.b])
            sp.dma(st_o[s][:, :], ST[:], oslot, reads=[ST.b])
            sp.dma(pool_o[s][:, :], x_d[s][LP - 15:LP, :], oslot)

        ccslot = P.slot()
        run_sample()
        run_prompt()
        fin = Buf("fin")
        fin.rd = {oslot.key: oslot.n}
        P.finish([fin, xtok[0], ftail, ctail, ST] if False else [fin])
    return nc


def _fm(a, nchunk):
    r, c = a.shape
    return np.ascontiguousarray(a.reshape(r, nchunk, 128).transpose(2, 1, 0))


def _prep_inputs(x_prompt, x_sample, p_prompt, p_sample, cache_pool, cache_ssm_conv, state_ssm, cache_ffn_conv,
                 pool_w, pool_scale, ssm_in_proj, ssm_conv_w, ssm_conv_b, ssm_dt_bias, ssm_A_log, ssm_D, ssm_norm_w,
                 ssm_out_proj, ln_mix_g, ln_mix_b, ffn_up, ffn_conv_w, ffn_conv_b, ffn_down, ln_ffn_g, ln_ffn_b,
                 ple_proj, ple_gate_w, ple_gate_b, n_cores=8):
    f = lambda a: np.ascontiguousarray(np.asarray(a, dtype=np.float32))
    L = x_prompt.shape[1]
    LS = x_sample.shape[1]
    shared = {
        "pool_w": f(pool_w[0]), "pool_scale": f(pool_scale), "in_proj": f(ssm_in_proj[0]),
        "scw": f(np.asarray(ssm_conv_w[0]).reshape(4, 32, 128).transpose(2, 1, 0)),
        "scb": f(np.asarray(ssm_conv_b[0]).reshape(32, 128).T),
        "dtb": f(ssm_dt_bias), "alog": f(ssm_A_log), "dsk": f(ssm_D),
        "normw": f(np.asarray(ssm_norm_w[0]).reshape(16, 128).T),
        "out_proj": f(ssm_out_proj[0]),
        "lng": f(np.stack([ln_mix_g[0], ln_ffn_g[0], ln_mix_g[1], ln_ffn_g[1]])),
        "lnb": f(np.stack([ln_mix_b[0], ln_ffn_b[0], ln_mix_b[1], ln_ffn_b[1]])),
        "fcw": f(np.asarray(ffn_conv_w).reshape(2, 3, 2, 22, 128).transpose(4, 0, 2, 3, 1)),
        "fcb": f(np.asarray(ffn_conv_b).reshape(2, 2, 22, 128).transpose(3, 0, 1, 2)),
        "gate_b": f(ple_gate_b),
    }
    for i in range(2):
        shared["ffn_up%d" % i] = f(ffn_up[i])
        shared["ffn_down%d" % i] = f(ffn_down[i])
        shared["ple_proj%d" % i] = f(ple_proj[i])
        shared["gate_w%d" % i] = f(ple_gate_w[i])
    wins = (2, 4, 8, 16)
    invc_p = np.zeros((128, 8, 16), np.float32)
    invc_s = np.zeros((128, 8, 16), np.float32)
    for c in range(8):
        w = wins[c // 2]
        invc_p[:, c, :] = 1.0 / np.minimum(w, np.arange(16) + 1)
        invc_s[:, c, :] = 1.0 / w
    in_maps = []
    nseq = x_prompt.shape[0]
    nseg = n_cores // nseq
    Lo = L // nseg
    NOWN = Lo // 128
    NST = 3 * NOWN
    NBLK = NST + 1 + NOWN
    LP = NBLK * 128

    def ext(a, start, n):
        out = np.zeros((n,) + a.shape[1:], np.float32)
        lo = max(start, 0)
        if start + n > lo:
            out[lo - start:] = a[lo:start + n]
        return out

    for core in range(n_cores):
        bp = core // nseg
        seg = core % nseg
        bs = core % x_sample.shape[0]
        start = (seg * NOWN - NST - 1) * 128
        nv = NST + 1 - seg * NOWN
        m = dict(shared)
        xp = f(x_prompt[bp])
        m["x0"] = ext(xp, start, LP)
        m["xT0"] = _fm(ext(xp, start - 15, LP + 15), 8)
        xs = f(x_sample[bs])
        m["x1"] = xs
        m["xT1"] = _fm(np.concatenate([f(cache_pool[0, bs]), xs], 0), 8)
        for i in range(2):
            m["pT0_%d" % i] = _fm(ext(f(p_prompt[i, bp]), start, LP), 2)
            m["pT1_%d" % i] = _fm(f(p_sample[i, bs]), 2)
        m["ftail0"] = np.zeros((128, 2, 2, 22, 2), np.float32)
        m["ftail1"] = f(np.asarray(cache_ffn_conv[:, bs]).reshape(2, 2, 2, 22, 128).transpose(4, 0, 2, 3, 1))
        m["ctail0"] = np.zeros((128, 32, 3), np.float32)
        m["ctail1"] = f(np.asarray(cache_ssm_conv[0, bs]).reshape(3, 32, 128).transpose(2, 1, 0))
        m["st0"] = np.zeros((128, 2048), np.float32)
        m["st1"] = f(np.asarray(state_ssm[0, bs]).reshape(2048, 128).T)
        m["invc0"] = np.stack([invc_p if k == seg else invc_s for k in range(4)])
        m["invc1"] = invc_s
        vfl = np.ones((128, NBLK), np.float32)
        vfl[:, :nv] = 0.0
        m["vfl"] = vfl
        in_maps.append(m)
    return in_maps, Lo, LS


def _unft(a):
    return np.ascontiguousarray(a.transpose(1, 4, 2, 3, 0).reshape(2, 2, 5632))


def _assemble(results, nb_p, nb_s):
    nseg = len(results) // nb_p
    last = [b * nseg + nseg - 1 for b in range(nb_p)]
    y_p = np.stack([np.concatenate([results[b * nseg + g]["y0"] for g in range(nseg)], 0) for b in range(nb_p)])
    y_s = np.stack([results[b]["y1"] for b in range(nb_s)])
    pool_p = np.stack([results[c]["pool_o0"] for c in last])[None]
    pool_s = np.stack([results[b]["pool_o1"] for b in range(nb_s)])[None]
    unc = lambda a: np.ascontiguousarray(a.transpose(2, 1, 0).reshape(3, 4096))
    sconv_p = np.stack([unc(results[c]["ctail_o0"]) for c in last])[None]
    sconv_s = np.stack([unc(results[b]["ctail_o1"]) for b in range(nb_s)])[None]
    uns = lambda a: np.ascontiguousarray(a.T.reshape(32, 64, 128))
    st_p = np.stack([uns(results[c]["st_o0"]) for c in last])[None]
    st_s = np.stack([uns(results[b]["st_o1"]) for b in range(nb_s)])[None]
    ffn_p = np.stack([_unft(results[c]["ftail_o0"]) for c in last], 1)
    ffn_s = np.stack([_unft(results[b]["ftail_o1"]) for b in range(nb_s)], 1)
    return (y_p, y_s, pool_p, pool_s, sconv_p, sconv_s, st_p, st_s, ffn_p, ffn_s)


_NC_CACHE = {}


def kernel(**inputs):
    in_maps, L, LS = _prep_inputs(**inputs)
    key = (L, LS)
    if key not in _NC_CACHE:
        _NC_CACHE[key] = build_program(L, LS)
    nc = _NC_CACHE[key]
    res = run_bass_kernel_spmd(nc, in_maps, core_ids=list(range(8)))
    outs = _assemble(res.results, inputs["x_prompt"].shape[0], inputs["x_sample"].shape[0])
    return tuple(np.asarray(o, dtype=np.float32) for o in outs)
```

```python
import numpy as np
import concourse.bass as bass
import concourse.mybir as mybir
from concourse.bass_utils import run_bass_kernel_spmd

F32 = mybir.dt.float32
BF16 = mybir.dt.bfloat16
AF = mybir.ActivationFunctionType
ALU = mybir.AluOpType


class Buf:
    __slots__ = ("name", "lw", "rd")

    def __init__(self, name):
        self.name = name
        self.lw = {}
        self.rd = {}


class Eng:
    def __init__(self, prog, name, handle, sem, is_pe=False, compute=True):
        self.prog = prog
        self.name = name
        self.h = handle
        self.sem = sem
        self.n = 0
        self.seen = {}
        self.q = []
        self.is_pe = is_pe
        self.compute = compute

    def _sync(self, deps):
        for key, cnt in deps.items():
            if key == self.name and self.is_pe:
                continue
            if self.seen.get(key, 0) >= cnt:
                continue
            self.seen[key] = cnt
            sem = self.prog.sems[key]
            self.q.append(("w", sem, cnt))

    def op(self, fn, reads=(), writes=()):
        deps = {}
        for b in reads:
            for k, c in b.lw.items():
                if deps.get(k, 0) < c:
                    deps[k] = c
        for b in writes:
            for k, c in b.lw.items():
                if deps.get(k, 0) < c:
                    deps[k] = c
            for k, c in b.rd.items():
                if deps.get(k, 0) < c:
                    deps[k] = c
        self._sync(deps)
        self.n += 1
        self.q.append(("o", fn, self.sem, 1))
        for b in reads:
            if b.rd.get(self.name, 0) < self.n:
                b.rd[self.name] = self.n
        for b in writes:
            b.lw = {self.name: self.n}
            b.rd = {}

    def dma(self, out_ap, in_ap, slot, reads=(), writes=()):
        deps = {}
        for b in reads:
            for k, c in b.lw.items():
                if deps.get(k, 0) < c:
                    deps[k] = c
        for b in writes:
            for k, c in b.lw.items():
                if deps.get(k, 0) < c:
                    deps[k] = c
            for k, c in b.rd.items():
                if deps.get(k, 0) < c:
                    deps[k] = c
        self._sync(deps)
        slot.n += 16
        self.q.append(("d", out_ap, in_ap, slot.sem))
        for b in reads:
            if b.rd.get(slot.key, 0) < slot.n:
                b.rd[slot.key] = slot.n
        for b in writes:
            b.lw = dict(b.lw) if False else {slot.key: slot.n}
            b.rd = {}

    def collective(self, tin, tout, slot, reads=(), writes=()):
        deps = {}
        for b in reads:
            for k, c in b.lw.items():
                if deps.get(k, 0) < c:
                    deps[k] = c
        for b in writes:
            for k, c in list(b.lw.items()) + list(b.rd.items()):
                if deps.get(k, 0) < c:
                    deps[k] = c
        self._sync(deps)
        slot.n += 1
        self.q.append(("c", tin, tout, slot.sem))
        for b in reads:
            b.rd[slot.key] = slot.n
        for b in writes:
            b.lw = {slot.key: slot.n}
            b.rd = {}

    def wait_all(self, bufs):
        deps = {}
        for b in bufs:
            for k, c in list(b.lw.items()) + list(b.rd.items()):
                if deps.get(k, 0) < c:
                    deps[k] = c
        self._sync(deps)

    def emit(self, h):
        for it in self.q:
            if it[0] == "w":
                h.wait_ge(it[1], it[2])
            elif it[0] == "o":
                ins = it[1](h)
                ins.then_inc(it[2], it[3])
            elif it[0] == "c":
                h.collective_compute("AllGather", ALU.bypass, replica_groups=[list(range(8))],
                                     ins=[it[1].ap().opt()], outs=[it[2].ap().opt()]).then_inc(it[3], 1)
            else:
                h.dma_start(out=it[1], in_=it[2]).then_inc(it[3], 16)


class Slot:
    def __init__(self, prog, key, sem):
        self.key = key
        self.sem = sem
        self.n = 0


class Prog:
    def __init__(self, nc, stack):
        self.nc = nc
        self.stack = stack
        self.sems = {}
        self.nslots = 0
        mk = lambda nm: stack.enter_context(nc.semaphore(nm))
        self.pe = Eng(self, "pe", nc.tensor, mk("s_pe"), is_pe=True)
        self.act = Eng(self, "act", nc.scalar, mk("s_act"))
        self.dve = Eng(self, "dve", nc.vector, mk("s_dve"))
        self.pool = Eng(self, "pool", nc.gpsimd, mk("s_pool"))
        self.sp = Eng(self, "sp", nc.sync, None, compute=False)
        for e in (self.pe, self.act, self.dve, self.pool):
            self.sems[e.name] = e.sem

    def slot(self, name=None):
        self.nslots += 1
        key = "dma%d" % self.nslots
        sem = self.stack.enter_context(self.nc.semaphore("s_" + key))
        self.sems[key] = sem
        return Slot(self, key, sem)

    def sb(self, name, shape, dtype):
        return self.stack.enter_context(self.nc.sbuf_tensor(name, list(shape), dtype))

    def ps(self, name, shape, dtype):
        return self.stack.enter_context(self.nc.psum_tensor(name, list(shape), dtype))

    def finish(self, final_bufs):
        self.sp.wait_all(final_bufs)
        with self.nc.Block() as block:
            @block.tensor
            def _(e):
                self.pe.emit(e)

            @block.scalar
            def _(e):
                self.act.emit(e)

            @block.vector
            def _(e):
                self.dve.emit(e)

            @block.gpsimd
            def _(e):
                self.pool.emit(e)

            @block.sync
            def _(e):
                self.sp.emit(e)

D = 1024
DFF = 2816
NJ = 22
DIN = 2048
CONVD = 4096
NH = 32
ALPHA_ = (2 * 2) ** 0.25
LN_EPS_ = 1e-5
RMS_EPS_ = 1e-5


def build_program(L, LS=32):
    import contextlib
    nc = bass.Bass("TRN2", target_bir_lowering=False)
    NOWN = L // 128
    NST = 3 * NOWN
    NBLK = NST + 1 + NOWN
    LP = NBLK * 128

    def din(name, shape, dt=F32):
        return nc.dram_tensor(name, list(shape), dt, kind="ExternalInput").ap()

    def dout(name, shape, dt=F32):
        return nc.dram_tensor(name, list(shape), dt, kind="ExternalOutput").ap()

    def dint(name, shape, dt=BF16):
        return nc.dram_tensor(name, list(shape), dt).ap()

    Ls = [LP, LS]
    xT_d = [din("xT%d" % s, [128, 8, 15 + Ls[s]]) for s in range(2)]
    x_d = [din("x%d" % s, [Ls[s], D]) for s in range(2)]
    pT_d = [[din("pT%d_%d" % (s, i), [128, 2, Ls[s]]) for i in range(2)] for s in range(2)]
    ftail_d = [din("ftail%d" % s, [128, 2, 2, NJ, 2]) for s in range(2)]
    ctail_d = [din("ctail%d" % s, [128, 32, 3]) for s in range(2)]
    st_d = [din("st%d" % s, [128, DIN]) for s in range(2)]
    invc_d = [din("invc0", [4, 128, 8, 16]), din("invc1", [128, 8, 16])]
    pool_w_d = din("pool_w", [4, 256, 256])
    pool_scale_d = din("pool_scale", [1, D])
    in_proj_d = din("in_proj", [D, 6176])
    scw_d = din("scw", [128, 32, 4])
    scb_d = din("scb", [128, 32])
    dtb_d = din("dtb", [1, NH])
    alog_d = din("alog", [1, NH])
    dsk_d = din("dsk", [1, NH])
    normw_d = din("normw", [128, 16])
    out_proj_d = din("out_proj", [DIN, D])
    lng_d = din("lng", [4, D])
    lnb_d = din("lnb", [4, D])
    lngf_d = din("lngf", [128, 4, 8])
    lnbf_d = din("lnbf", [128, 4, 8])
    ffn_up_d = [din("ffn_up%d" % i, [D, 2 * DFF]) for i in range(2)]
    fcw_d = din("fcw", [128, 2, 2, NJ, 3])
    fcb_d = din("fcb", [128, 2, 2, NJ])
    ffn_down_d = [din("ffn_down%d" % i, [DFF, D]) for i in range(2)]
    ple_proj_d = [din("ple_proj%d" % i, [256, D]) for i in range(2)]
    gate_w_d = [din("gate_w%d" % i, [D, D]) for i in range(2)]
    gate_b_d = din("gate_b", [2, D])
    y_d = [dout("y0", [L, D]), dout("y1", [LS, D])]
    vfl_d = din("vfl", [128, NBLK])
    pool_o = [dout("pool_o%d" % s, [15, D]) for s in range(2)]
    ftail_o = [dout("ftail_o%d" % s, [128, 2, 2, NJ, 2]) for s in range(2)]
    ctail_o = [dout("ctail_o%d" % s, [128, 32, 3]) for s in range(2)]
    st_o = [dout("st_o%d" % s, [128, DIN]) for s in range(2)]
    wup_s = [dint("wup_s%d" % i, [NJ, 128, 8, 256]) for i in range(2)]
    wdn_s = [dint("wdn_s%d" % i, [NJ, 128, D]) for i in range(2)]
    gate_s = [dint("gate_s%d" % i, [8, 128, D]) for i in range(2)]
    proj_s = [dint("proj_s%d" % i, [128, 2, D]) for i in range(2)]
    gb_s = dint("gb_s", [2, D])
    inz_s = dint("inz_s", [8, 128, DIN])
    inx_s = dint("inx_s", [32, 128, 8, 128])
    indt_s = dint("indt_s", [128, 8, NH])
    outp_s = dint("outp_s", [16, 128, D])

    with contextlib.ExitStack() as stack:
        P = Prog(nc, stack)
        pe, act, dve, pool, sp = P.pe, P.act, P.dve, P.pool, P.sp

        class T_:
            def __init__(self, name, shape, dt, psum=False):
                self.t = (P.ps if psum else P.sb)(name, shape, dt)
                self.b = Buf(name)

            def __getitem__(self, k):
                return self.t[k]

            def store(self, dst, src, writes=()):
                if not hasattr(self, "sslot"):
                    self.sslot = P.slot()
                sp.dma(dst, src, self.sslot, reads=[self.b], writes=list(writes))

            def load(self, dst, src, reads=()):
                if not hasattr(self, "slot"):
                    self.slot = P.slot()
                sp.dma(dst, src, self.slot, reads=list(reads), writes=[self.b])

        cnt = [0]

        def SB(shape, dt=F32, name=None):
            cnt[0] += 1
            return T_(name or ("t%d" % cnt[0]), shape, dt)

        WB = {}
        WS = {}

        def conv(key, dst, src):
            if key not in WB:
                WB[key] = Buf("w_" + key)
                WS[key] = P.slot()
            pool.dma(dst, src, WS[key], writes=[WB[key]])

        def conv_layer(i):
            for j in range(NJ):
                for gu in range(2):
                    src = ffn_up_d[i][:, gu * DFF + j * 128: gu * DFF + (j + 1) * 128].rearrange("(k p) n -> p k n", p=128)
                    conv("wup%d" % i, wup_s[i][j, :, :, gu * 128:(gu + 1) * 128], src)
            for j0 in range(0, NJ, 2):
                conv("wdn%d" % i, wdn_s[i][j0:j0 + 2].rearrange("j p n -> p j n"),
                     ffn_down_d[i][j0 * 128:(j0 + 2) * 128, :].rearrange("(j p) n -> p j n", p=128))
            for k0 in range(0, 8, 2):
                conv("gate%d" % i, gate_s[i][k0:k0 + 2].rearrange("j p n -> p j n"),
                     gate_w_d[i][k0 * 128:(k0 + 2) * 128, :].rearrange("(j p) n -> p j n", p=128))
            conv("proj%d" % i, proj_s[i][:, :, :], ple_proj_d[i].rearrange("(k p) n -> p k n", p=128))

        conv("gb", gb_s[:, :], gate_b_d[:, :])
        conv("indt", indt_s[:, :, :], in_proj_d[:, 6144:6176].rearrange("(k p) n -> p k n", p=128))
        conv_layer(0)
        for k in range(8):
            conv("inz", inz_s[k], in_proj_d[k * 128:(k + 1) * 128, 0:DIN])
        for c in range(32):
            conv("inx", inx_s[c], in_proj_d[:, DIN + c * 128: DIN + (c + 1) * 128].rearrange("(k p) n -> p k n", p=128))
        for k0 in range(0, 16, 2):
            conv("outp", outp_s[k0:k0 + 2].rearrange("j p n -> p j n"),
                 out_proj_d[k0 * 128:(k0 + 2) * 128, :].rearrange("(j p) n -> p j n", p=128))
        conv_layer(1)

        ld = P.slot()
        identf = SB([128, 128]); ident = SB([128, 128], BF16)
        tri = SB([128, 128]); trib = SB([128, 128], BF16); onesf = SB([128, 128]); mstr = SB([128, 128], BF16)
        mtmp = identf
        onesb = SB([1, 128], BF16)
        pool.op(lambda e: e.memset(onesf[:], 1.0), writes=[onesf.b])
        pool.op(lambda e: e.memset(onesb[:], 1.0), writes=[onesb.b])
        pool.op(lambda e: e.affine_select(out=identf[:], in_=onesf[:], pattern=[[-1, 128]], compare_op=ALU.is_equal,
                                          fill=0.0, base=0, channel_multiplier=1), reads=[onesf.b], writes=[identf.b])
        dve.op(lambda e: e.tensor_copy(out=ident[:], in_=identf[:]), reads=[identf.b], writes=[ident.b])
        pool.op(lambda e: e.affine_select(out=tri[:], in_=onesf[:], pattern=[[1, 128]], compare_op=ALU.is_ge,
                                          fill=0.0, base=0, channel_multiplier=-1), reads=[onesf.b], writes=[tri.b])
        dve.op(lambda e: e.tensor_copy(out=trib[:], in_=tri[:]), reads=[tri.b], writes=[trib.b])
        pool.op(lambda e: e.affine_select(out=mtmp[:], in_=onesf[:], pattern=[[-1, 128]], compare_op=ALU.is_gt,
                                          fill=0.0, base=0, channel_multiplier=1), reads=[onesf.b], writes=[mtmp.b])
        dve.op(lambda e: e.tensor_copy(out=mstr[:], in_=mtmp[:]), reads=[mtmp.b], writes=[mstr.b])

        lng = [SB([128, D]) for _ in range(4)]
        lnb = [SB([128, D]) for _ in range(4)]
        for q in range(4):
            lng[q].load(lng[q][:], lng_d[q:q + 1, :].partition_broadcast(128))
            lnb[q].load(lnb[q][:], lnb_d[q:q + 1, :].partition_broadcast(128))
        lngf = SB([128, 4, 8]); lnbf = SB([128, 4, 8]); epsT = SB([128, 1])
        lngf.load(lngf[:], lngf_d[:, :, :]); lnbf.load(lnbf[:], lnbf_d[:, :, :])
        pool.op(lambda e: e.memset(epsT[:], LN_EPS_), writes=[epsT.b])
        gbb = SB([1, 2, D], BF16)
        gbb.load(gbb[:], gb_s[:, :].rearrange("(o a) n -> o a n", o=1), reads=[WB["gb"]])
        dtb = SB([128, NH]); abc = SB([128, NH]); dbc = SB([128, NH])
        dtb.load(dtb[:], dtb_d[0:1, :].partition_broadcast(128))
        abc.load(abc[:], alog_d[0:1, :].partition_broadcast(128))
        dbc.load(dbc[:], dsk_d[0:1, :].partition_broadcast(128))
        act.op(lambda e: e.activation(out=abc[:], in_=abc[:], func=AF.Exp), reads=[abc.b], writes=[abc.b])
        dve.op(lambda e: e.tensor_scalar_mul(out=abc[:], in0=abc[:], scalar1=-1.0), reads=[abc.b], writes=[abc.b])
        scw = SB([128, 32, 4]); scb = SB([128, 32]); normw = SB([128, 16])
        fcw = SB([128, 2, 2, NJ, 3]); fcb = SB([128, 2, 2, NJ])
        scw.load(scw[:], scw_d[:, :, :])
        scb.load(scb[:], scb_d[:, :])
        normw.load(normw[:], normw_d[:, :])
        fcw.load(fcw[:], fcw_d[:, :, :, :, :])
        fcb.load(fcb[:], fcb_d[:, :, :, :])
        poolw = SB([128, 8, 256], BF16)
        POOLW_SETUP = True
        wdt = SB([128, 8, NH], BF16)
        wdt.load(wdt[:], indt_s[:, :, :], reads=[WB["indt"]])

        pA = T_("pA", [128, 2048], F32, psum=True)
        pB = T_("pB", [128, 1024], F32, psum=True)
        pT = T_("pT", [128, 1024], BF16, psum=True)
        pS = T_("pS", [128, 512], F32, psum=True)
        pAh = [Buf("pA0"), Buf("pA1")]
        pBh = [Buf("pB0"), Buf("pB1")]
        pA23 = [Buf("pA2"), Buf("pA3")]

        NSLOT = 7
        wslots = [SB([128, 2048], BF16, name="wslot%d" % q) for q in range(NSLOT)]
        for w_ in wslots:
            w_.slot = P.slot()

        class WView:
            def __init__(self, slot_t, view):
                self.v = view
                self.b = slot_t.b

            def __getitem__(self, k):
                return self.v[k]

        class Ring:
            def __init__(self, kind, lo, hi, srcbuf=None):
                self.kind = kind
                self.lo = lo
                self.depth = hi - lo

            def view(self, t):
                k = self.kind
                if k == "up":
                    return t.t[:, :].rearrange("p (k n) -> p k n", n=256)
                if k == "row1024":
                    return t.t[:, 0:1024]
                if k == "proj":
                    return t.t[:, :].rearrange("p (k n) -> p k n", n=1024)
                if k == "row2048":
                    return t.t[:, :]
                if k == "inx":
                    return t.t[:, 0:1024].rearrange("p (k n) -> p k n", n=128)
                raise KeyError(k)

            def load(self, i, src, wb):
                t = wslots[self.lo + i % self.depth]
                v = self.view(t)
                sp.dma(v, src, t.slot, reads=[wb], writes=[t.b])
                return WView(t, v)

        r_up = Ring("up", 0, 3)
        r_dn = Ring("row1024", 3, 6)
        r_gate = Ring("row1024", 0, 6)
        r_proj = Ring("proj", 6, 7)
        r_inz = Ring("row2048", 0, 6)
        r_inx = Ring("inx", 0, 6)
        r_out = Ring("row1024", 0, 6)

        def stream(n, ring, srcf, body, wb):
            tl = {}
            for i in range(min(ring.depth, n)):
                tl[i] = ring.load(i, srcf(i), wb)
            for i in range(n):
                body(i, tl[i])
                if i + ring.depth < n:
                    tl[i + ring.depth] = ring.load(i + ring.depth, srcf(i + ring.depth), wb)

        xin = SB([128, 8, 143])
        xtok = [SB([128, D]) for _ in range(3)]
        sA = SB([128, 2, 143]); sBf = SB([128, 2, 143]); t16 = SB([128, 2, 16])
        pooled = SB([128, 8, 128], BF16)
        xT = SB([128, 8, 128], BF16)
        xb16 = SB([128, D], BF16)
        U1 = SB([128, 32 * 128], BF16)
        hb = [[SB([128, 130]) for _ in range(2)] for _ in range(4)]
        acc = [[SB([128, 128]) for _ in range(2)] for _ in range(4)]
        sgs = [SB([128, 128]) for _ in range(2)]
        for hp_ in hb:
            for h__ in hp_:
                h__.tb = Buf("tail")
        pst = SB([128, 2, 128]); pTb = SB([128, 2, 128], BF16)
        gate = SB([128, D]); tmpt = SB([128, D])
        stats = SB([128, 2, 6]); mv = SB([128, 8])
        ftail = SB([128, 2, 2, NJ, 2]); ctail = SB([128, 32, 3])
        invc = SB([128, 8, 16])
        sz = SB([128, DIN]); ybuf = SB([128, DIN]); ytmp = SB([128, DIN])
        xsT = SB([128, 16, 128], BF16); BT = SB([128, 8, 128], BF16); CT = SB([128, 8, 128], BF16)
        cbuf = [SB([128, 131]) for _ in range(4)]; cacc = [SB([128, 128]) for _ in range(4)]
        for c__ in cbuf:
            c__.tb = Buf("ctailb")
        xdt = SB([128, DIN], BF16); xdtd = SB([128, DIN], BF16); Btok = SB([128, 1024], BF16)
        cbm = SB([128, 8, 128], BF16)
        rseg = [SB([128, 8, 128], BF16) for _ in range(2)]
        _ex = SB([128, 8, 128], BF16); expL = [_ex, _ex]
        ST = SB([128, DIN]); STb = SB([128, DIN], BF16)
        ynb = SB([128, DIN], BF16); ynT = SB([128, 16, 128], BF16)
        sm = SB([128, 8, NH])
        ss = SB([128, 4, 8])
        io = P.slot(); io2 = P.slot(); oslot = P.slot()
        vfl = SB([128, NBLK]); vfl.load(vfl[:], vfl_d[:, :])
        invt = [SB([128, 8, 16]) for _ in range(4)]
        for q in range(4):
            invt[q].load(invt[q][:], invc_d[0][q])
        scr = Buf("scratch")
        sz.load(sz[:].rearrange("p (c d) -> p c d", d=256), pool_w_d.rearrange("g (h p) d -> p (g h) d", p=128))
        ytmp.load(ytmp[:, 0:D], pool_scale_d[0:1, :].partition_broadcast(128))
        for c in range(8):
            g = c // 2
            dve.op(lambda e, c=c, g=g: e.tensor_tensor(out=poolw[:, c, :], in0=sz[:, c * 256:(c + 1) * 256],
                                                       in1=ytmp[:, g * 256:(g + 1) * 256], op=ALU.mult),
                   reads=[sz.b, ytmp.b], writes=[poolw.b])
        outbufs = []

        def layer_norm(T, u, q, xo, use_xT=True):
            for h in range(2):
                dve.op(lambda e, h=h: e.bn_stats(out=stats[:T, h, :], in_=u[:T, h * 512:(h + 1) * 512]),
                       reads=[u.b], writes=[stats.b])
            dve.op(lambda e: e.bn_aggr(out=mv[:T, 0:2], in_=stats[:T].rearrange("p a b -> p (a b)")), reads=[stats.b], writes=[mv.b])
            act.op(lambda e: e.activation(out=mv[:T, 3:4], in_=mv[:T, 1:2], func=AF.Sqrt, bias=epsT[:T, 0:1]), reads=[mv.b, epsT.b], writes=[mv.b])
            dve.op(lambda e: e.reciprocal(out=mv[:T, 4:5], in_=mv[:T, 3:4]), reads=[mv.b], writes=[mv.b])
            dve.op(lambda e: e.tensor_scalar(out=xb16[:T, :], in0=u[:T, :], scalar1=mv[:T, 0:1], scalar2=mv[:T, 4:5],
                                             op0=ALU.subtract, op1=ALU.mult), reads=[u.b, mv.b], writes=[xb16.b])
            dve.op(lambda e: e.scalar_tensor_tensor(out=mv[:T, 5:6], in0=mv[:T, 0:1], scalar=-1.0, in1=mv[:T, 4:5],
                                                    op0=ALU.mult, op1=ALU.mult), reads=[mv.b], writes=[mv.b])
            for c in range(8):
                pe.op(lambda e, c=c: e.transpose(out=pT[:, c * 128:c * 128 + T], in_=xb16[:T, c * 128:(c + 1) * 128],
                                                 identity=ident[:T, :T]), reads=[xb16.b, ident.b], writes=[pT.b])
            act.op(lambda e: e.activation(out=tmpt[:T, :], in_=u[:T, :], func=AF.Identity, scale=mv[:T, 4:5], bias=mv[:T, 5:6]),
                   reads=[u.b, mv.b], writes=[tmpt.b])
            for c in range(8):
                act.op(lambda e, c=c: e.activation(out=xT[:, c, :T], in_=pT[:, c * 128:c * 128 + T], func=AF.Identity,
                                                   scale=lngf[:, q, c:c + 1], bias=lnbf[:, q, c:c + 1]),
                       reads=[pT.b, lngf.b, lnbf.b], writes=[xT.b])
            pool.op(lambda e: e.tensor_tensor(out=tmpt[:T, :], in0=tmpt[:T, :], in1=lng[q][:T, :], op=ALU.mult),
                    reads=[tmpt.b, lng[q].b], writes=[tmpt.b])
            pool.op(lambda e: e.tensor_tensor(out=xo[:T, :], in0=tmpt[:T, :], in1=lnb[q][:T, :], op=ALU.add),
                    reads=[tmpt.b, lnb[q].b], writes=[xo.b])

        def to_T(T, xo):
            act.op(lambda e: e.activation(out=xb16[:T, :], in_=xo[:T, :], func=AF.Copy), reads=[xo.b], writes=[xb16.b])
            for c in range(8):
                pe.op(lambda e, c=c: e.transpose(out=pT[:, c * 128:c * 128 + T], in_=xb16[:T, c * 128:(c + 1) * 128],
                                                 identity=ident[:T, :T]), reads=[xb16.b, ident.b], writes=[pT.b])
            dve.op(lambda e: e.tensor_copy(out=xT[:, :, :T], in_=pT[:].rearrange("p (c t) -> p c t", t=128)[:, :, :T]),
                   reads=[pT.b], writes=[xT.b])

        def ffn(T, i):
            hT = U1
            st_ = {}

            def up(j, w):
                par = j % 2
                if par == 0:
                    pp_, po_, pbs_ = pB, 0, [[pBh[0]], [pBh[1]]]
                else:
                    pp_, po_, pbs_ = pA, 1024, [[pA23[0], pAh[1]], [pA23[1], pAh[1]]]
                for gu in range(2):
                    for k in range(8):
                        pe.op(lambda e, gu=gu, k=k: e.matmul(pp_[:, po_ + gu * 512:po_ + gu * 512 + T], lhsT=w[:, k, gu * 128:(gu + 1) * 128],
                                                             rhs=xT[:, k, :T], start=(k == 0), stop=(k == 7)),
                              reads=[w.b, xT.b], writes=pbs_[gu])
                p4 = j % 4
                sg = sgs[j % 2]
                for gu in range(2):
                    h_ = hb[p4][gu]; a_ = acc[p4][gu]
                    act.op(lambda e, gu=gu, h_=h_: e.activation(out=h_[:, 2:2 + T], in_=pp_[:, po_ + gu * 512:po_ + gu * 512 + T], func=AF.Copy),
                           reads=pbs_[gu], writes=[h_.b])
                    act.op(lambda e, gu=gu, a_=a_: e.activation(out=a_[:, :T], in_=pp_[:, po_ + gu * 512:po_ + gu * 512 + T], func=AF.Identity,
                                                                scale=fcw[:, i, gu, j, 2:3], bias=fcb[:, i, gu, j:j + 1]),
                           reads=pbs_[gu] + [fcw.b, fcb.b], writes=[a_.b])
                    pool.op(lambda e, gu=gu, h_=h_: e.tensor_copy(out=h_[:, 0:2], in_=ftail[:, i, gu, j, :]),
                            reads=[ftail.b], writes=[h_.tb])
                    pool.op(lambda e, gu=gu, h_=h_: e.tensor_copy(out=ftail[:, i, gu, j, :], in_=h_[:, T:T + 2]),
                            reads=[h_.b], writes=[ftail.b])
                for tap in (1, 0):
                    for gu in range(2):
                        h_ = hb[p4][gu]; a_ = acc[p4][gu]
                        dve.op(lambda e, gu=gu, h_=h_, a_=a_, tap=tap: e.scalar_tensor_tensor(
                            out=a_[:, :T], in0=h_[:, tap:tap + T], scalar=fcw[:, i, gu, j, tap:tap + 1], in1=a_[:, :T],
                            op0=ALU.mult, op1=ALU.add), reads=[h_.b, h_.tb, fcw.b, a_.b], writes=[a_.b])
                act.op(lambda e: e.activation(out=sg[:, :T], in_=acc[p4][0][:, :T], func=AF.Silu), reads=[acc[p4][0].b], writes=[sg.b])
                pool.op(lambda e: e.tensor_tensor(out=hT[:, j * 128:j * 128 + T], in0=sg[:, :T], in1=acc[p4][1][:, :T], op=ALU.mult),
                        reads=[sg.b, acc[p4][1].b], writes=[hT.b])

            def down(j, w):
                for h in range(2):
                    pe.op(lambda e, h=h: e.matmul(pA[:T, h * 512:(h + 1) * 512], lhsT=hT[:, j * 128:j * 128 + T],
                                                  rhs=w[:, h * 512:(h + 1) * 512], start=(j == 0), stop=(j == NJ - 1)),
                          reads=[hT.b, w.b], writes=[pAh[0]])

            upt = {}; dnt = {}
            for jj in range(min(3, NJ)):
                upt[jj] = r_up.load(jj, wup_s[i][jj], WB["wup%d" % i]); dnt[jj] = r_dn.load(jj, wdn_s[i][jj], WB["wdn%d" % i])
            for j in range(NJ):
                up(j, upt[j])
                down(j, dnt[j])
                if j + 3 < NJ:
                    upt[j + 3] = r_up.load(j + 3, wup_s[i][j + 3], WB["wup%d" % i]); dnt[j + 3] = r_dn.load(j + 3, wdn_s[i][j + 3], WB["wdn%d" % i])

        def ple(T, i, s, t0, x2, xo):
            for h in range(2):
                pe.op(lambda e, h=h: e.matmul(pA[:T, h * 512:(h + 1) * 512], lhsT=onesb[0:1, :T], rhs=gbb[0:1, i, h * 512:(h + 1) * 512],
                                              start=True, stop=False), reads=[onesb.b, gbb.b], writes=[pAh[0]])

            def body(k, w):
                for h in range(2):
                    pe.op(lambda e, h=h: e.matmul(pA[:T, h * 512:(h + 1) * 512], lhsT=xT[:, k, :T], rhs=w[:, h * 512:(h + 1) * 512],
                                                  start=False, stop=(k == 7)), reads=[xT.b, w.b], writes=[pAh[0]])
            stream(8, r_gate, lambda k: gate_s[i][k], body, WB["gate%d" % i])
            pst.load(pst[:, :, :T], pT_d[s][i][:, :, t0:t0 + T])
            pool.op(lambda e: e.tensor_copy(out=pTb[:, :, :T], in_=pst[:, :, :T]), reads=[pst.b], writes=[pTb.b])
            wp = r_proj.load(0, proj_s[i][:, :, :], WB["proj%d" % i])
            for h in range(2):
                for k in range(2):
                    pe.op(lambda e, h=h, k=k: e.matmul(pA[:T, 1024 + h * 512:1024 + (h + 1) * 512], lhsT=pTb[:, k, :T],
                                                       rhs=wp[:, k, h * 512:(h + 1) * 512], start=(k == 0), stop=(k == 1)),
                          reads=[pTb.b, wp.b], writes=[pAh[1]])
            act.op(lambda e: e.activation(out=gate[:T, :], in_=pA[:T, 0:1024], func=AF.Sigmoid), reads=[pAh[0]], writes=[gate.b])
            dve.op(lambda e: e.tensor_tensor(out=gate[:T, :], in0=gate[:T, :], in1=pA[:T, 1024:2048], op=ALU.mult),
                   reads=[gate.b, pAh[1]], writes=[gate.b])
            pool.op(lambda e: e.tensor_tensor(out=xo[:T, :], in0=gate[:T, :], in1=x2[:T, :], op=ALU.add),
                    reads=[gate.b, x2.b], writes=[xo.b])

        def residual(T, xres, pbuf, uo):
            dve.op(lambda e: e.scalar_tensor_tensor(out=uo[:T, :], in0=xres[:T, :], scalar=ALPHA_, in1=pA[:T, 0:1024],
                                                    op0=ALU.mult, op1=ALU.add), reads=[xres.b, pbuf], writes=[uo.b])

        def bc3(ap2, T, n):
            return ap2.unsqueeze(2).to_broadcast([T, NH, n])

        def block_p1(s, t0, T, first, mode, blk, vcol=None, invtab=None):
            invc_ = invtab if invtab is not None else invc
            xa, xb, xc = xtok
            xin.load(xin[:, :, :15 + T], xT_d[s][:, :, t0:t0 + 15 + T])
            xa.load(xa[:T, :], x_d[s][t0:t0 + T, :])
            W = 15 + T
            for g in range(4):
                cs_ = slice(2 * g, 2 * g + 2)
                wsz = 2 ** (g + 1)
                pool.op(lambda e, cs_=cs_: e.tensor_tensor(out=sA[:, :, 1:W], in0=xin[:, cs_, 1:W], in1=xin[:, cs_, 0:W - 1], op=ALU.add),
                        reads=[xin.b], writes=[sA.b])
                cur, oth = sA, sBf
                sh = 1
                for lvl in range(g):
                    sh2 = sh * 2
                    lo = 2 * sh2 - 1
                    pool.op(lambda e, cur=cur, oth=oth, lo=lo, sh2=sh2: e.tensor_tensor(
                        out=oth[:, :, lo:W], in0=cur[:, :, lo:W], in1=cur[:, :, lo - sh2:W - sh2], op=ALU.add),
                        reads=[cur.b], writes=[oth.b])
                    cur, oth = oth, cur
                    sh = sh2
                dve.op(lambda e, cur=cur, cs_=cs_, wsz=wsz: e.scalar_tensor_tensor(
                    out=pooled[:, cs_, :T], in0=cur[:, :, 15:W], scalar=1.0 / wsz, in1=xin[:, cs_, 15:W],
                    op0=ALU.mult, op1=ALU.subtract), reads=[cur.b, xin.b], writes=[pooled.b])
                if first:
                    dve.op(lambda e, cur=cur, cs_=cs_: e.tensor_tensor(out=t16[:], in0=cur[:, :, 15:31], in1=invc_[:, cs_, :], op=ALU.mult),
                           reads=[cur.b, invc_.b], writes=[t16.b])
                    dve.op(lambda e, cs_=cs_: e.tensor_tensor(out=pooled[:, cs_, 0:16], in0=t16[:], in1=xin[:, cs_, 15:31], op=ALU.subtract),
                           reads=[t16.b, xin.b], writes=[pooled.b])
            for g in range(4):
                for h in range(2):
                    c = 2 * g + h
                    pe.op(lambda e, g=g, h=h, c=c: e.matmul(pA[:T, g * 256:(g + 1) * 256], lhsT=pooled[:, c, :T], rhs=poolw[:, c, :],
                                                            start=(h == 0), stop=(h == 1)),
                          reads=[pooled.b, poolw.b], writes=[pAh[0]])
            residual(T, xa, pAh[0], xb)
            layer_norm(T, xb, 0, xc)
            ffn(T, 0)
            residual(T, xc, pAh[0], xb)
            layer_norm(T, xb, 1, xa)
            ple(T, 0, s, t0, xa, xc)
            to_T(T, xc)

            if mode != "state":
                def zbody(k, w):
                    for cc in range(4):
                        pe.op(lambda e, cc=cc: e.matmul(pA[:T, cc * 512:(cc + 1) * 512], lhsT=xT[:, k, :T], rhs=w[:, cc * 512:(cc + 1) * 512],
                                                        start=(k == 0), stop=(k == 7)), reads=[xT.b, w.b], writes=[pAh[cc // 2]])
                stream(8, r_inz, lambda k: inz_s[k], zbody, WB["inz"])
                act.op(lambda e: e.activation(out=sz[:T, :], in_=pA[:T, :], func=AF.Silu), reads=[pAh[0], pAh[1]], writes=[sz.b])
            for k in range(8):
                pe.op(lambda e, k=k: e.matmul(pS[:T, 0:NH], lhsT=xT[:, k, :T], rhs=wdt[:, k, :], start=(k == 0), stop=(k == 7)),
                      reads=[xT.b, wdt.b], writes=[pS.b])
            S_ = lambda n: sm[:T, n, :]
            dve.op(lambda e: e.tensor_tensor(out=S_(0), in0=pS[:T, 0:NH], in1=dtb[:T, :], op=ALU.add), reads=[pS.b, dtb.b], writes=[sm.b])
            dve.op(lambda e: e.scalar_tensor_tensor(out=S_(1), in0=S_(0), scalar=-1.0, in1=S_(0), op0=ALU.mult, op1=ALU.max), reads=[sm.b], writes=[sm.b])
            act.op(lambda e: e.activation(out=S_(1), in_=S_(1), func=AF.Exp, scale=-1.0), reads=[sm.b], writes=[sm.b])
            act.op(lambda e: e.activation(out=S_(1), in_=S_(1), func=AF.Ln, bias=1.0), reads=[sm.b], writes=[sm.b])
            dve.op(lambda e: e.tensor_scalar_max(out=S_(0), in0=S_(0), scalar1=0.0), reads=[sm.b], writes=[sm.b])
            dve.op(lambda e: e.tensor_tensor(out=S_(2), in0=S_(0), in1=S_(1), op=ALU.add), reads=[sm.b], writes=[sm.b])
            if vcol is not None:
                dve.op(lambda e: e.tensor_scalar_mul(out=S_(2), in0=S_(2), scalar1=vfl[:T, vcol:vcol + 1]), reads=[sm.b, vfl.b], writes=[sm.b])
            dve.op(lambda e: e.tensor_tensor(out=S_(3), in0=S_(2), in1=abc[:T, :], op=ALU.mult), reads=[sm.b, abc.b], writes=[sm.b])

            def xbody(c, w):
                par = c % 2
                for k in range(8):
                    pe.op(lambda e, k=k: e.matmul(pB[:, par * 512:par * 512 + T], lhsT=w[:, k, :], rhs=xT[:, k, :T],
                                                  start=(k == 0), stop=(k == 7)), reads=[w.b, xT.b], writes=[pBh[par]])
                cb_ = cbuf[c % 4]; ca_ = cacc[c % 4]
                act.op(lambda e: e.activation(out=cb_[:, 3:3 + T], in_=pB[:, par * 512:par * 512 + T], func=AF.Copy),
                       reads=[pBh[par]], writes=[cb_.b])
                act.op(lambda e: e.activation(out=ca_[:, :T], in_=pB[:, par * 512:par * 512 + T], func=AF.Identity,
                                              scale=scw[:, c, 3:4], bias=scb[:, c:c + 1]),
                       reads=[pBh[par], scw.b, scb.b], writes=[ca_.b])
                pool.op(lambda e: e.tensor_copy(out=cb_[:, 0:3], in_=ctail[:, c, :]), reads=[ctail.b], writes=[cb_.tb])
                pool.op(lambda e: e.tensor_copy(out=ctail[:, c, :], in_=cb_[:, T:T + 3]), reads=[cb_.b], writes=[ctail.b])
                for tap in (2, 1, 0):
                    dve.op(lambda e, tap=tap: e.scalar_tensor_tensor(out=ca_[:, :T], in0=cb_[:, tap:tap + T], scalar=scw[:, c, tap:tap + 1],
                                                                     in1=ca_[:, :T], op0=ALU.mult, op1=ALU.add),
                           reads=[cb_.b, cb_.tb, scw.b, ca_.b], writes=[ca_.b])
                if c < 16:
                    dst, db = xsT[:, c, :T], xsT.b
                elif c < 24:
                    dst, db = BT[:, c - 16, :T], BT.b
                else:
                    dst, db = CT[:, c - 24, :T], CT.b
                act.op(lambda e: e.activation(out=dst, in_=ca_[:, :T], func=AF.Silu), reads=[ca_.b], writes=[db])
            stream(24 if mode == "state" else 32, r_inx, lambda c: inx_s[c], xbody, WB["inx"])
            for r in range(2):
                for c8 in range(8):
                    c = r * 8 + c8
                    pe.op(lambda e, c=c, c8=c8: e.transpose(out=pT[:T, c8 * 128:(c8 + 1) * 128], in_=xsT[:, c, :T], identity=ident[:, :]),
                          reads=[xsT.b, ident.b], writes=[pT.b])
                hs = slice(r * 16, (r + 1) * 16)
                dve.op(lambda e, r=r, hs=hs: e.tensor_tensor(
                    out=xdt[:T, r * 1024:(r + 1) * 1024].rearrange("p (h d) -> p h d", d=64),
                    in0=pT[:T, :].rearrange("p (h d) -> p h d", d=64),
                    in1=sm[:T, 2, hs].unsqueeze(2).to_broadcast([T, 16, 64]), op=ALU.mult),
                    reads=[pT.b, sm.b], writes=[xdt.b])
                if mode != "state":
                  dve.op(lambda e, r=r, hs=hs: e.tensor_tensor(
                    out=ybuf[:T, r * 1024:(r + 1) * 1024].rearrange("p (h d) -> p h d", d=64),
                    in0=pT[:T, :].rearrange("p (h d) -> p h d", d=64),
                    in1=dbc[:T, hs].unsqueeze(2).to_broadcast([T, 16, 64]), op=ALU.mult),
                    reads=[pT.b, dbc.b], writes=[ybuf.b])
            for g in range(8):
                pe.op(lambda e, g=g: e.transpose(out=pT[:T, g * 128:(g + 1) * 128], in_=BT[:, g, :T], identity=ident[:, :]),
                      reads=[BT.b, ident.b], writes=[pT.b])
            dve.op(lambda e: e.tensor_copy(out=Btok[:T, :], in_=pT[:T, :]), reads=[pT.b], writes=[Btok.b])
            pe.op(lambda e: e.matmul(pS[:T, 32:64], lhsT=tri[:T, :T], rhs=sm[:T, 3, :], start=True, stop=True),
                  reads=[tri.b, sm.b], writes=[pS.b])
            pe.op(lambda e: e.matmul(pS[:, 64:96], lhsT=onesf[:T, :], rhs=sm[:T, 3, :], start=True, stop=True),
                  reads=[onesf.b, sm.b], writes=[pS.b])
            act.op(lambda e: e.activation(out=S_(4), in_=pS[:T, 32:64], func=AF.Exp), reads=[pS.b], writes=[sm.b])
            act.op(lambda e: e.activation(out=S_(5), in_=pS[:T, 32:64], func=AF.Copy), reads=[pS.b], writes=[sm.b])
            dve.op(lambda e: e.tensor_tensor(out=S_(6), in0=pS[:T, 64:96], in1=S_(5), op=ALU.subtract), reads=[pS.b, sm.b], writes=[sm.b])
            act.op(lambda e: e.activation(out=S_(6), in_=S_(6), func=AF.Exp), reads=[sm.b], writes=[sm.b])
            act.op(lambda e: e.activation(out=sm[:, 7, :], in_=pS[:, 64:96], func=AF.Exp), reads=[pS.b], writes=[sm.b])
            if False:
                dve.op(lambda e: e.tensor_tensor(out=S_(0), in0=pS[:T, 32:64], in1=carry[:T, :], op=ALU.add), reads=[pS.b, carry.b], writes=[sm.b])
                act.op(lambda e: e.activation(out=S_(0), in_=S_(0), func=AF.Exp), reads=[sm.b], writes=[sm.b])
                dve.op(lambda e: e.tensor_tensor(out=carry[:, :], in0=carry[:, :], in1=pS[:, 64:96], op=ALU.add), reads=[pS.b, carry.b], writes=[carry.b])
            if mode != "state":
                for g in range(8):
                    pe.op(lambda e, g=g: e.matmul(pB[:T, g * 128:g * 128 + T], lhsT=BT[:, g, :T], rhs=CT[:, g, :T], start=True, stop=True),
                          reads=[BT.b, CT.b], writes=[pBh[g // 4]])
                dve.op(lambda e: e.tensor_tensor(out=cbm[:T, :, :T], in0=pB[:T, :].rearrange("p (g q) -> p g q", q=128)[:, :, :T],
                                                 in1=tri[:T, :T].unsqueeze(1).to_broadcast([T, 8, T]), op=ALU.mult),
                       reads=[pBh[0], pBh[1], tri.b], writes=[cbm.b])
                MT = U1
                for r in range(4):
                    rs_ = rseg[r % 2]; ex_ = expL[r % 2]
                    pool.op(lambda e, r=r, rs_=rs_: e.tensor_tensor(
                        out=rs_[:T, :, :T], in0=sm[:T, 3, r * 8:(r + 1) * 8].unsqueeze(2).to_broadcast([T, 8, T]),
                        in1=trib[:T, :T].unsqueeze(1).to_broadcast([T, 8, T]), op=ALU.mult),
                        reads=[sm.b, trib.b], writes=[rs_.b])
                    hp = r % 2
                    for q2 in range(2):
                        pe.op(lambda e, rs_=rs_, q2=q2, hp=hp: e.matmul(
                            pA[:T, hp * 1024 + q2 * 512: hp * 1024 + q2 * 512 + 4 * T].rearrange("p (h q) -> p h q", q=T),
                            lhsT=mstr[:T, :T], rhs=rs_[:T, q2 * 4:(q2 + 1) * 4, :T], start=True, stop=True),
                            reads=[mstr.b, rs_.b], writes=[pAh[hp]])
                    for q2 in range(2):
                        act.op(lambda e, ex_=ex_, q2=q2, hp=hp: e.activation(
                            out=ex_[:T, q2 * 4:(q2 + 1) * 4, :T],
                            in_=pA[:T, hp * 1024 + q2 * 512: hp * 1024 + q2 * 512 + 4 * T].rearrange("p (h q) -> p h q", q=T),
                            func=AF.Exp), reads=[pAh[hp]], writes=[ex_.b])
                    for g2 in range(2):
                        gg = 2 * r + g2
                        dve.op(lambda e, ex_=ex_, g2=g2, gg=gg, r=r: e.tensor_tensor(
                            out=MT[:T, (r * 8 + g2 * 4) * 128:(r * 8 + g2 * 4 + 4) * 128].rearrange("p (h q) -> p h q", q=128)[:, :, :T],
                            in0=ex_[:T, g2 * 4:(g2 + 1) * 4, :T],
                            in1=cbm[:T, gg:gg + 1, :T].to_broadcast([T, 4, T]), op=ALU.mult),
                            reads=[ex_.b, cbm.b], writes=[MT.b])
                for g in range(8):
                    pe.op(lambda e, g=g: e.matmul(pA[:T, g * 256:(g + 1) * 256], lhsT=CT[:, g, :T], rhs=STb[:, g * 256:(g + 1) * 256],
                                                  start=True, stop=True), reads=[CT.b, STb.b], writes=[pAh[g // 4]])
                dve.op(lambda e: e.tensor_tensor(out=ytmp[:T, :].rearrange("p (h d) -> p h d", d=64),
                                                 in0=pA[:T, :].rearrange("p (h d) -> p h d", d=64),
                                                 in1=bc3(sm[:T, 4, :], T, 64), op=ALU.mult),
                       reads=[pAh[0], pAh[1], sm.b], writes=[ytmp.b])
                pool.op(lambda e: e.tensor_tensor(out=ybuf[:T, :], in0=ybuf[:T, :], in1=ytmp[:T, :], op=ALU.add),
                        reads=[ybuf.b, ytmp.b], writes=[ybuf.b])
                for h in range(NH):
                    pe.op(lambda e, h=h: e.matmul(pA[:T, h * 64:(h + 1) * 64], lhsT=MT[:T, h * 128:h * 128 + T], rhs=xdt[:T, h * 64:(h + 1) * 64],
                                                  start=True, stop=True), reads=[MT.b, xdt.b], writes=[pAh[h // 16]])
                dve.op(lambda e: e.tensor_tensor(out=ybuf[:T, :], in0=ybuf[:T, :], in1=pA[:T, :], op=ALU.add),
                       reads=[ybuf.b, pAh[0], pAh[1]], writes=[ybuf.b])
            pool.op(lambda e: e.tensor_tensor(out=xdtd[:T, :].rearrange("p (h d) -> p h d", d=64),
                                              in0=xdt[:T, :].rearrange("p (h d) -> p h d", d=64),
                                              in1=bc3(sm[:T, 6, :], T, 64), op=ALU.mult), reads=[xdt.b, sm.b], writes=[xdtd.b])
            for g in range(8):
                pe.op(lambda e, g=g: e.matmul(pA[:, g * 256:(g + 1) * 256], lhsT=Btok[:T, g * 128:(g + 1) * 128],
                                              rhs=xdtd[:T, g * 256:(g + 1) * 256], start=True, stop=True),
                      reads=[Btok.b, xdtd.b], writes=[pAh[g // 4]])
            pool.op(lambda e: e.tensor_tensor(out=ST[:, :].rearrange("p (h d) -> p h d", d=64),
                                              in0=ST[:, :].rearrange("p (h d) -> p h d", d=64),
                                              in1=sm[:, 7, :].unsqueeze(2).to_broadcast([128, NH, 64]), op=ALU.mult),
                    reads=[ST.b, sm.b], writes=[ST.b])
            dve.op(lambda e: e.tensor_tensor(out=ST[:, :], in0=ST[:, :], in1=pA[:, :], op=ALU.add),
                   reads=[ST.b, pAh[0], pAh[1]], writes=[ST.b])
            act.op(lambda e: e.activation(out=STb[:, :], in_=ST[:, :], func=AF.Copy), reads=[ST.b], writes=[STb.b])
            if False:
                xc.store(x3s[blk], xc[:, :], writes=[scr])
                sz.store(szs[blk], sz[:, :], writes=[scr])
                ybuf.store(yss[blk], ybuf[:, :], writes=[scr])
                CT.store(cts[blk], CT[:, :, :], writes=[scr])
                sm.store(ess[blk], sm[:, 0, :], writes=[scr])

        def block_p2(s, t0, T, mode, blk, yrow):
            xa, xb, xc = xtok
            if False:
                xc.load(xc[:, :], x3s[blk], reads=[scr])
                sz.load(sz[:, :], szs[blk], reads=[scr])
                ybuf.load(ybuf[:, :], yss[blk], reads=[scr])
                CT.load(CT[:, :, :], cts[blk], reads=[scr])
                sm.load(sm[:, 0, :], ess[blk], reads=[scr])
                for g in range(8):
                    pe.op(lambda e, g=g: e.matmul(pA[:T, g * 256:(g + 1) * 256], lhsT=CT[:, g, :T], rhs=STb[:, g * 256:(g + 1) * 256],
                                                  start=True, stop=True), reads=[CT.b, STb.b], writes=[pAh[g // 4]])
                dve.op(lambda e: e.tensor_tensor(out=ytmp[:T, :].rearrange("p (h d) -> p h d", d=64),
                                                 in0=pA[:T, :].rearrange("p (h d) -> p h d", d=64),
                                                 in1=bc3(sm[:T, 0, :], T, 64), op=ALU.mult),
                       reads=[pAh[0], pAh[1], sm.b], writes=[ytmp.b])
                pool.op(lambda e: e.tensor_tensor(out=ybuf[:T, :], in0=ybuf[:T, :], in1=ytmp[:T, :], op=ALU.add),
                        reads=[ybuf.b, ytmp.b], writes=[ybuf.b])
            pool.op(lambda e: e.tensor_tensor(out=ybuf[:T, :], in0=ybuf[:T, :], in1=sz[:T, :], op=ALU.mult),
                    reads=[ybuf.b, sz.b], writes=[ybuf.b])
            for g in range(8):
                act.op(lambda e, g=g: e.activation(out=ytmp[:T, g * 256:(g + 1) * 256], in_=ybuf[:T, g * 256:(g + 1) * 256], func=AF.Square,
                                                   accum_out=ss[:T, 0, g:g + 1]), reads=[ybuf.b], writes=[ytmp.b, ss.b])
            dve.op(lambda e: e.tensor_scalar(out=ss[:T, 1, :], in0=ss[:T, 0, :], scalar1=1.0 / 256, scalar2=RMS_EPS_,
                                             op0=ALU.mult, op1=ALU.add), reads=[ss.b], writes=[ss.b])
            act.op(lambda e: e.activation(out=ss[:T, 2, :], in_=ss[:T, 1, :], func=AF.Sqrt), reads=[ss.b], writes=[ss.b])
            dve.op(lambda e: e.reciprocal(out=ss[:T, 3, :], in_=ss[:T, 2, :]), reads=[ss.b], writes=[ss.b])
            dve.op(lambda e: e.tensor_tensor(out=ynb[:T, :].rearrange("p (g d) -> p g d", d=256),
                                             in0=ybuf[:T, :].rearrange("p (g d) -> p g d", d=256),
                                             in1=ss[:T, 3, :].unsqueeze(2).to_broadcast([T, 8, 256]), op=ALU.mult),
                   reads=[ybuf.b, ss.b], writes=[ynb.b])
            for r in range(2):
                for c8 in range(8):
                    c = r * 8 + c8
                    pe.op(lambda e, c=c, c8=c8: e.transpose(out=pT[:, c8 * 128:c8 * 128 + T], in_=ynb[:T, c * 128:(c + 1) * 128],
                                                            identity=ident[:T, :T]), reads=[ynb.b, ident.b], writes=[pT.b])
                for c8 in range(8):
                    c = r * 8 + c8
                    act.op(lambda e, c=c, c8=c8: e.activation(out=ynT[:, c, :T], in_=pT[:, c8 * 128:c8 * 128 + T], func=AF.Identity,
                                                              scale=normw[:, c:c + 1]), reads=[pT.b, normw.b], writes=[ynT.b])

            def obody(k, w):
                for h in range(2):
                    pe.op(lambda e, h=h: e.matmul(pA[:T, h * 512:(h + 1) * 512], lhsT=ynT[:, k, :T], rhs=w[:, h * 512:(h + 1) * 512],
                                                  start=(k == 0), stop=(k == 15)), reads=[ynT.b, w.b], writes=[pAh[0]])
            stream(16, r_out, lambda k: outp_s[k], obody, WB["outp"])
            residual(T, xc, pAh[0], xb)
            layer_norm(T, xb, 2, xa)
            ffn(T, 1)
            residual(T, xa, pAh[0], xb)
            layer_norm(T, xb, 3, xc)
            ple(T, 1, s, t0, xc, xa)
            if mode != "halo":
                sp.dma(y_d[s][yrow:yrow + T, :], xa[:T, :], oslot, reads=[xa.b])

        def run_sample():
            s = 1
            ftail.load(ftail[:], ftail_d[s][:, :, :, :, :])
            ctail.load(ctail[:], ctail_d[s][:, :, :])
            ST.load(ST[:], st_d[s][:, :])
            invc.load(invc[:], invc_d[s][:, :, :])
            act.op(lambda e: e.activation(out=STb[:, :], in_=ST[:, :], func=AF.Copy), reads=[ST.b], writes=[STb.b])
            block_p1(s, 0, LS, True, "sample", 0)
            block_p2(s, 0, LS, "sample", 0, 0)
            sp.dma(ftail_o[s][:, :, :, :, :], ftail[:], oslot, reads=[ftail.b])
            sp.dma(ctail_o[s][:, :, :], ctail[:], oslot, reads=[ctail.b])
            sp.dma(st_o[s][:, :], ST[:], oslot, reads=[ST.b])
            sp.dma(pool_o[s][:, :], x_d[s][LS - 15:LS, :], oslot)

        def mask_all(b):
            for ap_, tb in ((ftail[:].rearrange("p a b c d -> p (a b c d)"), ftail), (ctail[:].rearrange("p a b -> p (a b)"), ctail)):
                dve.op(lambda e, ap_=ap_: e.tensor_scalar_mul(out=ap_, in0=ap_, scalar1=vfl[:, b:b + 1]),
                       reads=[tb.b, vfl.b], writes=[tb.b])

        def run_prompt():
            s = 0
            ftail.load(ftail[:], ftail_d[s][:, :, :, :, :])
            ctail.load(ctail[:], ctail_d[s][:, :, :])
            ST.load(ST[:], st_d[s][:, :])
            act.op(lambda e: e.activation(out=STb[:, :], in_=ST[:, :], func=AF.Copy), reads=[ST.b], writes=[STb.b])
            firsts = {NST + 1 - k * NOWN: k for k in range(4)}
            for b in range(NBLK):
                mode = "state" if b < NST else ("halo" if b == NST else "own")
                block_p1(s, b * 128, 128, b in firsts, mode, b, vcol=(b if mode != "own" else None),
                         invtab=invt[firsts[b]] if b in firsts else None)
                if mode != "state":
                    block_p2(s, b * 128, 128, mode, b, (b - NST - 1) * 128)
                if (b + 1) in firsts:
                    mask_all(b)
            sp.dma(ftail_o[s][:, :, :, :, :], ftail[:], oslot, reads=[ftail.b])
            sp.dma(ctail_o[s][:, :, :], ctail[:], oslot, reads=[ctail.b])
            sp.dma(st_o[s][:, :], ST[:], oslot, reads=[ST.b])
            sp.dma(pool_o[s][:, :], x_d[s][LP - 15:LP, :], oslot)

        ccslot = P.slot()
        run_sample()
        run_prompt()
        fin = Buf("fin")
        fin.rd = {oslot.key: oslot.n}
        P.finish([fin, xtok[0], ftail, ctail, ST] if False else [fin])
    return nc


def _fm(a, nchunk):
    r, c = a.shape
    return np.ascontiguousarray(a.reshape(r, nchunk, 128).transpose(2, 1, 0))


def _prep_inputs(x_prompt, x_sample, p_prompt, p_sample, cache_pool, cache_ssm_conv, state_ssm, cache_ffn_conv,
                 pool_w, pool_scale, ssm_in_proj, ssm_conv_w, ssm_conv_b, ssm_dt_bias, ssm_A_log, ssm_D, ssm_norm_w,
                 ssm_out_proj, ln_mix_g, ln_mix_b, ffn_up, ffn_conv_w, ffn_conv_b, ffn_down, ln_ffn_g, ln_ffn_b,
                 ple_proj, ple_gate_w, ple_gate_b, n_cores=8):
    f = lambda a: np.ascontiguousarray(np.asarray(a, dtype=np.float32))
    L = x_prompt.shape[1]
    LS = x_sample.shape[1]
    shared = {
        "pool_w": f(pool_w[0]), "pool_scale": f(pool_scale), "in_proj": f(ssm_in_proj[0]),
        "scw": f(np.asarray(ssm_conv_w[0]).reshape(4, 32, 128).transpose(2, 1, 0)),
        "scb": f(np.asarray(ssm_conv_b[0]).reshape(32, 128).T),
        "dtb": f(ssm_dt_bias), "alog": f(ssm_A_log), "dsk": f(ssm_D),
        "normw": f(np.asarray(ssm_norm_w[0]).reshape(16, 128).T),
        "out_proj": f(ssm_out_proj[0]),
        "lng": f(np.stack([ln_mix_g[0], ln_ffn_g[0], ln_mix_g[1], ln_ffn_g[1]])),
        "lnb": f(np.stack([ln_mix_b[0], ln_ffn_b[0], ln_mix_b[1], ln_ffn_b[1]])),
        "lngf": f(np.stack([ln_mix_g[0], ln_ffn_g[0], ln_mix_g[1], ln_ffn_g[1]]).reshape(4, 8, 128).transpose(2, 0, 1)),
        "lnbf": f(np.stack([ln_mix_b[0], ln_ffn_b[0], ln_mix_b[1], ln_ffn_b[1]]).reshape(4, 8, 128).transpose(2, 0, 1)),
        "fcw": f(np.asarray(ffn_conv_w).reshape(2, 3, 2, 22, 128).transpose(4, 0, 2, 3, 1)),
        "fcb": f(np.asarray(ffn_conv_b).reshape(2, 2, 22, 128).transpose(3, 0, 1, 2)),
        "gate_b": f(ple_gate_b),
    }
    for i in range(2):
        shared["ffn_up%d" % i] = f(ffn_up[i])
        shared["ffn_down%d" % i] = f(ffn_down[i])
        shared["ple_proj%d" % i] = f(ple_proj[i])
        shared["gate_w%d" % i] = f(ple_gate_w[i])
    wins = (2, 4, 8, 16)
    invc_p = np.zeros((128, 8, 16), np.float32)
    invc_s = np.zeros((128, 8, 16), np.float32)
    for c in range(8):
        w = wins[c // 2]
        invc_p[:, c, :] = 1.0 / np.minimum(w, np.arange(16) + 1)
        invc_s[:, c, :] = 1.0 / w
    in_maps = []
    nseq = x_prompt.shape[0]
    nseg = n_cores // nseq
    Lo = L // nseg
    NOWN = Lo // 128
    NST = 3 * NOWN
    NBLK = NST + 1 + NOWN
    LP = NBLK * 128

    def ext(a, start, n):
        out = np.zeros((n,) + a.shape[1:], np.float32)
        lo = max(start, 0)
        if start + n > lo:
            out[lo - start:] = a[lo:start + n]
        return out

    for core in range(n_cores):
        bp = core // nseg
        seg = core % nseg
        bs = core % x_sample.shape[0]
        start = (seg * NOWN - NST - 1) * 128
        nv = NST + 1 - seg * NOWN
        m = dict(shared)
        xp = f(x_prompt[bp])
        m["x0"] = ext(xp, start, LP)
        m["xT0"] = _fm(ext(xp, start - 15, LP + 15), 8)
        xs = f(x_sample[bs])
        m["x1"] = xs
        m["xT1"] = _fm(np.concatenate([f(cache_pool[0, bs]), xs], 0), 8)
        for i in range(2):
            m["pT0_%d" % i] = _fm(ext(f(p_prompt[i, bp]), start, LP), 2)
            m["pT1_%d" % i] = _fm(f(p_sample[i, bs]), 2)
        m["ftail0"] = np.zeros((128, 2, 2, 22, 2), np.float32)
        m["ftail1"] = f(np.asarray(cache_ffn_conv[:, bs]).reshape(2, 2, 2, 22, 128).transpose(4, 0, 2, 3, 1))
        m["ctail0"] = np.zeros((128, 32, 3), np.float32)
        m["ctail1"] = f(np.asarray(cache_ssm_conv[0, bs]).reshape(3, 32, 128).transpose(2, 1, 0))
        m["st0"] = np.zeros((128, 2048), np.float32)
        m["st1"] = f(np.asarray(state_ssm[0, bs]).reshape(2048, 128).T)
        m["invc0"] = np.stack([invc_p if k == seg else invc_s for k in range(4)])
        m["invc1"] = invc_s
        vfl = np.ones((128, NBLK), np.float32)
        vfl[:, :nv] = 0.0
        m["vfl"] = vfl
        in_maps.append(m)
    return in_maps, Lo, LS


def _unft(a):
    return np.ascontiguousarray(a.transpose(1, 4, 2, 3, 0).reshape(2, 2, 5632))


def _assemble(results, nb_p, nb_s):
    nseg = len(results) // nb_p
    last = [b * nseg + nseg - 1 for b in range(nb_p)]
    y_p = np.stack([np.concatenate([results[b * nseg + g]["y0"] for g in range(nseg)], 0) for b in range(nb_p)])
    y_s = np.stack([results[b]["y1"] for b in range(nb_s)])
    pool_p = np.stack([results[c]["pool_o0"] for c in last])[None]
    pool_s = np.stack([results[b]["pool_o1"] for b in range(nb_s)])[None]
    unc = lambda a: np.ascontiguousarray(a.transpose(2, 1, 0).reshape(3, 4096))
    sconv_p = np.stack([unc(results[c]["ctail_o0"]) for c in last])[None]
    sconv_s = np.stack([unc(results[b]["ctail_o1"]) for b in range(nb_s)])[None]
    uns = lambda a: np.ascontiguousarray(a.T.reshape(32, 64, 128))
    st_p = np.stack([uns(results[c]["st_o0"]) for c in last])[None]
    st_s = np.stack([uns(results[b]["st_o1"]) for b in range(nb_s)])[None]
    ffn_p = np.stack([_unft(results[c]["ftail_o0"]) for c in last], 1)
    ffn_s = np.stack([_unft(results[b]["ftail_o1"]) for b in range(nb_s)], 1)
    return (y_p, y_s, pool_p, pool_s, sconv_p, sconv_s, st_p, st_s, ffn_p, ffn_s)


_NC_CACHE = {}


def kernel(**inputs):
    in_maps, L, LS = _prep_inputs(**inputs)
    key = (L, LS)
    if key not in _NC_CACHE:
        _NC_CACHE[key] = build_program(L, LS)
    nc = _NC_CACHE[key]
    res = run_bass_kernel_spmd(nc, in_maps, core_ids=list(range(8)))
    outs = _assemble(res.results, inputs["x_prompt"].shape[0], inputs["x_sample"].shape[0])
    return tuple(np.asarray(o, dtype=np.float32) for o in outs)
```

```python
import numpy as np
import concourse.bass as bass
import concourse.mybir as mybir
from concourse.bass_utils import run_bass_kernel_spmd

F32 = mybir.dt.float32
BF16 = mybir.dt.bfloat16
AF = mybir.ActivationFunctionType
ALU = mybir.AluOpType


class Buf:
    __slots__ = ("name", "lw", "rd")

    def __init__(self, name):
        self.name = name
        self.lw = {}
        self.rd = {}


class Eng:
    def __init__(self, prog, name, handle, sem, is_pe=False, compute=True):
        self.prog = prog
        self.name = name
        self.h = handle
        self.sem = sem
        self.n = 0
        self.seen = {}
        self.q = []
        self.is_pe = is_pe
        self.compute = compute

    def _sync(self, deps):
        for key, cnt in deps.items():
            if key == self.name and self.is_pe:
                continue
            if self.seen.get(key, 0) >= cnt:
                continue
            self.seen[key] = cnt
            sem = self.prog.sems[key]
            self.q.append(("w", sem, cnt))

    def op(self, fn, reads=(), writes=()):
        deps = {}
        for b in reads:
            for k, c in b.lw.items():
                if deps.get(k, 0) < c:
                    deps[k] = c
        for b in writes:
            for k, c in b.lw.items():
                if deps.get(k, 0) < c:
                    deps[k] = c
            for k, c in b.rd.items():
                if deps.get(k, 0) < c:
                    deps[k] = c
        self._sync(deps)
        self.n += 1
        self.q.append(("o", fn, self.sem, 1))
        for b in reads:
            if b.rd.get(self.name, 0) < self.n:
                b.rd[self.name] = self.n
        for b in writes:
            b.lw = {self.name: self.n}
            b.rd = {}

    def dma(self, out_ap, in_ap, slot, reads=(), writes=()):
        deps = {}
        for b in reads:
            for k, c in b.lw.items():
                if deps.get(k, 0) < c:
                    deps[k] = c
        for b in writes:
            for k, c in b.lw.items():
                if deps.get(k, 0) < c:
                    deps[k] = c
            for k, c in b.rd.items():
                if deps.get(k, 0) < c:
                    deps[k] = c
        self._sync(deps)
        slot.n += 16
        self.q.append(("d", out_ap, in_ap, slot.sem))
        for b in reads:
            if b.rd.get(slot.key, 0) < slot.n:
                b.rd[slot.key] = slot.n
        for b in writes:
            b.lw = dict(b.lw) if False else {slot.key: slot.n}
            b.rd = {}

    def collective(self, tin, tout, slot, reads=(), writes=()):
        deps = {}
        for b in reads:
            for k, c in b.lw.items():
                if deps.get(k, 0) < c:
                    deps[k] = c
        for b in writes:
            for k, c in list(b.lw.items()) + list(b.rd.items()):
                if deps.get(k, 0) < c:
                    deps[k] = c
        self._sync(deps)
        slot.n += 1
        self.q.append(("c", tin, tout, slot.sem))
        for b in reads:
            b.rd[slot.key] = slot.n
        for b in writes:
            b.lw = {slot.key: slot.n}
            b.rd = {}

    def wait_all(self, bufs):
        deps = {}
        for b in bufs:
            for k, c in list(b.lw.items()) + list(b.rd.items()):
                if deps.get(k, 0) < c:
                    deps[k] = c
        self._sync(deps)

    def emit(self, h):
        for it in self.q:
            if it[0] == "w":
                h.wait_ge(it[1], it[2])
            elif it[0] == "o":
                ins = it[1](h)
                ins.then_inc(it[2], it[3])
            elif it[0] == "c":
                h.collective_compute("AllGather", ALU.bypass, replica_groups=[list(range(8))],
                                     ins=[it[1].ap().opt()], outs=[it[2].ap().opt()]).then_inc(it[3], 1)
            else:
                h.dma_start(out=it[1], in_=it[2]).then_inc(it[3], 16)


class Slot:
    def __init__(self, prog, key, sem):
        self.key = key
        self.sem = sem
        self.n = 0


class Prog:
    def __init__(self, nc, stack):
        self.nc = nc
        self.stack = stack
        self.sems = {}
        self.nslots = 0
        mk = lambda nm: stack.enter_context(nc.semaphore(nm))
        self.pe = Eng(self, "pe", nc.tensor, mk("s_pe"), is_pe=True)
        self.act = Eng(self, "act", nc.scalar, mk("s_act"))
        self.dve = Eng(self, "dve", nc.vector, mk("s_dve"))
        self.pool = Eng(self, "pool", nc.gpsimd, mk("s_pool"))
        self.sp = Eng(self, "sp", nc.sync, None, compute=False)
        for e in (self.pe, self.act, self.dve, self.pool):
            self.sems[e.name] = e.sem

    def slot(self, name=None):
        self.nslots += 1
        key = "dma%d" % self.nslots
        sem = self.stack.enter_context(self.nc.semaphore("s_" + key))
        self.sems[key] = sem
        return Slot(self, key, sem)

    def sb(self, name, shape, dtype):
        return self.stack.enter_context(self.nc.sbuf_tensor(name, list(shape), dtype))

    def ps(self, name, shape, dtype):
        return self.stack.enter_context(self.nc.psum_tensor(name, list(shape), dtype))

    def finish(self, final_bufs):
        self.sp.wait_all(final_bufs)
        with self.nc.Block() as block:
            @block.tensor
            def _(e):
                self.pe.emit(e)

            @block.scalar
            def _(e):
                self.act.emit(e)

            @block.vector
            def _(e):
                self.dve.emit(e)

            @block.gpsimd
            def _(e):
                self.pool.emit(e)

            @block.sync
            def _(e):
                self.sp.emit(e)

D = 1024
DFF = 2816
NJ = 22
DIN = 2048
CONVD = 4096
NH = 32
ALPHA_ = (2 * 2) ** 0.25
LN_EPS_ = 1e-5
RMS_EPS_ = 1e-5


def build_program(L, LS=32):
    import contextlib
    nc = bass.Bass("TRN2", target_bir_lowering=False)
    NOWN = L // 128
    NST = 3 * NOWN
    NBLK = NST + 1 + NOWN
    LP = NBLK * 128

    def din(name, shape, dt=F32):
        return nc.dram_tensor(name, list(shape), dt, kind="ExternalInput").ap()

    def dout(name, shape, dt=F32):
        return nc.dram_tensor(name, list(shape), dt, kind="ExternalOutput").ap()

    def dint(name, shape, dt=BF16):
        return nc.dram_tensor(name, list(shape), dt).ap()

    Ls = [LP, LS]
    xT_d = [din("xT%d" % s, [128, 8, 15 + Ls[s]]) for s in range(2)]
    x_d = [din("x%d" % s, [Ls[s], D]) for s in range(2)]
    pT_d = [[din("pT%d_%d" % (s, i), [128, 2, Ls[s]]) for i in range(2)] for s in range(2)]
    ftail_d = [din("ftail%d" % s, [128, 2, 2, NJ, 2]) for s in range(2)]
    ctail_d = [din("ctail%d" % s, [128, 32, 3]) for s in range(2)]
    st_d = [din("st%d" % s, [128, DIN]) for s in range(2)]
    invc_d = [din("invc0", [4, 128, 8, 16]), din("invc1", [128, 8, 16])]
    pool_w_d = din("pool_w", [4, 256, 256])
    pool_scale_d = din("pool_scale", [1, D])
    in_proj_d = din("in_proj", [D, 6176])
    scw_d = din("scw", [128, 32, 4])
    scb_d = din("scb", [128, 32])
    dtb_d = din("dtb", [1, NH])
    alog_d = din("alog", [1, NH])
    dsk_d = din("dsk", [1, NH])
    normw_d = din("normw", [128, 16])
    out_proj_d = din("out_proj", [DIN, D])
    lng_d = din("lng", [4, D])
    lnb_d = din("lnb", [4, D])
    lngf_d = din("lngf", [128, 4, 8])
    lnbf_d = din("lnbf", [128, 4, 8])
    ffn_up_d = [din("ffn_up%d" % i, [D, 2 * DFF]) for i in range(2)]
    fcw_d = din("fcw", [128, 2, 2, NJ, 3])
    fcb_d = din("fcb", [128, 2, 2, NJ])
    ffn_down_d = [din("ffn_down%d" % i, [DFF, D]) for i in range(2)]
    ple_proj_d = [din("ple_proj%d" % i, [256, D]) for i in range(2)]
    gate_w_d = [din("gate_w%d" % i, [D, D]) for i in range(2)]
    gate_b_d = din("gate_b", [2, D])
    y_d = [dout("y0", [L, D]), dout("y1", [LS, D])]
    vfl_d = din("vfl", [128, NBLK])
    pool_o = [dout("pool_o%d" % s, [15, D]) for s in range(2)]
    ftail_o = [dout("ftail_o%d" % s, [128, 2, 2, NJ, 2]) for s in range(2)]
    ctail_o = [dout("ctail_o%d" % s, [128, 32, 3]) for s in range(2)]
    st_o = [dout("st_o%d" % s, [128, DIN]) for s in range(2)]
    wup_s = [dint("wup_s%d" % i, [NJ, 128, 8, 256]) for i in range(2)]
    wdn_s = [dint("wdn_s%d" % i, [NJ, 128, D]) for i in range(2)]
    gate_s = [dint("gate_s%d" % i, [8, 128, D]) for i in range(2)]
    proj_s = [dint("proj_s%d" % i, [128, 2, D]) for i in range(2)]
    gb_s = dint("gb_s", [2, D])
    inz_s = dint("inz_s", [8, 128, DIN])
    inx_s = dint("inx_s", [32, 128, 8, 128])
    indt_s = dint("indt_s", [128, 8, NH])
    outp_s = dint("outp_s", [16, 128, D])

    with contextlib.ExitStack() as stack:
        P = Prog(nc, stack)
        pe, act, dve, pool, sp = P.pe, P.act, P.dve, P.pool, P.sp

        class T_:
            def __init__(self, name, shape, dt, psum=False):
                self.t = (P.ps if psum else P.sb)(name, shape, dt)
                self.b = Buf(name)

            def __getitem__(self, k):
                return self.t[k]

            def store(self, dst, src, writes=()):
                if not hasattr(self, "sslot"):
                    self.sslot = P.slot()
                sp.dma(dst, src, self.sslot, reads=[self.b], writes=list(writes))

            def load(self, dst, src, reads=()):
                if not hasattr(self, "slot"):
                    self.slot = P.slot()
                sp.dma(dst, src, self.slot, reads=list(reads), writes=[self.b])

        cnt = [0]

        def SB(shape, dt=F32, name=None):
            cnt[0] += 1
            return T_(name or ("t%d" % cnt[0]), shape, dt)

        WB = {}
        WS = {}

        def conv(key, dst, src):
            if key not in WB:
                WB[key] = Buf("w_" + key)
                WS[key] = P.slot()
            pool.dma(dst, src, WS[key], writes=[WB[key]])

        def conv_layer(i):
            for j in range(NJ):
                for gu in range(2):
                    src = ffn_up_d[i][:, gu * DFF + j * 128: gu * DFF + (j + 1) * 128].rearrange("(k p) n -> p k n", p=128)
                    conv("wup%d" % i, wup_s[i][j, :, :, gu * 128:(gu + 1) * 128], src)
            for j0 in range(0, NJ, 2):
                conv("wdn%d" % i, wdn_s[i][j0:j0 + 2].rearrange("j p n -> p j n"),
                     ffn_down_d[i][j0 * 128:(j0 + 2) * 128, :].rearrange("(j p) n -> p j n", p=128))
            for k0 in range(0, 8, 2):
                conv("gate%d" % i, gate_s[i][k0:k0 + 2].rearrange("j p n -> p j n"),
                     gate_w_d[i][k0 * 128:(k0 + 2) * 128, :].rearrange("(j p) n -> p j n", p=128))
            conv("proj%d" % i, proj_s[i][:, :, :], ple_proj_d[i].rearrange("(k p) n -> p k n", p=128))

        conv("gb", gb_s[:, :], gate_b_d[:, :])
        conv("indt", indt_s[:, :, :], in_proj_d[:, 6144:6176].rearrange("(k p) n -> p k n", p=128))
        conv_layer(0)
        for k in range(8):
            conv("inz", inz_s[k], in_proj_d[k * 128:(k + 1) * 128, 0:DIN])
        for c in range(32):
            conv("inx", inx_s[c], in_proj_d[:, DIN + c * 128: DIN + (c + 1) * 128].rearrange("(k p) n -> p k n", p=128))
        for k0 in range(0, 16, 2):
            conv("outp", outp_s[k0:k0 + 2].rearrange("j p n -> p j n"),
                 out_proj_d[k0 * 128:(k0 + 2) * 128, :].rearrange("(j p) n -> p j n", p=128))
        conv_layer(1)

        ld = P.slot()
        identf = SB([128, 128]); ident = SB([128, 128], BF16)
        tri = SB([128, 128]); trib = SB([128, 128], BF16); onesf = SB([128, 128]); mstr = SB([128, 128], BF16)
        mtmp = identf
        onesb = SB([1, 128], BF16)
        pool.op(lambda e: e.memset(onesf[:], 1.0), writes=[onesf.b])
        pool.op(lambda e: e.memset(onesb[:], 1.0), writes=[onesb.b])
        pool.op(lambda e: e.affine_select(out=identf[:], in_=onesf[:], pattern=[[-1, 128]], compare_op=ALU.is_equal,
                                          fill=0.0, base=0, channel_multiplier=1), reads=[onesf.b], writes=[identf.b])
        dve.op(lambda e: e.tensor_copy(out=ident[:], in_=identf[:]), reads=[identf.b], writes=[ident.b])
        pool.op(lambda e: e.affine_select(out=tri[:], in_=onesf[:], pattern=[[1, 128]], compare_op=ALU.is_ge,
                                          fill=0.0, base=0, channel_multiplier=-1), reads=[onesf.b], writes=[tri.b])
        dve.op(lambda e: e.tensor_copy(out=trib[:], in_=tri[:]), reads=[tri.b], writes=[trib.b])
        pool.op(lambda e: e.affine_select(out=mtmp[:], in_=onesf[:], pattern=[[-1, 128]], compare_op=ALU.is_gt,
                                          fill=0.0, base=0, channel_multiplier=1), reads=[onesf.b], writes=[mtmp.b])
        dve.op(lambda e: e.tensor_copy(out=mstr[:], in_=mtmp[:]), reads=[mtmp.b], writes=[mstr.b])

        lng = [SB([128, D]) for _ in range(4)]
        lnb = [SB([128, D]) for _ in range(4)]
        for q in range(4):
            lng[q].load(lng[q][:], lng_d[q:q + 1, :].partition_broadcast(128))
            lnb[q].load(lnb[q][:], lnb_d[q:q + 1, :].partition_broadcast(128))
        lngf = SB([128, 4, 8]); lnbf = SB([128, 4, 8]); epsT = SB([128, 1])
        lngf.load(lngf[:], lngf_d[:, :, :]); lnbf.load(lnbf[:], lnbf_d[:, :, :])
        pool.op(lambda e: e.memset(epsT[:], LN_EPS_), writes=[epsT.b])
        gbb = SB([1, 2, D], BF16)
        gbb.load(gbb[:], gb_s[:, :].rearrange("(o a) n -> o a n", o=1), reads=[WB["gb"]])
        dtb = SB([128, NH]); abc = SB([128, NH]); dbc = SB([128, NH])
        dtb.load(dtb[:], dtb_d[0:1, :].partition_broadcast(128))
        abc.load(abc[:], alog_d[0:1, :].partition_broadcast(128))
        dbc.load(dbc[:], dsk_d[0:1, :].partition_broadcast(128))
        act.op(lambda e: e.activation(out=abc[:], in_=abc[:], func=AF.Exp), reads=[abc.b], writes=[abc.b])
        dve.op(lambda e: e.tensor_scalar_mul(out=abc[:], in0=abc[:], scalar1=-1.0), reads=[abc.b], writes=[abc.b])
        scw = SB([128, 32, 4]); scb = SB([128, 32]); normw = SB([128, 16])
        fcw = SB([128, 2, 2, NJ, 3]); fcb = SB([128, 2, 2, NJ])
        scw.load(scw[:], scw_d[:, :, :])
        scb.load(scb[:], scb_d[:, :])
        normw.load(normw[:], normw_d[:, :])
        fcw.load(fcw[:], fcw_d[:, :, :, :, :])
        fcb.load(fcb[:], fcb_d[:, :, :, :])
        poolw = SB([128, 8, 256], BF16)
        POOLW_SETUP = True
        wdt = SB([128, 8, NH], BF16)
        wdt.load(wdt[:], indt_s[:, :, :], reads=[WB["indt"]])

        pA = T_("pA", [128, 2048], F32, psum=True)
        pB = T_("pB", [128, 1024], F32, psum=True)
        pT = T_("pT", [128, 1024], BF16, psum=True)
        pS = T_("pS", [128, 512], F32, psum=True)
        pAh = [Buf("pA0"), Buf("pA1")]
        pBh = [Buf("pB0"), Buf("pB1")]
        pA23 = [Buf("pA2"), Buf("pA3")]

        NSLOT = 7
        wslots = [SB([128, 2048], BF16, name="wslot%d" % q) for q in range(NSLOT)]
        for w_ in wslots:
            w_.slot = P.slot()

        class WView:
            def __init__(self, slot_t, view):
                self.v = view
                self.b = slot_t.b

            def __getitem__(self, k):
                return self.v[k]

        class Ring:
            def __init__(self, kind, lo, hi, srcbuf=None):
                self.kind = kind
                self.lo = lo
                self.depth = hi - lo

            def view(self, t):
                k = self.kind
                if k == "up":
                    return t.t[:, :].rearrange("p (k n) -> p k n", n=256)
                if k == "row1024":
                    return t.t[:, 0:1024]
                if k == "proj":
                    return t.t[:, :].rearrange("p (k n) -> p k n", n=1024)
                if k == "row2048":
                    return t.t[:, :]
                if k == "inx":
                    return t.t[:, 0:1024].rearrange("p (k n) -> p k n", n=128)
                raise KeyError(k)

            def load(self, i, src, wb):
                t = wslots[self.lo + i % self.depth]
                v = self.view(t)
                sp.dma(v, src, t.slot, reads=[wb], writes=[t.b])
                return WView(t, v)

        r_up = Ring("up", 0, 3)
        r_dn = Ring("row1024", 3, 6)
        r_gate = Ring("row1024", 0, 6)
        r_proj = Ring("proj", 6, 7)
        r_inz = Ring("row2048", 0, 6)
        r_inx = Ring("inx", 0, 6)
        r_out = Ring("row1024", 0, 6)

        def stream(n, ring, srcf, body, wb):
            tl = {}
            for i in range(min(ring.depth, n)):
                tl[i] = ring.load(i, srcf(i), wb)
            for i in range(n):
                body(i, tl[i])
                if i + ring.depth < n:
                    tl[i + ring.depth] = ring.load(i + ring.depth, srcf(i + ring.depth), wb)

        xin = SB([128, 8, 143])
        xtok = [SB([128, D]) for _ in range(3)]
        sA = SB([128, 2, 143]); sBf = SB([128, 2, 143]); t16 = SB([128, 2, 16])
        pooled = SB([128, 8, 128], BF16)
        xT = SB([128, 8, 128], BF16)
        xb16 = SB([128, D], BF16)
        U1 = SB([128, 32 * 128], BF16)
        hb = [[SB([128, 130]) for _ in range(2)] for _ in range(4)]
        acc = [[SB([128, 128]) for _ in range(2)] for _ in range(4)]
        sgs = [SB([128, 128]) for _ in range(2)]
        for hp_ in hb:
            for h__ in hp_:
                h__.tb = Buf("tail")
        pst = SB([128, 2, 128]); pTb = SB([128, 2, 128], BF16)
        gate = SB([128, D]); tmpt = SB([128, D])
        stats = SB([128, 2, 6]); mv = SB([128, 8])
        ftail = SB([128, 2, 2, NJ, 2]); ctail = SB([128, 32, 3])
        invc = SB([128, 8, 16])
        sz = SB([128, DIN]); ybuf = SB([128, DIN]); ytmp = SB([128, DIN])
        xsT = SB([128, 16, 128], BF16); BT = SB([128, 8, 128], BF16); CT = SB([128, 8, 128], BF16)
        cbuf = [SB([128, 131]) for _ in range(4)]; cacc = [SB([128, 128]) for _ in range(4)]
        for c__ in cbuf:
            c__.tb = Buf("ctailb")
        xdt = SB([128, DIN], BF16); xdtd = SB([128, DIN], BF16); Btok = SB([128, 1024], BF16)
        cbm = SB([128, 8, 128], BF16)
        rseg = [SB([128, 8, 128], BF16) for _ in range(2)]
        _ex = SB([128, 8, 128], BF16); expL = [_ex, _ex]
        ST = SB([128, DIN]); STb = SB([128, DIN], BF16)
        ynb = SB([128, DIN], BF16); ynT = SB([128, 16, 128], BF16)
        sm = SB([128, 8, NH])
        ss = SB([128, 4, 8])
        io = P.slot(); io2 = P.slot(); oslot = P.slot()
        vfl = SB([128, NBLK]); vfl.load(vfl[:], vfl_d[:, :])
        invt = [SB([128, 8, 16]) for _ in range(4)]
        for q in range(4):
            invt[q].load(invt[q][:], invc_d[0][q])
        scr = Buf("scratch")
        sz.load(sz[:].rearrange("p (c d) -> p c d", d=256), pool_w_d.rearrange("g (h p) d -> p (g h) d", p=128))
        ytmp.load(ytmp[:, 0:D], pool_scale_d[0:1, :].partition_broadcast(128))
        for c in range(8):
            g = c // 2
            dve.op(lambda e, c=c, g=g: e.tensor_tensor(out=poolw[:, c, :], in0=sz[:, c * 256:(c + 1) * 256],
                                                       in1=ytmp[:, g * 256:(g + 1) * 256], op=ALU.mult),
                   reads=[sz.b, ytmp.b], writes=[poolw.b])
        outbufs = []

        def layer_norm(T, u, q, xo, use_xT=True):
            for h in range(2):
                dve.op(lambda e, h=h: e.bn_stats(out=stats[:T, h, :], in_=u[:T, h * 512:(h + 1) * 512]),
                       reads=[u.b], writes=[stats.b])
            dve.op(lambda e: e.bn_aggr(out=mv[:T, 0:2], in_=stats[:T].rearrange("p a b -> p (a b)")), reads=[stats.b], writes=[mv.b])
            act.op(lambda e: e.activation(out=mv[:T, 3:4], in_=mv[:T, 1:2], func=AF.Sqrt, bias=epsT[:T, 0:1]), reads=[mv.b, epsT.b], writes=[mv.b])
            dve.op(lambda e: e.reciprocal(out=mv[:T, 4:5], in_=mv[:T, 3:4]), reads=[mv.b], writes=[mv.b])
            dve.op(lambda e: e.tensor_scalar(out=xb16[:T, :], in0=u[:T, :], scalar1=mv[:T, 0:1], scalar2=mv[:T, 4:5],
                                             op0=ALU.subtract, op1=ALU.mult), reads=[u.b, mv.b], writes=[xb16.b])
            dve.op(lambda e: e.scalar_tensor_tensor(out=mv[:T, 5:6], in0=mv[:T, 0:1], scalar=-1.0, in1=mv[:T, 4:5],
                                                    op0=ALU.mult, op1=ALU.mult), reads=[mv.b], writes=[mv.b])
            for c in range(8):
                pe.op(lambda e, c=c: e.transpose(out=pT[:, c * 128:c * 128 + T], in_=xb16[:T, c * 128:(c + 1) * 128],
                                                 identity=ident[:T, :T]), reads=[xb16.b, ident.b], writes=[pT.b])
            act.op(lambda e: e.activation(out=tmpt[:T, :], in_=u[:T, :], func=AF.Identity, scale=mv[:T, 4:5], bias=mv[:T, 5:6]),
                   reads=[u.b, mv.b], writes=[tmpt.b])
            for c in range(8):
                act.op(lambda e, c=c: e.activation(out=xT[:, c, :T], in_=pT[:, c * 128:c * 128 + T], func=AF.Identity,
                                                   scale=lngf[:, q, c:c + 1], bias=lnbf[:, q, c:c + 1]),
                       reads=[pT.b, lngf.b, lnbf.b], writes=[xT.b])
            pool.op(lambda e: e.tensor_tensor(out=tmpt[:T, :], in0=tmpt[:T, :], in1=lng[q][:T, :], op=ALU.mult),
                    reads=[tmpt.b, lng[q].b], writes=[tmpt.b])
            pool.op(lambda e: e.tensor_tensor(out=xo[:T, :], in0=tmpt[:T, :], in1=lnb[q][:T, :], op=ALU.add),
                    reads=[tmpt.b, lnb[q].b], writes=[xo.b])

        def to_T(T, xo):
            act.op(lambda e: e.activation(out=xb16[:T, :], in_=xo[:T, :], func=AF.Copy), reads=[xo.b], writes=[xb16.b])
            for c in range(8):
                pe.op(lambda e, c=c: e.transpose(out=pT[:, c * 128:c * 128 + T], in_=xb16[:T, c * 128:(c + 1) * 128],
                                                 identity=ident[:T, :T]), reads=[xb16.b, ident.b], writes=[pT.b])
            dve.op(lambda e: e.tensor_copy(out=xT[:, :, :T], in_=pT[:].rearrange("p (c t) -> p c t", t=128)[:, :, :T]),
                   reads=[pT.b], writes=[xT.b])

        hTb = [Buf("hT%d" % j_) for j_ in range(NJ)]

        def ffn(T, i):
            hT = U1
            st_ = {}

            def up(j, w):
                par = j % 2
                if par == 0:
                    pp_, po_, pbs_ = pB, 0, [[pBh[0]], [pBh[1]]]
                else:
                    pp_, po_, pbs_ = pA, 1024, [[pA23[0], pAh[1]], [pA23[1], pAh[1]]]
                for gu in range(2):
                    for k in range(8):
                        pe.op(lambda e, gu=gu, k=k: e.matmul(pp_[:, po_ + gu * 512:po_ + gu * 512 + T], lhsT=w[:, k, gu * 128:(gu + 1) * 128],
                                                             rhs=xT[:, k, :T], start=(k == 0), stop=(k == 7)),
                              reads=[w.b, xT.b], writes=pbs_[gu])
                p4 = j % 4
                sg = sgs[j % 2]
                for gu in range(2):
                    h_ = hb[p4][gu]; a_ = acc[p4][gu]
                    act.op(lambda e, gu=gu, h_=h_: e.activation(out=h_[:, 2:2 + T], in_=pp_[:, po_ + gu * 512:po_ + gu * 512 + T], func=AF.Copy),
                           reads=pbs_[gu], writes=[h_.b])
                    act.op(lambda e, gu=gu, a_=a_: e.activation(out=a_[:, :T], in_=pp_[:, po_ + gu * 512:po_ + gu * 512 + T], func=AF.Identity,
                                                                scale=fcw[:, i, gu, j, 2:3], bias=fcb[:, i, gu, j:j + 1]),
                           reads=pbs_[gu] + [fcw.b, fcb.b], writes=[a_.b])
                    pool.op(lambda e, gu=gu, h_=h_: e.tensor_copy(out=h_[:, 0:2], in_=ftail[:, i, gu, j, :]),
                            reads=[ftail.b], writes=[h_.tb])
                    pool.op(lambda e, gu=gu, h_=h_: e.tensor_copy(out=ftail[:, i, gu, j, :], in_=h_[:, T:T + 2]),
                            reads=[h_.b], writes=[ftail.b])
                for tap in (1, 0):
                    for gu in range(2):
                        h_ = hb[p4][gu]; a_ = acc[p4][gu]
                        dve.op(lambda e, gu=gu, h_=h_, a_=a_, tap=tap: e.scalar_tensor_tensor(
                            out=a_[:, :T], in0=h_[:, tap:tap + T], scalar=fcw[:, i, gu, j, tap:tap + 1], in1=a_[:, :T],
                            op0=ALU.mult, op1=ALU.add), reads=[h_.b, h_.tb, fcw.b, a_.b], writes=[a_.b])
                act.op(lambda e: e.activation(out=sg[:, :T], in_=acc[p4][0][:, :T], func=AF.Silu), reads=[acc[p4][0].b], writes=[sg.b])
                pool.op(lambda e: e.tensor_tensor(out=hT[:, j * 128:j * 128 + T], in0=sg[:, :T], in1=acc[p4][1][:, :T], op=ALU.mult),
                        reads=[sg.b, acc[p4][1].b], writes=[hTb[j]] + ([hT.b] if j == 0 else []))

            def down(j, w):
                for h in range(2):
                    pe.op(lambda e, h=h: e.matmul(pA[:T, h * 512:(h + 1) * 512], lhsT=hT[:, j * 128:j * 128 + T],
                                                  rhs=w[:, h * 512:(h + 1) * 512], start=(j == 0), stop=(j == NJ - 1)),
                          reads=[hTb[j], w.b] + ([hT.b] if j == NJ - 1 else []), writes=[pAh[0]])

            DLY = 2
            upt = {}; dnt = {}
            for jj in range(min(3, NJ)):
                upt[jj] = r_up.load(jj, wup_s[i][jj], WB["wup%d" % i]); dnt[jj] = r_dn.load(jj, wdn_s[i][jj], WB["wdn%d" % i])
            for it_ in range(NJ + DLY):
                if it_ < NJ:
                    up(it_, upt[it_])
                    if it_ + 3 < NJ:
                        upt[it_ + 3] = r_up.load(it_ + 3, wup_s[i][it_ + 3], WB["wup%d" % i])
                jd = it_ - DLY
                if jd >= 0:
                    down(jd, dnt[jd])
                    if jd + 3 < NJ:
                        dnt[jd + 3] = r_dn.load(jd + 3, wdn_s[i][jd + 3], WB["wdn%d" % i])

        def ple(T, i, s, t0, x2, xo):
            for h in range(2):
                pe.op(lambda e, h=h: e.matmul(pA[:T, h * 512:(h + 1) * 512], lhsT=onesb[0:1, :T], rhs=gbb[0:1, i, h * 512:(h + 1) * 512],
                                              start=True, stop=False), reads=[onesb.b, gbb.b], writes=[pAh[0]])

            def body(k, w):
                for h in range(2):
                    pe.op(lambda e, h=h: e.matmul(pA[:T, h * 512:(h + 1) * 512], lhsT=xT[:, k, :T], rhs=w[:, h * 512:(h + 1) * 512],
                                                  start=False, stop=(k == 7)), reads=[xT.b, w.b], writes=[pAh[0]])
            stream(8, r_gate, lambda k: gate_s[i][k], body, WB["gate%d" % i])
            pst.load(pst[:, :, :T], pT_d[s][i][:, :, t0:t0 + T])
            pool.op(lambda e: e.tensor_copy(out=pTb[:, :, :T], in_=pst[:, :, :T]), reads=[pst.b], writes=[pTb.b])
            wp = r_proj.load(0, proj_s[i][:, :, :], WB["proj%d" % i])
            for h in range(2):
                for k in range(2):
                    pe.op(lambda e, h=h, k=k: e.matmul(pA[:T, 1024 + h * 512:1024 + (h + 1) * 512], lhsT=pTb[:, k, :T],
                                                       rhs=wp[:, k, h * 512:(h + 1) * 512], start=(k == 0), stop=(k == 1)),
                          reads=[pTb.b, wp.b], writes=[pAh[1]])
            act.op(lambda e: e.activation(out=gate[:T, :], in_=pA[:T, 0:1024], func=AF.Sigmoid), reads=[pAh[0]], writes=[gate.b])
            dve.op(lambda e: e.tensor_tensor(out=gate[:T, :], in0=gate[:T, :], in1=pA[:T, 1024:2048], op=ALU.mult),
                   reads=[gate.b, pAh[1]], writes=[gate.b])
            pool.op(lambda e: e.tensor_tensor(out=xo[:T, :], in0=gate[:T, :], in1=x2[:T, :], op=ALU.add),
                    reads=[gate.b, x2.b], writes=[xo.b])

        def residual(T, xres, pbuf, uo):
            dve.op(lambda e: e.scalar_tensor_tensor(out=uo[:T, :], in0=xres[:T, :], scalar=ALPHA_, in1=pA[:T, 0:1024],
                                                    op0=ALU.mult, op1=ALU.add), reads=[xres.b, pbuf], writes=[uo.b])

        def bc3(ap2, T, n):
            return ap2.unsqueeze(2).to_broadcast([T, NH, n])

        def block_p1(s, t0, T, first, mode, blk, vcol=None, invtab=None):
            invc_ = invtab if invtab is not None else invc
            xa, xb, xc = xtok
            xin.load(xin[:, :, :15 + T], xT_d[s][:, :, t0:t0 + 15 + T])
            xa.load(xa[:T, :], x_d[s][t0:t0 + T, :])
            W = 15 + T
            for g in range(4):
                cs_ = slice(2 * g, 2 * g + 2)
                wsz = 2 ** (g + 1)
                pool.op(lambda e, cs_=cs_: e.tensor_tensor(out=sA[:, :, 1:W], in0=xin[:, cs_, 1:W], in1=xin[:, cs_, 0:W - 1], op=ALU.add),
                        reads=[xin.b], writes=[sA.b])
                cur, oth = sA, sBf
                sh = 1
                for lvl in range(g):
                    sh2 = sh * 2
                    lo = 2 * sh2 - 1
                    pool.op(lambda e, cur=cur, oth=oth, lo=lo, sh2=sh2: e.tensor_tensor(
                        out=oth[:, :, lo:W], in0=cur[:, :, lo:W], in1=cur[:, :, lo - sh2:W - sh2], op=ALU.add),
                        reads=[cur.b], writes=[oth.b])
                    cur, oth = oth, cur
                    sh = sh2
                dve.op(lambda e, cur=cur, cs_=cs_, wsz=wsz: e.scalar_tensor_tensor(
                    out=pooled[:, cs_, :T], in0=cur[:, :, 15:W], scalar=1.0 / wsz, in1=xin[:, cs_, 15:W],
                    op0=ALU.mult, op1=ALU.subtract), reads=[cur.b, xin.b], writes=[pooled.b])
                if first:
                    dve.op(lambda e, cur=cur, cs_=cs_: e.tensor_tensor(out=t16[:], in0=cur[:, :, 15:31], in1=invc_[:, cs_, :], op=ALU.mult),
                           reads=[cur.b, invc_.b], writes=[t16.b])
                    dve.op(lambda e, cs_=cs_: e.tensor_tensor(out=pooled[:, cs_, 0:16], in0=t16[:], in1=xin[:, cs_, 15:31], op=ALU.subtract),
                           reads=[t16.b, xin.b], writes=[pooled.b])
            for g in range(4):
                for h in range(2):
                    c = 2 * g + h
                    pe.op(lambda e, g=g, h=h, c=c: e.matmul(pA[:T, g * 256:(g + 1) * 256], lhsT=pooled[:, c, :T], rhs=poolw[:, c, :],
                                                            start=(h == 0), stop=(h == 1)),
                          reads=[pooled.b, poolw.b], writes=[pAh[0]])
            residual(T, xa, pAh[0], xb)
            layer_norm(T, xb, 0, xc)
            ffn(T, 0)
            residual(T, xc, pAh[0], xb)
            layer_norm(T, xb, 1, xa)
            ple(T, 0, s, t0, xa, xc)
            to_T(T, xc)

            if mode != "state":
                def zbody(k, w):
                    for cc in range(4):
                        pe.op(lambda e, cc=cc: e.matmul(pA[:T, cc * 512:(cc + 1) * 512], lhsT=xT[:, k, :T], rhs=w[:, cc * 512:(cc + 1) * 512],
                                                        start=(k == 0), stop=(k == 7)), reads=[xT.b, w.b], writes=[pAh[cc // 2]])
                stream(8, r_inz, lambda k: inz_s[k], zbody, WB["inz"])
                act.op(lambda e: e.activation(out=sz[:T, :], in_=pA[:T, :], func=AF.Silu), reads=[pAh[0], pAh[1]], writes=[sz.b])
            for k in range(8):
                pe.op(lambda e, k=k: e.matmul(pS[:T, 0:NH], lhsT=xT[:, k, :T], rhs=wdt[:, k, :], start=(k == 0), stop=(k == 7)),
                      reads=[xT.b, wdt.b], writes=[pS.b])
            S_ = lambda n: sm[:T, n, :]
            dve.op(lambda e: e.tensor_tensor(out=S_(0), in0=pS[:T, 0:NH], in1=dtb[:T, :], op=ALU.add), reads=[pS.b, dtb.b], writes=[sm.b])
            dve.op(lambda e: e.scalar_tensor_tensor(out=S_(1), in0=S_(0), scalar=-1.0, in1=S_(0), op0=ALU.mult, op1=ALU.max), reads=[sm.b], writes=[sm.b])
            act.op(lambda e: e.activation(out=S_(1), in_=S_(1), func=AF.Exp, scale=-1.0), reads=[sm.b], writes=[sm.b])
            act.op(lambda e: e.activation(out=S_(1), in_=S_(1), func=AF.Ln, bias=1.0), reads=[sm.b], writes=[sm.b])
            dve.op(lambda e: e.tensor_scalar_max(out=S_(0), in0=S_(0), scalar1=0.0), reads=[sm.b], writes=[sm.b])
            dve.op(lambda e: e.tensor_tensor(out=S_(2), in0=S_(0), in1=S_(1), op=ALU.add), reads=[sm.b], writes=[sm.b])
            if vcol is not None:
                dve.op(lambda e: e.tensor_scalar_mul(out=S_(2), in0=S_(2), scalar1=vfl[:T, vcol:vcol + 1]), reads=[sm.b, vfl.b], writes=[sm.b])
            dve.op(lambda e: e.tensor_tensor(out=S_(3), in0=S_(2), in1=abc[:T, :], op=ALU.mult), reads=[sm.b, abc.b], writes=[sm.b])

            def xbody(c, w):
                par = c % 2
                for k in range(8):
                    pe.op(lambda e, k=k: e.matmul(pB[:, par * 512:par * 512 + T], lhsT=w[:, k, :], rhs=xT[:, k, :T],
                                                  start=(k == 0), stop=(k == 7)), reads=[w.b, xT.b], writes=[pBh[par]])
                cb_ = cbuf[c % 4]; ca_ = cacc[c % 4]
                act.op(lambda e: e.activation(out=cb_[:, 3:3 + T], in_=pB[:, par * 512:par * 512 + T], func=AF.Copy),
                       reads=[pBh[par]], writes=[cb_.b])
                act.op(lambda e: e.activation(out=ca_[:, :T], in_=pB[:, par * 512:par * 512 + T], func=AF.Identity,
                                              scale=scw[:, c, 3:4], bias=scb[:, c:c + 1]),
                       reads=[pBh[par], scw.b, scb.b], writes=[ca_.b])
                pool.op(lambda e: e.tensor_copy(out=cb_[:, 0:3], in_=ctail[:, c, :]), reads=[ctail.b], writes=[cb_.tb])
                pool.op(lambda e: e.tensor_copy(out=ctail[:, c, :], in_=cb_[:, T:T + 3]), reads=[cb_.b], writes=[ctail.b])
                for tap in (2, 1, 0):
                    dve.op(lambda e, tap=tap: e.scalar_tensor_tensor(out=ca_[:, :T], in0=cb_[:, tap:tap + T], scalar=scw[:, c, tap:tap + 1],
                                                                     in1=ca_[:, :T], op0=ALU.mult, op1=ALU.add),
                           reads=[cb_.b, cb_.tb, scw.b, ca_.b], writes=[ca_.b])
                if c < 16:
                    dst, db = xsT[:, c, :T], xsT.b
                elif c < 24:
                    dst, db = BT[:, c - 16, :T], BT.b
                else:
                    dst, db = CT[:, c - 24, :T], CT.b
                act.op(lambda e: e.activation(out=dst, in_=ca_[:, :T], func=AF.Silu), reads=[ca_.b], writes=[db])
            stream(24 if mode == "state" else 32, r_inx, lambda c: inx_s[c], xbody, WB["inx"])
            for r in range(2):
                for c8 in range(8):
                    c = r * 8 + c8
                    pe.op(lambda e, c=c, c8=c8: e.transpose(out=pT[:T, c8 * 128:(c8 + 1) * 128], in_=xsT[:, c, :T], identity=ident[:, :]),
                          reads=[xsT.b, ident.b], writes=[pT.b])
                hs = slice(r * 16, (r + 1) * 16)
                dve.op(lambda e, r=r, hs=hs: e.tensor_tensor(
                    out=xdt[:T, r * 1024:(r + 1) * 1024].rearrange("p (h d) -> p h d", d=64),
                    in0=pT[:T, :].rearrange("p (h d) -> p h d", d=64),
                    in1=sm[:T, 2, hs].unsqueeze(2).to_broadcast([T, 16, 64]), op=ALU.mult),
                    reads=[pT.b, sm.b], writes=[xdt.b])
                if mode != "state":
                  dve.op(lambda e, r=r, hs=hs: e.tensor_tensor(
                    out=ybuf[:T, r * 1024:(r + 1) * 1024].rearrange("p (h d) -> p h d", d=64),
                    in0=pT[:T, :].rearrange("p (h d) -> p h d", d=64),
                    in1=dbc[:T, hs].unsqueeze(2).to_broadcast([T, 16, 64]), op=ALU.mult),
                    reads=[pT.b, dbc.b], writes=[ybuf.b])
            for g in range(8):
                pe.op(lambda e, g=g: e.transpose(out=pT[:T, g * 128:(g + 1) * 128], in_=BT[:, g, :T], identity=ident[:, :]),
                      reads=[BT.b, ident.b], writes=[pT.b])
            dve.op(lambda e: e.tensor_copy(out=Btok[:T, :], in_=pT[:T, :]), reads=[pT.b], writes=[Btok.b])
            pe.op(lambda e: e.matmul(pS[:T, 32:64], lhsT=tri[:T, :T], rhs=sm[:T, 3, :], start=True, stop=True),
                  reads=[tri.b, sm.b], writes=[pS.b])
            pe.op(lambda e: e.matmul(pS[:, 64:96], lhsT=onesf[:T, :], rhs=sm[:T, 3, :], start=True, stop=True),
                  reads=[onesf.b, sm.b], writes=[pS.b])
            act.op(lambda e: e.activation(out=S_(4), in_=pS[:T, 32:64], func=AF.Exp), reads=[pS.b], writes=[sm.b])
            act.op(lambda e: e.activation(out=S_(5), in_=pS[:T, 32:64], func=AF.Copy), reads=[pS.b], writes=[sm.b])
            dve.op(lambda e: e.tensor_tensor(out=S_(6), in0=pS[:T, 64:96], in1=S_(5), op=ALU.subtract), reads=[pS.b, sm.b], writes=[sm.b])
            act.op(lambda e: e.activation(out=S_(6), in_=S_(6), func=AF.Exp), reads=[sm.b], writes=[sm.b])
            act.op(lambda e: e.activation(out=sm[:, 7, :], in_=pS[:, 64:96], func=AF.Exp), reads=[pS.b], writes=[sm.b])
            if False:
                dve.op(lambda e: e.tensor_tensor(out=S_(0), in0=pS[:T, 32:64], in1=carry[:T, :], op=ALU.add), reads=[pS.b, carry.b], writes=[sm.b])
                act.op(lambda e: e.activation(out=S_(0), in_=S_(0), func=AF.Exp), reads=[sm.b], writes=[sm.b])
                dve.op(lambda e: e.tensor_tensor(out=carry[:, :], in0=carry[:, :], in1=pS[:, 64:96], op=ALU.add), reads=[pS.b, carry.b], writes=[carry.b])
            if mode != "state":
                for g in range(8):
                    pe.op(lambda e, g=g: e.matmul(pB[:T, g * 128:g * 128 + T], lhsT=BT[:, g, :T], rhs=CT[:, g, :T], start=True, stop=True),
                          reads=[BT.b, CT.b], writes=[pBh[g // 4]])
                dve.op(lambda e: e.tensor_tensor(out=cbm[:T, :, :T], in0=pB[:T, :].rearrange("p (g q) -> p g q", q=128)[:, :, :T],
                                                 in1=tri[:T, :T].unsqueeze(1).to_broadcast([T, 8, T]), op=ALU.mult),
                       reads=[pBh[0], pBh[1], tri.b], writes=[cbm.b])
                MT = U1
                for r in range(4):
                    rs_ = rseg[r % 2]; ex_ = expL[r % 2]
                    pool.op(lambda e, r=r, rs_=rs_: e.tensor_tensor(
                        out=rs_[:T, :, :T], in0=sm[:T, 3, r * 8:(r + 1) * 8].unsqueeze(2).to_broadcast([T, 8, T]),
                        in1=trib[:T, :T].unsqueeze(1).to_broadcast([T, 8, T]), op=ALU.mult),
                        reads=[sm.b, trib.b], writes=[rs_.b])
                    hp = r % 2
                    for q2 in range(2):
                        pe.op(lambda e, rs_=rs_, q2=q2, hp=hp: e.matmul(
                            pA[:T, hp * 1024 + q2 * 512: hp * 1024 + q2 * 512 + 4 * T].rearrange("p (h q) -> p h q", q=T),
                            lhsT=mstr[:T, :T], rhs=rs_[:T, q2 * 4:(q2 + 1) * 4, :T], start=True, stop=True),
                            reads=[mstr.b, rs_.b], writes=[pAh[hp]])
                    for q2 in range(2):
                        act.op(lambda e, ex_=ex_, q2=q2, hp=hp: e.activation(
                            out=ex_[:T, q2 * 4:(q2 + 1) * 4, :T],
                            in_=pA[:T, hp * 1024 + q2 * 512: hp * 1024 + q2 * 512 + 4 * T].rearrange("p (h q) -> p h q", q=T),
                            func=AF.Exp), reads=[pAh[hp]], writes=[ex_.b])
                    for g2 in range(2):
                        gg = 2 * r + g2
                        dve.op(lambda e, ex_=ex_, g2=g2, gg=gg, r=r: e.tensor_tensor(
                            out=MT[:T, (r * 8 + g2 * 4) * 128:(r * 8 + g2 * 4 + 4) * 128].rearrange("p (h q) -> p h q", q=128)[:, :, :T],
                            in0=ex_[:T, g2 * 4:(g2 + 1) * 4, :T],
                            in1=cbm[:T, gg:gg + 1, :T].to_broadcast([T, 4, T]), op=ALU.mult),
                            reads=[ex_.b, cbm.b], writes=[MT.b])
                for g in range(8):
                    pe.op(lambda e, g=g: e.matmul(pA[:T, g * 256:(g + 1) * 256], lhsT=CT[:, g, :T], rhs=STb[:, g * 256:(g + 1) * 256],
                                                  start=True, stop=True), reads=[CT.b, STb.b], writes=[pAh[g // 4]])
                dve.op(lambda e: e.tensor_tensor(out=ytmp[:T, :].rearrange("p (h d) -> p h d", d=64),
                                                 in0=pA[:T, :].rearrange("p (h d) -> p h d", d=64),
                                                 in1=bc3(sm[:T, 4, :], T, 64), op=ALU.mult),
                       reads=[pAh[0], pAh[1], sm.b], writes=[ytmp.b])
                pool.op(lambda e: e.tensor_tensor(out=ybuf[:T, :], in0=ybuf[:T, :], in1=ytmp[:T, :], op=ALU.add),
                        reads=[ybuf.b, ytmp.b], writes=[ybuf.b])
                for h in range(NH):
                    pe.op(lambda e, h=h: e.matmul(pA[:T, h * 64:(h + 1) * 64], lhsT=MT[:T, h * 128:h * 128 + T], rhs=xdt[:T, h * 64:(h + 1) * 64],
                                                  start=True, stop=True), reads=[MT.b, xdt.b], writes=[pAh[h // 16]])
                dve.op(lambda e: e.tensor_tensor(out=ybuf[:T, :], in0=ybuf[:T, :], in1=pA[:T, :], op=ALU.add),
                       reads=[ybuf.b, pAh[0], pAh[1]], writes=[ybuf.b])
            pool.op(lambda e: e.tensor_tensor(out=xdtd[:T, :].rearrange("p (h d) -> p h d", d=64),
                                              in0=xdt[:T, :].rearrange("p (h d) -> p h d", d=64),
                                              in1=bc3(sm[:T, 6, :], T, 64), op=ALU.mult), reads=[xdt.b, sm.b], writes=[xdtd.b])
            for g in range(8):
                pe.op(lambda e, g=g: e.matmul(pA[:, g * 256:(g + 1) * 256], lhsT=Btok[:T, g * 128:(g + 1) * 128],
                                              rhs=xdtd[:T, g * 256:(g + 1) * 256], start=True, stop=True),
                      reads=[Btok.b, xdtd.b], writes=[pAh[g // 4]])
            pool.op(lambda e: e.tensor_tensor(out=ST[:, :].rearrange("p (h d) -> p h d", d=64),
                                              in0=ST[:, :].rearrange("p (h d) -> p h d", d=64),
                                              in1=sm[:, 7, :].unsqueeze(2).to_broadcast([128, NH, 64]), op=ALU.mult),
                    reads=[ST.b, sm.b], writes=[ST.b])
            dve.op(lambda e: e.tensor_tensor(out=ST[:, :], in0=ST[:, :], in1=pA[:, :], op=ALU.add),
                   reads=[ST.b, pAh[0], pAh[1]], writes=[ST.b])
            act.op(lambda e: e.activation(out=STb[:, :], in_=ST[:, :], func=AF.Copy), reads=[ST.b], writes=[STb.b])
            if False:
                xc.store(x3s[blk], xc[:, :], writes=[scr])
                sz.store(szs[blk], sz[:, :], writes=[scr])
                ybuf.store(yss[blk], ybuf[:, :], writes=[scr])
                CT.store(cts[blk], CT[:, :, :], writes=[scr])
                sm.store(ess[blk], sm[:, 0, :], writes=[scr])

        def block_p2(s, t0, T, mode, blk, yrow):
            xa, xb, xc = xtok
            if False:
                xc.load(xc[:, :], x3s[blk], reads=[scr])
                sz.load(sz[:, :], szs[blk], reads=[scr])
                ybuf.load(ybuf[:, :], yss[blk], reads=[scr])
                CT.load(CT[:, :, :], cts[blk], reads=[scr])
                sm.load(sm[:, 0, :], ess[blk], reads=[scr])
                for g in range(8):
                    pe.op(lambda e, g=g: e.matmul(pA[:T, g * 256:(g + 1) * 256], lhsT=CT[:, g, :T], rhs=STb[:, g * 256:(g + 1) * 256],
                                                  start=True, stop=True), reads=[CT.b, STb.b], writes=[pAh[g // 4]])
                dve.op(lambda e: e.tensor_tensor(out=ytmp[:T, :].rearrange("p (h d) -> p h d", d=64),
                                                 in0=pA[:T, :].rearrange("p (h d) -> p h d", d=64),
                                                 in1=bc3(sm[:T, 0, :], T, 64), op=ALU.mult),
                       reads=[pAh[0], pAh[1], sm.b], writes=[ytmp.b])
                pool.op(lambda e: e.tensor_tensor(out=ybuf[:T, :], in0=ybuf[:T, :], in1=ytmp[:T, :], op=ALU.add),
                        reads=[ybuf.b, ytmp.b], writes=[ybuf.b])
            pool.op(lambda e: e.tensor_tensor(out=ybuf[:T, :], in0=ybuf[:T, :], in1=sz[:T, :], op=ALU.mult),
                    reads=[ybuf.b, sz.b], writes=[ybuf.b])
            for g in range(8):
                act.op(lambda e, g=g: e.activation(out=ytmp[:T, g * 256:(g + 1) * 256], in_=ybuf[:T, g * 256:(g + 1) * 256], func=AF.Square,
                                                   accum_out=ss[:T, 0, g:g + 1]), reads=[ybuf.b], writes=[ytmp.b, ss.b])
            dve.op(lambda e: e.tensor_scalar(out=ss[:T, 1, :], in0=ss[:T, 0, :], scalar1=1.0 / 256, scalar2=RMS_EPS_,
                                             op0=ALU.mult, op1=ALU.add), reads=[ss.b], writes=[ss.b])
            act.op(lambda e: e.activation(out=ss[:T, 2, :], in_=ss[:T, 1, :], func=AF.Sqrt), reads=[ss.b], writes=[ss.b])
            dve.op(lambda e: e.reciprocal(out=ss[:T, 3, :], in_=ss[:T, 2, :]), reads=[ss.b], writes=[ss.b])
            dve.op(lambda e: e.tensor_tensor(out=ynb[:T, :].rearrange("p (g d) -> p g d", d=256),
                                             in0=ybuf[:T, :].rearrange("p (g d) -> p g d", d=256),
                                             in1=ss[:T, 3, :].unsqueeze(2).to_broadcast([T, 8, 256]), op=ALU.mult),
                   reads=[ybuf.b, ss.b], writes=[ynb.b])
            for r in range(2):
                for c8 in range(8):
                    c = r * 8 + c8
                    pe.op(lambda e, c=c, c8=c8: e.transpose(out=pT[:, c8 * 128:c8 * 128 + T], in_=ynb[:T, c * 128:(c + 1) * 128],
                                                            identity=ident[:T, :T]), reads=[ynb.b, ident.b], writes=[pT.b])
                for c8 in range(8):
                    c = r * 8 + c8
                    act.op(lambda e, c=c, c8=c8: e.activation(out=ynT[:, c, :T], in_=pT[:, c8 * 128:c8 * 128 + T], func=AF.Identity,
                                                              scale=normw[:, c:c + 1]), reads=[pT.b, normw.b], writes=[ynT.b])

            def obody(k, w):
                for h in range(2):
                    pe.op(lambda e, h=h: e.matmul(pA[:T, h * 512:(h + 1) * 512], lhsT=ynT[:, k, :T], rhs=w[:, h * 512:(h + 1) * 512],
                                                  start=(k == 0), stop=(k == 15)), reads=[ynT.b, w.b], writes=[pAh[0]])
            stream(16, r_out, lambda k: outp_s[k], obody, WB["outp"])
            residual(T, xc, pAh[0], xb)
            layer_norm(T, xb, 2, xa)
            ffn(T, 1)
            residual(T, xa, pAh[0], xb)
            layer_norm(T, xb, 3, xc)
            ple(T, 1, s, t0, xc, xa)
            if mode != "halo":
                sp.dma(y_d[s][yrow:yrow + T, :], xa[:T, :], oslot, reads=[xa.b])

        def run_sample():
            s = 1
            ftail.load(ftail[:], ftail_d[s][:, :, :, :, :])
            ctail.load(ctail[:], ctail_d[s][:, :, :])
            ST.load(ST[:], st_d[s][:, :])
            invc.load(invc[:], invc_d[s][:, :, :])
            act.op(lambda e: e.activation(out=STb[:, :], in_=ST[:, :], func=AF.Copy), reads=[ST.b], writes=[STb.b])
            block_p1(s, 0, LS, True, "sample", 0)
            block_p2(s, 0, LS, "sample", 0, 0)
            sp.dma(ftail_o[s][:, :, :, :, :], ftail[:], oslot, reads=[ftail.b])
            sp.dma(ctail_o[s][:, :, :], ctail[:], oslot, reads=[ctail.b])
            sp.dma(st_o[s][:, :], ST[:], oslot, reads=[ST.b])
            sp.dma(pool_o[s][:, :], x_d[s][LS - 15:LS, :], oslot)

        def mask_all(b):
            for ap_, tb in ((ftail[:].rearrange("p a b c d -> p (a b c d)"), ftail), (ctail[:].rearrange("p a b -> p (a b)"), ctail)):
                dve.op(lambda e, ap_=ap_: e.tensor_scalar_mul(out=ap_, in0=ap_, scalar1=vfl[:, b:b + 1]),
                       reads=[tb.b, vfl.b], writes=[tb.b])

        def run_prompt():
            s = 0
            ftail.load(ftail[:], ftail_d[s][:, :, :, :, :])
            ctail.load(ctail[:], ctail_d[s][:, :, :])
            ST.load(ST[:], st_d[s][:, :])
            act.op(lambda e: e.activation(out=STb[:, :], in_=ST[:, :], func=AF.Copy), reads=[ST.b], writes=[STb.b])
            firsts = {NST + 1 - k * NOWN: k for k in range(4)}
            for b in range(NBLK):
                mode = "state" if b < NST else ("halo" if b == NST else "own")
                block_p1(s, b * 128, 128, b in firsts, mode, b, vcol=(b if mode != "own" else None),
                         invtab=invt[firsts[b]] if b in firsts else None)
                if mode != "state":
                    block_p2(s, b * 128, 128, mode, b, (b - NST - 1) * 128)
                if (b + 1) in firsts:
                    mask_all(b)
            sp.dma(ftail_o[s][:, :, :, :, :], ftail[:], oslot, reads=[ftail.b])
            sp.dma(ctail_o[s][:, :, :], ctail[:], oslot, reads=[ctail.b])
            sp.dma(st_o[s][:, :], ST[:], oslot, reads=[ST.b])
            sp.dma(pool_o[s][:, :], x_d[s][LP - 15:LP, :], oslot)

        ccslot = P.slot()
        run_sample()
        run_prompt()
        fin = Buf("fin")
        fin.rd = {oslot.key: oslot.n}
        P.finish([fin, xtok[0], ftail, ctail, ST] if False else [fin])
    return nc


def _fm(a, nchunk):
    r, c = a.shape
    return np.ascontiguousarray(a.reshape(r, nchunk, 128).transpose(2, 1, 0))


def _prep_inputs(x_prompt, x_sample, p_prompt, p_sample, cache_pool, cache_ssm_conv, state_ssm, cache_ffn_conv,
                 pool_w, pool_scale, ssm_in_proj, ssm_conv_w, ssm_conv_b, ssm_dt_bias, ssm_A_log, ssm_D, ssm_norm_w,
                 ssm_out_proj, ln_mix_g, ln_mix_b, ffn_up, ffn_conv_w, ffn_conv_b, ffn_down, ln_ffn_g, ln_ffn_b,
                 ple_proj, ple_gate_w, ple_gate_b, n_cores=8):
    f = lambda a: np.ascontiguousarray(np.asarray(a, dtype=np.float32))
    L = x_prompt.shape[1]
    LS = x_sample.shape[1]
    shared = {
        "pool_w": f(pool_w[0]), "pool_scale": f(pool_scale), "in_proj": f(ssm_in_proj[0]),
        "scw": f(np.asarray(ssm_conv_w[0]).reshape(4, 32, 128).transpose(2, 1, 0)),
        "scb": f(np.asarray(ssm_conv_b[0]).reshape(32, 128).T),
        "dtb": f(ssm_dt_bias), "alog": f(ssm_A_log), "dsk": f(ssm_D),
        "normw": f(np.asarray(ssm_norm_w[0]).reshape(16, 128).T),
        "out_proj": f(ssm_out_proj[0]),
        "lng": f(np.stack([ln_mix_g[0], ln_ffn_g[0], ln_mix_g[1], ln_ffn_g[1]])),
        "lnb": f(np.stack([ln_mix_b[0], ln_ffn_b[0], ln_mix_b[1], ln_ffn_b[1]])),
        "lngf": f(np.stack([ln_mix_g[0], ln_ffn_g[0], ln_mix_g[1], ln_ffn_g[1]]).reshape(4, 8, 128).transpose(2, 0, 1)),
        "lnbf": f(np.stack([ln_mix_b[0], ln_ffn_b[0], ln_mix_b[1], ln_ffn_b[1]]).reshape(4, 8, 128).transpose(2, 0, 1)),
        "fcw": f(np.asarray(ffn_conv_w).reshape(2, 3, 2, 22, 128).transpose(4, 0, 2, 3, 1)),
        "fcb": f(np.asarray(ffn_conv_b).reshape(2, 2, 22, 128).transpose(3, 0, 1, 2)),
        "gate_b": f(ple_gate_b),
    }
    for i in range(2):
        shared["ffn_up%d" % i] = f(ffn_up[i])
        shared["ffn_down%d" % i] = f(ffn_down[i])
        shared["ple_proj%d" % i] = f(ple_proj[i])
        shared["gate_w%d" % i] = f(ple_gate_w[i])
    wins = (2, 4, 8, 16)
    invc_p = np.zeros((128, 8, 16), np.float32)
    invc_s = np.zeros((128, 8, 16), np.float32)
    for c in range(8):
        w = wins[c // 2]
        invc_p[:, c, :] = 1.0 / np.minimum(w, np.arange(16) + 1)
        invc_s[:, c, :] = 1.0 / w
    in_maps = []
    nseq = x_prompt.shape[0]
    nseg = n_cores // nseq
    Lo = L // nseg
    NOWN = Lo // 128
    NST = 3 * NOWN
    NBLK = NST + 1 + NOWN
    LP = NBLK * 128

    def ext(a, start, n):
        out = np.zeros((n,) + a.shape[1:], np.float32)
        lo = max(start, 0)
        if start + n > lo:
            out[lo - start:] = a[lo:start + n]
        return out

    for core in range(n_cores):
        bp = core // nseg
        seg = core % nseg
        bs = core % x_sample.shape[0]
        start = (seg * NOWN - NST - 1) * 128
        nv = NST + 1 - seg * NOWN
        m = dict(shared)
        xp = f(x_prompt[bp])
        m["x0"] = ext(xp, start, LP)
        m["xT0"] = _fm(ext(xp, start - 15, LP + 15), 8)
        xs = f(x_sample[bs])
        m["x1"] = xs
        m["xT1"] = _fm(np.concatenate([f(cache_pool[0, bs]), xs], 0), 8)
        for i in range(2):
            m["pT0_%d" % i] = _fm(ext(f(p_prompt[i, bp]), start, LP), 2)
            m["pT1_%d" % i] = _fm(f(p_sample[i, bs]), 2)
        m["ftail0"] = np.zeros((128, 2, 2, 22, 2), np.float32)
        m["ftail1"] = f(np.asarray(cache_ffn_conv[:, bs]).reshape(2, 2, 2, 22, 128).transpose(4, 0, 2, 3, 1))
        m["ctail0"] = np.zeros((128, 32, 3), np.float32)
        m["ctail1"] = f(np.asarray(cache_ssm_conv[0, bs]).reshape(3, 32, 128).transpose(2, 1, 0))
        m["st0"] = np.zeros((128, 2048), np.float32)
        m["st1"] = f(np.asarray(state_ssm[0, bs]).reshape(2048, 128).T)
        m["invc0"] = np.stack([invc_p if k == seg else invc_s for k in range(4)])
        m["invc1"] = invc_s
        vfl = np.ones((128, NBLK), np.float32)
        vfl[:, :nv] = 0.0
        m["vfl"] = vfl
        in_maps.append(m)
    return in_maps, Lo, LS


def _unft(a):
    return np.ascontiguousarray(a.transpose(1, 4, 2, 3, 0).reshape(2, 2, 5632))


def _assemble(results, nb_p, nb_s):
    nseg = len(results) // nb_p
    last = [b * nseg + nseg - 1 for b in range(nb_p)]
    y_p = np.stack([np.concatenate([results[b * nseg + g]["y0"] for g in range(nseg)], 0) for b in range(nb_p)])
    y_s = np.stack([results[b]["y1"] for b in range(nb_s)])
    pool_p = np.stack([results[c]["pool_o0"] for c in last])[None]
    pool_s = np.stack([results[b]["pool_o1"] for b in range(nb_s)])[None]
    unc = lambda a: np.ascontiguousarray(a.transpose(2, 1, 0).reshape(3, 4096))
    sconv_p = np.stack([unc(results[c]["ctail_o0"]) for c in last])[None]
    sconv_s = np.stack([unc(results[b]["ctail_o1"]) for b in range(nb_s)])[None]
    uns = lambda a: np.ascontiguousarray(a.T.reshape(32, 64, 128))
    st_p = np.stack([uns(results[c]["st_o0"]) for c in last])[None]
    st_s = np.stack([uns(results[b]["st_o1"]) for b in range(nb_s)])[None]
    ffn_p = np.stack([_unft(results[c]["ftail_o0"]) for c in last], 1)
    ffn_s = np.stack([_unft(results[b]["ftail_o1"]) for b in range(nb_s)], 1)
    return (y_p, y_s, pool_p, pool_s, sconv_p, sconv_s, st_p, st_s, ffn_p, ffn_s)


_NC_CACHE = {}


def kernel(**inputs):
    in_maps, L, LS = _prep_inputs(**inputs)
    key = (L, LS)
    if key not in _NC_CACHE:
        _NC_CACHE[key] = build_program(L, LS)
    nc = _NC_CACHE[key]
    res = run_bass_kernel_spmd(nc, in_maps, core_ids=list(range(8)))
    outs = _assemble(res.results, inputs["x_prompt"].shape[0], inputs["x_sample"].shape[0])
    return tuple(np.asarray(o, dtype=np.float32) for o in outs)
```

```python
import numpy as np
import concourse.bass as bass
import concourse.mybir as mybir
from concourse.bass_utils import run_bass_kernel_spmd

F32 = mybir.dt.float32
BF16 = mybir.dt.bfloat16
AF = mybir.ActivationFunctionType
ALU = mybir.AluOpType


class Buf:
    __slots__ = ("name", "lw", "rd")

    def __init__(self, name):
        self.name = name
        self.lw = {}
        self.rd = {}


class Eng:
    def __init__(self, prog, name, handle, sem, is_pe=False, compute=True):
        self.prog = prog
        self.name = name
        self.h = handle
        self.sem = sem
        self.n = 0
        self.seen = {}
        self.q = []
        self.is_pe = is_pe
        self.compute = compute

    def _sync(self, deps):
        for key, cnt in deps.items():
            if key == self.name and self.is_pe:
                continue
            if self.seen.get(key, 0) >= cnt:
                continue
            self.seen[key] = cnt
            sem = self.prog.sems[key]
            self.q.append(("w", sem, cnt))

    def op(self, fn, reads=(), writes=()):
        deps = {}
        for b in reads:
            for k, c in b.lw.items():
                if deps.get(k, 0) < c:
                    deps[k] = c
        for b in writes:
            for k, c in b.lw.items():
                if deps.get(k, 0) < c:
                    deps[k] = c
            for k, c in b.rd.items():
                if deps.get(k, 0) < c:
                    deps[k] = c
        self._sync(deps)
        self.n += 1
        self.q.append(("o", fn, self.sem, 1))
        for b in reads:
            if b.rd.get(self.name, 0) < self.n:
                b.rd[self.name] = self.n
        for b in writes:
            b.lw = {self.name: self.n}
            b.rd = {}

    def dma(self, out_ap, in_ap, slot, reads=(), writes=()):
        deps = {}
        for b in reads:
            for k, c in b.lw.items():
                if deps.get(k, 0) < c:
                    deps[k] = c
        for b in writes:
            for k, c in b.lw.items():
                if deps.get(k, 0) < c:
                    deps[k] = c
            for k, c in b.rd.items():
                if deps.get(k, 0) < c:
                    deps[k] = c
        self._sync(deps)
        slot.n += 16
        self.q.append(("d", out_ap, in_ap, slot.sem))
        for b in reads:
            if b.rd.get(slot.key, 0) < slot.n:
                b.rd[slot.key] = slot.n
        for b in writes:
            b.lw = dict(b.lw) if False else {slot.key: slot.n}
            b.rd = {}

    def collective(self, tin, tout, slot, reads=(), writes=()):
        deps = {}
        for b in reads:
            for k, c in b.lw.items():
                if deps.get(k, 0) < c:
                    deps[k] = c
        for b in writes:
            for k, c in list(b.lw.items()) + list(b.rd.items()):
                if deps.get(k, 0) < c:
                    deps[k] = c
        self._sync(deps)
        slot.n += 1
        self.q.append(("c", tin, tout, slot.sem))
        for b in reads:
            b.rd[slot.key] = slot.n
        for b in writes:
            b.lw = {slot.key: slot.n}
            b.rd = {}

    def wait_all(self, bufs):
        deps = {}
        for b in bufs:
            for k, c in list(b.lw.items()) + list(b.rd.items()):
                if deps.get(k, 0) < c:
                    deps[k] = c
        self._sync(deps)

    def emit(self, h):
        for it in self.q:
            if it[0] == "w":
                h.wait_ge(it[1], it[2])
            elif it[0] == "o":
                ins = it[1](h)
                ins.then_inc(it[2], it[3])
            elif it[0] == "c":
                h.collective_compute("AllGather", ALU.bypass, replica_groups=[list(range(8))],
                                     ins=[it[1].ap().opt()], outs=[it[2].ap().opt()]).then_inc(it[3], 1)
            else:
                h.dma_start(out=it[1], in_=it[2]).then_inc(it[3], 16)


class Slot:
    def __init__(self, prog, key, sem):
        self.key = key
        self.sem = sem
        self.n = 0


class Prog:
    def __init__(self, nc, stack):
        self.nc = nc
        self.stack = stack
        self.sems = {}
        self.nslots = 0
        mk = lambda nm: stack.enter_context(nc.semaphore(nm))
        self.pe = Eng(self, "pe", nc.tensor, mk("s_pe"), is_pe=True)
        self.act = Eng(self, "act", nc.scalar, mk("s_act"))
        self.dve = Eng(self, "dve", nc.vector, mk("s_dve"))
        self.pool = Eng(self, "pool", nc.gpsimd, mk("s_pool"))
        self.sp = Eng(self, "sp", nc.sync, None, compute=False)
        for e in (self.pe, self.act, self.dve, self.pool):
            self.sems[e.name] = e.sem

    def slot(self, name=None):
        self.nslots += 1
        key = "dma%d" % self.nslots
        sem = self.stack.enter_context(self.nc.semaphore("s_" + key))
        self.sems[key] = sem
        return Slot(self, key, sem)

    def sb(self, name, shape, dtype):
        return self.stack.enter_context(self.nc.sbuf_tensor(name, list(shape), dtype))

    def ps(self, name, shape, dtype):
        return self.stack.enter_context(self.nc.psum_tensor(name, list(shape), dtype))

    def finish(self, final_bufs):
        self.sp.wait_all(final_bufs)
        with self.nc.Block() as block:
            @block.tensor
            def _(e):
                self.pe.emit(e)

            @block.scalar
            def _(e):
                self.act.emit(e)

            @block.vector
            def _(e):
                self.dve.emit(e)

            @block.gpsimd
            def _(e):
                self.pool.emit(e)

            @block.sync
            def _(e):
                self.sp.emit(e)

D = 1024
DFF = 2816
NJ = 22
DIN = 2048
CONVD = 4096
NH = 32
ALPHA_ = (2 * 2) ** 0.25
LN_EPS_ = 1e-5
RMS_EPS_ = 1e-5


def build_program(L, LS=32):
    import contextlib
    nc = bass.Bass("TRN2", target_bir_lowering=False)
    NOWN = L // 128
    NST = 3 * NOWN
    NBLK = NST + 1 + NOWN
    LP = NBLK * 128

    def din(name, shape, dt=F32):
        return nc.dram_tensor(name, list(shape), dt, kind="ExternalInput").ap()

    def dout(name, shape, dt=F32):
        return nc.dram_tensor(name, list(shape), dt, kind="ExternalOutput").ap()

    def dint(name, shape, dt=BF16):
        return nc.dram_tensor(name, list(shape), dt).ap()

    Ls = [LP, LS]
    xT_d = [din("xT%d" % s, [128, 8, 15 + Ls[s]]) for s in range(2)]
    x_d = [din("x%d" % s, [Ls[s], D]) for s in range(2)]
    pT_d = [[din("pT%d_%d" % (s, i), [128, 2, Ls[s]]) for i in range(2)] for s in range(2)]
    ftail_d = [din("ftail%d" % s, [128, 2, 2, NJ, 2]) for s in range(2)]
    ctail_d = [din("ctail%d" % s, [128, 32, 3]) for s in range(2)]
    st_d = [din("st%d" % s, [128, DIN]) for s in range(2)]
    invc_d = [din("invc0", [4, 128, 8, 16]), din("invc1", [128, 8, 16])]
    pool_w_d = din("pool_w", [4, 256, 256])
    pool_scale_d = din("pool_scale", [1, D])
    in_proj_d = din("in_proj", [D, 6176])
    scw_d = din("scw", [128, 32, 4])
    scb_d = din("scb", [128, 32])
    dtb_d = din("dtb", [1, NH])
    alog_d = din("alog", [1, NH])
    dsk_d = din("dsk", [1, NH])
    normw_d = din("normw", [128, 16])
    out_proj_d = din("out_proj", [DIN, D])
    lng_d = din("lng", [4, D])
    lnb_d = din("lnb", [4, D])
    lngf_d = din("lngf", [128, 4, 8])
    lnbf_d = din("lnbf", [128, 4, 8])
    ffn_up_d = [din("ffn_up%d" % i, [D, 2 * DFF]) for i in range(2)]
    fcw_d = din("fcw", [128, 2, 2, NJ, 3])
    fcb_d = din("fcb", [128, 2, 2, NJ])
    ffn_down_d = [din("ffn_down%d" % i, [DFF, D]) for i in range(2)]
    ple_proj_d = [din("ple_proj%d" % i, [256, D]) for i in range(2)]
    gate_w_d = [din("gate_w%d" % i, [D, D]) for i in range(2)]
    gate_b_d = din("gate_b", [2, D])
    y_d = [dout("y0", [L, D]), dout("y1", [LS, D])]
    vfl_d = din("vfl", [128, NBLK])
    pool_o = [dout("pool_o%d" % s, [15, D]) for s in range(2)]
    ftail_o = [dout("ftail_o%d" % s, [128, 2, 2, NJ, 2]) for s in range(2)]
    ctail_o = [dout("ctail_o%d" % s, [128, 32, 3]) for s in range(2)]
    st_o = [dout("st_o%d" % s, [128, DIN]) for s in range(2)]
    wup_s = [dint("wup_s%d" % i, [NJ, 128, 8, 256]) for i in range(2)]
    wdn_s = [dint("wdn_s%d" % i, [NJ, 128, D]) for i in range(2)]
    gate_s = [dint("gate_s%d" % i, [8, 128, D]) for i in range(2)]
    proj_s = [dint("proj_s%d" % i, [128, 2, D]) for i in range(2)]
    gb_s = dint("gb_s", [2, D])
    inz_s = dint("inz_s", [8, 128, DIN])
    inx_s = dint("inx_s", [32, 128, 8, 128])
    indt_s = dint("indt_s", [128, 8, NH])
    outp_s = dint("outp_s", [16, 128, D])

    with contextlib.ExitStack() as stack:
        P = Prog(nc, stack)
        pe, act, dve, pool, sp = P.pe, P.act, P.dve, P.pool, P.sp

        class T_:
            def __init__(self, name, shape, dt, psum=False):
                self.t = (P.ps if psum else P.sb)(name, shape, dt)
                self.b = Buf(name)

            def __getitem__(self, k):
                return self.t[k]

            def store(self, dst, src, writes=()):
                if not hasattr(self, "sslot"):
                    self.sslot = P.slot()
                sp.dma(dst, src, self.sslot, reads=[self.b], writes=list(writes))

            def load(self, dst, src, reads=()):
                if not hasattr(self, "slot"):
                    self.slot = P.slot()
                sp.dma(dst, src, self.slot, reads=list(reads), writes=[self.b])

        cnt = [0]

        def SB(shape, dt=F32, name=None):
            cnt[0] += 1
            return T_(name or ("t%d" % cnt[0]), shape, dt)

        WB = {}
        WS = {}

        def conv(key, dst, src):
            if key not in WB:
                WB[key] = Buf("w_" + key)
                WS[key] = P.slot()
            pool.dma(dst, src, WS[key], writes=[WB[key]])

        def conv_layer(i):
            for j in range(NJ):
                for gu in range(2):
                    src = ffn_up_d[i][:, gu * DFF + j * 128: gu * DFF + (j + 1) * 128].rearrange("(k p) n -> p k n", p=128)
                    conv("wup%d" % i, wup_s[i][j, :, :, gu * 128:(gu + 1) * 128], src)
            for j0 in range(0, NJ, 2):
                conv("wdn%d" % i, wdn_s[i][j0:j0 + 2].rearrange("j p n -> p j n"),
                     ffn_down_d[i][j0 * 128:(j0 + 2) * 128, :].rearrange("(j p) n -> p j n", p=128))
            for k0 in range(0, 8, 2):
                conv("gate%d" % i, gate_s[i][k0:k0 + 2].rearrange("j p n -> p j n"),
                     gate_w_d[i][k0 * 128:(k0 + 2) * 128, :].rearrange("(j p) n -> p j n", p=128))
            conv("proj%d" % i, proj_s[i][:, :, :], ple_proj_d[i].rearrange("(k p) n -> p k n", p=128))

        conv("gb", gb_s[:, :], gate_b_d[:, :])
        conv("indt", indt_s[:, :, :], in_proj_d[:, 6144:6176].rearrange("(k p) n -> p k n", p=128))
        conv_layer(0)
        for k in range(8):
            conv("inz", inz_s[k], in_proj_d[k * 128:(k + 1) * 128, 0:DIN])
        for c in range(32):
            conv("inx", inx_s[c], in_proj_d[:, DIN + c * 128: DIN + (c + 1) * 128].rearrange("(k p) n -> p k n", p=128))
        for k0 in range(0, 16, 2):
            conv("outp", outp_s[k0:k0 + 2].rearrange("j p n -> p j n"),
                 out_proj_d[k0 * 128:(k0 + 2) * 128, :].rearrange("(j p) n -> p j n", p=128))
        conv_layer(1)

        ld = P.slot()
        identf = SB([128, 128]); ident = SB([128, 128], BF16)
        tri = SB([128, 128]); trib = SB([128, 128], BF16); onesf = SB([128, 128]); mstr = SB([128, 128], BF16)
        mtmp = identf
        onesb = SB([1, 128], BF16)
        pool.op(lambda e: e.memset(onesf[:], 1.0), writes=[onesf.b])
        pool.op(lambda e: e.memset(onesb[:], 1.0), writes=[onesb.b])
        pool.op(lambda e: e.affine_select(out=identf[:], in_=onesf[:], pattern=[[-1, 128]], compare_op=ALU.is_equal,
                                          fill=0.0, base=0, channel_multiplier=1), reads=[onesf.b], writes=[identf.b])
        dve.op(lambda e: e.tensor_copy(out=ident[:], in_=identf[:]), reads=[identf.b], writes=[ident.b])
        pool.op(lambda e: e.affine_select(out=tri[:], in_=onesf[:], pattern=[[1, 128]], compare_op=ALU.is_ge,
                                          fill=0.0, base=0, channel_multiplier=-1), reads=[onesf.b], writes=[tri.b])
        dve.op(lambda e: e.tensor_copy(out=trib[:], in_=tri[:]), reads=[tri.b], writes=[trib.b])
        pool.op(lambda e: e.affine_select(out=mtmp[:], in_=onesf[:], pattern=[[-1, 128]], compare_op=ALU.is_gt,
                                          fill=0.0, base=0, channel_multiplier=1), reads=[onesf.b], writes=[mtmp.b])
        dve.op(lambda e: e.tensor_copy(out=mstr[:], in_=mtmp[:]), reads=[mtmp.b], writes=[mstr.b])

        lng = [SB([128, D]) for _ in range(4)]
        lnb = [SB([128, D]) for _ in range(4)]
        for q in range(4):
            lng[q].load(lng[q][:], lng_d[q:q + 1, :].partition_broadcast(128))
            lnb[q].load(lnb[q][:], lnb_d[q:q + 1, :].partition_broadcast(128))
        lngf = SB([128, 4, 8]); lnbf = SB([128, 4, 8]); epsT = SB([128, 1])
        lngf.load(lngf[:], lngf_d[:, :, :]); lnbf.load(lnbf[:], lnbf_d[:, :, :])
        pool.op(lambda e: e.memset(epsT[:], LN_EPS_), writes=[epsT.b])
        gbb = SB([1, 2, D], BF16)
        gbb.load(gbb[:], gb_s[:, :].rearrange("(o a) n -> o a n", o=1), reads=[WB["gb"]])
        dtb = SB([128, NH]); abc = SB([128, NH]); dbc = SB([128, NH])
        dtb.load(dtb[:], dtb_d[0:1, :].partition_broadcast(128))
        abc.load(abc[:], alog_d[0:1, :].partition_broadcast(128))
        dbc.load(dbc[:], dsk_d[0:1, :].partition_broadcast(128))
        act.op(lambda e: e.activation(out=abc[:], in_=abc[:], func=AF.Exp), reads=[abc.b], writes=[abc.b])
        dve.op(lambda e: e.tensor_scalar_mul(out=abc[:], in0=abc[:], scalar1=-1.0), reads=[abc.b], writes=[abc.b])
        scw = SB([128, 32, 4]); scb = SB([128, 32]); normw = SB([128, 16])
        fcw = SB([128, 2, 2, NJ, 3]); fcb = SB([128, 2, 2, NJ])
        scw.load(scw[:], scw_d[:, :, :])
        scb.load(scb[:], scb_d[:, :])
        normw.load(normw[:], normw_d[:, :])
        fcw.load(fcw[:], fcw_d[:, :, :, :, :])
        fcb.load(fcb[:], fcb_d[:, :, :, :])
        poolw = SB([128, 8, 256], BF16)
        POOLW_SETUP = True
        wdt = SB([128, 8, NH], BF16)
        wdt.load(wdt[:], indt_s[:, :, :], reads=[WB["indt"]])

        pA = T_("pA", [128, 2048], F32, psum=True)
        pB = T_("pB", [128, 1024], F32, psum=True)
        pT = T_("pT", [128, 1024], BF16, psum=True)
        pS = T_("pS", [128, 512], F32, psum=True)
        pAh = [Buf("pA0"), Buf("pA1")]
        pBh = [Buf("pB0"), Buf("pB1")]
        pA23 = [Buf("pA2"), Buf("pA3")]

        NSLOT = 7
        wslots = [SB([128, 2048], BF16, name="wslot%d" % q) for q in range(NSLOT)]
        for w_ in wslots:
            w_.slot = P.slot()

        class WView:
            def __init__(self, slot_t, view):
                self.v = view
                self.b = slot_t.b

            def __getitem__(self, k):
                return self.v[k]

        class Ring:
            def __init__(self, kind, lo, hi, srcbuf=None):
                self.kind = kind
                self.lo = lo
                self.depth = hi - lo

            def view(self, t):
                k = self.kind
                if k == "up":
                    return t.t[:, :].rearrange("p (k n) -> p k n", n=256)
                if k == "row1024":
                    return t.t[:, 0:1024]
                if k == "proj":
                    return t.t[:, :].rearrange("p (k n) -> p k n", n=1024)
                if k == "row2048":
                    return t.t[:, :]
                if k == "inx":
                    return t.t[:, 0:1024].rearrange("p (k n) -> p k n", n=128)
                raise KeyError(k)

            def load(self, i, src, wb):
                t = wslots[self.lo + i % self.depth]
                v = self.view(t)
                sp.dma(v, src, t.slot, reads=[wb], writes=[t.b])
                return WView(t, v)

        r_up = Ring("up", 0, 3)
        r_dn = Ring("row1024", 3, 6)
        r_gate = Ring("row1024", 0, 6)
        r_proj = Ring("proj", 6, 7)
        r_inz = Ring("row2048", 0, 6)
        r_inx = Ring("inx", 0, 6)
        r_out = Ring("row1024", 0, 6)

        def stream(n, ring, srcf, body, wb):
            tl = {}
            for i in range(min(ring.depth, n)):
                tl[i] = ring.load(i, srcf(i), wb)
            for i in range(n):
                body(i, tl[i])
                if i + ring.depth < n:
                    tl[i + ring.depth] = ring.load(i + ring.depth, srcf(i + ring.depth), wb)

        xin = SB([128, 8, 143])
        xtok = [SB([128, D]) for _ in range(3)]
        sA = SB([128, 2, 143]); sBf = SB([128, 2, 143]); t16 = SB([128, 2, 16])
        pooled = SB([128, 8, 128], BF16)
        xT = SB([128, 8, 128], BF16)
        xb16 = SB([128, D], BF16)
        U1 = SB([128, 32 * 128], BF16)
        hb = [[SB([128, 130]) for _ in range(2)] for _ in range(4)]
        acc = [[SB([128, 128]) for _ in range(2)] for _ in range(4)]
        sgs = [SB([128, 128]) for _ in range(2)]
        for hp_ in hb:
            for h__ in hp_:
                h__.tb = Buf("tail")
        pst = SB([128, 2, 128]); pTb = SB([128, 2, 128], BF16)
        gate = SB([128, D]); tmpt = SB([128, D])
        stats = SB([128, 2, 6]); mv = SB([128, 8])
        ftail = SB([128, 2, 2, NJ, 2]); ctail = SB([128, 32, 3])
        invc = SB([128, 8, 16])
        sz = SB([128, DIN]); ybuf = SB([128, DIN]); ytmp = SB([128, DIN])
        xsT = SB([128, 16, 128], BF16); BT = SB([128, 8, 128], BF16); CT = SB([128, 8, 128], BF16)
        cbuf = [SB([128, 131]) for _ in range(4)]; cacc = [SB([128, 128]) for _ in range(4)]
        for c__ in cbuf:
            c__.tb = Buf("ctailb")
        xdt = SB([128, DIN], BF16); xdtd = SB([128, DIN], BF16); Btok = SB([128, 1024], BF16)
        cbm = SB([128, 8, 128], BF16)
        rseg = [SB([128, 8, 128], BF16) for _ in range(2)]
        _ex = SB([128, 8, 128], BF16); expL = [_ex, _ex]
        ST = SB([128, DIN]); STb = SB([128, DIN], BF16)
        ynb = SB([128, DIN], BF16); ynT = SB([128, 16, 128], BF16)
        sm = SB([128, 8, NH])
        ss = SB([128, 4, 8])
        io = P.slot(); io2 = P.slot(); oslot = P.slot()
        vfl = SB([128, NBLK]); vfl.load(vfl[:], vfl_d[:, :])
        invt = [SB([128, 8, 16]) for _ in range(4)]
        for q in range(4):
            invt[q].load(invt[q][:], invc_d[0][q])
        scr = Buf("scratch")
        sz.load(sz[:].rearrange("p (c d) -> p c d", d=256), pool_w_d.rearrange("g (h p) d -> p (g h) d", p=128))
        ytmp.load(ytmp[:, 0:D], pool_scale_d[0:1, :].partition_broadcast(128))
        for c in range(8):
            g = c // 2
            dve.op(lambda e, c=c, g=g: e.tensor_tensor(out=poolw[:, c, :], in0=sz[:, c * 256:(c + 1) * 256],
                                                       in1=ytmp[:, g * 256:(g + 1) * 256], op=ALU.mult),
                   reads=[sz.b, ytmp.b], writes=[poolw.b])
        outbufs = []

        def layer_norm(T, u, q, xo, use_xT=True):
            for h in range(2):
                dve.op(lambda e, h=h: e.bn_stats(out=stats[:T, h, :], in_=u[:T, h * 512:(h + 1) * 512]),
                       reads=[u.b], writes=[stats.b])
            dve.op(lambda e: e.bn_aggr(out=mv[:T, 0:2], in_=stats[:T].rearrange("p a b -> p (a b)")), reads=[stats.b], writes=[mv.b])
            act.op(lambda e: e.activation(out=mv[:T, 3:4], in_=mv[:T, 1:2], func=AF.Sqrt, bias=epsT[:T, 0:1]), reads=[mv.b, epsT.b], writes=[mv.b])
            dve.op(lambda e: e.reciprocal(out=mv[:T, 4:5], in_=mv[:T, 3:4]), reads=[mv.b], writes=[mv.b])
            dve.op(lambda e: e.tensor_scalar(out=xb16[:T, :], in0=u[:T, :], scalar1=mv[:T, 0:1], scalar2=mv[:T, 4:5],
                                             op0=ALU.subtract, op1=ALU.mult), reads=[u.b, mv.b], writes=[xb16.b])
            dve.op(lambda e: e.scalar_tensor_tensor(out=mv[:T, 5:6], in0=mv[:T, 0:1], scalar=-1.0, in1=mv[:T, 4:5],
                                                    op0=ALU.mult, op1=ALU.mult), reads=[mv.b], writes=[mv.b])
            for c in range(8):
                pe.op(lambda e, c=c: e.transpose(out=pT[:, c * 128:c * 128 + T], in_=xb16[:T, c * 128:(c + 1) * 128],
                                                 identity=ident[:T, :T]), reads=[xb16.b, ident.b], writes=[pT.b])
            act.op(lambda e: e.activation(out=tmpt[:T, :], in_=u[:T, :], func=AF.Identity, scale=mv[:T, 4:5], bias=mv[:T, 5:6]),
                   reads=[u.b, mv.b], writes=[tmpt.b])
            for c in range(8):
                act.op(lambda e, c=c: e.activation(out=xT[:, c, :T], in_=pT[:, c * 128:c * 128 + T], func=AF.Identity,
                                                   scale=lngf[:, q, c:c + 1], bias=lnbf[:, q, c:c + 1]),
                       reads=[pT.b, lngf.b, lnbf.b], writes=[xT.b])
            pool.op(lambda e: e.tensor_tensor(out=tmpt[:T, :], in0=tmpt[:T, :], in1=lng[q][:T, :], op=ALU.mult),
                    reads=[tmpt.b, lng[q].b], writes=[tmpt.b])
            pool.op(lambda e: e.tensor_tensor(out=xo[:T, :], in0=tmpt[:T, :], in1=lnb[q][:T, :], op=ALU.add),
                    reads=[tmpt.b, lnb[q].b], writes=[xo.b])

        def to_T(T, xo):
            act.op(lambda e: e.activation(out=xb16[:T, :], in_=xo[:T, :], func=AF.Copy), reads=[xo.b], writes=[xb16.b])
            for c in range(8):
                pe.op(lambda e, c=c: e.transpose(out=pT[:, c * 128:c * 128 + T], in_=xb16[:T, c * 128:(c + 1) * 128],
                                                 identity=ident[:T, :T]), reads=[xb16.b, ident.b], writes=[pT.b])
            dve.op(lambda e: e.tensor_copy(out=xT[:, :, :T], in_=pT[:].rearrange("p (c t) -> p c t", t=128)[:, :, :T]),
                   reads=[pT.b], writes=[xT.b])

        hTb = [Buf("hT%d" % j_) for j_ in range(NJ)]

        def ffn(T, i):
            hT = U1
            st_ = {}

            def up(j, w):
                par = j % 2
                if par == 0:
                    pp_, po_, pbs_ = pB, 0, [[pBh[0]], [pBh[1]]]
                else:
                    pp_, po_, pbs_ = pA, 1024, [[pA23[0], pAh[1]], [pA23[1], pAh[1]]]
                for gu in range(2):
                    for k in range(8):
                        pe.op(lambda e, gu=gu, k=k: e.matmul(pp_[:, po_ + gu * 512:po_ + gu * 512 + T], lhsT=w[:, k, gu * 128:(gu + 1) * 128],
                                                             rhs=xT[:, k, :T], start=(k == 0), stop=(k == 7)),
                              reads=[w.b, xT.b], writes=pbs_[gu])
                p4 = j % 4
                sg = sgs[j % 2]
                for gu in range(2):
                    h_ = hb[p4][gu]; a_ = acc[p4][gu]
                    act.op(lambda e, gu=gu, h_=h_: e.activation(out=h_[:, 2:2 + T], in_=pp_[:, po_ + gu * 512:po_ + gu * 512 + T], func=AF.Copy),
                           reads=pbs_[gu], writes=[h_.b])
                    act.op(lambda e, gu=gu, a_=a_: e.activation(out=a_[:, :T], in_=pp_[:, po_ + gu * 512:po_ + gu * 512 + T], func=AF.Identity,
                                                                scale=fcw[:, i, gu, j, 2:3], bias=fcb[:, i, gu, j:j + 1]),
                           reads=pbs_[gu] + [fcw.b, fcb.b], writes=[a_.b])
                    pool.op(lambda e, gu=gu, h_=h_: e.tensor_copy(out=h_[:, 0:2], in_=ftail[:, i, gu, j, :]),
                            reads=[ftail.b], writes=[h_.tb])
                    pool.op(lambda e, gu=gu, h_=h_: e.tensor_copy(out=ftail[:, i, gu, j, :], in_=h_[:, T:T + 2]),
                            reads=[h_.b], writes=[ftail.b])
                for tap in (1, 0):
                    for gu in range(2):
                        h_ = hb[p4][gu]; a_ = acc[p4][gu]
                        dve.op(lambda e, gu=gu, h_=h_, a_=a_, tap=tap: e.scalar_tensor_tensor(
                            out=a_[:, :T], in0=h_[:, tap:tap + T], scalar=fcw[:, i, gu, j, tap:tap + 1], in1=a_[:, :T],
                            op0=ALU.mult, op1=ALU.add), reads=[h_.b, h_.tb, fcw.b, a_.b], writes=[a_.b])

            def back(j):
                p4 = j % 4
                sg = sgs[j % 2]
                act.op(lambda e: e.activation(out=sg[:, :T], in_=acc[p4][0][:, :T], func=AF.Silu), reads=[acc[p4][0].b], writes=[sg.b])
                pool.op(lambda e: e.tensor_tensor(out=hT[:, j * 128:j * 128 + T], in0=sg[:, :T], in1=acc[p4][1][:, :T], op=ALU.mult),
                        reads=[sg.b, acc[p4][1].b], writes=[hTb[j]] + ([hT.b] if j == 0 else []))

            def down(j, w):
                for h in range(2):
                    pe.op(lambda e, h=h: e.matmul(pA[:T, h * 512:(h + 1) * 512], lhsT=hT[:, j * 128:j * 128 + T],
                                                  rhs=w[:, h * 512:(h + 1) * 512], start=(j == 0), stop=(j == NJ - 1)),
                          reads=[hTb[j], w.b] + ([hT.b] if j == NJ - 1 else []), writes=[pAh[0]])

            DLY = 2
            upt = {}; dnt = {}
            for jj in range(min(3, NJ)):
                upt[jj] = r_up.load(jj, wup_s[i][jj], WB["wup%d" % i]); dnt[jj] = r_dn.load(jj, wdn_s[i][jj], WB["wdn%d" % i])
            for it_ in range(NJ + DLY):
                if it_ < NJ:
                    up(it_, upt[it_])
                    if it_ + 3 < NJ:
                        upt[it_ + 3] = r_up.load(it_ + 3, wup_s[i][it_ + 3], WB["wup%d" % i])
                if 0 <= it_ - 1 < NJ:
                    back(it_ - 1)
                jd = it_ - DLY
                if jd >= 0:
                    down(jd, dnt[jd])
                    if jd + 3 < NJ:
                        dnt[jd + 3] = r_dn.load(jd + 3, wdn_s[i][jd + 3], WB["wdn%d" % i])

        def ple(T, i, s, t0, x2, xo):
            for h in range(2):
                pe.op(lambda e, h=h: e.matmul(pA[:T, h * 512:(h + 1) * 512], lhsT=onesb[0:1, :T], rhs=gbb[0:1, i, h * 512:(h + 1) * 512],
                                              start=True, stop=False), reads=[onesb.b, gbb.b], writes=[pAh[0]])

            def body(k, w):
                for h in range(2):
                    pe.op(lambda e, h=h: e.matmul(pA[:T, h * 512:(h + 1) * 512], lhsT=xT[:, k, :T], rhs=w[:, h * 512:(h + 1) * 512],
                                                  start=False, stop=(k == 7)), reads=[xT.b, w.b], writes=[pAh[0]])
            stream(8, r_gate, lambda k: gate_s[i][k], body, WB["gate%d" % i])
            pst.load(pst[:, :, :T], pT_d[s][i][:, :, t0:t0 + T])
            pool.op(lambda e: e.tensor_copy(out=pTb[:, :, :T], in_=pst[:, :, :T]), reads=[pst.b], writes=[pTb.b])
            wp = r_proj.load(0, proj_s[i][:, :, :], WB["proj%d" % i])
            for h in range(2):
                for k in range(2):
                    pe.op(lambda e, h=h, k=k: e.matmul(pA[:T, 1024 + h * 512:1024 + (h + 1) * 512], lhsT=pTb[:, k, :T],
                                                       rhs=wp[:, k, h * 512:(h + 1) * 512], start=(k == 0), stop=(k == 1)),
                          reads=[pTb.b, wp.b], writes=[pAh[1]])
            act.op(lambda e: e.activation(out=gate[:T, :], in_=pA[:T, 0:1024], func=AF.Sigmoid), reads=[pAh[0]], writes=[gate.b])
            dve.op(lambda e: e.tensor_tensor(out=gate[:T, :], in0=gate[:T, :], in1=pA[:T, 1024:2048], op=ALU.mult),
                   reads=[gate.b, pAh[1]], writes=[gate.b])
            pool.op(lambda e: e.tensor_tensor(out=xo[:T, :], in0=gate[:T, :], in1=x2[:T, :], op=ALU.add),
                    reads=[gate.b, x2.b], writes=[xo.b])

        def residual(T, xres, pbuf, uo):
            dve.op(lambda e: e.scalar_tensor_tensor(out=uo[:T, :], in0=xres[:T, :], scalar=ALPHA_, in1=pA[:T, 0:1024],
                                                    op0=ALU.mult, op1=ALU.add), reads=[xres.b, pbuf], writes=[uo.b])

        def bc3(ap2, T, n):
            return ap2.unsqueeze(2).to_broadcast([T, NH, n])

        def block_p1(s, t0, T, first, mode, blk, vcol=None, invtab=None):
            invc_ = invtab if invtab is not None else invc
            xa, xb, xc = xtok
            xin.load(xin[:, :, :15 + T], xT_d[s][:, :, t0:t0 + 15 + T])
            xa.load(xa[:T, :], x_d[s][t0:t0 + T, :])
            W = 15 + T
            for g in range(4):
                cs_ = slice(2 * g, 2 * g + 2)
                wsz = 2 ** (g + 1)
                pool.op(lambda e, cs_=cs_: e.tensor_tensor(out=sA[:, :, 1:W], in0=xin[:, cs_, 1:W], in1=xin[:, cs_, 0:W - 1], op=ALU.add),
                        reads=[xin.b], writes=[sA.b])
                cur, oth = sA, sBf
                sh = 1
                for lvl in range(g):
                    sh2 = sh * 2
                    lo = 2 * sh2 - 1
                    pool.op(lambda e, cur=cur, oth=oth, lo=lo, sh2=sh2: e.tensor_tensor(
                        out=oth[:, :, lo:W], in0=cur[:, :, lo:W], in1=cur[:, :, lo - sh2:W - sh2], op=ALU.add),
                        reads=[cur.b], writes=[oth.b])
                    cur, oth = oth, cur
                    sh = sh2
                dve.op(lambda e, cur=cur, cs_=cs_, wsz=wsz: e.scalar_tensor_tensor(
                    out=pooled[:, cs_, :T], in0=cur[:, :, 15:W], scalar=1.0 / wsz, in1=xin[:, cs_, 15:W],
                    op0=ALU.mult, op1=ALU.subtract), reads=[cur.b, xin.b], writes=[pooled.b])
                if first:
                    dve.op(lambda e, cur=cur, cs_=cs_: e.tensor_tensor(out=t16[:], in0=cur[:, :, 15:31], in1=invc_[:, cs_, :], op=ALU.mult),
                           reads=[cur.b, invc_.b], writes=[t16.b])
                    dve.op(lambda e, cs_=cs_: e.tensor_tensor(out=pooled[:, cs_, 0:16], in0=t16[:], in1=xin[:, cs_, 15:31], op=ALU.subtract),
                           reads=[t16.b, xin.b], writes=[pooled.b])
            for g in range(4):
                for h in range(2):
                    c = 2 * g + h
                    pe.op(lambda e, g=g, h=h, c=c: e.matmul(pA[:T, g * 256:(g + 1) * 256], lhsT=pooled[:, c, :T], rhs=poolw[:, c, :],
                                                            start=(h == 0), stop=(h == 1)),
                          reads=[pooled.b, poolw.b], writes=[pAh[0]])
            residual(T, xa, pAh[0], xb)
            layer_norm(T, xb, 0, xc)
            ffn(T, 0)
            residual(T, xc, pAh[0], xb)
            layer_norm(T, xb, 1, xa)
            ple(T, 0, s, t0, xa, xc)
            to_T(T, xc)

            if mode != "state":
                def zbody(k, w):
                    for cc in range(4):
                        pe.op(lambda e, cc=cc: e.matmul(pA[:T, cc * 512:(cc + 1) * 512], lhsT=xT[:, k, :T], rhs=w[:, cc * 512:(cc + 1) * 512],
                                                        start=(k == 0), stop=(k == 7)), reads=[xT.b, w.b], writes=[pAh[cc // 2]])
                stream(8, r_inz, lambda k: inz_s[k], zbody, WB["inz"])
                act.op(lambda e: e.activation(out=sz[:T, :], in_=pA[:T, :], func=AF.Silu), reads=[pAh[0], pAh[1]], writes=[sz.b])
            for k in range(8):
                pe.op(lambda e, k=k: e.matmul(pS[:T, 0:NH], lhsT=xT[:, k, :T], rhs=wdt[:, k, :], start=(k == 0), stop=(k == 7)),
                      reads=[xT.b, wdt.b], writes=[pS.b])
            S_ = lambda n: sm[:T, n, :]
            dve.op(lambda e: e.tensor_tensor(out=S_(0), in0=pS[:T, 0:NH], in1=dtb[:T, :], op=ALU.add), reads=[pS.b, dtb.b], writes=[sm.b])
            dve.op(lambda e: e.scalar_tensor_tensor(out=S_(1), in0=S_(0), scalar=-1.0, in1=S_(0), op0=ALU.mult, op1=ALU.max), reads=[sm.b], writes=[sm.b])
            act.op(lambda e: e.activation(out=S_(1), in_=S_(1), func=AF.Exp, scale=-1.0), reads=[sm.b], writes=[sm.b])
            act.op(lambda e: e.activation(out=S_(1), in_=S_(1), func=AF.Ln, bias=1.0), reads=[sm.b], writes=[sm.b])
            dve.op(lambda e: e.tensor_scalar_max(out=S_(0), in0=S_(0), scalar1=0.0), reads=[sm.b], writes=[sm.b])
            dve.op(lambda e: e.tensor_tensor(out=S_(2), in0=S_(0), in1=S_(1), op=ALU.add), reads=[sm.b], writes=[sm.b])
            if vcol is not None:
                dve.op(lambda e: e.tensor_scalar_mul(out=S_(2), in0=S_(2), scalar1=vfl[:T, vcol:vcol + 1]), reads=[sm.b, vfl.b], writes=[sm.b])
            dve.op(lambda e: e.tensor_tensor(out=S_(3), in0=S_(2), in1=abc[:T, :], op=ALU.mult), reads=[sm.b, abc.b], writes=[sm.b])

            def xbody(c, w):
                par = c % 2
                for k in range(8):
                    pe.op(lambda e, k=k: e.matmul(pB[:, par * 512:par * 512 + T], lhsT=w[:, k, :], rhs=xT[:, k, :T],
                                                  start=(k == 0), stop=(k == 7)), reads=[w.b, xT.b], writes=[pBh[par]])
                cb_ = cbuf[c % 4]; ca_ = cacc[c % 4]
                act.op(lambda e: e.activation(out=cb_[:, 3:3 + T], in_=pB[:, par * 512:par * 512 + T], func=AF.Copy),
                       reads=[pBh[par]], writes=[cb_.b])
                act.op(lambda e: e.activation(out=ca_[:, :T], in_=pB[:, par * 512:par * 512 + T], func=AF.Identity,
                                              scale=scw[:, c, 3:4], bias=scb[:, c:c + 1]),
                       reads=[pBh[par], scw.b, scb.b], writes=[ca_.b])
                pool.op(lambda e: e.tensor_copy(out=cb_[:, 0:3], in_=ctail[:, c, :]), reads=[ctail.b], writes=[cb_.tb])
                pool.op(lambda e: e.tensor_copy(out=ctail[:, c, :], in_=cb_[:, T:T + 3]), reads=[cb_.b], writes=[ctail.b])
                for tap in (2, 1, 0):
                    dve.op(lambda e, tap=tap: e.scalar_tensor_tensor(out=ca_[:, :T], in0=cb_[:, tap:tap + T], scalar=scw[:, c, tap:tap + 1],
                                                                     in1=ca_[:, :T], op0=ALU.mult, op1=ALU.add),
                           reads=[cb_.b, cb_.tb, scw.b, ca_.b], writes=[ca_.b])
                def fin(c=c, ca_=ca_):
                    if c < 16:
                        dst, db = xsT[:, c, :T], xsT.b
                    elif c < 24:
                        dst, db = BT[:, c - 16, :T], BT.b
                    else:
                        dst, db = CT[:, c - 24, :T], CT.b
                    act.op(lambda e: e.activation(out=dst, in_=ca_[:, :T], func=AF.Silu), reads=[ca_.b], writes=[db])
                if xpend:
                    xpend.pop()()
                xpend.append(fin)
            xpend = []
            stream(24 if mode == "state" else 32, r_inx, lambda c: inx_s[c], xbody, WB["inx"])
            while xpend:
                xpend.pop()()
            for r in range(2):
                for c8 in range(8):
                    c = r * 8 + c8
                    pe.op(lambda e, c=c, c8=c8: e.transpose(out=pT[:T, c8 * 128:(c8 + 1) * 128], in_=xsT[:, c, :T], identity=ident[:, :]),
                          reads=[xsT.b, ident.b], writes=[pT.b])
                hs = slice(r * 16, (r + 1) * 16)
                dve.op(lambda e, r=r, hs=hs: e.tensor_tensor(
                    out=xdt[:T, r * 1024:(r + 1) * 1024].rearrange("p (h d) -> p h d", d=64),
                    in0=pT[:T, :].rearrange("p (h d) -> p h d", d=64),
                    in1=sm[:T, 2, hs].unsqueeze(2).to_broadcast([T, 16, 64]), op=ALU.mult),
                    reads=[pT.b, sm.b], writes=[xdt.b])
                if mode != "state":
                  dve.op(lambda e, r=r, hs=hs: e.tensor_tensor(
                    out=ybuf[:T, r * 1024:(r + 1) * 1024].rearrange("p (h d) -> p h d", d=64),
                    in0=pT[:T, :].rearrange("p (h d) -> p h d", d=64),
                    in1=dbc[:T, hs].unsqueeze(2).to_broadcast([T, 16, 64]), op=ALU.mult),
                    reads=[pT.b, dbc.b], writes=[ybuf.b])
            for g in range(8):
                pe.op(lambda e, g=g: e.transpose(out=pT[:T, g * 128:(g + 1) * 128], in_=BT[:, g, :T], identity=ident[:, :]),
                      reads=[BT.b, ident.b], writes=[pT.b])
            dve.op(lambda e: e.tensor_copy(out=Btok[:T, :], in_=pT[:T, :]), reads=[pT.b], writes=[Btok.b])
            pe.op(lambda e: e.matmul(pS[:T, 32:64], lhsT=tri[:T, :T], rhs=sm[:T, 3, :], start=True, stop=True),
                  reads=[tri.b, sm.b], writes=[pS.b])
            pe.op(lambda e: e.matmul(pS[:, 64:96], lhsT=onesf[:T, :], rhs=sm[:T, 3, :], start=True, stop=True),
                  reads=[onesf.b, sm.b], writes=[pS.b])
            act.op(lambda e: e.activation(out=S_(4), in_=pS[:T, 32:64], func=AF.Exp), reads=[pS.b], writes=[sm.b])
            act.op(lambda e: e.activation(out=S_(5), in_=pS[:T, 32:64], func=AF.Copy), reads=[pS.b], writes=[sm.b])
            dve.op(lambda e: e.tensor_tensor(out=S_(6), in0=pS[:T, 64:96], in1=S_(5), op=ALU.subtract), reads=[pS.b, sm.b], writes=[sm.b])
            act.op(lambda e: e.activation(out=S_(6), in_=S_(6), func=AF.Exp), reads=[sm.b], writes=[sm.b])
            act.op(lambda e: e.activation(out=sm[:, 7, :], in_=pS[:, 64:96], func=AF.Exp), reads=[pS.b], writes=[sm.b])
            if False:
                dve.op(lambda e: e.tensor_tensor(out=S_(0), in0=pS[:T, 32:64], in1=carry[:T, :], op=ALU.add), reads=[pS.b, carry.b], writes=[sm.b])
                act.op(lambda e: e.activation(out=S_(0), in_=S_(0), func=AF.Exp), reads=[sm.b], writes=[sm.b])
                dve.op(lambda e: e.tensor_tensor(out=carry[:, :], in0=carry[:, :], in1=pS[:, 64:96], op=ALU.add), reads=[pS.b, carry.b], writes=[carry.b])
            if mode != "state":
                for g in range(8):
                    pe.op(lambda e, g=g: e.matmul(pB[:T, g * 128:g * 128 + T], lhsT=BT[:, g, :T], rhs=CT[:, g, :T], start=True, stop=True),
                          reads=[BT.b, CT.b], writes=[pBh[g // 4]])
                dve.op(lambda e: e.tensor_tensor(out=cbm[:T, :, :T], in0=pB[:T, :].rearrange("p (g q) -> p g q", q=128)[:, :, :T],
                                                 in1=tri[:T, :T].unsqueeze(1).to_broadcast([T, 8, T]), op=ALU.mult),
                       reads=[pBh[0], pBh[1], tri.b], writes=[cbm.b])
                MT = U1
                for r in range(4):
                    rs_ = rseg[r % 2]; ex_ = expL[r % 2]
                    pool.op(lambda e, r=r, rs_=rs_: e.tensor_tensor(
                        out=rs_[:T, :, :T], in0=sm[:T, 3, r * 8:(r + 1) * 8].unsqueeze(2).to_broadcast([T, 8, T]),
                        in1=trib[:T, :T].unsqueeze(1).to_broadcast([T, 8, T]), op=ALU.mult),
                        reads=[sm.b, trib.b], writes=[rs_.b])
                    hp = r % 2
                    for q2 in range(2):
                        pe.op(lambda e, rs_=rs_, q2=q2, hp=hp: e.matmul(
                            pA[:T, hp * 1024 + q2 * 512: hp * 1024 + q2 * 512 + 4 * T].rearrange("p (h q) -> p h q", q=T),
                            lhsT=mstr[:T, :T], rhs=rs_[:T, q2 * 4:(q2 + 1) * 4, :T], start=True, stop=True),
                            reads=[mstr.b, rs_.b], writes=[pAh[hp]])
                    for q2 in range(2):
                        act.op(lambda e, ex_=ex_, q2=q2, hp=hp: e.activation(
                            out=ex_[:T, q2 * 4:(q2 + 1) * 4, :T],
                            in_=pA[:T, hp * 1024 + q2 * 512: hp * 1024 + q2 * 512 + 4 * T].rearrange("p (h q) -> p h q", q=T),
                            func=AF.Exp), reads=[pAh[hp]], writes=[ex_.b])
                    for g2 in range(2):
                        gg = 2 * r + g2
                        dve.op(lambda e, ex_=ex_, g2=g2, gg=gg, r=r: e.tensor_tensor(
                            out=MT[:T, (r * 8 + g2 * 4) * 128:(r * 8 + g2 * 4 + 4) * 128].rearrange("p (h q) -> p h q", q=128)[:, :, :T],
                            in0=ex_[:T, g2 * 4:(g2 + 1) * 4, :T],
                            in1=cbm[:T, gg:gg + 1, :T].to_broadcast([T, 4, T]), op=ALU.mult),
                            reads=[ex_.b, cbm.b], writes=[MT.b])
                for g in range(8):
                    pe.op(lambda e, g=g: e.matmul(pA[:T, g * 256:(g + 1) * 256], lhsT=CT[:, g, :T], rhs=STb[:, g * 256:(g + 1) * 256],
                                                  start=True, stop=True), reads=[CT.b, STb.b], writes=[pAh[g // 4]])
                dve.op(lambda e: e.tensor_tensor(out=ytmp[:T, :].rearrange("p (h d) -> p h d", d=64),
                                                 in0=pA[:T, :].rearrange("p (h d) -> p h d", d=64),
                                                 in1=bc3(sm[:T, 4, :], T, 64), op=ALU.mult),
                       reads=[pAh[0], pAh[1], sm.b], writes=[ytmp.b])
                pool.op(lambda e: e.tensor_tensor(out=ybuf[:T, :], in0=ybuf[:T, :], in1=ytmp[:T, :], op=ALU.add),
                        reads=[ybuf.b, ytmp.b], writes=[ybuf.b])
                for h in range(NH):
                    pe.op(lambda e, h=h: e.matmul(pA[:T, h * 64:(h + 1) * 64], lhsT=MT[:T, h * 128:h * 128 + T], rhs=xdt[:T, h * 64:(h + 1) * 64],
                                                  start=True, stop=True), reads=[MT.b, xdt.b], writes=[pAh[h // 16]])
                dve.op(lambda e: e.tensor_tensor(out=ybuf[:T, :], in0=ybuf[:T, :], in1=pA[:T, :], op=ALU.add),
                       reads=[ybuf.b, pAh[0], pAh[1]], writes=[ybuf.b])
            pool.op(lambda e: e.tensor_tensor(out=xdtd[:T, :].rearrange("p (h d) -> p h d", d=64),
                                              in0=xdt[:T, :].rearrange("p (h d) -> p h d", d=64),
                                              in1=bc3(sm[:T, 6, :], T, 64), op=ALU.mult), reads=[xdt.b, sm.b], writes=[xdtd.b])
            for g in range(8):
                pe.op(lambda e, g=g: e.matmul(pA[:, g * 256:(g + 1) * 256], lhsT=Btok[:T, g * 128:(g + 1) * 128],
                                              rhs=xdtd[:T, g * 256:(g + 1) * 256], start=True, stop=True),
                      reads=[Btok.b, xdtd.b], writes=[pAh[g // 4]])
            pool.op(lambda e: e.tensor_tensor(out=ST[:, :].rearrange("p (h d) -> p h d", d=64),
                                              in0=ST[:, :].rearrange("p (h d) -> p h d", d=64),
                                              in1=sm[:, 7, :].unsqueeze(2).to_broadcast([128, NH, 64]), op=ALU.mult),
                    reads=[ST.b, sm.b], writes=[ST.b])
            dve.op(lambda e: e.tensor_tensor(out=ST[:, :], in0=ST[:, :], in1=pA[:, :], op=ALU.add),
                   reads=[ST.b, pAh[0], pAh[1]], writes=[ST.b])
            act.op(lambda e: e.activation(out=STb[:, :], in_=ST[:, :], func=AF.Copy), reads=[ST.b], writes=[STb.b])
            if False:
                xc.store(x3s[blk], xc[:, :], writes=[scr])
                sz.store(szs[blk], sz[:, :], writes=[scr])
                ybuf.store(yss[blk], ybuf[:, :], writes=[scr])
                CT.store(cts[blk], CT[:, :, :], writes=[scr])
                sm.store(ess[blk], sm[:, 0, :], writes=[scr])

        def block_p2(s, t0, T, mode, blk, yrow):
            xa, xb, xc = xtok
            if False:
                xc.load(xc[:, :], x3s[blk], reads=[scr])
                sz.load(sz[:, :], szs[blk], reads=[scr])
                ybuf.load(ybuf[:, :], yss[blk], reads=[scr])
                CT.load(CT[:, :, :], cts[blk], reads=[scr])
                sm.load(sm[:, 0, :], ess[blk], reads=[scr])
                for g in range(8):
                    pe.op(lambda e, g=g: e.matmul(pA[:T, g * 256:(g + 1) * 256], lhsT=CT[:, g, :T], rhs=STb[:, g * 256:(g + 1) * 256],
                                                  start=True, stop=True), reads=[CT.b, STb.b], writes=[pAh[g // 4]])
                dve.op(lambda e: e.tensor_tensor(out=ytmp[:T, :].rearrange("p (h d) -> p h d", d=64),
                                                 in0=pA[:T, :].rearrange("p (h d) -> p h d", d=64),
                                                 in1=bc3(sm[:T, 0, :], T, 64), op=ALU.mult),
                       reads=[pAh[0], pAh[1], sm.b], writes=[ytmp.b])
                pool.op(lambda e: e.tensor_tensor(out=ybuf[:T, :], in0=ybuf[:T, :], in1=ytmp[:T, :], op=ALU.add),
                        reads=[ybuf.b, ytmp.b], writes=[ybuf.b])
            pool.op(lambda e: e.tensor_tensor(out=ybuf[:T, :], in0=ybuf[:T, :], in1=sz[:T, :], op=ALU.mult),
                    reads=[ybuf.b, sz.b], writes=[ybuf.b])
            for g in range(8):
                act.op(lambda e, g=g: e.activation(out=ytmp[:T, g * 256:(g + 1) * 256], in_=ybuf[:T, g * 256:(g + 1) * 256], func=AF.Square,
                                                   accum_out=ss[:T, 0, g:g + 1]), reads=[ybuf.b], writes=[ytmp.b, ss.b])
            dve.op(lambda e: e.tensor_scalar(out=ss[:T, 1, :], in0=ss[:T, 0, :], scalar1=1.0 / 256, scalar2=RMS_EPS_,
                                             op0=ALU.mult, op1=ALU.add), reads=[ss.b], writes=[ss.b])
            act.op(lambda e: e.activation(out=ss[:T, 2, :], in_=ss[:T, 1, :], func=AF.Sqrt), reads=[ss.b], writes=[ss.b])
            dve.op(lambda e: e.reciprocal(out=ss[:T, 3, :], in_=ss[:T, 2, :]), reads=[ss.b], writes=[ss.b])
            dve.op(lambda e: e.tensor_tensor(out=ynb[:T, :].rearrange("p (g d) -> p g d", d=256),
                                             in0=ybuf[:T, :].rearrange("p (g d) -> p g d", d=256),
                                             in1=ss[:T, 3, :].unsqueeze(2).to_broadcast([T, 8, 256]), op=ALU.mult),
                   reads=[ybuf.b, ss.b], writes=[ynb.b])
            for r in range(2):
                for c8 in range(8):
                    c = r * 8 + c8
                    pe.op(lambda e, c=c, c8=c8: e.transpose(out=pT[:, c8 * 128:c8 * 128 + T], in_=ynb[:T, c * 128:(c + 1) * 128],
                                                            identity=ident[:T, :T]), reads=[ynb.b, ident.b], writes=[pT.b])
                for c8 in range(8):
                    c = r * 8 + c8
                    act.op(lambda e, c=c, c8=c8: e.activation(out=ynT[:, c, :T], in_=pT[:, c8 * 128:c8 * 128 + T], func=AF.Identity,
                                                              scale=normw[:, c:c + 1]), reads=[pT.b, normw.b], writes=[ynT.b])

            def obody(k, w):
                for h in range(2):
                    pe.op(lambda e, h=h: e.matmul(pA[:T, h * 512:(h + 1) * 512], lhsT=ynT[:, k, :T], rhs=w[:, h * 512:(h + 1) * 512],
                                                  start=(k == 0), stop=(k == 15)), reads=[ynT.b, w.b], writes=[pAh[0]])
            stream(16, r_out, lambda k: outp_s[k], obody, WB["outp"])
            residual(T, xc, pAh[0], xb)
            layer_norm(T, xb, 2, xa)
            ffn(T, 1)
            residual(T, xa, pAh[0], xb)
            layer_norm(T, xb, 3, xc)
            ple(T, 1, s, t0, xc, xa)
            if mode != "halo":
                sp.dma(y_d[s][yrow:yrow + T, :], xa[:T, :], oslot, reads=[xa.b])

        def run_sample():
            s = 1
            ftail.load(ftail[:], ftail_d[s][:, :, :, :, :])
            ctail.load(ctail[:], ctail_d[s][:, :, :])
            ST.load(ST[:], st_d[s][:, :])
            invc.load(invc[:], invc_d[s][:, :, :])
            act.op(lambda e: e.activation(out=STb[:, :], in_=ST[:, :], func=AF.Copy), reads=[ST.b], writes=[STb.b])
            block_p1(s, 0, LS, True, "sample", 0)
            block_p2(s, 0, LS, "sample", 0, 0)
            sp.dma(ftail_o[s][:, :, :, :, :], ftail[:], oslot, reads=[ftail.b])
            sp.dma(ctail_o[s][:, :, :], ctail[:], oslot, reads=[ctail.b])
            sp.dma(st_o[s][:, :], ST[:], oslot, reads=[ST.b])
            sp.dma(pool_o[s][:, :], x_d[s][LS - 15:LS, :], oslot)

        def mask_all(b):
            for ap_, tb in ((ftail[:].rearrange("p a b c d -> p (a b c d)"), ftail), (ctail[:].rearrange("p a b -> p (a b)"), ctail)):
                dve.op(lambda e, ap_=ap_: e.tensor_scalar_mul(out=ap_, in0=ap_, scalar1=vfl[:, b:b + 1]),
                       reads=[tb.b, vfl.b], writes=[tb.b])

        def run_prompt():
            s = 0
            ftail.load(ftail[:], ftail_d[s][:, :, :, :, :])
            ctail.load(ctail[:], ctail_d[s][:, :, :])
            ST.load(ST[:], st_d[s][:, :])
            act.op(lambda e: e.activation(out=STb[:, :], in_=ST[:, :], func=AF.Copy), reads=[ST.b], writes=[STb.b])
            firsts = {NST + 1 - k * NOWN: k for k in range(4)}
            for b in range(NBLK):
                mode = "state" if b < NST else ("halo" if b == NST else "own")
                block_p1(s, b * 128, 128, b in firsts, mode, b, vcol=(b if mode != "own" else None),
                         invtab=invt[firsts[b]] if b in firsts else None)
                if mode != "state":
                    block_p2(s, b * 128, 128, mode, b, (b - NST - 1) * 128)
                if (b + 1) in firsts:
                    mask_all(b)
            sp.dma(ftail_o[s][:, :, :, :, :], ftail[:], oslot, reads=[ftail.b])
            sp.dma(ctail_o[s][:, :, :], ctail[:], oslot, reads=[ctail.b])
            sp.dma(st_o[s][:, :], ST[:], oslot, reads=[ST.b])
            sp.dma(pool_o[s][:, :], x_d[s][LP - 15:LP, :], oslot)

        ccslot = P.slot()
        run_sample()
        run_prompt()
        fin = Buf("fin")
        fin.rd = {oslot.key: oslot.n}
        P.finish([fin, xtok[0], ftail, ctail, ST] if False else [fin])
    return nc


def _fm(a, nchunk):
    r, c = a.shape
    return np.ascontiguousarray(a.reshape(r, nchunk, 128).transpose(2, 1, 0))


def _prep_inputs(x_prompt, x_sample, p_prompt, p_sample, cache_pool, cache_ssm_conv, state_ssm, cache_ffn_conv,
                 pool_w, pool_scale, ssm_in_proj, ssm_conv_w, ssm_conv_b, ssm_dt_bias, ssm_A_log, ssm_D, ssm_norm_w,
                 ssm_out_proj, ln_mix_g, ln_mix_b, ffn_up, ffn_conv_w, ffn_conv_b, ffn_down, ln_ffn_g, ln_ffn_b,
                 ple_proj, ple_gate_w, ple_gate_b, n_cores=8):
    f = lambda a: np.ascontiguousarray(np.asarray(a, dtype=np.float32))
    L = x_prompt.shape[1]
    LS = x_sample.shape[1]
    shared = {
        "pool_w": f(pool_w[0]), "pool_scale": f(pool_scale), "in_proj": f(ssm_in_proj[0]),
        "scw": f(np.asarray(ssm_conv_w[0]).reshape(4, 32, 128).transpose(2, 1, 0)),
        "scb": f(np.asarray(ssm_conv_b[0]).reshape(32, 128).T),
        "dtb": f(ssm_dt_bias), "alog": f(ssm_A_log), "dsk": f(ssm_D),
        "normw": f(np.asarray(ssm_norm_w[0]).reshape(16, 128).T),
        "out_proj": f(ssm_out_proj[0]),
        "lng": f(np.stack([ln_mix_g[0], ln_ffn_g[0], ln_mix_g[1], ln_ffn_g[1]])),
        "lnb": f(np.stack([ln_mix_b[0], ln_ffn_b[0], ln_mix_b[1], ln_ffn_b[1]])),
        "lngf": f(np.stack([ln_mix_g[0], ln_ffn_g[0], ln_mix_g[1], ln_ffn_g[1]]).reshape(4, 8, 128).transpose(2, 0, 1)),
        "lnbf": f(np.stack([ln_mix_b[0], ln_ffn_b[0], ln_mix_b[1], ln_ffn_b[1]]).reshape(4, 8, 128).transpose(2, 0, 1)),
        "fcw": f(np.asarray(ffn_conv_w).reshape(2, 3, 2, 22, 128).transpose(4, 0, 2, 3, 1)),
        "fcb": f(np.asarray(ffn_conv_b).reshape(2, 2, 22, 128).transpose(3, 0, 1, 2)),
        "gate_b": f(ple_gate_b),
    }
    for i in range(2):
        shared["ffn_up%d" % i] = f(ffn_up[i])
        shared["ffn_down%d" % i] = f(ffn_down[i])
        shared["ple_proj%d" % i] = f(ple_proj[i])
        shared["gate_w%d" % i] = f(ple_gate_w[i])
    wins = (2, 4, 8, 16)
    invc_p = np.zeros((128, 8, 16), np.float32)
    invc_s = np.zeros((128, 8, 16), np.float32)
    for c in range(8):
        w = wins[c // 2]
        invc_p[:, c, :] = 1.0 / np.minimum(w, np.arange(16) + 1)
        invc_s[:, c, :] = 1.0 / w
    in_maps = []
    nseq = x_prompt.shape[0]
    nseg = n_cores // nseq
    Lo = L // nseg
    NOWN = Lo // 128
    NST = 3 * NOWN
    NBLK = NST + 1 + NOWN
    LP = NBLK * 128

    def ext(a, start, n):
        out = np.zeros((n,) + a.shape[1:], np.float32)
        lo = max(start, 0)
        if start + n > lo:
            out[lo - start:] = a[lo:start + n]
        return out

    for core in range(n_cores):
        bp = core // nseg
        seg = core % nseg
        bs = core % x_sample.shape[0]
        start = (seg * NOWN - NST - 1) * 128
        nv = NST + 1 - seg * NOWN
        m = dict(shared)
        xp = f(x_prompt[bp])
        m["x0"] = ext(xp, start, LP)
        m["xT0"] = _fm(ext(xp, start - 15, LP + 15), 8)
        xs = f(x_sample[bs])
        m["x1"] = xs
        m["xT1"] = _fm(np.concatenate([f(cache_pool[0, bs]), xs], 0), 8)
        for i in range(2):
            m["pT0_%d" % i] = _fm(ext(f(p_prompt[i, bp]), start, LP), 2)
            m["pT1_%d" % i] = _fm(f(p_sample[i, bs]), 2)
        m["ftail0"] = np.zeros((128, 2, 2, 22, 2), np.float32)
        m["ftail1"] = f(np.asarray(cache_ffn_conv[:, bs]).reshape(2, 2, 2, 22, 128).transpose(4, 0, 2, 3, 1))
        m["ctail0"] = np.zeros((128, 32, 3), np.float32)
        m["ctail1"] = f(np.asarray(cache_ssm_conv[0, bs]).reshape(3, 32, 128).transpose(2, 1, 0))
        m["st0"] = np.zeros((128, 2048), np.float32)
        m["st1"] = f(np.asarray(state_ssm[0, bs]).reshape(2048, 128).T)
        m["invc0"] = np.stack([invc_p if k == seg else invc_s for k in range(4)])
        m["invc1"] = invc_s
        vfl = np.ones((128, NBLK), np.float32)
        vfl[:, :nv] = 0.0
        m["vfl"] = vfl
        in_maps.append(m)
    return in_maps, Lo, LS


def _unft(a):
    return np.ascontiguousarray(a.transpose(1, 4, 2, 3, 0).reshape(2, 2, 5632))


def _assemble(results, nb_p, nb_s):
    nseg = len(results) // nb_p
    last = [b * nseg + nseg - 1 for b in range(nb_p)]
    y_p = np.stack([np.concatenate([results[b * nseg + g]["y0"] for g in range(nseg)], 0) for b in range(nb_p)])
    y_s = np.stack([results[b]["y1"] for b in range(nb_s)])
    pool_p = np.stack([results[c]["pool_o0"] for c in last])[None]
    pool_s = np.stack([results[b]["pool_o1"] for b in range(nb_s)])[None]
    unc = lambda a: np.ascontiguousarray(a.transpose(2, 1, 0).reshape(3, 4096))
    sconv_p = np.stack([unc(results[c]["ctail_o0"]) for c in last])[None]
    sconv_s = np.stack([unc(results[b]["ctail_o1"]) for b in range(nb_s)])[None]
    uns = lambda a: np.ascontiguousarray(a.T.reshape(32, 64, 128))
    st_p = np.stack([uns(results[c]["st_o0"]) for c in last])[None]
    st_s = np.stack([uns(results[b]["st_o1"]) for b in range(nb_s)])[None]
    ffn_p = np.stack([_unft(results[c]["ftail_o0"]) for c in last], 1)
    ffn_s = np.stack([_unft(results[b]["ftail_o1"]) for b in range(nb_s)], 1)
    return (y_p, y_s, pool_p, pool_s, sconv_p, sconv_s, st_p, st_s, ffn_p, ffn_s)


_NC_CACHE = {}


def kernel(**inputs):
    in_maps, L, LS = _prep_inputs(**inputs)
    key = (L, LS)
    if key not in _NC_CACHE:
        _NC_CACHE[key] = build_program(L, LS)
    nc = _NC_CACHE[key]
    res = run_bass_kernel_spmd(nc, in_maps, core_ids=list(range(8)))
    outs = _assemble(res.results, inputs["x_prompt"].shape[0], inputs["x_sample"].shape[0])
    return tuple(np.asarray(o, dtype=np.float32) for o in outs)
```

```python
import numpy as np
import concourse.bass as bass
import concourse.mybir as mybir
from concourse.bass_utils import run_bass_kernel_spmd

F32 = mybir.dt.float32
BF16 = mybir.dt.bfloat16
AF = mybir.ActivationFunctionType
ALU = mybir.AluOpType


class Buf:
    __slots__ = ("name", "lw", "rd")

    def __init__(self, name):
        self.name = name
        self.lw = {}
        self.rd = {}


class Eng:
    def __init__(self, prog, name, handle, sem, is_pe=False, compute=True):
        self.prog = prog
        self.name = name
        self.h = handle
        self.sem = sem
        self.n = 0
        self.seen = {}
        self.q = []
        self.is_pe = is_pe
        self.compute = compute

    def _sync(self, deps):
        for key, cnt in deps.items():
            if key == self.name and self.is_pe:
                continue
            if self.seen.get(key, 0) >= cnt:
                continue
            self.seen[key] = cnt
            sem = self.prog.sems[key]
            self.q.append(("w", sem, cnt))

    def op(self, fn, reads=(), writes=()):
        deps = {}
        for b in reads:
            for k, c in b.lw.items():
                if deps.get(k, 0) < c:
                    deps[k] = c
        for b in writes:
            for k, c in b.lw.items():
                if deps.get(k, 0) < c:
                    deps[k] = c
            for k, c in b.rd.items():
                if deps.get(k, 0) < c:
                    deps[k] = c
        self._sync(deps)
        self.n += 1
        self.q.append(("o", fn, self.sem, 1))
        for b in reads:
            if b.rd.get(self.name, 0) < self.n:
                b.rd[self.name] = self.n
        for b in writes:
            b.lw = {self.name: self.n}
            b.rd = {}

    def dma(self, out_ap, in_ap, slot, reads=(), writes=()):
        deps = {}
        for b in reads:
            for k, c in b.lw.items():
                if deps.get(k, 0) < c:
                    deps[k] = c
        for b in writes:
            for k, c in b.lw.items():
                if deps.get(k, 0) < c:
                    deps[k] = c
            for k, c in b.rd.items():
                if deps.get(k, 0) < c:
                    deps[k] = c
        self._sync(deps)
        slot.n += 16
        self.q.append(("d", out_ap, in_ap, slot.sem))
        for b in reads:
            if b.rd.get(slot.key, 0) < slot.n:
                b.rd[slot.key] = slot.n
        for b in writes:
            b.lw = dict(b.lw) if False else {slot.key: slot.n}
            b.rd = {}

    def collective(self, tin, tout, slot, reads=(), writes=()):
        deps = {}
        for b in reads:
            for k, c in b.lw.items():
                if deps.get(k, 0) < c:
                    deps[k] = c
        for b in writes:
            for k, c in list(b.lw.items()) + list(b.rd.items()):
                if deps.get(k, 0) < c:
                    deps[k] = c
        self._sync(deps)
        slot.n += 1
        self.q.append(("c", tin, tout, slot.sem))
        for b in reads:
            b.rd[slot.key] = slot.n
        for b in writes:
            b.lw = {slot.key: slot.n}
            b.rd = {}

    def wait_all(self, bufs):
        deps = {}
        for b in bufs:
            for k, c in list(b.lw.items()) + list(b.rd.items()):
                if deps.get(k, 0) < c:
                    deps[k] = c
        self._sync(deps)

    def emit(self, h):
        for it in self.q:
            if it[0] == "w":
                h.wait_ge(it[1], it[2])
            elif it[0] == "o":
                ins = it[1](h)
                ins.then_inc(it[2], it[3])
            elif it[0] == "c":
                h.collective_compute("AllGather", ALU.bypass, replica_groups=[list(range(8))],
                                     ins=[it[1].ap().opt()], outs=[it[2].ap().opt()]).then_inc(it[3], 1)
            else:
                h.dma_start(out=it[1], in_=it[2]).then_inc(it[3], 16)


class Slot:
    def __init__(self, prog, key, sem):
        self.key = key
        self.sem = sem
        self.n = 0


class Prog:
    def __init__(self, nc, stack):
        self.nc = nc
        self.stack = stack
        self.sems = {}
        self.nslots = 0
        mk = lambda nm: stack.enter_context(nc.semaphore(nm))
        self.pe = Eng(self, "pe", nc.tensor, mk("s_pe"), is_pe=True)
        self.act = Eng(self, "act", nc.scalar, mk("s_act"))
        self.dve = Eng(self, "dve", nc.vector, mk("s_dve"))
        self.pool = Eng(self, "pool", nc.gpsimd, mk("s_pool"))
        self.sp = Eng(self, "sp", nc.sync, None, compute=False)
        for e in (self.pe, self.act, self.dve, self.pool):
            self.sems[e.name] = e.sem

    def slot(self, name=None):
        self.nslots += 1
        key = "dma%d" % self.nslots
        sem = self.stack.enter_context(self.nc.semaphore("s_" + key))
        self.sems[key] = sem
        return Slot(self, key, sem)

    def sb(self, name, shape, dtype):
        return self.stack.enter_context(self.nc.sbuf_tensor(name, list(shape), dtype))

    def ps(self, name, shape, dtype):
        return self.stack.enter_context(self.nc.psum_tensor(name, list(shape), dtype))

    def finish(self, final_bufs):
        self.sp.wait_all(final_bufs)
        with self.nc.Block() as block:
            @block.tensor
            def _(e):
                self.pe.emit(e)

            @block.scalar
            def _(e):
                self.act.emit(e)

            @block.vector
            def _(e):
                self.dve.emit(e)

            @block.gpsimd
            def _(e):
                self.pool.emit(e)

            @block.sync
            def _(e):
                self.sp.emit(e)

D = 1024
DFF = 2816
NJ = 22
DIN = 2048
CONVD = 4096
NH = 32
ALPHA_ = (2 * 2) ** 0.25
LN_EPS_ = 1e-5
RMS_EPS_ = 1e-5


def build_program(L, LS=32):
    import contextlib
    nc = bass.Bass("TRN2", target_bir_lowering=False)
    NOWN = L // 128
    NST = 3 * NOWN
    NBLK = NST + 1 + NOWN
    LP = NBLK * 128

    def din(name, shape, dt=F32):
        return nc.dram_tensor(name, list(shape), dt, kind="ExternalInput").ap()

    def dout(name, shape, dt=F32):
        return nc.dram_tensor(name, list(shape), dt, kind="ExternalOutput").ap()

    def dint(name, shape, dt=BF16):
        return nc.dram_tensor(name, list(shape), dt).ap()

    Ls = [LP, LS]
    xT_d = [din("xT%d" % s, [128, 8, 15 + Ls[s]]) for s in range(2)]
    x_d = [din("x%d" % s, [Ls[s], D]) for s in range(2)]
    pT_d = [[din("pT%d_%d" % (s, i), [128, 2, Ls[s]]) for i in range(2)] for s in range(2)]
    ftail_d = [din("ftail%d" % s, [128, 2, 2, NJ, 2]) for s in range(2)]
    ctail_d = [din("ctail%d" % s, [128, 32, 3]) for s in range(2)]
    st_d = [din("st%d" % s, [128, DIN]) for s in range(2)]
    invc_d = [din("invc0", [4, 128, 8, 16]), din("invc1", [128, 8, 16])]
    pool_w_d = din("pool_w", [4, 256, 256])
    pool_scale_d = din("pool_scale", [1, D])
    in_proj_d = din("in_proj", [D, 6176])
    scw_d = din("scw", [128, 32, 4])
    scb_d = din("scb", [128, 32])
    dtb_d = din("dtb", [1, NH])
    alog_d = din("alog", [1, NH])
    dsk_d = din("dsk", [1, NH])
    normw_d = din("normw", [128, 16])
    out_proj_d = din("out_proj", [DIN, D])
    lng_d = din("lng", [4, D])
    lnb_d = din("lnb", [4, D])
    lngf_d = din("lngf", [128, 4, 8])
    lnbf_d = din("lnbf", [128, 4, 8])
    ffn_up_d = [din("ffn_up%d" % i, [D, 2 * DFF]) for i in range(2)]
    fcw_d = din("fcw", [128, 2, 2, NJ, 3])
    fcb_d = din("fcb", [128, 2, 2, NJ])
    ffn_down_d = [din("ffn_down%d" % i, [DFF, D]) for i in range(2)]
    ple_proj_d = [din("ple_proj%d" % i, [256, D]) for i in range(2)]
    gate_w_d = [din("gate_w%d" % i, [D, D]) for i in range(2)]
    gate_b_d = din("gate_b", [2, D])
    y_d = [dout("y0", [L, D]), dout("y1", [LS, D])]
    vfl_d = din("vfl", [128, NBLK])
    pool_o = [dout("pool_o%d" % s, [15, D]) for s in range(2)]
    ftail_o = [dout("ftail_o%d" % s, [128, 2, 2, NJ, 2]) for s in range(2)]
    ctail_o = [dout("ctail_o%d" % s, [128, 32, 3]) for s in range(2)]
    st_o = [dout("st_o%d" % s, [128, DIN]) for s in range(2)]
    wup_s = [dint("wup_s%d" % i, [NJ, 128, 8, 256]) for i in range(2)]
    wdn_s = [dint("wdn_s%d" % i, [NJ, 128, D]) for i in range(2)]
    gate_s = [dint("gate_s%d" % i, [8, 128, D]) for i in range(2)]
    proj_s = [dint("proj_s%d" % i, [128, 2, D]) for i in range(2)]
    gb_s = dint("gb_s", [2, D])
    inz_s = dint("inz_s", [8, 128, DIN])
    inx_s = dint("inx_s", [32, 128, 8, 128])
    indt_s = dint("indt_s", [128, 8, NH])
    outp_s = dint("outp_s", [16, 128, D])

    with contextlib.ExitStack() as stack:
        P = Prog(nc, stack)
        pe, act, dve, pool, sp = P.pe, P.act, P.dve, P.pool, P.sp

        class T_:
            def __init__(self, name, shape, dt, psum=False):
                self.t = (P.ps if psum else P.sb)(name, shape, dt)
                self.b = Buf(name)

            def __getitem__(self, k):
                return self.t[k]

            def store(self, dst, src, writes=()):
                if not hasattr(self, "sslot"):
                    self.sslot = P.slot()
                sp.dma(dst, src, self.sslot, reads=[self.b], writes=list(writes))

            def load(self, dst, src, reads=()):
                if not hasattr(self, "slot"):
                    self.slot = P.slot()
                sp.dma(dst, src, self.slot, reads=list(reads), writes=[self.b])

        cnt = [0]

        def SB(shape, dt=F32, name=None):
            cnt[0] += 1
            return T_(name or ("t%d" % cnt[0]), shape, dt)

        WB = {}
        WS = {}

        def conv(key, dst, src):
            if key not in WB:
                WB[key] = Buf("w_" + key)
                WS[key] = P.slot()
            pool.dma(dst, src, WS[key], writes=[WB[key]])

        def conv_layer(i):
            for j in range(NJ):
                for gu in range(2):
                    src = ffn_up_d[i][:, gu * DFF + j * 128: gu * DFF + (j + 1) * 128].rearrange("(k p) n -> p k n", p=128)
                    conv("wup%d" % i, wup_s[i][j, :, :, gu * 128:(gu + 1) * 128], src)
            for j0 in range(0, NJ, 2):
                conv("wdn%d" % i, wdn_s[i][j0:j0 + 2].rearrange("j p n -> p j n"),
                     ffn_down_d[i][j0 * 128:(j0 + 2) * 128, :].rearrange("(j p) n -> p j n", p=128))
            for k0 in range(0, 8, 2):
                conv("gate%d" % i, gate_s[i][k0:k0 + 2].rearrange("j p n -> p j n"),
                     gate_w_d[i][k0 * 128:(k0 + 2) * 128, :].rearrange("(j p) n -> p j n", p=128))
            conv("proj%d" % i, proj_s[i][:, :, :], ple_proj_d[i].rearrange("(k p) n -> p k n", p=128))

        conv("gb", gb_s[:, :], gate_b_d[:, :])
        conv("indt", indt_s[:, :, :], in_proj_d[:, 6144:6176].rearrange("(k p) n -> p k n", p=128))
        conv_layer(0)
        for k in range(8):
            conv("inz", inz_s[k], in_proj_d[k * 128:(k + 1) * 128, 0:DIN])
        for c in range(32):
            conv("inx", inx_s[c], in_proj_d[:, DIN + c * 128: DIN + (c + 1) * 128].rearrange("(k p) n -> p k n", p=128))
        for k0 in range(0, 16, 2):
            conv("outp", outp_s[k0:k0 + 2].rearrange("j p n -> p j n"),
                 out_proj_d[k0 * 128:(k0 + 2) * 128, :].rearrange("(j p) n -> p j n", p=128))
        conv_layer(1)

        ld = P.slot()
        identf = SB([128, 128]); ident = SB([128, 128], BF16)
        tri = SB([128, 128]); trib = SB([128, 128], BF16); onesf = SB([128, 128]); mstr = SB([128, 128], BF16)
        mtmp = identf
        onesb = SB([1, 128], BF16)
        pool.op(lambda e: e.memset(onesf[:], 1.0), writes=[onesf.b])
        pool.op(lambda e: e.memset(onesb[:], 1.0), writes=[onesb.b])
        pool.op(lambda e: e.affine_select(out=identf[:], in_=onesf[:], pattern=[[-1, 128]], compare_op=ALU.is_equal,
                                          fill=0.0, base=0, channel_multiplier=1), reads=[onesf.b], writes=[identf.b])
        dve.op(lambda e: e.tensor_copy(out=ident[:], in_=identf[:]), reads=[identf.b], writes=[ident.b])
        pool.op(lambda e: e.affine_select(out=tri[:], in_=onesf[:], pattern=[[1, 128]], compare_op=ALU.is_ge,
                                          fill=0.0, base=0, channel_multiplier=-1), reads=[onesf.b], writes=[tri.b])
        dve.op(lambda e: e.tensor_copy(out=trib[:], in_=tri[:]), reads=[tri.b], writes=[trib.b])
        pool.op(lambda e: e.affine_select(out=mtmp[:], in_=onesf[:], pattern=[[-1, 128]], compare_op=ALU.is_gt,
                                          fill=0.0, base=0, channel_multiplier=1), reads=[onesf.b], writes=[mtmp.b])
        dve.op(lambda e: e.tensor_copy(out=mstr[:], in_=mtmp[:]), reads=[mtmp.b], writes=[mstr.b])

        lng = [SB([128, D]) for _ in range(4)]
        lnb = [SB([128, D]) for _ in range(4)]
        for q in range(4):
            lng[q].load(lng[q][:], lng_d[q:q + 1, :].partition_broadcast(128))
            lnb[q].load(lnb[q][:], lnb_d[q:q + 1, :].partition_broadcast(128))
        lngf = SB([128, 4, 8]); lnbf = SB([128, 4, 8]); epsT = SB([128, 1])
        lngf.load(lngf[:], lngf_d[:, :, :]); lnbf.load(lnbf[:], lnbf_d[:, :, :])
        pool.op(lambda e: e.memset(epsT[:], LN_EPS_), writes=[epsT.b])
        gbb = SB([1, 2, D], BF16)
        gbb.load(gbb[:], gb_s[:, :].rearrange("(o a) n -> o a n", o=1), reads=[WB["gb"]])
        dtb = SB([128, NH]); abc = SB([128, NH]); dbc = SB([128, NH])
        dtb.load(dtb[:], dtb_d[0:1, :].partition_broadcast(128))
        abc.load(abc[:], alog_d[0:1, :].partition_broadcast(128))
        dbc.load(dbc[:], dsk_d[0:1, :].partition_broadcast(128))
        act.op(lambda e: e.activation(out=abc[:], in_=abc[:], func=AF.Exp), reads=[abc.b], writes=[abc.b])
        dve.op(lambda e: e.tensor_scalar_mul(out=abc[:], in0=abc[:], scalar1=-1.0), reads=[abc.b], writes=[abc.b])
        scw = SB([128, 32, 4]); scb = SB([128, 32]); normw = SB([128, 16])
        fcw = SB([128, 2, 2, NJ, 3]); fcb = SB([128, 2, 2, NJ])
        scw.load(scw[:], scw_d[:, :, :])
        scb.load(scb[:], scb_d[:, :])
        normw.load(normw[:], normw_d[:, :])
        fcw.load(fcw[:], fcw_d[:, :, :, :, :])
        fcb.load(fcb[:], fcb_d[:, :, :, :])
        poolw = SB([128, 8, 256], BF16)
        POOLW_SETUP = True
        wdt = SB([128, 8, NH], BF16)
        wdt.load(wdt[:], indt_s[:, :, :], reads=[WB["indt"]])

        pA = T_("pA", [128, 2048], F32, psum=True)
        pB = T_("pB", [128, 1024], F32, psum=True)
        pT = T_("pT", [128, 1024], BF16, psum=True)
        pS = T_("pS", [128, 512], F32, psum=True)
        pAh = [Buf("pA0"), Buf("pA1")]
        pBh = [Buf("pB0"), Buf("pB1")]
        pA23 = [Buf("pA2"), Buf("pA3")]

        NSLOT = 7
        wslots = [SB([128, 2048], BF16, name="wslot%d" % q) for q in range(NSLOT)]
        for w_ in wslots:
            w_.slot = P.slot()

        class WView:
            def __init__(self, slot_t, view):
                self.v = view
                self.b = slot_t.b

            def __getitem__(self, k):
                return self.v[k]

        class Ring:
            def __init__(self, kind, lo, hi, srcbuf=None):
                self.kind = kind
                self.lo = lo
                self.depth = hi - lo

            def view(self, t):
                k = self.kind
                if k == "up":
                    return t.t[:, :].rearrange("p (k n) -> p k n", n=256)
                if k == "row1024":
                    return t.t[:, 0:1024]
                if k == "proj":
                    return t.t[:, :].rearrange("p (k n) -> p k n", n=1024)
                if k == "row2048":
                    return t.t[:, :]
                if k == "inx":
                    return t.t[:, 0:1024].rearrange("p (k n) -> p k n", n=128)
                raise KeyError(k)

            def load(self, i, src, wb):
                t = wslots[self.lo + i % self.depth]
                v = self.view(t)
                sp.dma(v, src, t.slot, reads=[wb], writes=[t.b])
                return WView(t, v)

        r_up = Ring("up", 0, 3)
        r_dn = Ring("row1024", 3, 6)
        r_gate = Ring("row1024", 0, 6)
        r_proj = Ring("proj", 6, 7)
        r_inz = Ring("row2048", 0, 6)
        r_inx = Ring("inx", 0, 6)
        r_out = Ring("row1024", 0, 6)

        def stream(n, ring, srcf, body, wb):
            tl = {}
            for i in range(min(ring.depth, n)):
                tl[i] = ring.load(i, srcf(i), wb)
            for i in range(n):
                body(i, tl[i])
                if i + ring.depth < n:
                    tl[i + ring.depth] = ring.load(i + ring.depth, srcf(i + ring.depth), wb)

        xin = SB([128, 8, 143])
        xtok = [SB([128, D]) for _ in range(3)]
        sA = SB([128, 2, 143]); sBf = SB([128, 2, 143]); t16 = SB([128, 2, 16])
        pooled = SB([128, 8, 128], BF16)
        xT = SB([128, 8, 128], BF16)
        xb16 = SB([128, D], BF16)
        U1 = SB([128, 32 * 128], BF16)
        hb = [[SB([128, 130]) for _ in range(2)] for _ in range(4)]
        acc = [[SB([128, 128]) for _ in range(2)] for _ in range(4)]
        sgs = [SB([128, 128]) for _ in range(2)]
        for hp_ in hb:
            for h__ in hp_:
                h__.tb = Buf("tail")
        pst = SB([128, 2, 128]); pTb = SB([128, 2, 128], BF16)
        gate = SB([128, D]); tmpt = SB([128, D])
        stats = SB([128, 2, 6]); mv = SB([128, 8])
        ftail = SB([128, 2, 2, NJ, 2]); ctail = SB([128, 32, 3])
        invc = SB([128, 8, 16])
        sz = SB([128, DIN]); ybuf = SB([128, DIN]); ytmp = SB([128, DIN])
        xsT = SB([128, 16, 128], BF16); BT = SB([128, 8, 128], BF16); CT = SB([128, 8, 128], BF16)
        cbuf = [SB([128, 131]) for _ in range(4)]; cacc = [SB([128, 128]) for _ in range(4)]
        for c__ in cbuf:
            c__.tb = Buf("ctailb")
        xdt = SB([128, DIN], BF16); xdtd = SB([128, DIN], BF16); Btok = SB([128, 1024], BF16)
        cbm = SB([128, 8, 128], BF16)
        rseg = [SB([128, 8, 128], BF16) for _ in range(2)]
        _ex = SB([128, 8, 128], BF16); expL = [_ex, _ex]
        ST = SB([128, DIN]); STb = SB([128, DIN], BF16)
        ynb = SB([128, DIN], BF16); ynT = SB([128, 16, 128], BF16)
        sm = SB([128, 8, NH])
        ss = SB([128, 4, 8])
        io = P.slot(); io2 = P.slot(); oslot = P.slot()
        vfl = SB([128, NBLK]); vfl.load(vfl[:], vfl_d[:, :])
        invt = [SB([128, 8, 16]) for _ in range(4)]
        for q in range(4):
            invt[q].load(invt[q][:], invc_d[0][q])
        scr = Buf("scratch")
        sz.load(sz[:].rearrange("p (c d) -> p c d", d=256), pool_w_d.rearrange("g (h p) d -> p (g h) d", p=128))
        ytmp.load(ytmp[:, 0:D], pool_scale_d[0:1, :].partition_broadcast(128))
        for c in range(8):
            g = c // 2
            dve.op(lambda e, c=c, g=g: e.tensor_tensor(out=poolw[:, c, :], in0=sz[:, c * 256:(c + 1) * 256],
                                                       in1=ytmp[:, g * 256:(g + 1) * 256], op=ALU.mult),
                   reads=[sz.b, ytmp.b], writes=[poolw.b])
        outbufs = []

        def layer_norm(T, u, q, xo, use_xT=True):
            for h in range(2):
                dve.op(lambda e, h=h: e.bn_stats(out=stats[:T, h, :], in_=u[:T, h * 512:(h + 1) * 512]),
                       reads=[u.b], writes=[stats.b])
            dve.op(lambda e: e.bn_aggr(out=mv[:T, 0:2], in_=stats[:T].rearrange("p a b -> p (a b)")), reads=[stats.b], writes=[mv.b])
            act.op(lambda e: e.activation(out=mv[:T, 3:4], in_=mv[:T, 1:2], func=AF.Sqrt, bias=epsT[:T, 0:1]), reads=[mv.b, epsT.b], writes=[mv.b])
            dve.op(lambda e: e.reciprocal(out=mv[:T, 4:5], in_=mv[:T, 3:4]), reads=[mv.b], writes=[mv.b])
            dve.op(lambda e: e.tensor_scalar(out=xb16[:T, :], in0=u[:T, :], scalar1=mv[:T, 0:1], scalar2=mv[:T, 4:5],
                                             op0=ALU.subtract, op1=ALU.mult), reads=[u.b, mv.b], writes=[xb16.b])
            dve.op(lambda e: e.scalar_tensor_tensor(out=mv[:T, 5:6], in0=mv[:T, 0:1], scalar=-1.0, in1=mv[:T, 4:5],
                                                    op0=ALU.mult, op1=ALU.mult), reads=[mv.b], writes=[mv.b])
            for c in range(8):
                pe.op(lambda e, c=c: e.transpose(out=pT[:, c * 128:c * 128 + T], in_=xb16[:T, c * 128:(c + 1) * 128],
                                                 identity=ident[:T, :T]), reads=[xb16.b, ident.b], writes=[pT.b])
            act.op(lambda e: e.activation(out=tmpt[:T, :], in_=u[:T, :], func=AF.Identity, scale=mv[:T, 4:5], bias=mv[:T, 5:6]),
                   reads=[u.b, mv.b], writes=[tmpt.b])
            for c in range(8):
                act.op(lambda e, c=c: e.activation(out=xT[:, c, :T], in_=pT[:, c * 128:c * 128 + T], func=AF.Identity,
                                                   scale=lngf[:, q, c:c + 1], bias=lnbf[:, q, c:c + 1]),
                       reads=[pT.b, lngf.b, lnbf.b], writes=[xT.b])
            pool.op(lambda e: e.tensor_tensor(out=tmpt[:T, :], in0=tmpt[:T, :], in1=lng[q][:T, :], op=ALU.mult),
                    reads=[tmpt.b, lng[q].b], writes=[tmpt.b])
            pool.op(lambda e: e.tensor_tensor(out=xo[:T, :], in0=tmpt[:T, :], in1=lnb[q][:T, :], op=ALU.add),
                    reads=[tmpt.b, lnb[q].b], writes=[xo.b])

        def to_T(T, xo):
            act.op(lambda e: e.activation(out=xb16[:T, :], in_=xo[:T, :], func=AF.Copy), reads=[xo.b], writes=[xb16.b])
            for c in range(8):
                pe.op(lambda e, c=c: e.transpose(out=pT[:, c * 128:c * 128 + T], in_=xb16[:T, c * 128:(c + 1) * 128],
                                                 identity=ident[:T, :T]), reads=[xb16.b, ident.b], writes=[pT.b])
            dve.op(lambda e: e.tensor_copy(out=xT[:, :, :T], in_=pT[:].rearrange("p (c t) -> p c t", t=128)[:, :, :T]),
                   reads=[pT.b], writes=[xT.b])

        hTb = [Buf("hT%d" % j_) for j_ in range(NJ)]

        def ffn(T, i):
            hT = U1
            st_ = {}

            def up(j, w):
                par = j % 2
                if par == 0:
                    pp_, po_, pbs_ = pB, 0, [[pBh[0]], [pBh[1]]]
                else:
                    pp_, po_, pbs_ = pA, 1024, [[pA23[0], pAh[1]], [pA23[1], pAh[1]]]
                for gu in range(2):
                    for k in range(8):
                        pe.op(lambda e, gu=gu, k=k: e.matmul(pp_[:, po_ + gu * 512:po_ + gu * 512 + T], lhsT=w[:, k, gu * 128:(gu + 1) * 128],
                                                             rhs=xT[:, k, :T], start=(k == 0), stop=(k == 7)),
                              reads=[w.b, xT.b], writes=pbs_[gu])
                p4 = j % 4
                sg = sgs[j % 2]
                for gu in range(2):
                    h_ = hb[p4][gu]; a_ = acc[p4][gu]
                    act.op(lambda e, gu=gu, h_=h_: e.activation(out=h_[:, 2:2 + T], in_=pp_[:, po_ + gu * 512:po_ + gu * 512 + T], func=AF.Copy),
                           reads=pbs_[gu], writes=[h_.b])
                    act.op(lambda e, gu=gu, a_=a_: e.activation(out=a_[:, :T], in_=pp_[:, po_ + gu * 512:po_ + gu * 512 + T], func=AF.Identity,
                                                                scale=fcw[:, i, gu, j, 2:3], bias=fcb[:, i, gu, j:j + 1]),
                           reads=pbs_[gu] + [fcw.b, fcb.b], writes=[a_.b])
                    pool.op(lambda e, gu=gu, h_=h_: e.tensor_copy(out=h_[:, 0:2], in_=ftail[:, i, gu, j, :]),
                            reads=[ftail.b], writes=[h_.tb])
                    pool.op(lambda e, gu=gu, h_=h_: e.tensor_copy(out=ftail[:, i, gu, j, :], in_=h_[:, T:T + 2]),
                            reads=[h_.b], writes=[ftail.b])
                for tap in (1, 0):
                    for gu in range(2):
                        h_ = hb[p4][gu]; a_ = acc[p4][gu]
                        dve.op(lambda e, gu=gu, h_=h_, a_=a_, tap=tap: e.scalar_tensor_tensor(
                            out=a_[:, :T], in0=h_[:, tap:tap + T], scalar=fcw[:, i, gu, j, tap:tap + 1], in1=a_[:, :T],
                            op0=ALU.mult, op1=ALU.add), reads=[h_.b, h_.tb, fcw.b, a_.b], writes=[a_.b])

            def back(j):
                p4 = j % 4
                sg = sgs[j % 2]
                act.op(lambda e: e.activation(out=sg[:, :T], in_=acc[p4][0][:, :T], func=AF.Silu), reads=[acc[p4][0].b], writes=[sg.b])
                pool.op(lambda e: e.tensor_tensor(out=hT[:, j * 128:j * 128 + T], in0=sg[:, :T], in1=acc[p4][1][:, :T], op=ALU.mult),
                        reads=[sg.b, acc[p4][1].b], writes=[hTb[j]] + ([hT.b] if j == 0 else []))

            def down(j, w):
                for h in range(2):
                    pe.op(lambda e, h=h: e.matmul(pA[:T, h * 512:(h + 1) * 512], lhsT=hT[:, j * 128:j * 128 + T],
                                                  rhs=w[:, h * 512:(h + 1) * 512], start=(j == 0), stop=(j == NJ - 1)),
                          reads=[hTb[j], w.b] + ([hT.b] if j == NJ - 1 else []), writes=[pAh[0]])

            DLY = 3
            upt = {}; dnt = {}
            for jj in range(min(3, NJ)):
                upt[jj] = r_up.load(jj, wup_s[i][jj], WB["wup%d" % i]); dnt[jj] = r_dn.load(jj, wdn_s[i][jj], WB["wdn%d" % i])
            for it_ in range(NJ + DLY):
                if it_ < NJ:
                    up(it_, upt[it_])
                    if it_ + 3 < NJ:
                        upt[it_ + 3] = r_up.load(it_ + 3, wup_s[i][it_ + 3], WB["wup%d" % i])
                if 0 <= it_ - 1 < NJ:
                    back(it_ - 1)
                jd = it_ - DLY
                if jd >= 0:
                    down(jd, dnt[jd])
                    if jd + 3 < NJ:
                        dnt[jd + 3] = r_dn.load(jd + 3, wdn_s[i][jd + 3], WB["wdn%d" % i])

        def ple(T, i, s, t0, x2, xo):
            for h in range(2):
                pe.op(lambda e, h=h: e.matmul(pA[:T, h * 512:(h + 1) * 512], lhsT=onesb[0:1, :T], rhs=gbb[0:1, i, h * 512:(h + 1) * 512],
                                              start=True, stop=False), reads=[onesb.b, gbb.b], writes=[pAh[0]])

            def body(k, w):
                for h in range(2):
                    pe.op(lambda e, h=h: e.matmul(pA[:T, h * 512:(h + 1) * 512], lhsT=xT[:, k, :T], rhs=w[:, h * 512:(h + 1) * 512],
                                                  start=False, stop=(k == 7)), reads=[xT.b, w.b], writes=[pAh[0]])
            stream(8, r_gate, lambda k: gate_s[i][k], body, WB["gate%d" % i])
            pst.load(pst[:, :, :T], pT_d[s][i][:, :, t0:t0 + T])
            pool.op(lambda e: e.tensor_copy(out=pTb[:, :, :T], in_=pst[:, :, :T]), reads=[pst.b], writes=[pTb.b])
            wp = r_proj.load(0, proj_s[i][:, :, :], WB["proj%d" % i])
            for h in range(2):
                for k in range(2):
                    pe.op(lambda e, h=h, k=k: e.matmul(pA[:T, 1024 + h * 512:1024 + (h + 1) * 512], lhsT=pTb[:, k, :T],
                                                       rhs=wp[:, k, h * 512:(h + 1) * 512], start=(k == 0), stop=(k == 1)),
                          reads=[pTb.b, wp.b], writes=[pAh[1]])
            act.op(lambda e: e.activation(out=gate[:T, :], in_=pA[:T, 0:1024], func=AF.Sigmoid), reads=[pAh[0]], writes=[gate.b])
            dve.op(lambda e: e.tensor_tensor(out=gate[:T, :], in0=gate[:T, :], in1=pA[:T, 1024:2048], op=ALU.mult),
                   reads=[gate.b, pAh[1]], writes=[gate.b])
            pool.op(lambda e: e.tensor_tensor(out=xo[:T, :], in0=gate[:T, :], in1=x2[:T, :], op=ALU.add),
                    reads=[gate.b, x2.b], writes=[xo.b])

        def residual(T, xres, pbuf, uo):
            dve.op(lambda e: e.scalar_tensor_tensor(out=uo[:T, :], in0=xres[:T, :], scalar=ALPHA_, in1=pA[:T, 0:1024],
                                                    op0=ALU.mult, op1=ALU.add), reads=[xres.b, pbuf], writes=[uo.b])

        def bc3(ap2, T, n):
            return ap2.unsqueeze(2).to_broadcast([T, NH, n])

        def block_p1(s, t0, T, first, mode, blk, vcol=None, invtab=None):
            invc_ = invtab if invtab is not None else invc
            xa, xb, xc = xtok
            xin.load(xin[:, :, :15 + T], xT_d[s][:, :, t0:t0 + 15 + T])
            xa.load(xa[:T, :], x_d[s][t0:t0 + T, :])
            W = 15 + T
            for g in range(4):
                cs_ = slice(2 * g, 2 * g + 2)
                wsz = 2 ** (g + 1)
                pool.op(lambda e, cs_=cs_: e.tensor_tensor(out=sA[:, :, 1:W], in0=xin[:, cs_, 1:W], in1=xin[:, cs_, 0:W - 1], op=ALU.add),
                        reads=[xin.b], writes=[sA.b])
                cur, oth = sA, sBf
                sh = 1
                for lvl in range(g):
                    sh2 = sh * 2
                    lo = 2 * sh2 - 1
                    pool.op(lambda e, cur=cur, oth=oth, lo=lo, sh2=sh2: e.tensor_tensor(
                        out=oth[:, :, lo:W], in0=cur[:, :, lo:W], in1=cur[:, :, lo - sh2:W - sh2], op=ALU.add),
                        reads=[cur.b], writes=[oth.b])
                    cur, oth = oth, cur
                    sh = sh2
                dve.op(lambda e, cur=cur, cs_=cs_, wsz=wsz: e.scalar_tensor_tensor(
                    out=pooled[:, cs_, :T], in0=cur[:, :, 15:W], scalar=1.0 / wsz, in1=xin[:, cs_, 15:W],
                    op0=ALU.mult, op1=ALU.subtract), reads=[cur.b, xin.b], writes=[pooled.b])
                if first:
                    dve.op(lambda e, cur=cur, cs_=cs_: e.tensor_tensor(out=t16[:], in0=cur[:, :, 15:31], in1=invc_[:, cs_, :], op=ALU.mult),
                           reads=[cur.b, invc_.b], writes=[t16.b])
                    dve.op(lambda e, cs_=cs_: e.tensor_tensor(out=pooled[:, cs_, 0:16], in0=t16[:], in1=xin[:, cs_, 15:31], op=ALU.subtract),
                           reads=[t16.b, xin.b], writes=[pooled.b])
            for g in range(4):
                for h in range(2):
                    c = 2 * g + h
                    pe.op(lambda e, g=g, h=h, c=c: e.matmul(pA[:T, g * 256:(g + 1) * 256], lhsT=pooled[:, c, :T], rhs=poolw[:, c, :],
                                                            start=(h == 0), stop=(h == 1)),
                          reads=[pooled.b, poolw.b], writes=[pAh[0]])
            residual(T, xa, pAh[0], xb)
            layer_norm(T, xb, 0, xc)
            ffn(T, 0)
            residual(T, xc, pAh[0], xb)
            layer_norm(T, xb, 1, xa)
            ple(T, 0, s, t0, xa, xc)
            to_T(T, xc)

            if mode != "state":
                def zbody(k, w):
                    for cc in range(4):
                        pe.op(lambda e, cc=cc: e.matmul(pA[:T, cc * 512:(cc + 1) * 512], lhsT=xT[:, k, :T], rhs=w[:, cc * 512:(cc + 1) * 512],
                                                        start=(k == 0), stop=(k == 7)), reads=[xT.b, w.b], writes=[pAh[cc // 2]])
                stream(8, r_inz, lambda k: inz_s[k], zbody, WB["inz"])
                act.op(lambda e: e.activation(out=sz[:T, :], in_=pA[:T, :], func=AF.Silu), reads=[pAh[0], pAh[1]], writes=[sz.b])
            for k in range(8):
                pe.op(lambda e, k=k: e.matmul(pS[:T, 0:NH], lhsT=xT[:, k, :T], rhs=wdt[:, k, :], start=(k == 0), stop=(k == 7)),
                      reads=[xT.b, wdt.b], writes=[pS.b])
            S_ = lambda n: sm[:T, n, :]
            dve.op(lambda e: e.tensor_tensor(out=S_(0), in0=pS[:T, 0:NH], in1=dtb[:T, :], op=ALU.add), reads=[pS.b, dtb.b], writes=[sm.b])
            dve.op(lambda e: e.scalar_tensor_tensor(out=S_(1), in0=S_(0), scalar=-1.0, in1=S_(0), op0=ALU.mult, op1=ALU.max), reads=[sm.b], writes=[sm.b])
            act.op(lambda e: e.activation(out=S_(1), in_=S_(1), func=AF.Exp, scale=-1.0), reads=[sm.b], writes=[sm.b])
            act.op(lambda e: e.activation(out=S_(1), in_=S_(1), func=AF.Ln, bias=1.0), reads=[sm.b], writes=[sm.b])
            dve.op(lambda e: e.tensor_scalar_max(out=S_(0), in0=S_(0), scalar1=0.0), reads=[sm.b], writes=[sm.b])
            dve.op(lambda e: e.tensor_tensor(out=S_(2), in0=S_(0), in1=S_(1), op=ALU.add), reads=[sm.b], writes=[sm.b])
            if vcol is not None:
                dve.op(lambda e: e.tensor_scalar_mul(out=S_(2), in0=S_(2), scalar1=vfl[:T, vcol:vcol + 1]), reads=[sm.b, vfl.b], writes=[sm.b])
            dve.op(lambda e: e.tensor_tensor(out=S_(3), in0=S_(2), in1=abc[:T, :], op=ALU.mult), reads=[sm.b, abc.b], writes=[sm.b])

            def xbody(c, w):
                par = c % 2
                for k in range(8):
                    pe.op(lambda e, k=k: e.matmul(pB[:, par * 512:par * 512 + T], lhsT=w[:, k, :], rhs=xT[:, k, :T],
                                                  start=(k == 0), stop=(k == 7)), reads=[w.b, xT.b], writes=[pBh[par]])
                cb_ = cbuf[c % 4]; ca_ = cacc[c % 4]
                act.op(lambda e: e.activation(out=cb_[:, 3:3 + T], in_=pB[:, par * 512:par * 512 + T], func=AF.Copy),
                       reads=[pBh[par]], writes=[cb_.b])
                act.op(lambda e: e.activation(out=ca_[:, :T], in_=pB[:, par * 512:par * 512 + T], func=AF.Identity,
                                              scale=scw[:, c, 3:4], bias=scb[:, c:c + 1]),
                       reads=[pBh[par], scw.b, scb.b], writes=[ca_.b])
                pool.op(lambda e: e.tensor_copy(out=cb_[:, 0:3], in_=ctail[:, c, :]), reads=[ctail.b], writes=[cb_.tb])
                pool.op(lambda e: e.tensor_copy(out=ctail[:, c, :], in_=cb_[:, T:T + 3]), reads=[cb_.b], writes=[ctail.b])
                for tap in (2, 1, 0):
                    dve.op(lambda e, tap=tap: e.scalar_tensor_tensor(out=ca_[:, :T], in0=cb_[:, tap:tap + T], scalar=scw[:, c, tap:tap + 1],
                                                                     in1=ca_[:, :T], op0=ALU.mult, op1=ALU.add),
                           reads=[cb_.b, cb_.tb, scw.b, ca_.b], writes=[ca_.b])
                def fin(c=c, ca_=ca_):
                    if c < 16:
                        dst, db = xsT[:, c, :T], xsT.b
                    elif c < 24:
                        dst, db = BT[:, c - 16, :T], BT.b
                    else:
                        dst, db = CT[:, c - 24, :T], CT.b
                    act.op(lambda e: e.activation(out=dst, in_=ca_[:, :T], func=AF.Silu), reads=[ca_.b], writes=[db])
                if len(xpend) >= 2:
                    xpend.pop(0)()
                xpend.append(fin)
            xpend = []
            stream(24 if mode == "state" else 32, r_inx, lambda c: inx_s[c], xbody, WB["inx"])
            while xpend:
                xpend.pop(0)()
            for r in range(2):
                for c8 in range(8):
                    c = r * 8 + c8
                    pe.op(lambda e, c=c, c8=c8: e.transpose(out=pT[:T, c8 * 128:(c8 + 1) * 128], in_=xsT[:, c, :T], identity=ident[:, :]),
                          reads=[xsT.b, ident.b], writes=[pT.b])
                hs = slice(r * 16, (r + 1) * 16)
                dve.op(lambda e, r=r, hs=hs: e.tensor_tensor(
                    out=xdt[:T, r * 1024:(r + 1) * 1024].rearrange("p (h d) -> p h d", d=64),
                    in0=pT[:T, :].rearrange("p (h d) -> p h d", d=64),
                    in1=sm[:T, 2, hs].unsqueeze(2).to_broadcast([T, 16, 64]), op=ALU.mult),
                    reads=[pT.b, sm.b], writes=[xdt.b])
                if mode != "state":
                  dve.op(lambda e, r=r, hs=hs: e.tensor_tensor(
                    out=ybuf[:T, r * 1024:(r + 1) * 1024].rearrange("p (h d) -> p h d", d=64),
                    in0=pT[:T, :].rearrange("p (h d) -> p h d", d=64),
                    in1=dbc[:T, hs].unsqueeze(2).to_broadcast([T, 16, 64]), op=ALU.mult),
                    reads=[pT.b, dbc.b], writes=[ybuf.b])
            for g in range(8):
                pe.op(lambda e, g=g: e.transpose(out=pT[:T, g * 128:(g + 1) * 128], in_=BT[:, g, :T], identity=ident[:, :]),
                      reads=[BT.b, ident.b], writes=[pT.b])
            dve.op(lambda e: e.tensor_copy(out=Btok[:T, :], in_=pT[:T, :]), reads=[pT.b], writes=[Btok.b])
            pe.op(lambda e: e.matmul(pS[:T, 32:64], lhsT=tri[:T, :T], rhs=sm[:T, 3, :], start=True, stop=True),
                  reads=[tri.b, sm.b], writes=[pS.b])
            pe.op(lambda e: e.matmul(pS[:, 64:96], lhsT=onesf[:T, :], rhs=sm[:T, 3, :], start=True, stop=True),
                  reads=[onesf.b, sm.b], writes=[pS.b])
            act.op(lambda e: e.activation(out=S_(4), in_=pS[:T, 32:64], func=AF.Exp), reads=[pS.b], writes=[sm.b])
            act.op(lambda e: e.activation(out=S_(5), in_=pS[:T, 32:64], func=AF.Copy), reads=[pS.b], writes=[sm.b])
            dve.op(lambda e: e.tensor_tensor(out=S_(6), in0=pS[:T, 64:96], in1=S_(5), op=ALU.subtract), reads=[pS.b, sm.b], writes=[sm.b])
            act.op(lambda e: e.activation(out=S_(6), in_=S_(6), func=AF.Exp), reads=[sm.b], writes=[sm.b])
            act.op(lambda e: e.activation(out=sm[:, 7, :], in_=pS[:, 64:96], func=AF.Exp), reads=[pS.b], writes=[sm.b])
            if False:
                dve.op(lambda e: e.tensor_tensor(out=S_(0), in0=pS[:T, 32:64], in1=carry[:T, :], op=ALU.add), reads=[pS.b, carry.b], writes=[sm.b])
                act.op(lambda e: e.activation(out=S_(0), in_=S_(0), func=AF.Exp), reads=[sm.b], writes=[sm.b])
                dve.op(lambda e: e.tensor_tensor(out=carry[:, :], in0=carry[:, :], in1=pS[:, 64:96], op=ALU.add), reads=[pS.b, carry.b], writes=[carry.b])
            if mode != "state":
                for g in range(8):
                    pe.op(lambda e, g=g: e.matmul(pB[:T, g * 128:g * 128 + T], lhsT=BT[:, g, :T], rhs=CT[:, g, :T], start=True, stop=True),
                          reads=[BT.b, CT.b], writes=[pBh[g // 4]])
                dve.op(lambda e: e.tensor_tensor(out=cbm[:T, :, :T], in0=pB[:T, :].rearrange("p (g q) -> p g q", q=128)[:, :, :T],
                                                 in1=tri[:T, :T].unsqueeze(1).to_broadcast([T, 8, T]), op=ALU.mult),
                       reads=[pBh[0], pBh[1], tri.b], writes=[cbm.b])
                MT = U1
                for r in range(4):
                    rs_ = rseg[r % 2]; ex_ = expL[r % 2]
                    pool.op(lambda e, r=r, rs_=rs_: e.tensor_tensor(
                        out=rs_[:T, :, :T], in0=sm[:T, 3, r * 8:(r + 1) * 8].unsqueeze(2).to_broadcast([T, 8, T]),
                        in1=trib[:T, :T].unsqueeze(1).to_broadcast([T, 8, T]), op=ALU.mult),
                        reads=[sm.b, trib.b], writes=[rs_.b])
                    hp = r % 2
                    for q2 in range(2):
                        pe.op(lambda e, rs_=rs_, q2=q2, hp=hp: e.matmul(
                            pA[:T, hp * 1024 + q2 * 512: hp * 1024 + q2 * 512 + 4 * T].rearrange("p (h q) -> p h q", q=T),
                            lhsT=mstr[:T, :T], rhs=rs_[:T, q2 * 4:(q2 + 1) * 4, :T], start=True, stop=True),
                            reads=[mstr.b, rs_.b], writes=[pAh[hp]])
                    for q2 in range(2):
                        act.op(lambda e, ex_=ex_, q2=q2, hp=hp: e.activation(
                            out=ex_[:T, q2 * 4:(q2 + 1) * 4, :T],
                            in_=pA[:T, hp * 1024 + q2 * 512: hp * 1024 + q2 * 512 + 4 * T].rearrange("p (h q) -> p h q", q=T),
                            func=AF.Exp), reads=[pAh[hp]], writes=[ex_.b])
                    for g2 in range(2):
                        gg = 2 * r + g2
                        dve.op(lambda e, ex_=ex_, g2=g2, gg=gg, r=r: e.tensor_tensor(
                            out=MT[:T, (r * 8 + g2 * 4) * 128:(r * 8 + g2 * 4 + 4) * 128].rearrange("p (h q) -> p h q", q=128)[:, :, :T],
                            in0=ex_[:T, g2 * 4:(g2 + 1) * 4, :T],
                            in1=cbm[:T, gg:gg + 1, :T].to_broadcast([T, 4, T]), op=ALU.mult),
                            reads=[ex_.b, cbm.b], writes=[MT.b])
                for g in range(8):
                    pe.op(lambda e, g=g: e.matmul(pA[:T, g * 256:(g + 1) * 256], lhsT=CT[:, g, :T], rhs=STb[:, g * 256:(g + 1) * 256],
                                                  start=True, stop=True), reads=[CT.b, STb.b], writes=[pAh[g // 4]])
                dve.op(lambda e: e.tensor_tensor(out=ytmp[:T, :].rearrange("p (h d) -> p h d", d=64),
                                                 in0=pA[:T, :].rearrange("p (h d) -> p h d", d=64),
                                                 in1=bc3(sm[:T, 4, :], T, 64), op=ALU.mult),
                       reads=[pAh[0], pAh[1], sm.b], writes=[ytmp.b])
                pool.op(lambda e: e.tensor_tensor(out=ybuf[:T, :], in0=ybuf[:T, :], in1=ytmp[:T, :], op=ALU.add),
                        reads=[ybuf.b, ytmp.b], writes=[ybuf.b])
                for h in range(NH):
                    pe.op(lambda e, h=h: e.matmul(pA[:T, h * 64:(h + 1) * 64], lhsT=MT[:T, h * 128:h * 128 + T], rhs=xdt[:T, h * 64:(h + 1) * 64],
                                                  start=True, stop=True), reads=[MT.b, xdt.b], writes=[pAh[h // 16]])
                dve.op(lambda e: e.tensor_tensor(out=ybuf[:T, :], in0=ybuf[:T, :], in1=pA[:T, :], op=ALU.add),
                       reads=[ybuf.b, pAh[0], pAh[1]], writes=[ybuf.b])
            pool.op(lambda e: e.tensor_tensor(out=xdtd[:T, :].rearrange("p (h d) -> p h d", d=64),
                                              in0=xdt[:T, :].rearrange("p (h d) -> p h d", d=64),
                                              in1=bc3(sm[:T, 6, :], T, 64), op=ALU.mult), reads=[xdt.b, sm.b], writes=[xdtd.b])
            for g in range(8):
                pe.op(lambda e, g=g: e.matmul(pA[:, g * 256:(g + 1) * 256], lhsT=Btok[:T, g * 128:(g + 1) * 128],
                                              rhs=xdtd[:T, g * 256:(g + 1) * 256], start=True, stop=True),
                      reads=[Btok.b, xdtd.b], writes=[pAh[g // 4]])
            pool.op(lambda e: e.tensor_tensor(out=ST[:, :].rearrange("p (h d) -> p h d", d=64),
                                              in0=ST[:, :].rearrange("p (h d) -> p h d", d=64),
                                              in1=sm[:, 7, :].unsqueeze(2).to_broadcast([128, NH, 64]), op=ALU.mult),
                    reads=[ST.b, sm.b], writes=[ST.b])
            dve.op(lambda e: e.tensor_tensor(out=ST[:, :], in0=ST[:, :], in1=pA[:, :], op=ALU.add),
                   reads=[ST.b, pAh[0], pAh[1]], writes=[ST.b])
            act.op(lambda e: e.activation(out=STb[:, :], in_=ST[:, :], func=AF.Copy), reads=[ST.b], writes=[STb.b])
            if False:
                xc.store(x3s[blk], xc[:, :], writes=[scr])
                sz.store(szs[blk], sz[:, :], writes=[scr])
                ybuf.store(yss[blk], ybuf[:, :], writes=[scr])
                CT.store(cts[blk], CT[:, :, :], writes=[scr])
                sm.store(ess[blk], sm[:, 0, :], writes=[scr])

        def block_p2(s, t0, T, mode, blk, yrow):
            xa, xb, xc = xtok
            if False:
                xc.load(xc[:, :], x3s[blk], reads=[scr])
                sz.load(sz[:, :], szs[blk], reads=[scr])
                ybuf.load(ybuf[:, :], yss[blk], reads=[scr])
                CT.load(CT[:, :, :], cts[blk], reads=[scr])
                sm.load(sm[:, 0, :], ess[blk], reads=[scr])
                for g in range(8):
                    pe.op(lambda e, g=g: e.matmul(pA[:T, g * 256:(g + 1) * 256], lhsT=CT[:, g, :T], rhs=STb[:, g * 256:(g + 1) * 256],
                                                  start=True, stop=True), reads=[CT.b, STb.b], writes=[pAh[g // 4]])
                dve.op(lambda e: e.tensor_tensor(out=ytmp[:T, :].rearrange("p (h d) -> p h d", d=64),
                                                 in0=pA[:T, :].rearrange("p (h d) -> p h d", d=64),
                                                 in1=bc3(sm[:T, 0, :], T, 64), op=ALU.mult),
                       reads=[pAh[0], pAh[1], sm.b], writes=[ytmp.b])
                pool.op(lambda e: e.tensor_tensor(out=ybuf[:T, :], in0=ybuf[:T, :], in1=ytmp[:T, :], op=ALU.add),
                        reads=[ybuf.b, ytmp.b], writes=[ybuf.b])
            pool.op(lambda e: e.tensor_tensor(out=ybuf[:T, :], in0=ybuf[:T, :], in1=sz[:T, :], op=ALU.mult),
                    reads=[ybuf.b, sz.b], writes=[ybuf.b])
            for g in range(8):
                act.op(lambda e, g=g: e.activation(out=ytmp[:T, g * 256:(g + 1) * 256], in_=ybuf[:T, g * 256:(g + 1) * 256], func=AF.Square,
                                                   accum_out=ss[:T, 0, g:g + 1]), reads=[ybuf.b], writes=[ytmp.b, ss.b])
            dve.op(lambda e: e.tensor_scalar(out=ss[:T, 1, :], in0=ss[:T, 0, :], scalar1=1.0 / 256, scalar2=RMS_EPS_,
                                             op0=ALU.mult, op1=ALU.add), reads=[ss.b], writes=[ss.b])
            act.op(lambda e: e.activation(out=ss[:T, 2, :], in_=ss[:T, 1, :], func=AF.Sqrt), reads=[ss.b], writes=[ss.b])
            dve.op(lambda e: e.reciprocal(out=ss[:T, 3, :], in_=ss[:T, 2, :]), reads=[ss.b], writes=[ss.b])
            dve.op(lambda e: e.tensor_tensor(out=ynb[:T, :].rearrange("p (g d) -> p g d", d=256),
                                             in0=ybuf[:T, :].rearrange("p (g d) -> p g d", d=256),
                                             in1=ss[:T, 3, :].unsqueeze(2).to_broadcast([T, 8, 256]), op=ALU.mult),
                   reads=[ybuf.b, ss.b], writes=[ynb.b])
            for r in range(2):
                for c8 in range(8):
                    c = r * 8 + c8
                    pe.op(lambda e, c=c, c8=c8: e.transpose(out=pT[:, c8 * 128:c8 * 128 + T], in_=ynb[:T, c * 128:(c + 1) * 128],
                                                            identity=ident[:T, :T]), reads=[ynb.b, ident.b], writes=[pT.b])
                for c8 in range(8):
                    c = r * 8 + c8
                    act.op(lambda e, c=c, c8=c8: e.activation(out=ynT[:, c, :T], in_=pT[:, c8 * 128:c8 * 128 + T], func=AF.Identity,
                                                              scale=normw[:, c:c + 1]), reads=[pT.b, normw.b], writes=[ynT.b])

            def obody(k, w):
                for h in range(2):
                    pe.op(lambda e, h=h: e.matmul(pA[:T, h * 512:(h + 1) * 512], lhsT=ynT[:, k, :T], rhs=w[:, h * 512:(h + 1) * 512],
                                                  start=(k == 0), stop=(k == 15)), reads=[ynT.b, w.b], writes=[pAh[0]])
            stream(16, r_out, lambda k: outp_s[k], obody, WB["outp"])
            residual(T, xc, pAh[0], xb)
            layer_norm(T, xb, 2, xa)
            ffn(T, 1)
            residual(T, xa, pAh[0], xb)
            layer_norm(T, xb, 3, xc)
            ple(T, 1, s, t0, xc, xa)
            if mode != "halo":
                sp.dma(y_d[s][yrow:yrow + T, :], xa[:T, :], oslot, reads=[xa.b])

        def run_sample():
            s = 1
            ftail.load(ftail[:], ftail_d[s][:, :, :, :, :])
            ctail.load(ctail[:], ctail_d[s][:, :, :])
            ST.load(ST[:], st_d[s][:, :])
            invc.load(invc[:], invc_d[s][:, :, :])
            act.op(lambda e: e.activation(out=STb[:, :], in_=ST[:, :], func=AF.Copy), reads=[ST.b], writes=[STb.b])
            block_p1(s, 0, LS, True, "sample", 0)
            block_p2(s, 0, LS, "sample", 0, 0)
            sp.dma(ftail_o[s][:, :, :, :, :], ftail[:], oslot, reads=[ftail.b])
            sp.dma(ctail_o[s][:, :, :], ctail[:], oslot, reads=[ctail.b])
            sp.dma(st_o[s][:, :], ST[:], oslot, reads=[ST.b])
            sp.dma(pool_o[s][:, :], x_d[s][LS - 15:LS, :], oslot)

        def mask_all(b):
            for ap_, tb in ((ftail[:].rearrange("p a b c d -> p (a b c d)"), ftail), (ctail[:].rearrange("p a b -> p (a b)"), ctail)):
                dve.op(lambda e, ap_=ap_: e.tensor_scalar_mul(out=ap_, in0=ap_, scalar1=vfl[:, b:b + 1]),
                       reads=[tb.b, vfl.b], writes=[tb.b])

        def run_prompt():
            s = 0
            ftail.load(ftail[:], ftail_d[s][:, :, :, :, :])
            ctail.load(ctail[:], ctail_d[s][:, :, :])
            ST.load(ST[:], st_d[s][:, :])
            act.op(lambda e: e.activation(out=STb[:, :], in_=ST[:, :], func=AF.Copy), reads=[ST.b], writes=[STb.b])
            firsts = {NST + 1 - k * NOWN: k for k in range(4)}
            for b in range(NBLK):
                mode = "state" if b < NST else ("halo" if b == NST else "own")
                block_p1(s, b * 128, 128, b in firsts, mode, b, vcol=(b if mode != "own" else None),
                         invtab=invt[firsts[b]] if b in firsts else None)
                if mode != "state":
                    block_p2(s, b * 128, 128, mode, b, (b - NST - 1) * 128)
                if (b + 1) in firsts:
                    mask_all(b)
            sp.dma(ftail_o[s][:, :, :, :, :], ftail[:], oslot, reads=[ftail.b])
            sp.dma(ctail_o[s][:, :, :], ctail[:], oslot, reads=[ctail.b])
            sp.dma(st_o[s][:, :], ST[:], oslot, reads=[ST.b])
            sp.dma(pool_o[s][:, :], x_d[s][LP - 15:LP, :], oslot)

        ccslot = P.slot()
        run_sample()
        run_prompt()
        fin = Buf("fin")
        fin.rd = {oslot.key: oslot.n}
        P.finish([fin, xtok[0], ftail, ctail, ST] if False else [fin])
    return nc


def _fm(a, nchunk):
    r, c = a.shape
    return np.ascontiguousarray(a.reshape(r, nchunk, 128).transpose(2, 1, 0))


def _prep_inputs(x_prompt, x_sample, p_prompt, p_sample, cache_pool, cache_ssm_conv, state_ssm, cache_ffn_conv,
                 pool_w, pool_scale, ssm_in_proj, ssm_conv_w, ssm_conv_b, ssm_dt_bias, ssm_A_log, ssm_D, ssm_norm_w,
                 ssm_out_proj, ln_mix_g, ln_mix_b, ffn_up, ffn_conv_w, ffn_conv_b, ffn_down, ln_ffn_g, ln_ffn_b,
                 ple_proj, ple_gate_w, ple_gate_b, n_cores=8):
    f = lambda a: np.ascontiguousarray(np.asarray(a, dtype=np.float32))
    L = x_prompt.shape[1]
    LS = x_sample.shape[1]
    shared = {
        "pool_w": f(pool_w[0]), "pool_scale": f(pool_scale), "in_proj": f(ssm_in_proj[0]),
        "scw": f(np.asarray(ssm_conv_w[0]).reshape(4, 32, 128).transpose(2, 1, 0)),
        "scb": f(np.asarray(ssm_conv_b[0]).reshape(32, 128).T),
        "dtb": f(ssm_dt_bias), "alog": f(ssm_A_log), "dsk": f(ssm_D),
        "normw": f(np.asarray(ssm_norm_w[0]).reshape(16, 128).T),
        "out_proj": f(ssm_out_proj[0]),
        "lng": f(np.stack([ln_mix_g[0], ln_ffn_g[0], ln_mix_g[1], ln_ffn_g[1]])),
        "lnb": f(np.stack([ln_mix_b[0], ln_ffn_b[0], ln_mix_b[1], ln_ffn_b[1]])),
        "lngf": f(np.stack([ln_mix_g[0], ln_ffn_g[0], ln_mix_g[1], ln_ffn_g[1]]).reshape(4, 8, 128).transpose(2, 0, 1)),
        "lnbf": f(np.stack([ln_mix_b[0], ln_ffn_b[0], ln_mix_b[1], ln_ffn_b[1]]).reshape(4, 8, 128).transpose(2, 0, 1)),
        "fcw": f(np.asarray(ffn_conv_w).reshape(2, 3, 2, 22, 128).transpose(4, 0, 2, 3, 1)),
        "fcb": f(np.asarray(ffn_conv_b).reshape(2, 2, 22, 128).transpose(3, 0, 1, 2)),
        "gate_b": f(ple_gate_b),
    }
    for i in range(2):
        shared["ffn_up%d" % i] = f(ffn_up[i])
        shared["ffn_down%d" % i] = f(ffn_down[i])
        shared["ple_proj%d" % i] = f(ple_proj[i])
        shared["gate_w%d" % i] = f(ple_gate_w[i])
    wins = (2, 4, 8, 16)
    invc_p = np.zeros((128, 8, 16), np.float32)
    invc_s = np.zeros((128, 8, 16), np.float32)
    for c in range(8):
        w = wins[c // 2]
        invc_p[:, c, :] = 1.0 / np.minimum(w, np.arange(16) + 1)
        invc_s[:, c, :] = 1.0 / w
    in_maps = []
    nseq = x_prompt.shape[0]
    nseg = n_cores // nseq
    Lo = L // nseg
    NOWN = Lo // 128
    NST = 3 * NOWN
    NBLK = NST + 1 + NOWN
    LP = NBLK * 128

    def ext(a, start, n):
        out = np.zeros((n,) + a.shape[1:], np.float32)
        lo = max(start, 0)
        if start + n > lo:
            out[lo - start:] = a[lo:start + n]
        return out

    for core in range(n_cores):
        bp = core // nseg
        seg = core % nseg
        bs = core % x_sample.shape[0]
        start = (seg * NOWN - NST - 1) * 128
        nv = NST + 1 - seg * NOWN
        m = dict(shared)
        xp = f(x_prompt[bp])
        m["x0"] = ext(xp, start, LP)
        m["xT0"] = _fm(ext(xp, start - 15, LP + 15), 8)
        xs = f(x_sample[bs])
        m["x1"] = xs
        m["xT1"] = _fm(np.concatenate([f(cache_pool[0, bs]), xs], 0), 8)
        for i in range(2):
            m["pT0_%d" % i] = _fm(ext(f(p_prompt[i, bp]), start, LP), 2)
            m["pT1_%d" % i] = _fm(f(p_sample[i, bs]), 2)
        m["ftail0"] = np.zeros((128, 2, 2, 22, 2), np.float32)
        m["ftail1"] = f(np.asarray(cache_ffn_conv[:, bs]).reshape(2, 2, 2, 22, 128).transpose(4, 0, 2, 3, 1))
        m["ctail0"] = np.zeros((128, 32, 3), np.float32)
        m["ctail1"] = f(np.asarray(cache_ssm_conv[0, bs]).reshape(3, 32, 128).transpose(2, 1, 0))
        m["st0"] = np.zeros((128, 2048), np.float32)
        m["st1"] = f(np.asarray(state_ssm[0, bs]).reshape(2048, 128).T)
        m["invc0"] = np.stack([invc_p if k == seg else invc_s for k in range(4)])
        m["invc1"] = invc_s
        vfl = np.ones((128, NBLK), np.float32)
        vfl[:, :nv] = 0.0
        m["vfl"] = vfl
        in_maps.append(m)
    return in_maps, Lo, LS


def _unft(a):
    return np.ascontiguousarray(a.transpose(1, 4, 2, 3, 0).reshape(2, 2, 5632))


def _assemble(results, nb_p, nb_s):
    nseg = len(results) // nb_p
    last = [b * nseg + nseg - 1 for b in range(nb_p)]
    y_p = np.stack([np.concatenate([results[b * nseg + g]["y0"] for g in range(nseg)], 0) for b in range(nb_p)])
    y_s = np.stack([results[b]["y1"] for b in range(nb_s)])
    pool_p = np.stack([results[c]["pool_o0"] for c in last])[None]
    pool_s = np.stack([results[b]["pool_o1"] for b in range(nb_s)])[None]
    unc = lambda a: np.ascontiguousarray(a.transpose(2, 1, 0).reshape(3, 4096))
    sconv_p = np.stack([unc(results[c]["ctail_o0"]) for c in last])[None]
    sconv_s = np.stack([unc(results[b]["ctail_o1"]) for b in range(nb_s)])[None]
    uns = lambda a: np.ascontiguousarray(a.T.reshape(32, 64, 128))
    st_p = np.stack([uns(results[c]["st_o0"]) for c in last])[None]
    st_s = np.stack([uns(results[b]["st_o1"]) for b in range(nb_s)])[None]
    ffn_p = np.stack([_unft(results[c]["ftail_o0"]) for c in last], 1)
    ffn_s = np.stack([_unft(results[b]["ftail_o1"]) for b in range(nb_s)], 1)
    return (y_p, y_s, pool_p, pool_s, sconv_p, sconv_s, st_p, st_s, ffn_p, ffn_s)


_NC_CACHE = {}


def kernel(**inputs):
    in_maps, L, LS = _prep_inputs(**inputs)
    key = (L, LS)
    if key not in _NC_CACHE:
        _NC_CACHE[key] = build_program(L, LS)
    nc = _NC_CACHE[key]
    res = run_bass_kernel_spmd(nc, in_maps, core_ids=list(range(8)))
    outs = _assemble(res.results, inputs["x_prompt"].shape[0], inputs["x_sample"].shape[0])
    return tuple(np.asarray(o, dtype=np.float32) for o in outs)
```
